# Optimizing a Trainium2 kernel written in Bass

```python
import jax, jax.numpy as jnp
from jax import lax
import numpy as np

D_MODEL = 1024
BATCH = 16
SEQ = 2048
DEPTH = 2

GRID_W = 64
CTX_LEN = 256
HEAD_DIM = 64
N_MIXERS = 4
GROUP_WIDTH = D_MODEL // N_MIXERS
GROUP_HEADS = GROUP_WIDTH // HEAD_DIM
WIN_ROWS = 8
WIN_COLS = 16
COL_QBLOCK = 16
COL_KBLOCK = WIN_COLS + COL_QBLOCK
GMLP_CHUNK = 128
MLSTM_CHUNK = 128
CONV_W = 3
FNET_GROUPS = 4
ROPE_THETA = 10000.0
D_FF = 4 * D_MODEL
N_MOD = 6
EPS = 1e-6
NEG_INF = -1e30
N_GATES = 4 * GROUP_HEADS
OFF_A = 0
OFF_B = 3 * GROUP_WIDTH
OFF_C = 5 * GROUP_WIDTH
OFF_D = 9 * GROUP_WIDTH
OFF_G = 10 * GROUP_WIDTH
D_IN = OFF_G + N_GATES

kernel_name = 'hybrid_nat_gmlp_mlstm_fnet_dit_block'


def rms_norm(x, g):
    x32 = x.astype(jnp.float32)
    y = x32 * lax.rsqrt(jnp.mean(x32 * x32, axis=-1, keepdims=True) + EPS)
    return (y * g.astype(jnp.float32)).astype(x.dtype)


def split_heads(x):
    b, t, _ = x.shape
    return x.reshape(b, t, GROUP_HEADS, HEAD_DIM).transpose(0, 2, 1, 3)


def merge_heads(x):
    b, h, t, d = x.shape
    return x.transpose(0, 2, 1, 3).reshape(b, t, h * d)


def sq_relu_mlp(h, w1, w2):
    return jnp.square(jax.nn.relu(h @ w1)) @ w2


def dense_attention(q, k, v):
    s = jnp.einsum('bhqd,bhkd->bhqk', q, k).astype(jnp.float32) * HEAD_DIM ** -0.5
    p = jax.nn.softmax(s, axis=-1).astype(v.dtype)
    return jnp.einsum('bhqk,bhkd->bhqd', p, v)


def neighbourhood_attention(q, k, v, k_ctx, v_ctx, rpb):
    b, h, s, d = q.shape
    rows = s // GRID_W
    wr = min(WIN_ROWS, rows)
    ncb = GRID_W // COL_QBLOCK
    nk = wr * COL_KBLOCK
    r = jnp.arange(rows)
    row_idx = jnp.clip(r - wr // 2, 0, rows - wr)[:, None] + jnp.arange(wr)[None, :]
    c0 = jnp.arange(ncb) * COL_QBLOCK
    col_idx = jnp.clip(c0 - WIN_COLS // 2, 0, GRID_W - COL_KBLOCK)[:, None] + jnp.arange(COL_KBLOCK)[None, :]
    qcol = c0[:, None] + jnp.arange(COL_QBLOCK)[None, :]
    qstart = jnp.clip(qcol - WIN_COLS // 2, 0, GRID_W - WIN_COLS)
    in_win = (col_idx[:, None, :] >= qstart[:, :, None]) & (col_idx[:, None, :] < qstart[:, :, None] + WIN_COLS)
    d_row = row_idx[:, None, None, :, None] - r[:, None, None, None, None]
    d_col = col_idx[None, :, None, None, :] - qcol[None, :, :, None, None]
    bias = rpb[:, d_row + WIN_ROWS - 1, jnp.clip(d_col + WIN_COLS - 1, 0, 2 * WIN_COLS - 2)].astype(jnp.float32)
    bias = jnp.where(in_win[None, None, :, :, None, :], bias, NEG_INF).reshape(h, rows, ncb, COL_QBLOCK, nk)
    qb = q.reshape(b, h, rows, ncb, COL_QBLOCK, d)
    ri = row_idx[:, None, :, None]
    ci = col_idx[None, :, None, :]
    kb = k.reshape(b, h, rows, GRID_W, d)[:, :, ri, ci].reshape(b, h, rows, ncb, nk, d)
    vb = v.reshape(b, h, rows, GRID_W, d)[:, :, ri, ci].reshape(b, h, rows, ncb, nk, d)
    scale = HEAD_DIM ** -0.5
    s_loc = jnp.einsum('bhrcqd,bhrckd->bhrcqk', qb, kb).astype(jnp.float32) * scale + bias
    s_ctx = jnp.einsum('bhrcqd,bhkd->bhrcqk', qb, k_ctx).astype(jnp.float32) * scale
    p = jax.nn.softmax(jnp.concatenate([s_loc, s_ctx], axis=-1), axis=-1).astype(v.dtype)
    out = (jnp.einsum('bhrcqk,bhrckd->bhrcqd', p[..., :nk], vb)
           + jnp.einsum('bhrcqk,bhkd->bhrcqd', p[..., nk:], v_ctx))
    return out.reshape(b, h, s, d)


def spatial_gating(p_uz, w_s, b_s, g_z):
    uz = jax.nn.gelu(p_uz)
    u, z = uz[..., :GROUP_WIDTH], uz[..., GROUP_WIDTH:]
    z = rms_norm(z, g_z)
    b, t, _ = z.shape
    zc = z.reshape(b, t // GMLP_CHUNK, GMLP_CHUNK, GROUP_HEADS, HEAD_DIM)
    mixed = jnp.einsum('hpq,bnqhd->bnphd', w_s, zc) + b_s.T[:, :, None]
    return u * mixed.reshape(b, t, GROUP_WIDTH)


def short_conv(x, w):
    pad = CONV_W // 2
    return lax.conv_general_dilated(x, w[:, None, :].astype(x.dtype), (1,), [(pad, pad)],
                                    dimension_numbers=('NWC', 'WIO', 'NWC'),
                                    feature_group_count=x.shape[-1])


def rope_axis(x, pos):
    m = x.shape[-1] // 2
    inv = ROPE_THETA ** (-jnp.arange(m, dtype=jnp.float32) / m)
    ang = pos.astype(jnp.float32)[:, None] * inv[None, :]
    cos, sin = jnp.cos(ang), jnp.sin(ang)
    x1, x2 = x[..., :m].astype(jnp.float32), x[..., m:].astype(jnp.float32)
    return jnp.concatenate([x1 * cos - x2 * sin, x1 * sin + x2 * cos], axis=-1).astype(x.dtype)


def rope_2d(x, rows, cols):
    half = x.shape[-1] // 2
    return jnp.concatenate([rope_axis(x[..., :half], rows), rope_axis(x[..., half:], cols)], axis=-1)


def mlstm_scan(q, k, v, ig, lf, state, emit):
    b, h, t, d = q.shape
    n_chunks = t // MLSTM_CHUNK

    def to_chunks(a):
        a = a.astype(jnp.float32)
        return jnp.moveaxis(a.reshape(a.shape[:2] + (n_chunks, MLSTM_CHUNK) + a.shape[3:]), 2, 0)

    lower = jnp.tril(jnp.ones((MLSTM_CHUNK, MLSTM_CHUNK), dtype=bool))

    def step(carry, inp):
        c_mem, n_mem, m_mem = carry
        qc, kc, vc, ic, fc = inp
        bcum = jnp.cumsum(fc, axis=-1)
        b_end = bcum[..., -1]
        log_src = b_end[..., None] - bcum + ic
        m_new = jnp.maximum(b_end + m_mem, jnp.max(log_src, axis=-1))
        w_src = jnp.exp(log_src - m_new[..., None])
        decay = jnp.exp(b_end + m_mem - m_new)
        c_new = decay[..., None, None] * c_mem + jnp.einsum('bhs,bhsd,bhse->bhde', w_src, kc, vc)
        n_new = decay[..., None] * n_mem + jnp.einsum('bhs,bhsd->bhd', w_src, kc)
        if not emit:
            return (c_new, n_new, m_new), None
        log_w = jnp.where(lower, bcum[..., :, None] - bcum[..., None, :] + ic[..., None, :], -jnp.inf)
        log_inter = bcum + m_mem[..., None]
        m_t = jnp.maximum(jnp.max(log_w, axis=-1), log_inter)
        w_intra = jnp.einsum('bhtd,bhsd->bhts', qc, kc) * jnp.exp(log_w - m_t[..., None])
        w_inter = jnp.exp(log_inter - m_t)
        num = (jnp.einsum('bhts,bhsd->bhtd', w_intra, vc)
               + w_inter[..., None] * jnp.einsum('bhtd,bhde->bhte', qc, c_mem))
        den = jnp.sum(w_intra, axis=-1) + w_inter * jnp.einsum('bhtd,bhd->bht', qc, n_mem)
        h_out = num / jnp.maximum(jnp.abs(den), jnp.exp(-m_t))[..., None]
        return (c_new, n_new, m_new), h_out

    xs = (to_chunks(q), to_chunks(k), to_chunks(v), to_chunks(ig), to_chunks(lf))
    state, hs = lax.scan(step, state, xs)
    if not emit:
        return None, state
    return jnp.moveaxis(hs, 0, 2).reshape(b, h, t, d).astype(q.dtype), state


def mlstm_prep(p, w_conv, b_gate, pos):
    gw = GROUP_WIDTH
    qk = jax.nn.silu(short_conv(p[..., OFF_C:OFF_C + 2 * gw], w_conv))
    q = split_heads(qk[..., :gw])
    k = split_heads(qk[..., gw:])
    if pos is not None:
        q = rope_2d(q, pos[0], pos[1])
        k = rope_2d(k, pos[0], pos[1])
    k = k * HEAD_DIM ** -0.5
    v = split_heads(p[..., OFF_C + 2 * gw:OFF_C + 3 * gw])
    g = (p[..., OFF_G:OFF_G + N_GATES].astype(jnp.float32) + b_gate.astype(jnp.float32)).transpose(0, 2, 1)
    return q, k, v, g


def mlstm_bidirectional(ctx_in, lat_in, emit_ctx):
    qc, kc, vc, gc = ctx_in
    ql, kl, vl, gl = lat_in
    b = ql.shape[0]
    zero = (jnp.zeros((b, GROUP_HEADS, HEAD_DIM, HEAD_DIM), jnp.float32),
            jnp.zeros((b, GROUP_HEADS, HEAD_DIM), jnp.float32),
            jnp.zeros((b, GROUP_HEADS), jnp.float32))

    def run(rev):
        o = 2 * GROUP_HEADS * int(rev)
        flip = (lambda a: jnp.flip(a, axis=2)) if rev else (lambda a: a)
        ig_c, lf_c = gc[:, o:o + GROUP_HEADS], jax.nn.log_sigmoid(gc[:, o + GROUP_HEADS:o + 2 * GROUP_HEADS])
        ig_l, lf_l = gl[:, o:o + GROUP_HEADS], jax.nn.log_sigmoid(gl[:, o + GROUP_HEADS:o + 2 * GROUP_HEADS])
        h_c, st = mlstm_scan(flip(qc), flip(kc), flip(vc), flip(ig_c), flip(lf_c), zero, emit_ctx)
        h_l, _ = mlstm_scan(flip(ql), flip(kl), flip(vl), flip(ig_l), flip(lf_l), st, True)
        return (flip(h_c) if emit_ctx else None), flip(h_l)

    hc_f, hl_f = run(False)
    hc_b, hl_b = run(True)
    h_ctx = hc_f + hc_b if emit_ctx else None
    return h_ctx, hl_f + hl_b


def head_layer_norm(x, g):
    b, t, _ = x.shape
    xh = x.astype(jnp.float32).reshape(b, t, GROUP_HEADS, HEAD_DIM)
    mu = jnp.mean(xh, axis=-1, keepdims=True)
    var = jnp.mean(jnp.square(xh - mu), axis=-1, keepdims=True)
    y = ((xh - mu) * lax.rsqrt(var + EPS)).reshape(b, t, GROUP_WIDTH) * g.astype(jnp.float32)
    return y.astype(x.dtype)


def mlstm_output(h, p, g):
    o = jax.nn.sigmoid(p[..., OFF_C + 3 * GROUP_WIDTH:OFF_C + 4 * GROUP_WIDTH])
    return o * head_layer_norm(merge_heads(h), g)


def fourier_mix(f, w_fnet):
    b, t, _ = f.shape
    fg = f.astype(jnp.float32).reshape(b, t, FNET_GROUPS, GROUP_WIDTH // FNET_GROUPS)
    spec = jnp.fft.fft2(fg, axes=(1, 3), norm='ortho').real
    return spec.reshape(b, t, GROUP_WIDTH).astype(f.dtype) @ w_fnet


def token_mixers(h_lat, h_ctx, pos, w_in, b_gate, w_conv_qk, rpb, w_spatial, b_spatial,
                 g_gmlp, g_mlstm, w_fnet, w_out, with_ctx_out):
    gw = GROUP_WIDTH
    p_lat = h_lat @ w_in
    p_ctx = h_ctx @ w_in
    qa_l, ka_l, va_l = [split_heads(p_lat[..., OFF_A + i * gw:OFF_A + (i + 1) * gw]) for i in range(3)]
    qa_c, ka_c, va_c = [split_heads(p_ctx[..., OFF_A + i * gw:OFF_A + (i + 1) * gw]) for i in range(3)]
    a_lat = merge_heads(neighbourhood_attention(qa_l, ka_l, va_l, ka_c, va_c, rpb))
    b_lat = spatial_gating(p_lat[..., OFF_B:OFF_B + 2 * gw], w_spatial, b_spatial, g_gmlp)
    c_in_ctx = mlstm_prep(p_ctx, w_conv_qk, b_gate, None)
    c_in_lat = mlstm_prep(p_lat, w_conv_qk, b_gate, pos)
    hm_ctx, hm_lat = mlstm_bidirectional(c_in_ctx, c_in_lat, with_ctx_out)
    c_lat = mlstm_output(hm_lat, p_lat, g_mlstm)
    d_lat = fourier_mix(p_lat[..., OFF_D:OFF_D + gw], w_fnet)
    y_lat = jnp.concatenate([a_lat, b_lat, c_lat, d_lat], axis=-1) @ w_out
    if not with_ctx_out:
        return y_lat, None
    a_ctx = merge_heads(dense_attention(qa_c, ka_c, va_c))
    b_ctx = spatial_gating(p_ctx[..., OFF_B:OFF_B + 2 * gw], w_spatial, b_spatial, g_gmlp)
    c_ctx_out = mlstm_output(hm_ctx, p_ctx, g_mlstm)
    d_ctx = fourier_mix(p_ctx[..., OFF_D:OFF_D + gw], w_fnet)
    y_ctx = jnp.concatenate([a_ctx, b_ctx, c_ctx_out, d_ctx], axis=-1) @ w_out
    return y_lat, y_ctx


def setup_inputs(seed: int = 0) -> dict:
    key = jax.random.key(seed)
    ks = jax.random.split(key, 21)
    nrm = jax.random.normal
    gate_base = jnp.tile(jnp.concatenate([jnp.zeros((GROUP_HEADS,), jnp.float32),
                                          jnp.linspace(3.0, 6.0, GROUP_HEADS, dtype=jnp.float32)]), 2)
    return {
        'x': nrm(ks[0], (BATCH, SEQ, D_MODEL), jnp.float32),
        'c': nrm(ks[1], (BATCH, D_MODEL), jnp.float32),
        'ctx': nrm(ks[2], (BATCH, CTX_LEN, D_MODEL), jnp.float32),
        'c_ctx': nrm(ks[3], (D_MODEL,), jnp.float32),
        'w_ada': nrm(ks[4], (DEPTH, D_MODEL, N_MOD * D_MODEL), jnp.float32) * (0.5 * D_MODEL ** -0.5),
        'b_ada': 0.02 * nrm(ks[5], (DEPTH, N_MOD * D_MODEL), jnp.float32),
        'g_norm_mix': 1.0 + 0.05 * nrm(ks[6], (DEPTH, D_MODEL), jnp.float32),
        'g_norm_ffn': 1.0 + 0.05 * nrm(ks[7], (DEPTH, D_MODEL), jnp.float32),
        'w_in': nrm(ks[8], (DEPTH, D_MODEL, D_IN), jnp.float32) * D_MODEL ** -0.5,
        'b_gate': gate_base[None, :] + 0.1 * nrm(ks[9], (DEPTH, N_GATES), jnp.float32),
        'w_conv_qk': nrm(ks[10], (DEPTH, CONV_W, 2 * GROUP_WIDTH), jnp.float32) * CONV_W ** -0.5,
        'rpb': 0.2 * nrm(ks[11], (DEPTH, GROUP_HEADS, 2 * WIN_ROWS - 1, 2 * WIN_COLS - 1), jnp.float32),
        'w_spatial': nrm(ks[12], (DEPTH, GROUP_HEADS, GMLP_CHUNK, GMLP_CHUNK), jnp.float32) * GMLP_CHUNK ** -0.5,
        'b_spatial': 1.0 + 0.1 * nrm(ks[13], (DEPTH, GROUP_HEADS, GMLP_CHUNK), jnp.float32),
        'g_gmlp': 1.0 + 0.05 * nrm(ks[14], (DEPTH, GROUP_WIDTH), jnp.float32),
        'g_mlstm': 1.0 + 0.05 * nrm(ks[15], (DEPTH, GROUP_WIDTH), jnp.float32),
        'w_fnet': nrm(ks[16], (DEPTH, GROUP_WIDTH, GROUP_WIDTH), jnp.float32) * GROUP_WIDTH ** -0.5,
        'w_out': nrm(ks[17], (DEPTH, D_MODEL, D_MODEL), jnp.float32) * D_MODEL ** -0.5,
        'w_ff1': nrm(ks[18], (DEPTH, D_MODEL, D_FF), jnp.float32) * D_MODEL ** -0.5,
        'w_ff2': nrm(ks[19], (DEPTH, D_FF, D_MODEL), jnp.float32) * D_FF ** -0.5,
        'g_final': 1.0 + 0.05 * nrm(ks[20], (D_MODEL,), jnp.float32),
    }


def reference(x, c, ctx, c_ctx, w_ada, b_ada, g_norm_mix, g_norm_ffn, w_in, b_gate, w_conv_qk, rpb,
              w_spatial, b_spatial, g_gmlp, g_mlstm, w_fnet, w_out, w_ff1, w_ff2, g_final):
    seq = x.shape[1]
    t = jnp.arange(seq)
    pos = ((t // GRID_W).astype(jnp.float32), (t % GRID_W).astype(jnp.float32))
    for l in range(DEPTH):
        last = l == DEPTH - 1
        mod_lat = (jax.nn.silu(c) @ w_ada[l] + b_ada[l]).reshape(c.shape[0], N_MOD, 1, D_MODEL)
        mod_ctx = (jax.nn.silu(c_ctx) @ w_ada[l] + b_ada[l]).reshape(N_MOD, D_MODEL)
        h_lat = rms_norm(x, g_norm_mix[l]) * (1.0 + mod_lat[:, 1]) + mod_lat[:, 0]
        h_ctx = rms_norm(ctx, g_norm_mix[l]) * (1.0 + mod_ctx[1]) + mod_ctx[0]
        y_lat, y_ctx = token_mixers(h_lat, h_ctx, pos, w_in[l], b_gate[l], w_conv_qk[l], rpb[l],
                                    w_spatial[l], b_spatial[l], g_gmlp[l], g_mlstm[l], w_fnet[l],
                                    w_out[l], not last)
        x = x + mod_lat[:, 2] * y_lat
        h_lat = rms_norm(x, g_norm_ffn[l]) * (1.0 + mod_lat[:, 4]) + mod_lat[:, 3]
        x = x + mod_lat[:, 5] * sq_relu_mlp(h_lat, w_ff1[l], w_ff2[l])
        if not last:
            ctx = ctx + mod_ctx[2] * y_ctx
            h_ctx = rms_norm(ctx, g_norm_ffn[l]) * (1.0 + mod_ctx[4]) + mod_ctx[3]
            ctx = ctx + mod_ctx[5] * sq_relu_mlp(h_ctx, w_ff1[l], w_ff2[l])
    return rms_norm(x, g_final)
```

```python
import contextlib
import numpy as np
import ml_dtypes
import concourse.bass as bass
import concourse.mybir as mybir
from concourse.bass_utils import run_bass_kernel_spmd

F32 = mybir.dt.float32
BF16 = mybir.dt.bfloat16
AF = mybir.ActivationFunctionType
ALU = mybir.AluOpType
AX = mybir.AxisListType

D = 1024
T = 2304
TL = 2048
NCH = 18
DIN = 2576
DEPTH = 2
EPS = 1e-6
NEG = -30000.0
import os
SKIP = set(os.environ.get('MK_SKIP', '').split(','))
CSTOP = int(os.environ.get('MK_CSTOP', '9'))
ASTOP = int(os.environ.get('MK_ASTOP', '9'))
OFF_A, OFF_B, OFF_C, OFF_D, OFF_G = 0, 768, 1280, 2304, 2560


class Sch:
    def __init__(self, nc, stack, ndma=8, same_engine_sync=True):
        self.nc = nc
        self.E = {'pe': nc.tensor, 'act': nc.scalar, 'dve': nc.vector,
                  'pool': nc.gpsimd, 'sp': nc.sync}
        self.R = ndma
        self.same = same_engine_sync
        self.csem = {}
        self.dsem = {}
        for e in self.E:
            self.csem[e] = stack.enter_context(nc.semaphore('c_' + e))
            self.dsem[e] = [stack.enter_context(nc.semaphore('d_%s_%d' % (e, i)))
                            for i in range(ndma)]
        self.cc = {e: 0 for e in self.E}
        self.dc = {e: 0 for e in self.E}
        self.waited = {e: {} for e in self.E}
        self.lw = {}
        self.rd = {}
        self.bar = set()
        self.nops = 0
        self.nwaits = 0

    def _tok_sem(self, tok):
        kind, e, i = tok
        if kind == 'c':
            return (kind, e, 0), self.csem[e], i
        return (kind, e, i % self.R), self.dsem[e][i % self.R], 16 * (i // self.R + 1)

    def _wait(self, eng, tok):
        kind, e, i = tok
        if kind == 'c' and e == eng and (eng == 'pe' or not self.same):
            return
        key, sem, val = self._tok_sem(tok)
        if self.waited[eng].get(key, 0) >= val:
            return
        self.waited[eng][key] = val
        self.E[eng].wait_ge(sem, val)
        self.nwaits += 1

    def op(self, eng, fn, reads=(), writes=(), dma=False):
        deps = set(self.bar)
        for r in reads:
            if r in self.lw:
                deps.add(self.lw[r])
        for w in writes:
            if w in self.lw:
                deps.add(self.lw[w])
            for t in self.rd.get(w, ()):
                deps.add(t)
        if dma:
            k = self.dc[eng]
            self.dc[eng] += 1
            tok = ('d', eng, k)
            if k >= self.R:
                deps.add(('d', eng, k - self.R))
        else:
            self.cc[eng] += 1
            tok = ('c', eng, self.cc[eng])
        best = {}
        for t in deps:
            key, sem, val = self._tok_sem(t)
            if key not in best or best[key][1] < val:
                best[key] = (t, val)
        for key in sorted(best, key=str):
            self._wait(eng, best[key][0])
        inst = fn(self.E[eng])
        _, sem, _ = self._tok_sem(tok)
        inst.then_inc(sem, 16 if dma else 1)
        for w in writes:
            self.lw[w] = tok
            self.rd[w] = []
        for r in reads:
            self.rd.setdefault(r, []).append(tok)
        self.nops += 1
        return tok

    def barrier(self):
        self.bar = set()
        for e in self.E:
            if self.cc[e] > 0:
                self.bar.add(('c', e, self.cc[e]))
            for j in range(max(0, self.dc[e] - self.R), self.dc[e]):
                self.bar.add(('d', e, j))
        self.lw = {}
        self.rd = {}

    def finish(self, eng='sp'):
        self.barrier()
        best = {}
        for t in self.bar:
            key, sem, val = self._tok_sem(t)
            if key not in best or best[key][1] < val:
                best[key] = (t, val)
        for key in sorted(best, key=str):
            self._wait(eng, best[key][0])


class Rot:
    def __init__(self, items):
        self.items = items
        self.i = 0

    def next(self):
        it = self.items[self.i % len(self.items)]
        self.i += 1
        return it


def _bf(a):
    return np.ascontiguousarray(a.astype(ml_dtypes.bfloat16))


def _consts():
    c = {}
    c['ident'] = np.eye(128, dtype=np.float32)
    c['jrev'] = np.ascontiguousarray(np.eye(128, dtype=np.float32)[::-1])
    s = np.arange(128)
    mf = (s[:, None] <= s[None, :]).astype(np.float32)
    mb = (s[:, None] >= s[None, :]).astype(np.float32)
    c['masks'] = _bf(np.stack([mf, mb], 1))
    t = np.arange(TL)
    rows = (t // 64).astype(np.float32)
    cols = (t % 64).astype(np.float32)
    p = np.arange(128)
    d = p % 64
    half = d // 32
    i = (d % 16).astype(np.float32)
    inv = (10000.0 ** (-i / 16.0)).astype(np.float32)
    pos = np.where(half[:, None] == 0, rows[None, :], cols[None, :]).astype(np.float32)
    ang = pos * inv[:, None]
    c['rope'] = _bf(np.stack([np.cos(ang), np.sin(ang)], 1))
    rm = np.zeros((128, 128), np.float32)
    for m in range(128):
        dd = m % 32
        if dd < 16:
            rm[m + 16, m] = -1.0
        else:
            rm[m - 16, m] = 1.0
    c['rm'] = _bf(rm)
    for n, name in ((2048, 'l'), (256, 'c')):
        k = np.arange(n, dtype=np.float64)
        ang = 2.0 * np.pi * np.outer(k, k) / n
        c['dftc_' + name] = _bf(np.cos(ang))
        c['dfts_' + name] = _bf(-np.sin(ang))
    k = np.arange(64, dtype=np.float64)
    ang = 2.0 * np.pi * np.outer(k, k) / 64
    bc = np.zeros((128, 128)); bs = np.zeros((128, 128))
    for g in range(2):
        bc[g * 64:(g + 1) * 64, g * 64:(g + 1) * 64] = np.cos(ang)
        bs[g * 64:(g + 1) * 64, g * 64:(g + 1) * 64] = np.sin(ang)
    c['blk'] = _bf(np.stack([bc, bs], 1))
    sel = np.zeros((4, 2, 128), np.float32)
    for pr in range(2):
        sel[2 * pr, pr, :64] = 1.0
        sel[2 * pr + 1, pr, 64:] = 1.0
    c['sel'] = sel
    return c


def _nat_bias(rpb_l):
    out = np.full((5, 2, 128, 5, 2, 128), NEG, np.float32)
    tsel = [0, 1, 2, 14, 15]
    kk = np.arange(128)
    qq = np.arange(128)
    for ti, t in enumerate(tsel):
        cb = min(max(t - 2, 0), 11)
        for j in range(5):
            kr = (cb + j) * 2 + kk // 64
            kc = kk % 64
            r = 2 * t + qq // 64
            qc = qq % 64
            rs = np.clip(r - 4, 0, 24)
            row_ok = (kr[:, None] >= rs[None, :]) & (kr[:, None] < rs[None, :] + 8)
            qs = np.clip(qc - 8, 0, 48)
            col_ok = (kc[:, None] >= qs[None, :]) & (kc[:, None] < qs[None, :] + 16)
            ok = row_ok & col_ok
            drow = np.clip(kr[:, None] - r[None, :] + 7, 0, 14)
            dcol = np.clip(kc[:, None] - qc[None, :] + 15, 0, 30)
            for h in range(4):
                val = rpb_l[h][drow, dcol]
                out[ti, h // 2, :, j, h % 2, :] = np.where(ok, val, NEG)
    return out


def _fm(vec):
    v = vec.reshape(vec.shape[:-1] + (8, 128))
    return np.ascontiguousarray(np.moveaxis(v, -1, 0))


def build_nc(dbg=None):
    nc = bass.Bass("TRN2", target_bir_lowering=False)

    def din(name, shape, dt=F32):
        return nc.dram_tensor(name, list(shape), dt, kind="ExternalInput").ap()

    xin = din("xin", [2, 128, 8, T])
    cT = din("cT", [128, 8, 3])
    w_ada = din("w_ada", [2, D, 6 * D])
    badaT = din("badaT", [128, 2, 48])
    gvec = din("gvec", [128, 5, 8])
    w_in = din("w_in", [2, D, DIN])
    bg = din("bg", [4, 2, 4])
    wconv = din("wconv", [128, 2, 4, 3])
    natb = din("natb", [2, 5, 2, 128, 1280])
    wsT = din("wsT", [2, 128, 4, 128])
    bsT = din("bsT", [2, 128, 2, 128])
    ggm = din("ggm", [2, 256])
    gml = din("gml", [2, 256])
    w_fnet = din("w_fnet", [2, 256, 256])
    w_out = din("w_out", [2, D, D])
    w_ff1 = din("w_ff1", [2, D, 4 * D])
    w_ff2 = din("w_ff2", [2, 4 * D, D])
    c_ident = din("ident", [128, 128])
    c_jrev = din("jrev", [128, 128])
    c_masks = din("masks", [128, 2, 128], BF16)
    c_rope = din("rope", [128, 2, TL], BF16)
    c_rm = din("rm", [128, 128], BF16)
    c_dftc_l = din("dftc_l", [TL, TL], BF16)
    c_dfts_l = din("dfts_l", [TL, TL], BF16)
    c_dftc_c = din("dftc_c", [256, 256], BF16)
    c_dfts_c = din("dfts_c", [256, 256], BF16)
    c_blk = din("blk", [128, 2, 128], BF16)
    c_sel = din("sel", [4, 2, 128])
    outT = nc.dram_tensor("outT", [2, 128, 8, TL], F32, kind="ExternalOutput").ap()
    dbg_out = None
    if dbg:
        dbg_out = nc.dram_tensor("dbg", [128, 8, T], F32, kind="ExternalOutput").ap()

    with contextlib.ExitStack() as st:
        S = Sch(nc, st)

        uid = [0]

        def sb(stack, name, shape, dt):
            uid[0] += 1
            return stack.enter_context(nc.sbuf_tensor("s%d_%s" % (uid[0], name), list(shape), dt))

        def ps(stack, name, shape, dt=F32):
            uid[0] += 1
            return stack.enter_context(nc.psum_tensor("p%d_%s" % (uid[0], name), list(shape), dt))

        def dma(eng, out, in_, reads=(), writes=()):
            S.op(eng, lambda e: e.dma_start(out=out, in_=in_), reads=reads, writes=writes, dma=True)

        def mm(out, lhsT, rhs, start, stop, reads, writes):
            S.op('pe', lambda e: e.matmul(out, lhsT=lhsT, rhs=rhs, start=start, stop=stop),
                 reads=reads, writes=writes)

        def act(out, in_, func, reads, writes, scale=1.0, bias=None):
            if bias is None:
                S.op('act', lambda e: e.activation(out=out, in_=in_, func=func, scale=scale),
                     reads=reads, writes=writes)
            else:
                S.op('act', lambda e: e.activation(out=out, in_=in_, func=func, scale=scale, bias=bias),
                     reads=reads, writes=writes)

        def tt(eng, out, in0, in1, op, reads, writes):
            S.op(eng, lambda e: e.tensor_tensor(out=out, in0=in0, in1=in1, op=op), reads=reads, writes=writes)

        def stt(out, in0, scalar, in1, op0, op1, reads, writes):
            S.op('dve', lambda e: e.scalar_tensor_tensor(out=out, in0=in0, scalar=scalar, in1=in1, op0=op0, op1=op1),
                 reads=reads, writes=writes)

        def ts(eng, out, in0, s1, s2, op0, op1, reads, writes):
            if s2 is None:
                S.op(eng, lambda e: e.tensor_scalar(out=out, in0=in0, scalar1=s1, scalar2=None, op0=op0),
                     reads=reads, writes=writes)
            else:
                S.op(eng, lambda e: e.tensor_scalar(out=out, in0=in0, scalar1=s1, scalar2=s2, op0=op0, op1=op1),
                     reads=reads, writes=writes)

        def cp(eng, out, in_, reads, writes):
            S.op(eng, lambda e: e.tensor_copy(out=out, in_=in_), reads=reads, writes=writes)

        X = sb(st, "X", [128, 8, T], F32)
        CAT = sb(st, "CAT", [128, 8, T], BF16)
        ident = sb(st, "ident", [128, 128], F32)
        jrev = sb(st, "jrev", [128, 128], F32)
        identb = sb(st, "identb", [128, 128], BF16)
        masks = sb(st, "masks", [128, 2, 128], BF16)
        rm = sb(st, "rm", [128, 128], BF16)
        blk = sb(st, "blk", [128, 2, 128], BF16)
        sel = sb(st, "sel", [4, 2, 128], F32)
        onesb = sb(st, "onesb", [128, 128], BF16)
        onesf = sb(st, "onesf", [128, 512], F32)
        modT = sb(st, "modT", [128, 2, 48, 3], F32)
        gmT = sb(st, "gmT", [128, 2, 2, 8, 3], F32)
        gv = sb(st, "gv", [128, 5, 8], F32)
        bada = sb(st, "bada", [128, 2, 48], F32)
        csb = sb(st, "csb", [128, 8, 3], F32)
        scb = sb(st, "scb", [128, 8, 3], BF16)
        bgs = sb(st, "bgs", [4, 2, 4], F32)
        wcv = sb(st, "wcv", [128, 2, 4, 3], F32)

        dma('sp', ident[:], c_ident, writes=['ident'])
        dma('sp', jrev[:], c_jrev, writes=['jrev'])
        dma('sp', masks[:], c_masks, writes=['masks'])
        dma('sp', rm[:], c_rm, writes=['rm'])
        dma('sp', blk[:], c_blk, writes=['blk'])
        dma('sp', sel[:], c_sel, writes=['sel'])
        dma('sp', gv[:], gvec, writes=['gv'])
        dma('sp', bada[:], badaT, writes=['bada'])
        dma('sp', csb[:], cT, writes=['csb'])
        dma('sp', bgs[:], bg, writes=['bgs'])
        dma('sp', wcv[:], wconv, writes=['wcv'])
        S.op('dve', lambda e: e.memset(onesb[:], 1.0), writes=['onesb'])
        S.op('dve', lambda e: e.memset(onesf[:], 1.0), writes=['onesf'])
        cp('dve', identb[:], ident[:], ['ident'], ['identb'])
        act(scb[:], csb[:], AF.Silu, ['csb'], ['scb'])

        with contextlib.ExitStack() as ph:
            wa = [sb(ph, "wa%d" % i, [128, 8, 512], BF16) for i in range(2)]
            mps = [ps(ph, "mps%d" % i, [128, 4, 3]) for i in range(2)]
            it = 0
            for l in range(DEPTH):
                for pc in range(12):
                    b = it % 2
                    it += 1
                    dma('pool', wa[b][:], w_ada[l, :, pc * 512:(pc + 1) * 512].rearrange("(kc p) n -> p kc n", p=128),
                        writes=['wa%d' % b])
                    for oc in range(4):
                        for kc in range(8):
                            mm(mps[b][:, oc, :], wa[b][:, kc, oc * 128:(oc + 1) * 128], scb[:, kc, :],
                               kc == 0, kc == 7, ['wa%d' % b, 'scb'], ['mps%d' % b])
                    tt('dve', modT[:, l, pc * 4:(pc + 1) * 4, :], mps[b][:],
                       bada[:, l, pc * 4:(pc + 1) * 4].unsqueeze(2).broadcast_to([128, 4, 3]), ALU.add,
                       ['mps%d' % b, 'bada'], ['modT'])
            for l in range(DEPTH):
                for kind in range(2):
                    sc_j = 1 if kind == 0 else 4
                    for v in range(3):
                        stt(gmT[:, l, kind, :, v], modT[:, l, sc_j * 8:(sc_j + 1) * 8, v], 1.0,
                            gv[:, 2 * l + kind, :], ALU.add, ALU.mult, ['modT', 'gv'], ['gmT'])
        S.barrier()

        def modap(l, j, c, v):
            return modT[:, l, j * 8 + c, v:v + 1]

        hbuf = [sb(st, "hT%d" % i, [128, 8, 256], BF16) for i in range(2)]
        hrot = Rot([("hT%d" % i, hbuf[i]) for i in range(2)])
        sqb = [sb(st, "sq%d" % i, [128, 512], BF16) for i in range(2)]
        sqrot = Rot([("sq%d" % i, sqb[i]) for i in range(2)])
        rsb = [sb(st, "rs%d" % i, [128, 512], F32) for i in range(2)]
        rsrot = Rot([("rs%d" % i, rsb[i]) for i in range(2)])
        tmb = [sb(st, "tm%d" % i, [128, 512], F32) for i in range(2)]
        tmrot = Rot([("tm%d" % i, tmb[i]) for i in range(2)])
        ssq = [ps(st, "ssq%d" % i, [128, 512]) for i in range(1)]
        ssqrot = Rot([("ssq%d" % i, ssq[i]) for i in range(1)])

        def make_h(l, kind, t0, n, v, dst=None):
            if dst is None:
                hk, hT = hrot.next()
            else:
                hk, hT = dst
            pk, pst = ssqrot.next()
            for c in range(8):
                sk, sq = sqrot.next()
                act(sq[:, :n], X[:, c, t0:t0 + n], AF.Square, ['X%d' % c], [sk])
                mm(pst[:, :n], onesb[:], sq[:, :n], c == 0, c == 7, [sk, 'onesb'], [pk])
            rk, rs = rsrot.next()
            act(rs[:, :n], pst[:, :n], AF.Sqrt, [pk, 'epsb'], [rk], scale=1.0 / D, bias=epsb[:, 0:1])
            S.op('dve', lambda e: e.reciprocal(out=rs[:, :n], in_=rs[:, :n]), reads=[rk], writes=[rk])
            sh_j = 0 if kind == 0 else 3
            for c in range(8):
                tk, tm = tmrot.next()
                stt(tm[:, :n], X[:, c, t0:t0 + n], gmT[:, l, kind, c, v:v + 1], rs[:, :n], ALU.mult, ALU.mult,
                    ['X%d' % c, 'gmT', rk], [tk])
                act(hT[:, c, :n], tm[:, :n], AF.Identity, [tk, 'modT'], [hk], bias=modap(l, sh_j, c, v))
            return hk, hT

        epsb = sb(st, "epsb", [128, 1], F32)
        S.op('dve', lambda e: e.memset(epsb[:], EPS), writes=['epsb'])

        def proj_fm(hT, hk, n, w, wk, col0, ppool, evac):
            pk, pt = ppool.next()
            for kc in range(8):
                mm(pt[:, :n], w[:, kc, col0:col0 + 128], hT[:, kc, :n], kc == 0, kc == 7, [wk, hk], [pk])
            evac(pt[:, :n], pk)

        def proj_tm(hT, hk, sub, w, wk, col0, ncol, ppool, evac):
            pk, pt = ppool.next()
            for kc in range(8):
                mm(pt[:, :ncol], hT[:, kc, sub * 128:(sub + 1) * 128], w[:, kc, col0:col0 + ncol],
                   kc == 0, kc == 7, [wk, hk], [pk])
            evac(pt[:, :ncol], pk)

        def wload(wt, key, l, col0, ncol):
            dma('pool', wt, w_in[l, :, col0:col0 + ncol].rearrange("(kc p) n -> p kc n", p=128), writes=[key])

        ORD = [[16, 17] + list(range(16)), [17, 16] + list(range(15, -1, -1))]

        def mixer_c(s, l, last):
            with contextlib.ExitStack() as ph:
                cf03 = CAT[:, 0:4, :].rearrange("p c t -> p (c t)")
                vaug = cf03[:, 0:4752].rearrange("p (j h d) -> p j h d", j=18, h=4)[:, :, :, 0:65]
                wqk = cf03[:, 4752:4752 + 4096].rearrange("p (k n) -> p k n", k=8)
                ktm = CAT[:, 6:8, :].rearrange("p c t -> p (c t)").rearrange("p (j f) -> p j f", f=256)
                R1 = sb(ph, "R1", [128, 2 * T], F32)
                R2 = sb(ph, "R2", [128, 4, T], BF16)
                RAW = R1[:].bitcast(BF16).rearrange("p (c t) -> p c t", c=4)
                QK = R2
                wv = sb(ph, "wv", [128, 8, 256], BF16)
                wg = sb(ph, "wg", [128, 8, 16], BF16)
                gtm = sb(ph, "gtm", [128, 18, 16], F32)
                gmb = sb(ph, "gmb", [128, 256], F32)
                dma('sp', gmb[:], gml[l, :].partition_broadcast(128), writes=['gmb'])
                wload(wqk, 'wqk', l, OFF_C, 512)
                wload(wv[:], 'wv', l, OFF_C + 512, 256)
                wload(wg[:], 'wg', l, OFF_G, 16)
                S.op('pool', lambda e: e.memset(vaug[:, :, :, 64:65], 1.0), writes=['vaug1'])
                with contextlib.ExitStack() as p1:
                    pq = [ps(p1, "pq%d" % i, [128, 512]) for i in range(2)]
                    pqr = Rot([("pq%d" % i, pq[i]) for i in range(2)])
                    pvv = [ps(p1, "pvv%d" % i, [128, 512]) for i in range(2)]
                    pvr = Rot([("pvv%d" % i, pvv[i]) for i in range(2)])
                    rope = sb(p1, "rope", [128, 2, TL], BF16)
                    dma('sp', rope[:], c_rope, writes=['rope'])
                    for b in range(9):
                        t0 = 256 * b
                        v = s if b < 8 else 2
                        hk, hT = make_h(l, 0, t0, 256, v)
                        for oc in range(4):
                            proj_fm(hT, hk, 256, wqk, 'wqk', oc * 128, pqr,
                                    lambda p_, pk, oc=oc: act(RAW[:, oc, t0:t0 + 256], p_, AF.Copy, [pk], ['RAW%d' % oc]))
                        for sub in range(2):
                            j = 2 * b + sub
                            proj_tm(hT, hk, sub, wv, 'wv', 0, 256, pvr,
                                    lambda p_, pk, j=j: cp('dve', vaug[:, j, :, 0:64], p_.rearrange("p (h d) -> p h d", h=4),
                                                           [pk], ['vaug%d' % j]))
                            proj_tm(hT, hk, sub, wg, 'wg', 0, 16, pvr,
                                    lambda p_, pk, j=j: cp('dve', gtm[:, j, :], p_, [pk], ['gtm']))
                    if CSTOP == 1:
                        return
                    cvt = [sb(p1, "cvt%d" % i, [128, 512], F32) for i in range(2)]
                    cvr = Rot([("cvt%d" % i, cvt[i]) for i in range(2)])
                    cst = [sb(p1, "cst%d" % i, [128, 512], BF16) for i in range(2)]
                    csr = Rot([("cst%d" % i, cst[i]) for i in range(2)])
                    r2t = [sb(p1, "r2t%d" % i, [128, 512], F32) for i in range(2)]
                    r2r = Rot([("r2t%d" % i, r2t[i]) for i in range(2)])
                    r3t = [sb(p1, "r3t%d" % i, [128, 512], F32) for i in range(2)]
                    r3r = Rot([("r3t%d" % i, r3t[i]) for i in range(2)])
                    for oc in range(4):
                        scl = 0.125 if oc >= 2 else 1.0
                        for (g0, g1) in ((0, TL), (TL, T)):
                            for t0 in range(g0, g1, 512):
                                n = min(512, g1 - t0)
                                ck, ct = cvr.next()
                                ts('dve', ct[:, :n], RAW[:, oc, t0:t0 + n], wcv[:, l, oc, 1:2], None, ALU.mult, None,
                                   ['RAW%d' % oc, 'wcv'], [ck])
                                a = 1 if t0 == g0 else 0
                                stt(ct[:, a:n], RAW[:, oc, t0 + a - 1:t0 + n - 1], wcv[:, l, oc, 0:1], ct[:, a:n],
                                    ALU.mult, ALU.add, ['RAW%d' % oc, 'wcv', ck], [ck])
                                bnd = n - 1 if t0 + n == g1 else n
                                stt(ct[:, 0:bnd], RAW[:, oc, t0 + 1:t0 + 1 + bnd], wcv[:, l, oc, 2:3], ct[:, 0:bnd],
                                    ALU.mult, ALU.add, ['RAW%d' % oc, 'wcv', ck], [ck])
                                if g0 == 0:
                                    sk, cs_ = csr.next()
                                    act(cs_[:, :n], ct[:, :n], AF.Silu, [ck], [sk])
                                    pk, pt = pqr.next()
                                    mm(pt[:, :n], rm[:], cs_[:, :n], True, True, ['rm', sk], [pk])
                                    k2, t2 = r2r.next()
                                    stt(t2[:, :n], pt[:, :n], scl, rope[:, 1, t0:t0 + n], ALU.mult, ALU.mult, [pk, 'rope'], [k2])
                                    k3, t3 = r3r.next()
                                    stt(t3[:, :n], cs_[:, :n], scl, rope[:, 0, t0:t0 + n], ALU.mult, ALU.mult, [sk, 'rope'], [k3])
                                    tt('pool', QK[:, oc, t0:t0 + n], t2[:, :n], t3[:, :n], ALU.add, [k2, k3], ['QK%d' % oc])
                                else:
                                    act(QK[:, oc, t0:t0 + n], ct[:, :n], AF.Silu, [ck], ['QK%d' % oc], scale=1.0)
                                    if scl != 1.0:
                                        ts('dve', QK[:, oc, t0:t0 + n], QK[:, oc, t0:t0 + n], scl, None, ALU.mult, None,
                                           ['QK%d' % oc], ['QK%d' % oc])
                    for j in range(18):
                        pk, pt = pqr.next()
                        for kc in range(2):
                            mm(pt[:, kc * 128:(kc + 1) * 128], QK[:, 2 + kc, 128 * j:128 * j + 128], identb[:], True, True,
                               ['QK%d' % (2 + kc), 'identb'], [pk])
                        cp('dve', ktm[:, j, :], pt[:, 0:256], [pk], ['ktm%d' % j])
                S.barrier()
                if CSTOP == 2:
                    return
                R3 = sb(ph, "R3", [128, 18 * 256], F32)
                hsum = R3[:].rearrange("p (j f) -> p j f", f=256)
                colq = [sb(ph, "colq%d" % i, [128, 18, 12], F32) for i in range(2)]
                dcol = [sb(ph, "dcol%d" % i, [128, 2, 18], F32) for i in range(2)]
                with contextlib.ExitStack() as p3:
                    R1f = R1
                    rI = R1f[0:4, 0:T]
                    rF = R1f[0:4, T:2 * T]
                    rG = R3[0:4, 0:T]
                    rA = R3[0:4, T:2 * T]
                    rrow = sb(p3, "rrow", [4, 20], F32)
                    drow = sb(p3, "drow", [4, 18], F32)
                    colraw = sb(p3, "colraw", [128, 18, 12], F32)
                    prw = [ps(p3, "prw%d" % i, [128, 512]) for i in range(2)]
                    pcol = ps(p3, "pcol", [128, 18, 12])
                    pcol2 = ps(p3, "pcol2", [128, 18, 12])
                    pd = ps(p3, "pd", [128, 2, 18])
                    for dr in range(2):
                        tr_m = ident if dr == 0 else jrev
                        trk = 'ident' if dr == 0 else 'jrev'
                        for g in range(5):
                            idxs = list(range(4 * g, min(4 * g + 4, 18)))
                            for qi, (dst, pw) in enumerate(((rI, prw[0]), (rF, prw[1]))):
                                for ii, idx in enumerate(idxs):
                                    j = ORD[dr][idx]
                                    c0 = dr * 8 + qi * 4
                                    mm(pw[0:4, ii * 128:(ii + 1) * 128], gtm[:, j, c0:c0 + 4], tr_m[:], True, True,
                                       ['gtm', trk], ['prw%d' % qi])
                                n = 128 * len(idxs)
                                ts('dve', dst[:, 512 * g:512 * g + n], pw[0:4, 0:n], bgs[:, l, dr * 2 + qi:dr * 2 + qi + 1], None,
                                   ALU.add, None, ['prw%d' % qi, 'bgs'], ['row%d' % qi])
                        act(rF, rF, AF.Exp, ['row1'], ['row1'], scale=-1.0)
                        act(rF, rF, AF.Ln, ['row1'], ['row1'], bias=onesf[0:4, 0:1])
                        S.op('dve', lambda e: e.tensor_tensor_scan(out=rG, data0=onesf[0:4, 0:1].broadcast_to([4, T]), data1=rF,
                                                                   initial=0.0, op0=ALU.mult, op1=ALU.add),
                             reads=['row1', 'onesf'], writes=['row2'])
                        tt('dve', rI, rI, rG, ALU.add, ['row0', 'row2'], ['row0'])
                        S.op('dve', lambda e: e.tensor_tensor_scan(out=rA, data0=onesf[0:4, 0:1].broadcast_to([4, T]), data1=rI,
                                                                   initial=0.0, op0=ALU.mult, op1=ALU.max),
                             reads=['row0', 'onesf'], writes=['row3'])
                        S.op('dve', lambda e: e.memset(rrow[:, 0:1], 0.0), writes=['rrow'])
                        cp('dve', rrow[:, 1:19], rA.rearrange("p (j t) -> p j t", t=128)[:, :, 127], ['row3'], ['rrow'])
                        rfull = rrow[:, 0:18].unsqueeze(2).broadcast_to([4, 18, 128])
                        v3 = lambda r_: r_.rearrange("p (j t) -> p j t", t=128)
                        tt('dve', v3(rI), v3(rI), rfull, ALU.subtract, ['row0', 'rrow'], ['row0'])
                        act(rI, rI, AF.Exp, ['row0'], ['row0'])
                        tt('dve', rG, rG, rA, ALU.subtract, ['row2', 'row3'], ['row2'])
                        act(rG, rG, AF.Exp, ['row2'], ['row2'])
                        tt('dve', v3(rA), rfull, v3(rA), ALU.subtract, ['row3', 'rrow'], ['row3'])
                        act(rA, rA, AF.Exp, ['row3'], ['row3'])
                        tt('dve', drow[:, 0:18], rrow[:, 0:18], rrow[:, 1:19], ALU.subtract, ['rrow'], ['drow'])
                        act(drow[:], drow[:], AF.Exp, ['drow'], ['drow'])
                        for idx in range(18):
                            for qi, rw in enumerate((rI, rA, rG)):
                                mm(pcol[:, idx, qi * 4:(qi + 1) * 4], rw[:, idx * 128:(idx + 1) * 128], ident[0:4, 0:4], True, True,
                                   ['row0', 'row2', 'row3', 'ident'], ['pcol'])
                        if dr == 0:
                            cp('dve', colq[0][:], pcol[:], ['pcol'], ['colq0'])
                        else:
                            cp('dve', colraw[:], pcol[:], ['pcol'], ['colraw'])
                            mm(pcol2[:].rearrange("p a b -> p (a b)"), jrev[:], colraw[:].rearrange("p a b -> p (a b)"), True, True,
                               ['colraw', 'jrev'], ['pcol2'])
                            cp('dve', colq[1][:], pcol2[:], ['pcol2'], ['colq1'])
                        for pr in range(2):
                            mm(pd[:, pr, :], sel[:, pr, :], drow[:], True, True, ['sel', 'drow'], ['pd'])
                        cp('dve', dcol[dr][:], pd[:], ['pd'], ['dcol%d' % dr])
                S.barrier()
                if CSTOP == 3:
                    return
                with contextlib.ExitStack() as p4:
                    Cst = sb(p4, "Cst", [128, 2, 80], F32)
                    Cbf = sb(p4, "Cbf", [128, 4, 80], BF16)
                    Cbf4 = Cbf[:].rearrange("p (c u) d -> p c u d", u=2)
                    qz4_ = [sb(p4, "qzc%d" % i, [128, 256], BF16) for i in range(4)]
                    qzr4 = Rot([("qzc%d" % i, qz4_[i]) for i in range(4)])
                    for i in range(4):
                        S.op('pool', lambda e, i=i: e.memset(qz4_[i][:], 0.0), writes=['qzc%dz' % i, 'qzc%d' % i])
                    vs_ = [sb(p4, "vs%d" % i, [128, 4, 80], BF16) for i in range(2)]
                    vsr = Rot([("vs%d" % i, vs_[i]) for i in range(2)])
                    wt_ = [sb(p4, "wt%d" % i, [128, 4, 128], BF16) for i in range(2)]
                    wtr = Rot([("wt%d" % i, wt_[i]) for i in range(2)])
                    sm = [sb(p4, "sm%d" % i, [128, 16], F32) for i in range(2)]
                    smr = Rot([("sm%d" % i, sm[i]) for i in range(2)])
                    ho = [sb(p4, "ho%d" % i, [128, 4, 64], F32) for i in range(2)]
                    hor = Rot([("ho%d" % i, ho[i]) for i in range(2)])
                    stp = [ps(p4, "stp%d" % i, [128, 512]) for i in range(2)]
                    stpr = Rot([("stp%d" % i, stp[i]) for i in range(2)])
                    opp = [ps(p4, "opp%d" % i, [128, 4, 80]) for i in range(2)]
                    oppr = Rot([("opp%d" % i, opp[i]) for i in range(2)])
                    upp = [ps(p4, "upp%d" % i, [128, 2, 80]) for i in range(2)]
                    uppr = Rot([("upp%d" % i, upp[i]) for i in range(2)])
                    for di, dr in enumerate((1, 0)):
                        S.op('dve', lambda e: e.memset(Cst[:], 0.0), reads=['Cst'], writes=['Cst'])
                        S.op('dve', lambda e: e.memset(Cbf[:], 0.0), reads=['Cbf'], writes=['Cbf'])
                        for idx in range(18):
                            j = ORD[dr][idx]
                            tok = slice(128 * j, 128 * j + 128)
                            emit = not (last and j >= 16)
                            vk, vs = vsr.next()
                            tt('dve', vs[:, :, 0:65], vaug[:, j, :, :], colq[dr][:, idx, 0:4].unsqueeze(2).broadcast_to([128, 4, 65]),
                               ALU.mult, ['vaug%d' % j, 'vaug1', 'colq%d' % dr], [vk])
                            if emit:
                                sk, sp_ = stpr.next()
                                for c2 in range(2):
                                    zk, qz = qzr4.next()
                                    act(qz[0:64, 0:128], QK[0:64, c2, tok], AF.Copy, ['QK', zk + 'z'], [zk])
                                    act(qz[64:128, 128:256], QK[64:128, c2, tok], AF.Copy, ['QK', zk + 'z'], [zk])
                                    mm(sp_[:, c2 * 256:(c2 + 1) * 256], QK[:, 2 + c2, tok], qz[:], True, True, ['QK', zk], [sk])
                                wk, wt = wtr.next()
                                tt('dve', wt[:], sp_[:].rearrange("p (h t) -> p h t", h=4),
                                   masks[:, dr, :].unsqueeze(1).broadcast_to([128, 4, 128]), ALU.mult, [sk, 'masks'], [wk])
                                ok_, op_ = oppr.next()
                                for h in range(4):
                                    c2, po = h // 2, (h % 2) * 64
                                    mm(op_[:, h, 0:65], wt[:, h, :], vs[:, h, 0:65], True, False, [wk, vk], [ok_])
                                    mm(op_[:, h, 0:65], QK[:, c2, tok], Cbf[:, h, 0:65], False, True,
                                       ['QK', 'Cbf'], [ok_])
                                mk, m_ = smr.next()
                                eo = colq[dr][:, idx, 4:8]
                                eb = colq[dr][:, idx, 8:12]
                                tt('dve', m_[:, 0:4], op_[:, :, 64], eo, ALU.mult, [ok_, 'colq%d' % dr], [mk])
                                stt(m_[:, 4:8], m_[:, 0:4], -1.0, m_[:, 0:4], ALU.mult, ALU.max, [mk], [mk])
                                tt('dve', m_[:, 4:8], m_[:, 4:8], eb, ALU.max, [mk, 'colq%d' % dr], [mk])
                                S.op('dve', lambda e, m_=m_: e.reciprocal(out=m_[:, 8:12], in_=m_[:, 4:8]), reads=[mk], writes=[mk])
                                tt('dve', m_[:, 12:16], m_[:, 8:12], eo, ALU.mult, [mk, 'colq%d' % dr], [mk])
                                hs_j = hsum[:, j, :].rearrange("p (h d) -> p h d", h=4)
                                rcb = m_[:, 12:16].unsqueeze(2).broadcast_to([128, 4, 64])
                                if di == 0:
                                    tt('dve', hs_j, op_[:, :, 0:64], rcb, ALU.mult, [ok_, mk], ['hsum%d' % j])
                                else:
                                    hk2, h2 = hor.next()
                                    tt('dve', h2[:], op_[:, :, 0:64], rcb, ALU.mult, [ok_, mk], [hk2])
                                    tt('pool', hs_j, hs_j, h2[:], ALU.add, [hk2, 'hsum%d' % j], ['hsum%d' % j])
                            if idx < 17:
                                uk, up = uppr.next()
                                for pr in range(2):
                                    for hh in range(2):
                                        mm(up[hh * 64:(hh + 1) * 64, pr, 0:65], ktm[:, j, pr * 128 + hh * 64:pr * 128 + hh * 64 + 64],
                                           vs[:, 2 * pr + hh, 0:65], True, True, ['ktm%d' % j, vk], [uk])
                                tt('dve', Cst[:, :, 0:65], up[:, :, 0:65], Cst[:, :, 0:65], ALU.add, [uk, 'Cst'], ['Cst'])
                                tt('dve', Cst[:, :, 0:65], Cst[:, :, 0:65], dcol[dr][:, :, idx].unsqueeze(2).broadcast_to([128, 2, 65]), ALU.mult,
                                   ['Cst', 'dcol%d' % dr], ['Cst'])
                                cp('pool', Cbf4[0:64, :, 0, 0:65], Cst[0:64, :, 0:65], ['Cst'], ['Cbf'])
                                cp('pool', Cbf4[64:128, :, 1, 0:65], Cst[64:128, :, 0:65], ['Cst'], ['Cbf'])
                S.barrier()
                if CSTOP == 4:
                    return
                with contextlib.ExitStack() as p5:
                    OT = R2[:, 0:2, :]
                    wload(wv[:], 'wv', l, OFF_C + 768, 256)
                    pq = [ps(p5, "pq5_%d" % i, [128, 512]) for i in range(2)]
                    pqr = Rot([("pq5_%d" % i, pq[i]) for i in range(2)])
                    nb = 8 if last else 9
                    for b in range(nb):
                        t0 = 256 * b
                        v = s if b < 8 else 2
                        hk, hT = make_h(l, 0, t0, 256, v)
                        for oc in range(2):
                            proj_fm(hT, hk, 256, wv, 'wv', oc * 128, pqr,
                                    lambda p_, pk, oc=oc: act(OT[:, oc, t0:t0 + 256], p_, AF.Sigmoid, [pk], ['OT%d' % oc]))
                    st_ = [sb(p5, "lst%d" % i, [128, 16], F32) for i in range(2)]
                    str_ = Rot([("lst%d" % i, st_[i]) for i in range(2)])
                    xc_ = [sb(p5, "xc%d" % i, [128, 4, 64], F32) for i in range(2)]
                    xcr = Rot([("xc%d" % i, xc_[i]) for i in range(2)])
                    sq_ = [sb(p5, "xsq%d" % i, [128, 4, 64], F32) for i in range(2)]
                    sqr_ = Rot([("xsq%d" % i, sq_[i]) for i in range(2)])
                    for j in range(16 if last else 18):
                        tok = slice(128 * j, 128 * j + 128)
                        hs_j = hsum[:, j, :].rearrange("p (h d) -> p h d", h=4)
                        lk, ls = str_.next()
                        S.op('dve', lambda e, ls=ls, hs_j=hs_j: e.reduce_sum(out=ls[:, 0:4], in_=hs_j, axis=AX.X),
                             reads=['hsum%d' % j], writes=[lk])
                        ts('dve', ls[:, 0:4], ls[:, 0:4], 1.0 / 64, None, ALU.mult, None, [lk], [lk])
                        xk, xc = xcr.next()
                        tt('dve', xc[:], hs_j, ls[:, 0:4].unsqueeze(2).broadcast_to([128, 4, 64]), ALU.subtract,
                           ['hsum%d' % j, lk], [xk])
                        qk_, xq = sqr_.next()
                        tt('pool', xq[:], xc[:], xc[:], ALU.mult, [xk], [qk_])
                        S.op('dve', lambda e, ls=ls, xq=xq: e.reduce_sum(out=ls[:, 4:8], in_=xq[:], axis=AX.X),
                             reads=[qk_], writes=[lk])
                        ts('dve', ls[:, 4:8], ls[:, 4:8], 1.0 / 64, EPS, ALU.mult, ALU.add, [lk], [lk])
                        act(ls[:, 8:12], ls[:, 4:8], AF.Sqrt, [lk], [lk])
                        S.op('dve', lambda e, ls=ls: e.reciprocal(out=ls[:, 12:16], in_=ls[:, 8:12]), reads=[lk], writes=[lk])
                        tt('dve', xc[:], xc[:], ls[:, 12:16].unsqueeze(2).broadcast_to([128, 4, 64]), ALU.mult, [xk, lk], [xk])
                        tt('pool', xq[:].rearrange("p h d -> p (h d)"), xc[:].rearrange("p h d -> p (h d)"), gmb[:], ALU.mult,
                           [xk, 'gmb', qk_], [qk_])
                        pk, pt = pqr.next()
                        for kc in range(2):
                            mm(pt[:, kc * 128:(kc + 1) * 128], xq[:].rearrange("p h d -> p (h d)")[:, kc * 128:(kc + 1) * 128],
                               ident[:], True, True, [qk_, 'ident'], [pk])
                        tt('dve', CAT[:, 4:6, tok], pt[:, 0:256].rearrange("p (c t) -> p c t", c=2), OT[:, :, tok], ALU.mult,
                           [pk, 'OT0', 'OT1'], ['CATc'])

        def mixer_a(s, l, last):
            with contextlib.ExitStack() as ph:
                QA = sb(ph, "QA", [128, 2, T], BF16)
                KA = sb(ph, "KA", [128, 2, T], BF16)
                VA = sb(ph, "VA", [128, 18, 256], BF16)
                wq = sb(ph, "wqa", [128, 8, 768], BF16)
                wload(wq[:], 'wqa', l, OFF_A, 768)
                with contextlib.ExitStack() as p1:
                    pq = [ps(p1, "pqa%d" % i, [128, 512]) for i in range(2)]
                    pqr = Rot([("pqa%d" % i, pq[i]) for i in range(2)])
                    for b in range(9):
                        t0 = 256 * b
                        v = s if b < 8 else 2
                        hk, hT = make_h(l, 0, t0, 256, v)
                        for oc in range(4):
                            if oc < 2 and last and b == 8:
                                continue
                            dst = QA if oc < 2 else KA
                            proj_fm(hT, hk, 256, wq, 'wqa', oc * 128, pqr,
                                    lambda p_, pk, oc=oc, dst=dst: act(dst[:, oc % 2, t0:t0 + 256], p_, AF.Copy, [pk], ['QKA']))
                        for sub in range(2):
                            j = 2 * b + sub
                            proj_tm(hT, hk, sub, wq, 'wqa', 512, 256, pqr,
                                    lambda p_, pk, j=j: cp('dve', VA[:, j, :], p_, [pk], ['VA']))
                S.barrier()
                with contextlib.ExitStack() as p2:
                    bt = [sb(p2, "bt%d" % i, [128, 1280], F32) for i in range(2)]
                    btr = Rot([("bt%d" % i, bt[i]) for i in range(2)])
                    tf = [sb(p2, "tf%d" % i, [128, 1280], F32) for i in range(2)]
                    tfr = Rot([("tf%d" % i, tf[i]) for i in range(2)])
                    PT = [sb(p2, "PT%d" % i, [128, 7, 256], BF16) for i in range(2)]
                    ptr_ = Rot([("PT%d" % i, PT[i]) for i in range(2)])
                    rc_ = [sb(p2, "rca%d" % i, [128, 256], F32) for i in range(2)]
                    rcr = Rot([("rca%d" % i, rc_[i]) for i in range(2)])
                    sps = ps(p2, "sps", [128, 8, 256])
                    ov = [ps(p2, "ov%d" % i, [128, 512]) for i in range(1)]
                    ovr = Rot([("ov%d" % i, ov[i]) for i in range(1)])
                    dn = [ps(p2, "dn%d" % i, [128, 512]) for i in range(1)]
                    dnr = Rot([("dn%d" % i, dn[i]) for i in range(1)])
                    qz_ = [sb(p2, "qz%d" % i, [128, 256], BF16) for i in range(2)]
                    qzr = Rot([("qz%d" % i, qz_[i]) for i in range(2)])
                    for i in range(2):
                        S.op('pool', lambda e, i=i: e.memset(qz_[i][:], 0.0), writes=['qz%dz' % i, 'qz%d' % i])
                    nq = 16 if last else 18
                    qts = list(range(nq))
                    if ASTOP == 1:
                        qts = []
                    if ASTOP == 2:
                        qts = [16, 17]
                    if ASTOP == 3:
                        qts = [5]
                    for qt in qts:
                        ctxq = qt >= 16
                        tq = slice(128 * qt, 128 * qt + 128)
                        if ctxq:
                            kch = [16, 17]
                        else:
                            cb = min(max(qt - 2, 0), 11)
                            kch = list(range(cb, cb + 5)) + [16, 17]
                            typ = {0: 0, 1: 1, 14: 3, 15: 4}.get(qt, 2)
                        nk = len(kch)
                        for pr in range(2):
                            if not ctxq:
                                bk, bias_t = btr.next()
                                dma('sp', bias_t[:], natb[l, typ, pr], writes=[bk])
                            zk, qz = qzr.next()
                            cp('pool', qz[0:64, 0:128], QA[0:64, pr, tq], ['QKA', zk + 'z'], [zk])
                            cp('pool', qz[64:128, 128:256], QA[64:128, pr, tq], ['QKA', zk + 'z'], [zk])
                            for i, kc_ in enumerate(kch):
                                mm(sps[:, i, :], KA[:, pr, 128 * kc_:128 * kc_ + 128], qz[:], True, True, ['QKA', zk], ['sps'])
                            pk_, P_ = ptr_.next()
                            if not ctxq:
                                fk, tfl = tfr.next()
                                for (a0, a1) in ((0, 2), (2, 4), (4, 5)):
                                    stt(tfl[:, a0 * 256:a1 * 256], sps[:, a0:a1, :].rearrange("p a b -> p (a b)"), 0.125,
                                        bias_t[:, a0 * 256:a1 * 256], ALU.mult, ALU.add, ['sps', bk], [fk])
                                act(P_[:, 0:5, :].rearrange("p a b -> p (a b)"), tfl[:], AF.Exp, [fk], [pk_])
                                for a0 in (5, 6):
                                    act(P_[:, a0, :], sps[:, a0, :], AF.Exp, ['sps'], [pk_], scale=0.125)
                            else:
                                act(P_[:, 0:2, :].rearrange("p a b -> p (a b)"), sps[:, 0:2, :].rearrange("p a b -> p (a b)"),
                                    AF.Exp, ['sps'], [pk_], scale=0.125)
                            ok_, o_ = ovr.next()
                            dk_, d_ = dnr.next()
                            for i, kc_ in enumerate(kch):
                                mm(o_[:, 0:256], VA[:, kc_, pr * 128:(pr + 1) * 128], P_[:, i, :], i == 0, i == nk - 1,
                                   ['VA', pk_], [ok_])
                            for i, kc_ in enumerate(kch):
                                mm(d_[:, 0:256], onesb[:], P_[:, i, :], i == 0, i == nk - 1, ['onesb', pk_], [dk_])
                            rk_, r_ = rcr.next()
                            S.op('dve', lambda e, r_=r_, d_=d_: e.reciprocal(out=r_[:], in_=d_[:, 0:256]), reads=[dk_], writes=[rk_])
                            for hh in range(2):
                                po = hh * 64
                                tt('dve', CAT[po:po + 64, pr, tq], o_[po:po + 64, hh * 128:(hh + 1) * 128],
                                   r_[po:po + 64, hh * 128:(hh + 1) * 128], ALU.mult, [ok_, rk_], ['CATa'])

        def mixer_b(s, l, last):
            with contextlib.ExitStack() as ph:
                UB = sb(ph, "UB", [128, 2, T], BF16)
                ZB = sb(ph, "ZB", [128, 18, 256], BF16)
                wb_ = sb(ph, "wbb", [128, 8, 512], BF16)
                wsb = sb(ph, "wsb", [128, 4, 128], BF16)
                bsb = sb(ph, "bsb", [128, 2, 128], F32)
                ggb = sb(ph, "ggb", [128, 256], F32)
                wload(wb_[:], 'wbb', l, OFF_B, 512)
                dma('pool', wsb[:], wsT[l], writes=['wsb'])
                dma('sp', bsb[:], bsT[l], writes=['bsb'])
                dma('sp', ggb[:], ggm[l, :].partition_broadcast(128), writes=['ggb'])
                pq = [ps(ph, "pqb%d" % i, [128, 512]) for i in range(2)]
                pqr = Rot([("pqb%d" % i, pq[i]) for i in range(2)])
                zf = [sb(ph, "zf%d" % i, [128, 256], F32) for i in range(2)]
                zfr = Rot([("zf%d" % i, zf[i]) for i in range(2)])
                zq = [sb(ph, "zq%d" % i, [128, 256], F32) for i in range(2)]
                zqr = Rot([("zq%d" % i, zq[i]) for i in range(2)])
                zs = [sb(ph, "zs%d" % i, [128, 4], F32) for i in range(2)]
                zsr = Rot([("zs%d" % i, zs[i]) for i in range(2)])
                nb = 8 if last else 9
                for b in range(nb):
                    t0 = 256 * b
                    v = s if b < 8 else 2
                    hk, hT = make_h(l, 0, t0, 256, v)
                    for oc in range(2):
                        proj_fm(hT, hk, 256, wb_, 'wbb', oc * 128, pqr,
                                lambda p_, pk, oc=oc: act(UB[:, oc, t0:t0 + 256], p_, AF.Gelu_apprx_tanh, [pk], ['UB']))
                    for sub in range(2):
                        j = 2 * b + sub

                        def ev(p_, pk, j=j):
                            fk, z_ = zfr.next()
                            act(z_[:], p_, AF.Gelu_apprx_tanh, [pk], [fk])
                            qk_, q_ = zqr.next()
                            tt('pool', q_[:], z_[:], z_[:], ALU.mult, [fk], [qk_])
                            sk, s_ = zsr.next()
                            S.op('dve', lambda e: e.reduce_sum(out=s_[:, 0:1], in_=q_[:], axis=AX.X), reads=[qk_], writes=[sk])
                            ts('dve', s_[:, 0:1], s_[:, 0:1], 1.0 / 256, EPS, ALU.mult, ALU.add, [sk], [sk])
                            act(s_[:, 1:2], s_[:, 0:1], AF.Sqrt, [sk], [sk])
                            S.op('dve', lambda e: e.reciprocal(out=s_[:, 2:3], in_=s_[:, 1:2]), reads=[sk], writes=[sk])
                            stt(ZB[:, j, :], z_[:], s_[:, 2:3], ggb[:], ALU.mult, ALU.mult, [fk, sk, 'ggb'], ['ZB%d' % j])
                        proj_tm(hT, hk, sub, wb_, 'wbb', 256, 256, pqr, ev)
                mt = [sb(ph, "mt%d" % i, [128, 128], F32) for i in range(2)]
                mtr = Rot([("mt%d" % i, mt[i]) for i in range(2)])
                for j in range(16 if last else 18):
                    tok = slice(128 * j, 128 * j + 128)
                    for pr in range(2):
                        pk, pt = pqr.next()
                        for hh in range(2):
                            po = hh * 64
                            mm(pt[po:po + 64, 0:128], ZB[:, j, pr * 128 + po:pr * 128 + po + 64], wsb[:, 2 * pr + hh, :],
                               True, True, ['ZB%d' % j, 'wsb'], [pk])
                        mk, m_ = mtr.next()
                        tt('dve', m_[:], pt[:, 0:128], bsb[:, pr, :], ALU.add, [pk, 'bsb'], [mk])
                        tt('pool', CAT[:, 2 + pr, tok], m_[:], UB[:, pr, tok], ALU.mult, [mk, 'UB'], ['CATb'])

        def mixer_d(s, l, last):
            with contextlib.ExitStack() as ph:
                FT = sb(ph, "FT", [128, 18, 256], BF16)
                wfi = sb(ph, "wfi", [128, 8, 256], BF16)
                wfn = sb(ph, "wfn", [128, 2, 256], BF16)
                wload(wfi[:], 'wfi', l, OFF_D, 256)
                dma('pool', wfn[:], w_fnet[l].rearrange("(kc p) n -> p kc n", p=128), writes=['wfn'])
                pq = [ps(ph, "pqd%d" % i, [128, 512]) for i in range(2)]
                pqr = Rot([("pqd%d" % i, pq[i]) for i in range(2)])
                nb = 8 if last else 9
                for b in range(nb):
                    t0 = 256 * b
                    v = s if b < 8 else 2
                    hk, hT = make_h(l, 0, t0, 256, v)
                    for sub in range(2):
                        j = 2 * b + sub
                        proj_tm(hT, hk, sub, wfi, 'wfi', 0, 256, pqr,
                                lambda p_, pk, j=j: cp('dve', FT[:, j, :], p_, [pk], ['FT']))
                dc_ = [sb(ph, "dc%d" % i, [128, 16, 256], BF16) for i in range(2)]
                ds_ = [sb(ph, "ds%d" % i, [128, 16, 256], BF16) for i in range(2)]
                YC = [sb(ph, "YC%d" % i, [128, 2, 256], BF16) for i in range(2)]
                YS = [sb(ph, "YS%d" % i, [128, 2, 256], BF16) for i in range(2)]
                SPc = [sb(ph, "SP%d" % i, [128, 2, 256], BF16) for i in range(2)]
                ycp = ps(ph, "ycp", [128, 512])
                ysp = ps(ph, "ysp", [128, 512])
                spp = ps(ph, "spp", [128, 512])
                dpp = ps(ph, "dpp", [128, 512])
                it = 0
                segs = [(0, 16, TL, c_dftc_l, c_dfts_l)]
                if not last:
                    segs.append((16, 2, 256, c_dftc_c, c_dfts_c))
                for (jb, nchk, nT, mc, ms) in segs:
                    scale = float(1.0 / np.sqrt(64.0 * nT))
                    for tb in range(nT // 256):
                        bb = it % 2
                        it += 1
                        dma('sp', dc_[bb][:, 0:nchk, :], mc[:, tb * 256:(tb + 1) * 256].rearrange("(c p) n -> p c n", p=128),
                            writes=['dc%d' % bb])
                        dma('sp', ds_[bb][:, 0:nchk, :], ms[:, tb * 256:(tb + 1) * 256].rearrange("(c p) n -> p c n", p=128),
                            writes=['ds%d' % bb])
                        for fc in range(2):
                            for i in range(nchk):
                                mm(ycp[:, 0:256], FT[:, jb + i, fc * 128:(fc + 1) * 128], dc_[bb][:, i, :], i == 0, i == nchk - 1,
                                   ['FT', 'dc%d' % bb], ['ycp'])
                            for i in range(nchk):
                                mm(ysp[:, 0:256], FT[:, jb + i, fc * 128:(fc + 1) * 128], ds_[bb][:, i, :], i == 0, i == nchk - 1,
                                   ['FT', 'ds%d' % bb], ['ysp'])
                            act(YC[bb][:, fc, :], ycp[:, 0:256], AF.Copy, ['ycp'], ['YC%d' % bb])
                            cp('dve', YS[bb][:, fc, :], ysp[:, 0:256], ['ysp'], ['YS%d' % bb])
                            mm(spp[:, 0:256], blk[:, 0, :], YC[bb][:, fc, :], True, False, ['blk', 'YC%d' % bb], ['spp'])
                            mm(spp[:, 0:256], blk[:, 1, :], YS[bb][:, fc, :], False, True, ['blk', 'YS%d' % bb], ['spp'])
                            act(SPc[bb][:, fc, :], spp[:, 0:256], AF.Copy, ['spp'], ['SP%d' % bb], scale=scale)
                        for oc in range(2):
                            for fc in range(2):
                                mm(dpp[:, 0:256], wfn[:, fc, oc * 128:(oc + 1) * 128], SPc[bb][:, fc, :], fc == 0, fc == 1,
                                   ['wfn', 'SP%d' % bb], ['dpp'])
                            t0 = 128 * jb + tb * 256
                            cp('dve', CAT[:, 6 + oc, t0:t0 + 256], dpp[:, 0:256], ['dpp'], ['CATd'])

        class _Stop(Exception):
            pass

        def body():
          if dbg:
              S.op('pool', lambda e: e.memset(CAT[:], 0.0), writes=['CAT'])
              S.barrier()
          for s in range(2):
            for c in range(8):
                dma('sp', X[:, c, :], xin[s, :, c, :], writes=['X%d' % c])
            for l in range(DEPTH):
                last = (l == DEPTH - 1)
                if 'C' not in SKIP:
                    mixer_c(s, l, last)
                    S.barrier()
                if 'A' not in SKIP:
                    mixer_a(s, l, last)
                    S.barrier()
                if 'B' not in SKIP:
                    mixer_b(s, l, last)
                    S.barrier()
                if 'D' not in SKIP:
                    mixer_d(s, l, last)
                    S.barrier()
                if dbg == ('cat', s, l):
                    for c in range(8):
                        dma('pool', dbg_out[:, c, :], CAT[:, c, :], reads=['CAT'])
                    return
                S.barrier()
                for _once in ([] if 'wout' in SKIP else [0]):
                  with contextlib.ExitStack() as ph:
                    wo = sb(ph, "wo", [128, 8, D], BF16)
                    yps = [ps(ph, "yps%d" % i, [128, 512]) for i in range(2)]
                    for kc in range(8):
                        dma('pool', wo[:, kc, :], w_out[l, kc * 128:(kc + 1) * 128, :], writes=['wo'])
                    it = 0
                    nblk = 4 if last else 5
                    for b in range(nblk):
                        t0 = b * 512
                        n = 512 if b < 4 else 256
                        v = s if b < 4 else 2
                        for oc in range(8):
                            pb = it % 2
                            it += 1
                            for kc in range(8):
                                mm(yps[pb][:, :n], wo[:, kc, oc * 128:(oc + 1) * 128], CAT[:, kc, t0:t0 + n],
                                   kc == 0, kc == 7, ['wo', 'CAT'], ['yps%d' % pb])
                            stt(X[:, oc, t0:t0 + n], yps[pb][:, :n], modap(l, 2, oc, v), X[:, oc, t0:t0 + n],
                                ALU.mult, ALU.add, ['yps%d' % pb, 'modT', 'X%d' % oc], ['X%d' % oc])
                S.barrier()
                for _once in ([] if 'ffn' in SKIP else [0]):
                  with contextlib.ExitStack() as ph:
                    nblk = 4 if last else 5
                    H2 = CAT
                    for b in range(nblk):
                        t0 = b * 512
                        n = 512 if b < 4 else 256
                        v = s if b < 4 else 2
                        make_h(l, 1, t0, n, v, dst=('H2_%d' % b, H2[:, :, t0:t0 + n]))
                    w1 = [sb(ph, "w1_%d" % i, [128, 8, 512], BF16) for i in range(2)]
                    w2 = [sb(ph, "w2_%d" % i, [128, 4, D], BF16) for i in range(2)]
                    ag = [sb(ph, "ag%d" % i, [128, 4, 512], BF16) for i in range(2)]
                    rl = [sb(ph, "rl%d" % i, [128, 512], F32) for i in range(2)]
                    fps = [ps(ph, "fps%d" % i, [128, 512]) for i in range(2)]
                    ops_ = [ps(ph, "ops%d" % i, [128, 512]) for i in range(2)]
                    i1 = i2 = i3 = 0
                    for g in range(8):
                        wb = g % 2
                        dma('pool', w1[wb][:], w_ff1[l, :, g * 512:(g + 1) * 512].rearrange("(kc p) n -> p kc n", p=128),
                            writes=['w1_%d' % wb])
                        dma('pool', w2[wb][:], w_ff2[l, g * 512:(g + 1) * 512, :].rearrange("(kc p) n -> p kc n", p=128),
                            writes=['w2_%d' % wb])
                        for b in range(nblk):
                            t0 = b * 512
                            n = 512 if b < 4 else 256
                            v = s if b < 4 else 2
                            ab = i1 % 2
                            i1 += 1
                            for fc in range(4):
                                pb = i2 % 2
                                i2 += 1
                                for kc in range(8):
                                    mm(fps[pb][:, :n], w1[wb][:, kc, fc * 128:(fc + 1) * 128], H2[:, kc, t0:t0 + n],
                                       kc == 0, kc == 7, ['w1_%d' % wb, 'H2_%d' % b], ['fps%d' % pb])
                                act(rl[pb][:, :n], fps[pb][:, :n], AF.Relu, ['fps%d' % pb], ['rl%d' % pb])
                                tt('pool', ag[ab][:, fc, :n], rl[pb][:, :n], rl[pb][:, :n], ALU.mult,
                                   ['rl%d' % pb], ['ag%d_%d' % (ab, fc)])
                            for oc in range(8):
                                pb = i3 % 2
                                i3 += 1
                                for kc in range(4):
                                    mm(ops_[pb][:, :n], w2[wb][:, kc, oc * 128:(oc + 1) * 128], ag[ab][:, kc, :n],
                                       kc == 0, kc == 3, ['w2_%d' % wb, 'ag%d_%d' % (ab, kc)], ['ops%d' % pb])
                                stt(X[:, oc, t0:t0 + n], ops_[pb][:, :n], modap(l, 5, oc, v), X[:, oc, t0:t0 + n],
                                    ALU.mult, ALU.add, ['ops%d' % pb, 'modT', 'X%d' % oc], ['X%d' % oc])
                S.barrier()
                if dbg == ('x', s, l):
                    for c in range(8):
                        dma('sp', dbg_out[:, c, :], X[:, c, :], reads=['X%d' % c])
                    return
            with contextlib.ExitStack() as ph:
                ob = [sb(ph, "ob%d" % i, [128, 512], F32) for i in range(2)]
                it = 0
                for b in range(4):
                    t0 = b * 512
                    pk, pst = ssqrot.next()
                    for c in range(8):
                        sk, sq = sqrot.next()
                        act(sq[:], X[:, c, t0:t0 + 512], AF.Square, ['X%d' % c], [sk])
                        mm(pst[:], onesb[:], sq[:], c == 0, c == 7, [sk, 'onesb'], [pk])
                    rk, rs = rsrot.next()
                    act(rs[:], pst[:], AF.Sqrt, [pk, 'epsb'], [rk], scale=1.0 / D, bias=epsb[:, 0:1])
                    S.op('dve', lambda e: e.reciprocal(out=rs[:], in_=rs[:]), reads=[rk], writes=[rk])
                    for c in range(8):
                        o = it % 2
                        it += 1
                        stt(ob[o][:], X[:, c, t0:t0 + 512], gv[:, 4, c:c + 1], rs[:], ALU.mult, ALU.mult,
                            ['X%d' % c, 'gv', rk], ['ob%d' % o])
                        dma('sp', outT[s, :, c, t0:t0 + 512], ob[o][:], reads=['ob%d' % o])
            S.barrier()
        body()
        S.finish('sp')
        print("ops", S.nops, "waits", S.nwaits, {e: S.cc[e] for e in S.cc}, {e: S.dc[e] for e in S.dc})
    return nc


_NC_CACHE = {}


def _prep_shared(inp):
    f32 = lambda a: np.ascontiguousarray(np.asarray(a, dtype=np.float32))
    sh = dict(_consts())
    sh['w_ada'] = f32(inp['w_ada'])
    sh['badaT'] = np.ascontiguousarray(np.moveaxis(f32(inp['b_ada']).reshape(2, 48, 128), 2, 0))
    gl = [inp['g_norm_mix'][0], inp['g_norm_ffn'][0], inp['g_norm_mix'][1], inp['g_norm_ffn'][1], inp['g_final']]
    sh['gvec'] = np.ascontiguousarray(np.stack([f32(g).reshape(8, 128).T for g in gl], 1))
    sh['w_in'] = f32(inp['w_in'])
    bgate = f32(inp['b_gate'])
    sh['bg'] = np.ascontiguousarray(bgate.reshape(2, 4, 4).transpose(2, 0, 1))
    wc = f32(inp['w_conv_qk'])
    sh['wconv'] = np.ascontiguousarray(wc.reshape(2, 3, 4, 128).transpose(3, 0, 2, 1))
    rpb = f32(inp['rpb'])
    sh['natb'] = np.ascontiguousarray(np.stack([_nat_bias(rpb[l]).reshape(5, 2, 128, 1280) for l in range(2)], 0))
    sh['wsT'] = np.ascontiguousarray(f32(inp['w_spatial']).transpose(0, 3, 1, 2))
    bs = f32(inp['b_spatial'])
    bsT = np.zeros((2, 128, 2, 128), np.float32)
    for pr in range(2):
        for hh in range(2):
            bsT[:, hh * 64:(hh + 1) * 64, pr, :] = bs[:, 2 * pr + hh, None, :]
    sh['bsT'] = bsT
    sh['ggm'] = f32(inp['g_gmlp'])
    sh['gml'] = f32(inp['g_mlstm'])
    sh['w_fnet'] = f32(inp['w_fnet'])
    sh['w_out'] = f32(inp['w_out'])
    sh['w_ff1'] = f32(inp['w_ff1'])
    sh['w_ff2'] = f32(inp['w_ff2'])
    return sh


def _fmT(a):
    return np.ascontiguousarray(a.T.reshape(8, 128, a.shape[0]).transpose(1, 0, 2))


def kernel(dbg=None, **inp):
    x = np.asarray(inp['x'], np.float32)
    c = np.asarray(inp['c'], np.float32)
    ctx = np.asarray(inp['ctx'], np.float32)
    c_ctx = np.asarray(inp['c_ctx'], np.float32)
    sh = _prep_shared(inp)
    key = repr(dbg)
    if key not in _NC_CACHE:
        _NC_CACHE[key] = build_nc(dbg)
    nc = _NC_CACHE[key]
    in_maps = []
    ncores = int(os.environ.get('MK_CORES', '8'))
    for core in range(ncores):
        m = dict(sh)
        xs = []
        for i in range(2):
            b = 2 * core + i
            xs.append(_fmT(np.concatenate([x[b], ctx[b]], 0)))
        m['xin'] = np.ascontiguousarray(np.stack(xs, 0))
        vecs = [c[2 * core], c[2 * core + 1], c_ctx]
        m['cT'] = np.ascontiguousarray(np.stack([v.reshape(8, 128).T for v in vecs], 2))
        in_maps.append(m)
    res = run_bass_kernel_spmd(nc, in_maps, core_ids=list(range(ncores)))
    out = np.zeros((16, TL, D), np.float32)
    for core in range(ncores):
        o = res.results[core]['outT']
        for i in range(2):
            out[2 * core + i] = o[i].transpose(2, 1, 0).reshape(TL, D)
    if dbg:
        return out, [res.results[core]['dbg'] for core in range(ncores)]
    return out
```

```python
import contextlib
import numpy as np
import ml_dtypes
import concourse.bass as bass
import concourse.mybir as mybir
from concourse.bass_utils import run_bass_kernel_spmd

F32 = mybir.dt.float32
BF16 = mybir.dt.bfloat16
AF = mybir.ActivationFunctionType
ALU = mybir.AluOpType
AX = mybir.AxisListType

D = 1024
T = 2304
TL = 2048
NCH = 18
DIN = 2576
DEPTH = 2
EPS = 1e-6
NEG = -30000.0
import os
SKIP = set(os.environ.get('MK_SKIP', '').split(','))
CSTOP = int(os.environ.get('MK_CSTOP', '9'))
ASTOP = int(os.environ.get('MK_ASTOP', '9'))
OFF_A, OFF_B, OFF_C, OFF_D, OFF_G = 0, 768, 1280, 2304, 2560


class Sch:
    def __init__(self, nc, stack, ndma=8, same_engine_sync=True):
        self.nc = nc
        self.E = {'pe': nc.tensor, 'act': nc.scalar, 'dve': nc.vector,
                  'pool': nc.gpsimd, 'sp': nc.sync}
        self.R = ndma
        self.same = same_engine_sync
        self.csem = {}
        self.dsem = {}
        for e in self.E:
            self.csem[e] = stack.enter_context(nc.semaphore('c_' + e))
            self.dsem[e] = [stack.enter_context(nc.semaphore('d_%s_%d' % (e, i)))
                            for i in range(ndma)]
        self.cc = {e: 0 for e in self.E}
        self.dc = {e: 0 for e in self.E}
        self.waited = {e: {} for e in self.E}
        self.lw = {}
        self.rd = {}
        self.bar = set()
        self.nops = 0
        self.nwaits = 0

    def _tok_sem(self, tok):
        kind, e, i = tok
        if kind == 'c':
            return (kind, e, 0), self.csem[e], i
        return (kind, e, i % self.R), self.dsem[e][i % self.R], 16 * (i // self.R + 1)

    def _wait(self, eng, tok):
        kind, e, i = tok
        if kind == 'c' and e == eng and (eng == 'pe' or not self.same):
            return
        key, sem, val = self._tok_sem(tok)
        if self.waited[eng].get(key, 0) >= val:
            return
        self.waited[eng][key] = val
        self.E[eng].wait_ge(sem, val)
        self.nwaits += 1

    def op(self, eng, fn, reads=(), writes=(), dma=False):
        deps = set(self.bar)
        for r in reads:
            if r in self.lw:
                deps.add(self.lw[r])
        for w in writes:
            if w in self.lw:
                deps.add(self.lw[w])
            for t in self.rd.get(w, ()):
                deps.add(t)
        if dma:
            k = self.dc[eng]
            self.dc[eng] += 1
            tok = ('d', eng, k)
            if k >= self.R:
                deps.add(('d', eng, k - self.R))
        else:
            self.cc[eng] += 1
            tok = ('c', eng, self.cc[eng])
        best = {}
        for t in deps:
            key, sem, val = self._tok_sem(t)
            if key not in best or best[key][1] < val:
                best[key] = (t, val)
        for key in sorted(best, key=str):
            self._wait(eng, best[key][0])
        inst = fn(self.E[eng])
        _, sem, _ = self._tok_sem(tok)
        inst.then_inc(sem, 16 if dma else 1)
        for w in writes:
            self.lw[w] = tok
            self.rd[w] = []
        for r in reads:
            self.rd.setdefault(r, []).append(tok)
        self.nops += 1
        return tok

    def barrier(self):
        self.bar = set()
        for e in self.E:
            if self.cc[e] > 0:
                self.bar.add(('c', e, self.cc[e]))
            for j in range(max(0, self.dc[e] - self.R), self.dc[e]):
                self.bar.add(('d', e, j))
        self.lw = {}
        self.rd = {}

    def finish(self, eng='sp'):
        self.barrier()
        best = {}
        for t in self.bar:
            key, sem, val = self._tok_sem(t)
            if key not in best or best[key][1] < val:
                best[key] = (t, val)
        for key in sorted(best, key=str):
            self._wait(eng, best[key][0])


class Rot:
    def __init__(self, items):
        self.items = items
        self.i = 0

    def next(self):
        it = self.items[self.i % len(self.items)]
        self.i += 1
        return it


def _bf(a):
    return np.ascontiguousarray(a.astype(ml_dtypes.bfloat16))


def _consts():
    c = {}
    c['ident'] = np.eye(128, dtype=np.float32)
    c['jrev'] = np.ascontiguousarray(np.eye(128, dtype=np.float32)[::-1])
    s = np.arange(128)
    mf = (s[:, None] <= s[None, :]).astype(np.float32)
    mb = (s[:, None] >= s[None, :]).astype(np.float32)
    c['masks'] = _bf(np.stack([mf, mb], 1))
    t = np.arange(TL)
    rows = (t // 64).astype(np.float32)
    cols = (t % 64).astype(np.float32)
    p = np.arange(128)
    d = p % 64
    half = d // 32
    i = (d % 16).astype(np.float32)
    inv = (10000.0 ** (-i / 16.0)).astype(np.float32)
    pos = np.where(half[:, None] == 0, rows[None, :], cols[None, :]).astype(np.float32)
    ang = pos * inv[:, None]
    c['rope'] = _bf(np.stack([np.cos(ang), np.sin(ang)], 1))
    rm = np.zeros((128, 128), np.float32)
    for m in range(128):
        dd = m % 32
        if dd < 16:
            rm[m + 16, m] = -1.0
        else:
            rm[m - 16, m] = 1.0
    c['rm'] = _bf(rm)
    for n, name in ((2048, 'l'), (256, 'c')):
        k = np.arange(n, dtype=np.float64)
        ang = 2.0 * np.pi * np.outer(k, k) / n
        c['dftc_' + name] = _bf(np.cos(ang))
        c['dfts_' + name] = _bf(-np.sin(ang))
    k = np.arange(64, dtype=np.float64)
    ang = 2.0 * np.pi * np.outer(k, k) / 64
    bc = np.zeros((128, 128)); bs = np.zeros((128, 128))
    for g in range(2):
        bc[g * 64:(g + 1) * 64, g * 64:(g + 1) * 64] = np.cos(ang)
        bs[g * 64:(g + 1) * 64, g * 64:(g + 1) * 64] = np.sin(ang)
    c['blk'] = _bf(np.stack([bc, bs], 1))
    sel = np.zeros((4, 2, 128), np.float32)
    for pr in range(2):
        sel[2 * pr, pr, :64] = 1.0
        sel[2 * pr + 1, pr, 64:] = 1.0
    c['sel'] = sel
    return c


def _nat_bias(rpb_l):
    out = np.full((5, 2, 128, 5, 2, 128), NEG, np.float32)
    tsel = [0, 1, 2, 14, 15]
    kk = np.arange(128)
    qq = np.arange(128)
    for ti, t in enumerate(tsel):
        cb = min(max(t - 2, 0), 11)
        for j in range(5):
            kr = (cb + j) * 2 + kk // 64
            kc = kk % 64
            r = 2 * t + qq // 64
            qc = qq % 64
            rs = np.clip(r - 4, 0, 24)
            row_ok = (kr[:, None] >= rs[None, :]) & (kr[:, None] < rs[None, :] + 8)
            qs = np.clip(qc - 8, 0, 48)
            col_ok = (kc[:, None] >= qs[None, :]) & (kc[:, None] < qs[None, :] + 16)
            ok = row_ok & col_ok
            drow = np.clip(kr[:, None] - r[None, :] + 7, 0, 14)
            dcol = np.clip(kc[:, None] - qc[None, :] + 15, 0, 30)
            for h in range(4):
                val = rpb_l[h][drow, dcol]
                out[ti, h // 2, :, j, h % 2, :] = np.where(ok, val, NEG)
    return out


def _fm(vec):
    v = vec.reshape(vec.shape[:-1] + (8, 128))
    return np.ascontiguousarray(np.moveaxis(v, -1, 0))


def build_nc(dbg=None):
    nc = bass.Bass("TRN2", target_bir_lowering=False)

    def din(name, shape, dt=F32):
        return nc.dram_tensor(name, list(shape), dt, kind="ExternalInput").ap()

    xin = din("xin", [2, 128, 8, T])
    cT = din("cT", [128, 8, 3])
    w_ada = din("w_ada", [2, D, 6 * D])
    badaT = din("badaT", [128, 2, 48])
    gvec = din("gvec", [128, 5, 8])
    w_in = din("w_in", [2, D, DIN])
    bg = din("bg", [4, 2, 4])
    wconv = din("wconv", [128, 2, 4, 3])
    natb = din("natb", [2, 5, 2, 128, 1280])
    wsT = din("wsT", [2, 128, 4, 128])
    bsT = din("bsT", [2, 128, 2, 128])
    ggm = din("ggm", [2, 256])
    gml = din("gml", [2, 256])
    w_fnet = din("w_fnet", [2, 256, 256])
    w_out = din("w_out", [2, D, D])
    w_ff1 = din("w_ff1", [2, D, 4 * D])
    w_ff2 = din("w_ff2", [2, 4 * D, D])
    c_ident = din("ident", [128, 128])
    c_jrev = din("jrev", [128, 128])
    c_masks = din("masks", [128, 2, 128], BF16)
    c_rope = din("rope", [128, 2, TL], BF16)
    c_rm = din("rm", [128, 128], BF16)
    c_dftc_l = din("dftc_l", [TL, TL], BF16)
    c_dfts_l = din("dfts_l", [TL, TL], BF16)
    c_dftc_c = din("dftc_c", [256, 256], BF16)
    c_dfts_c = din("dfts_c", [256, 256], BF16)
    c_blk = din("blk", [128, 2, 128], BF16)
    c_sel = din("sel", [4, 2, 128])
    outT = nc.dram_tensor("outT", [2, 128, 8, TL], F32, kind="ExternalOutput").ap()
    dbg_out = None
    if dbg:
        dbg_out = nc.dram_tensor("dbg", [128, 8, T], F32, kind="ExternalOutput").ap()

    with contextlib.ExitStack() as st:
        S = Sch(nc, st)

        uid = [0]

        def sb(stack, name, shape, dt):
            uid[0] += 1
            return stack.enter_context(nc.sbuf_tensor("s%d_%s" % (uid[0], name), list(shape), dt))

        def ps(stack, name, shape, dt=F32):
            uid[0] += 1
            return stack.enter_context(nc.psum_tensor("p%d_%s" % (uid[0], name), list(shape), dt))

        def dma(eng, out, in_, reads=(), writes=()):
            S.op(eng, lambda e: e.dma_start(out=out, in_=in_), reads=reads, writes=writes, dma=True)

        def mm(out, lhsT, rhs, start, stop, reads, writes):
            S.op('pe', lambda e: e.matmul(out, lhsT=lhsT, rhs=rhs, start=start, stop=stop),
                 reads=reads, writes=writes)

        def act(out, in_, func, reads, writes, scale=1.0, bias=None):
            if bias is None:
                S.op('act', lambda e: e.activation(out=out, in_=in_, func=func, scale=scale),
                     reads=reads, writes=writes)
            else:
                S.op('act', lambda e: e.activation(out=out, in_=in_, func=func, scale=scale, bias=bias),
                     reads=reads, writes=writes)

        def tt(eng, out, in0, in1, op, reads, writes):
            S.op(eng, lambda e: e.tensor_tensor(out=out, in0=in0, in1=in1, op=op), reads=reads, writes=writes)

        def stt(out, in0, scalar, in1, op0, op1, reads, writes):
            S.op('dve', lambda e: e.scalar_tensor_tensor(out=out, in0=in0, scalar=scalar, in1=in1, op0=op0, op1=op1),
                 reads=reads, writes=writes)

        def ts(eng, out, in0, s1, s2, op0, op1, reads, writes):
            if s2 is None:
                S.op(eng, lambda e: e.tensor_scalar(out=out, in0=in0, scalar1=s1, scalar2=None, op0=op0),
                     reads=reads, writes=writes)
            else:
                S.op(eng, lambda e: e.tensor_scalar(out=out, in0=in0, scalar1=s1, scalar2=s2, op0=op0, op1=op1),
                     reads=reads, writes=writes)

        def cp(eng, out, in_, reads, writes):
            S.op(eng, lambda e: e.tensor_copy(out=out, in_=in_), reads=reads, writes=writes)

        X = sb(st, "X", [128, 8, T], F32)
        CAT = sb(st, "CAT", [128, 8, T], BF16)
        ident = sb(st, "ident", [128, 128], F32)
        jrev = sb(st, "jrev", [128, 128], F32)
        identb = sb(st, "identb", [128, 128], BF16)
        masks = sb(st, "masks", [128, 2, 128], BF16)
        rm = sb(st, "rm", [128, 128], BF16)
        blk = sb(st, "blk", [128, 2, 128], BF16)
        sel = sb(st, "sel", [4, 2, 128], F32)
        onesb = sb(st, "onesb", [128, 128], BF16)
        onesf = sb(st, "onesf", [128, 2], F32)
        RS = sb(st, "RS", [128, T], F32)
        modT = sb(st, "modT", [128, 2, 48, 3], F32)
        gmT = sb(st, "gmT", [128, 2, 2, 8, 3], F32)
        gv = sb(st, "gv", [128, 5, 8], F32)
        bada = sb(st, "bada", [128, 2, 48], F32)
        csb = sb(st, "csb", [128, 8, 3], F32)
        scb = sb(st, "scb", [128, 8, 3], BF16)
        bgs = sb(st, "bgs", [4, 2, 4], F32)
        wcv = sb(st, "wcv", [128, 2, 4, 3], F32)

        dma('sp', ident[:], c_ident, writes=['ident'])
        dma('sp', jrev[:], c_jrev, writes=['jrev'])
        dma('sp', masks[:], c_masks, writes=['masks'])
        dma('sp', rm[:], c_rm, writes=['rm'])
        dma('sp', blk[:], c_blk, writes=['blk'])
        dma('sp', sel[:], c_sel, writes=['sel'])
        dma('sp', gv[:], gvec, writes=['gv'])
        dma('sp', bada[:], badaT, writes=['bada'])
        dma('sp', csb[:], cT, writes=['csb'])
        dma('sp', bgs[:], bg, writes=['bgs'])
        dma('sp', wcv[:], wconv, writes=['wcv'])
        S.op('dve', lambda e: e.memset(onesb[:], 1.0), writes=['onesb'])
        S.op('dve', lambda e: e.memset(onesf[:], 1.0), writes=['onesf'])
        cp('dve', identb[:], ident[:], ['ident'], ['identb'])
        act(scb[:], csb[:], AF.Silu, ['csb'], ['scb'])

        with contextlib.ExitStack() as ph:
            wa = [sb(ph, "wa%d" % i, [128, 8, 512], BF16) for i in range(2)]
            mps = [ps(ph, "mps%d" % i, [128, 4, 3]) for i in range(2)]
            it = 0
            for l in range(DEPTH):
                for pc in range(12):
                    b = it % 2
                    it += 1
                    dma('pool', wa[b][:], w_ada[l, :, pc * 512:(pc + 1) * 512].rearrange("(kc p) n -> p kc n", p=128),
                        writes=['wa%d' % b])
                    for oc in range(4):
                        for kc in range(8):
                            mm(mps[b][:, oc, :], wa[b][:, kc, oc * 128:(oc + 1) * 128], scb[:, kc, :],
                               kc == 0, kc == 7, ['wa%d' % b, 'scb'], ['mps%d' % b])
                    tt('dve', modT[:, l, pc * 4:(pc + 1) * 4, :], mps[b][:],
                       bada[:, l, pc * 4:(pc + 1) * 4].unsqueeze(2).broadcast_to([128, 4, 3]), ALU.add,
                       ['mps%d' % b, 'bada'], ['modT'])
            for l in range(DEPTH):
                for kind in range(2):
                    sc_j = 1 if kind == 0 else 4
                    for v in range(3):
                        stt(gmT[:, l, kind, :, v], modT[:, l, sc_j * 8:(sc_j + 1) * 8, v], 1.0,
                            gv[:, 2 * l + kind, :], ALU.add, ALU.mult, ['modT', 'gv'], ['gmT'])
        S.barrier()

        def modap(l, j, c, v):
            return modT[:, l, j * 8 + c, v:v + 1]

        hbuf = [sb(st, "hT%d" % i, [128, 8, 256], BF16) for i in range(2)]
        hrot = Rot([("hT%d" % i, hbuf[i]) for i in range(2)])
        sqb = [sb(st, "sq%d" % i, [128, 256], BF16) for i in range(2)]
        sqrot = Rot([("sq%d" % i, sqb[i]) for i in range(2)])
        rsb = [sb(st, "rs%d" % i, [128, 256], F32) for i in range(2)]
        rsrot = Rot([("rs%d" % i, rsb[i]) for i in range(2)])
        tmb = [sb(st, "tm%d" % i, [128, 256], F32) for i in range(2)]
        tmrot = Rot([("tm%d" % i, tmb[i]) for i in range(2)])
        ssq = [ps(st, "ssq%d" % i, [128, 512]) for i in range(1)]
        ssqrot = Rot([("ssq%d" % i, ssq[i]) for i in range(1)])

        def make_h(l, kind, t0, n, v, dst=None):
            if dst is None:
                hk, hT = hrot.next()
            else:
                hk, hT = dst
            if kind == 0:
                return _mk_tail(l, kind, t0, n, v, hk, hT, 'RS', RS[:, t0:t0 + n])
            pk, pst = ssqrot.next()
            for c in range(8):
                sk, sq = sqrot.next()
                act(sq[:, :n], X[:, c, t0:t0 + n], AF.Square, ['X%d' % c], [sk])
                mm(pst[:, :n], onesb[:], sq[:, :n], c == 0, c == 7, [sk, 'onesb'], [pk])
            rk, rs = rsrot.next()
            act(rs[:, :n], pst[:, :n], AF.Sqrt, [pk, 'epsb'], [rk], scale=1.0 / D, bias=epsb[:, 0:1])
            S.op('dve', lambda e: e.reciprocal(out=rs[:, :n], in_=rs[:, :n]), reads=[rk], writes=[rk])
            return _mk_tail(l, kind, t0, n, v, hk, hT, rk, rs)

        def _mk_tail(l, kind, t0, n, v, hk, hT, rk, rs):
            sh_j = 0 if kind == 0 else 3
            for c in range(8):
                tk, tm = tmrot.next()
                stt(tm[:, :n], X[:, c, t0:t0 + n], gmT[:, l, kind, c, v:v + 1], rs[:, :n], ALU.mult, ALU.mult,
                    ['X%d' % c, 'gmT', rk], [tk])
                act(hT[:, c, :n], tm[:, :n], AF.Identity, [tk, 'modT'], [hk], bias=modap(l, sh_j, c, v))
            return hk, hT

        def compute_rs():
            for b in range(9):
                t0 = 256 * b
                n = 256
                pk, pst = ssqrot.next()
                for c in range(8):
                    sk, sq = sqrot.next()
                    act(sq[:, :n], X[:, c, t0:t0 + n], AF.Square, ['X%d' % c], [sk])
                    mm(pst[:, :n], onesb[:], sq[:, :n], c == 0, c == 7, [sk, 'onesb'], [pk])
                act(RS[:, t0:t0 + n], pst[:, :n], AF.Sqrt, [pk, 'epsb'], ['RS'], scale=1.0 / D, bias=epsb[:, 0:1])
                S.op('dve', lambda e, t0=t0, n=n: e.reciprocal(out=RS[:, t0:t0 + n], in_=RS[:, t0:t0 + n]),
                     reads=['RS'], writes=['RS'])

        epsb = sb(st, "epsb", [128, 1], F32)
        S.op('dve', lambda e: e.memset(epsb[:], EPS), writes=['epsb'])

        def proj_fm(hT, hk, n, w, wk, col0, ppool, evac):
            pk, pt = ppool.next()
            for kc in range(8):
                mm(pt[:, :n], w[:, kc, col0:col0 + 128], hT[:, kc, :n], kc == 0, kc == 7, [wk, hk], [pk])
            evac(pt[:, :n], pk)

        def proj_tm(hT, hk, sub, w, wk, col0, ncol, ppool, evac):
            pk, pt = ppool.next()
            for kc in range(8):
                mm(pt[:, :ncol], hT[:, kc, sub * 128:(sub + 1) * 128], w[:, kc, col0:col0 + ncol],
                   kc == 0, kc == 7, [wk, hk], [pk])
            evac(pt[:, :ncol], pk)

        def wload(wt, key, l, col0, ncol):
            dma('pool', wt, w_in[l, :, col0:col0 + ncol].rearrange("(kc p) n -> p kc n", p=128), writes=[key])

        ORD = [[16, 17] + list(range(16)), [17, 16] + list(range(15, -1, -1))]

        def mixer_c(s, l, last):
            with contextlib.ExitStack() as ph:
                cf03 = CAT[:, 0:4, :].rearrange("p c t -> p (c t)")
                vaug = cf03[:, 0:4752].rearrange("p (j h d) -> p j h d", j=18, h=4)[:, :, :, 0:65]
                wqk = cf03[:, 4752:4752 + 4096].rearrange("p (k n) -> p k n", k=8)
                ktm = CAT[:, 6:8, :].rearrange("p c t -> p (c t)").rearrange("p (j f) -> p j f", f=256)
                R1 = sb(ph, "R1", [128, 2 * T], F32)
                R2 = sb(ph, "R2", [128, 4, T], BF16)
                RAW = R1[:].bitcast(BF16).rearrange("p (c t) -> p c t", c=4)
                QK = R2
                wv = sb(ph, "wv", [128, 8, 256], BF16)
                wg = sb(ph, "wg", [128, 8, 16], BF16)
                gtm = sb(ph, "gtm", [128, 18, 16], F32)
                gmb = sb(ph, "gmb", [128, 256], F32)
                dma('sp', gmb[:], gml[l, :].partition_broadcast(128), writes=['gmb'])
                wload(wqk, 'wqk', l, OFF_C, 512)
                wload(wv[:], 'wv', l, OFF_C + 512, 256)
                wload(wg[:], 'wg', l, OFF_G, 16)
                S.op('pool', lambda e: e.memset(vaug[:, :, :, 64:65], 1.0), writes=['vaug1'])
                with contextlib.ExitStack() as p1:
                    pq = [ps(p1, "pq%d" % i, [128, 512]) for i in range(2)]
                    pqr = Rot([("pq%d" % i, pq[i]) for i in range(2)])
                    pvv = [ps(p1, "pvv%d" % i, [128, 512]) for i in range(2)]
                    pvr = Rot([("pvv%d" % i, pvv[i]) for i in range(2)])
                    rope = sb(p1, "rope", [128, 2, TL], BF16)
                    dma('sp', rope[:], c_rope, writes=['rope'])
                    for b in range(9):
                        t0 = 256 * b
                        v = s if b < 8 else 2
                        hk, hT = make_h(l, 0, t0, 256, v)
                        for oc in range(4):
                            proj_fm(hT, hk, 256, wqk, 'wqk', oc * 128, pqr,
                                    lambda p_, pk, oc=oc: act(RAW[:, oc, t0:t0 + 256], p_, AF.Copy, [pk], ['RAW%d' % oc]))
                        for sub in range(2):
                            j = 2 * b + sub
                            proj_tm(hT, hk, sub, wv, 'wv', 0, 256, pvr,
                                    lambda p_, pk, j=j: cp('dve', vaug[:, j, :, 0:64], p_.rearrange("p (h d) -> p h d", h=4),
                                                           [pk], ['vaug%d' % j]))
                            proj_tm(hT, hk, sub, wg, 'wg', 0, 16, pvr,
                                    lambda p_, pk, j=j: cp('dve', gtm[:, j, :], p_, [pk], ['gtm']))
                    if CSTOP == 1:
                        return
                    cvt = [sb(p1, "cvt%d" % i, [128, 512], F32) for i in range(2)]
                    cvr = Rot([("cvt%d" % i, cvt[i]) for i in range(2)])
                    cst = [sb(p1, "cst%d" % i, [128, 512], BF16) for i in range(2)]
                    csr = Rot([("cst%d" % i, cst[i]) for i in range(2)])
                    r2t = [sb(p1, "r2t%d" % i, [128, 512], F32) for i in range(2)]
                    r2r = Rot([("r2t%d" % i, r2t[i]) for i in range(2)])
                    r3t = [sb(p1, "r3t%d" % i, [128, 512], F32) for i in range(2)]
                    r3r = Rot([("r3t%d" % i, r3t[i]) for i in range(2)])
                    for oc in range(4):
                        scl = 0.125 if oc >= 2 else 1.0
                        for (g0, g1) in ((0, TL), (TL, T)):
                            for t0 in range(g0, g1, 512):
                                n = min(512, g1 - t0)
                                ck, ct = cvr.next()
                                ts('dve', ct[:, :n], RAW[:, oc, t0:t0 + n], wcv[:, l, oc, 1:2], None, ALU.mult, None,
                                   ['RAW%d' % oc, 'wcv'], [ck])
                                a = 1 if t0 == g0 else 0
                                stt(ct[:, a:n], RAW[:, oc, t0 + a - 1:t0 + n - 1], wcv[:, l, oc, 0:1], ct[:, a:n],
                                    ALU.mult, ALU.add, ['RAW%d' % oc, 'wcv', ck], [ck])
                                bnd = n - 1 if t0 + n == g1 else n
                                stt(ct[:, 0:bnd], RAW[:, oc, t0 + 1:t0 + 1 + bnd], wcv[:, l, oc, 2:3], ct[:, 0:bnd],
                                    ALU.mult, ALU.add, ['RAW%d' % oc, 'wcv', ck], [ck])
                                if g0 == 0:
                                    sk, cs_ = csr.next()
                                    act(cs_[:, :n], ct[:, :n], AF.Silu, [ck], [sk])
                                    pk, pt = pqr.next()
                                    mm(pt[:, :n], rm[:], cs_[:, :n], True, True, ['rm', sk], [pk])
                                    k2, t2 = r2r.next()
                                    stt(t2[:, :n], pt[:, :n], scl, rope[:, 1, t0:t0 + n], ALU.mult, ALU.mult, [pk, 'rope'], [k2])
                                    k3, t3 = r3r.next()
                                    stt(t3[:, :n], cs_[:, :n], scl, rope[:, 0, t0:t0 + n], ALU.mult, ALU.mult, [sk, 'rope'], [k3])
                                    tt('pool', QK[:, oc, t0:t0 + n], t2[:, :n], t3[:, :n], ALU.add, [k2, k3], ['QK%d' % oc])
                                else:
                                    act(QK[:, oc, t0:t0 + n], ct[:, :n], AF.Silu, [ck], ['QK%d' % oc], scale=1.0)
                                    if scl != 1.0:
                                        ts('dve', QK[:, oc, t0:t0 + n], QK[:, oc, t0:t0 + n], scl, None, ALU.mult, None,
                                           ['QK%d' % oc], ['QK%d' % oc])
                    for j in range(18):
                        pk, pt = pqr.next()
                        for kc in range(2):
                            mm(pt[:, kc * 128:(kc + 1) * 128], QK[:, 2 + kc, 128 * j:128 * j + 128], identb[:], True, True,
                               ['QK%d' % (2 + kc), 'identb'], [pk])
                        cp('dve', ktm[:, j, :], pt[:, 0:256], [pk], ['ktm%d' % j])
                S.barrier()
                if CSTOP == 2:
                    return
                R3 = sb(ph, "R3", [128, 18 * 256], F32)
                hsum = R3[:].rearrange("p (j f) -> p j f", f=256)
                colq = [sb(ph, "colq%d" % i, [128, 18, 12], F32) for i in range(2)]
                dcol = [sb(ph, "dcol%d" % i, [128, 2, 18], F32) for i in range(2)]
                with contextlib.ExitStack() as p3:
                    R1f = R1
                    rI = R1f[0:4, 0:T]
                    rF = R1f[0:4, T:2 * T]
                    rG = R3[0:4, 0:T]
                    rA = R3[0:4, T:2 * T]
                    rrow = sb(p3, "rrow", [4, 20], F32)
                    drow = sb(p3, "drow", [4, 18], F32)
                    colraw = sb(p3, "colraw", [128, 18, 12], F32)
                    prw = [ps(p3, "prw%d" % i, [128, 512]) for i in range(2)]
                    pcol = ps(p3, "pcol", [128, 18, 12])
                    pcol2 = ps(p3, "pcol2", [128, 18, 12])
                    pd = ps(p3, "pd", [128, 2, 18])
                    for dr in range(2):
                        tr_m = ident if dr == 0 else jrev
                        trk = 'ident' if dr == 0 else 'jrev'
                        for g in range(5):
                            idxs = list(range(4 * g, min(4 * g + 4, 18)))
                            for qi, (dst, pw) in enumerate(((rI, prw[0]), (rF, prw[1]))):
                                for ii, idx in enumerate(idxs):
                                    j = ORD[dr][idx]
                                    c0 = dr * 8 + qi * 4
                                    mm(pw[0:4, ii * 128:(ii + 1) * 128], gtm[:, j, c0:c0 + 4], tr_m[:], True, True,
                                       ['gtm', trk], ['prw%d' % qi])
                                n = 128 * len(idxs)
                                ts('dve', dst[:, 512 * g:512 * g + n], pw[0:4, 0:n], bgs[:, l, dr * 2 + qi:dr * 2 + qi + 1], None,
                                   ALU.add, None, ['prw%d' % qi, 'bgs'], ['row%d' % qi])
                        act(rF, rF, AF.Exp, ['row1'], ['row1'], scale=-1.0)
                        act(rF, rF, AF.Ln, ['row1'], ['row1'], bias=onesf[0:4, 0:1])
                        S.op('dve', lambda e: e.tensor_tensor_scan(out=rG, data0=onesf[0:4, 0:1].broadcast_to([4, T]), data1=rF,
                                                                   initial=0.0, op0=ALU.mult, op1=ALU.add),
                             reads=['row1', 'onesf'], writes=['row2'])
                        tt('dve', rI, rI, rG, ALU.add, ['row0', 'row2'], ['row0'])
                        S.op('dve', lambda e: e.tensor_tensor_scan(out=rA, data0=onesf[0:4, 0:1].broadcast_to([4, T]), data1=rI,
                                                                   initial=0.0, op0=ALU.mult, op1=ALU.max),
                             reads=['row0', 'onesf'], writes=['row3'])
                        S.op('dve', lambda e: e.memset(rrow[:, 0:1], 0.0), writes=['rrow'])
                        cp('dve', rrow[:, 1:19], rA.rearrange("p (j t) -> p j t", t=128)[:, :, 127], ['row3'], ['rrow'])
                        rfull = rrow[:, 0:18].unsqueeze(2).broadcast_to([4, 18, 128])
                        v3 = lambda r_: r_.rearrange("p (j t) -> p j t", t=128)
                        tt('dve', v3(rI), v3(rI), rfull, ALU.subtract, ['row0', 'rrow'], ['row0'])
                        act(rI, rI, AF.Exp, ['row0'], ['row0'])
                        tt('dve', rG, rG, rA, ALU.subtract, ['row2', 'row3'], ['row2'])
                        act(rG, rG, AF.Exp, ['row2'], ['row2'])
                        tt('dve', v3(rA), rfull, v3(rA), ALU.subtract, ['row3', 'rrow'], ['row3'])
                        act(rA, rA, AF.Exp, ['row3'], ['row3'])
                        tt('dve', drow[:, 0:18], rrow[:, 0:18], rrow[:, 1:19], ALU.subtract, ['rrow'], ['drow'])
                        act(drow[:], drow[:], AF.Exp, ['drow'], ['drow'])
                        for idx in range(18):
                            for qi, rw in enumerate((rI, rA, rG)):
                                mm(pcol[:, idx, qi * 4:(qi + 1) * 4], rw[:, idx * 128:(idx + 1) * 128], ident[0:4, 0:4], True, True,
                                   ['row0', 'row2', 'row3', 'ident'], ['pcol'])
                        if dr == 0:
                            cp('dve', colq[0][:], pcol[:], ['pcol'], ['colq0'])
                        else:
                            cp('dve', colraw[:], pcol[:], ['pcol'], ['colraw'])
                            mm(pcol2[:].rearrange("p a b -> p (a b)"), jrev[:], colraw[:].rearrange("p a b -> p (a b)"), True, True,
                               ['colraw', 'jrev'], ['pcol2'])
                            cp('dve', colq[1][:], pcol2[:], ['pcol2'], ['colq1'])
                        for pr in range(2):
                            mm(pd[:, pr, :], sel[:, pr, :], drow[:], True, True, ['sel', 'drow'], ['pd'])
                        cp('dve', dcol[dr][:], pd[:], ['pd'], ['dcol%d' % dr])
                S.barrier()
                if CSTOP == 3:
                    return
                with contextlib.ExitStack() as p4:
                    qz4_ = [sb(p4, "qzc%d" % i, [128, 256], BF16) for i in range(2)]
                    qzr4 = Rot([("qzc%d" % i, qz4_[i]) for i in range(2)])
                    for i in range(2):
                        S.op('pool', lambda e, i=i: e.memset(qz4_[i][:], 0.0), writes=['qzc%dz' % i, 'qzc%d' % i])
                    vs_ = [sb(p4, "vs%d" % i, [128, 4, 80], BF16) for i in range(2)]
                    vsr = Rot([("vs%d" % i, vs_[i]) for i in range(2)])
                    wt_ = [sb(p4, "wt%d" % i, [128, 4, 128], BF16) for i in range(2)]
                    wtr = Rot([("wt%d" % i, wt_[i]) for i in range(2)])
                    sm = [sb(p4, "sm%d" % i, [128, 16], F32) for i in range(2)]
                    smr = Rot([("sm%d" % i, sm[i]) for i in range(2)])
                    ho = [sb(p4, "ho%d" % i, [128, 4, 64], F32) for i in range(1)]
                    hor = Rot([("ho%d" % i, ho[i]) for i in range(1)])
                    stp = [ps(p4, "stp%d" % i, [128, 512]) for i in range(2)]
                    stpr = Rot([("stp%d" % i, stp[i]) for i in range(2)])
                    opp = [ps(p4, "opp%d" % i, [128, 4, 80]) for i in range(2)]
                    oppr = Rot([("opp%d" % i, opp[i]) for i in range(2)])
                    upp = [ps(p4, "upp%d" % i, [128, 2, 80]) for i in range(2)]
                    uppr = Rot([("upp%d" % i, upp[i]) for i in range(2)])
                    CstD, CbfD, Cbf4D = {}, {}, {}
                    for dr in range(2):
                        CstD[dr] = sb(p4, "CstD%d" % dr, [128, 2, 80], F32)
                        CbfD[dr] = sb(p4, "CbfD%d" % dr, [128, 4, 80], BF16)
                        Cbf4D[dr] = CbfD[dr][:].rearrange("p (c u) d -> p c u d", u=2)
                        S.op('dve', lambda e, dr=dr: e.memset(CstD[dr][:], 0.0), writes=['Cst%d' % dr])
                        S.op('dve', lambda e, dr=dr: e.memset(CbfD[dr][:], 0.0), writes=['Cbf%d' % dr])
                    written = set()
                    for idx in range(18):
                        for dr in (1, 0):
                            Cst, Cbf, Cbf4 = CstD[dr], CbfD[dr], Cbf4D[dr]
                            CK, BK = 'Cst%d' % dr, 'Cbf%d' % dr
                            j = ORD[dr][idx]
                            tok = slice(128 * j, 128 * j + 128)
                            emit = not (last and j >= 16)
                            vk, vs = vsr.next()
                            tt('dve', vs[:, :, 0:65], vaug[:, j, :, :], colq[dr][:, idx, 0:4].unsqueeze(2).broadcast_to([128, 4, 65]),
                               ALU.mult, ['vaug%d' % j, 'vaug1', 'colq%d' % dr], [vk])
                            if emit:
                                sk, sp_ = stpr.next()
                                for c2 in range(2):
                                    zk, qz = qzr4.next()
                                    act(qz[0:64, 0:128], QK[0:64, c2, tok], AF.Copy, ['QK', zk + 'z'], [zk])
                                    act(qz[64:128, 128:256], QK[64:128, c2, tok], AF.Copy, ['QK', zk + 'z'], [zk])
                                    mm(sp_[:, c2 * 256:(c2 + 1) * 256], QK[:, 2 + c2, tok], qz[:], True, True, ['QK', zk], [sk])
                                wk, wt = wtr.next()
                                tt('dve', wt[:], sp_[:].rearrange("p (h t) -> p h t", h=4),
                                   masks[:, dr, :].unsqueeze(1).broadcast_to([128, 4, 128]), ALU.mult, [sk, 'masks'], [wk])
                                ok_, op_ = oppr.next()
                                for h in range(4):
                                    c2, po = h // 2, (h % 2) * 64
                                    mm(op_[:, h, 0:65], wt[:, h, :], vs[:, h, 0:65], True, False, [wk, vk], [ok_])
                                    mm(op_[:, h, 0:65], QK[:, c2, tok], Cbf[:, h, 0:65], False, True,
                                       ['QK', BK], [ok_])
                                mk, m_ = smr.next()
                                eo = colq[dr][:, idx, 4:8]
                                eb = colq[dr][:, idx, 8:12]
                                tt('dve', m_[:, 0:4], op_[:, :, 64], eo, ALU.mult, [ok_, 'colq%d' % dr], [mk])
                                stt(m_[:, 4:8], m_[:, 0:4], -1.0, m_[:, 0:4], ALU.mult, ALU.max, [mk], [mk])
                                tt('dve', m_[:, 4:8], m_[:, 4:8], eb, ALU.max, [mk, 'colq%d' % dr], [mk])
                                S.op('dve', lambda e, m_=m_: e.reciprocal(out=m_[:, 8:12], in_=m_[:, 4:8]), reads=[mk], writes=[mk])
                                tt('dve', m_[:, 12:16], m_[:, 8:12], eo, ALU.mult, [mk, 'colq%d' % dr], [mk])
                                hs_j = hsum[:, j, :].rearrange("p (h d) -> p h d", h=4)
                                rcb = m_[:, 12:16].unsqueeze(2).broadcast_to([128, 4, 64])
                                if j not in written:
                                    written.add(j)
                                    tt('dve', hs_j, op_[:, :, 0:64], rcb, ALU.mult, [ok_, mk], ['hsum%d' % j])
                                else:
                                    hk2, h2 = hor.next()
                                    tt('dve', h2[:], op_[:, :, 0:64], rcb, ALU.mult, [ok_, mk], [hk2])
                                    tt('pool', hs_j, hs_j, h2[:], ALU.add, [hk2, 'hsum%d' % j], ['hsum%d' % j])
                            if idx < 17:
                                uk, up = uppr.next()
                                for pr in range(2):
                                    for hh in range(2):
                                        mm(up[hh * 64:(hh + 1) * 64, pr, 0:65], ktm[:, j, pr * 128 + hh * 64:pr * 128 + hh * 64 + 64],
                                           vs[:, 2 * pr + hh, 0:65], True, True, ['ktm%d' % j, vk], [uk])
                                tt('dve', Cst[:, :, 0:65], up[:, :, 0:65], Cst[:, :, 0:65], ALU.add, [uk, CK], [CK])
                                tt('dve', Cst[:, :, 0:65], Cst[:, :, 0:65], dcol[dr][:, :, idx].unsqueeze(2).broadcast_to([128, 2, 65]), ALU.mult,
                                   [CK, 'dcol%d' % dr], [CK])
                                cp('pool', Cbf4[0:64, :, 0, 0:65], Cst[0:64, :, 0:65], [CK], [BK])
                                cp('pool', Cbf4[64:128, :, 1, 0:65], Cst[64:128, :, 0:65], [CK], [BK])
                S.barrier()
                if CSTOP == 4:
                    return
                with contextlib.ExitStack() as p5:
                    OT = R2[:, 0:2, :]
                    wload(wv[:], 'wv', l, OFF_C + 768, 256)
                    pq = [ps(p5, "pq5_%d" % i, [128, 512]) for i in range(2)]
                    pqr = Rot([("pq5_%d" % i, pq[i]) for i in range(2)])
                    nb = 8 if last else 9
                    for b in range(nb):
                        t0 = 256 * b
                        v = s if b < 8 else 2
                        hk, hT = make_h(l, 0, t0, 256, v)
                        for oc in range(2):
                            proj_fm(hT, hk, 256, wv, 'wv', oc * 128, pqr,
                                    lambda p_, pk, oc=oc: act(OT[:, oc, t0:t0 + 256], p_, AF.Sigmoid, [pk], ['OT%d' % oc]))
                    st_ = [sb(p5, "lst%d" % i, [128, 16], F32) for i in range(2)]
                    str_ = Rot([("lst%d" % i, st_[i]) for i in range(2)])
                    xc_ = [sb(p5, "xc%d" % i, [128, 4, 64], F32) for i in range(2)]
                    xcr = Rot([("xc%d" % i, xc_[i]) for i in range(2)])
                    sq_ = [sb(p5, "xsq%d" % i, [128, 4, 64], F32) for i in range(2)]
                    sqr_ = Rot([("xsq%d" % i, sq_[i]) for i in range(2)])
                    for j in range(16 if last else 18):
                        tok = slice(128 * j, 128 * j + 128)
                        hs_j = hsum[:, j, :].rearrange("p (h d) -> p h d", h=4)
                        lk, ls = str_.next()
                        S.op('dve', lambda e, ls=ls, hs_j=hs_j: e.reduce_sum(out=ls[:, 0:4], in_=hs_j, axis=AX.X),
                             reads=['hsum%d' % j], writes=[lk])
                        ts('dve', ls[:, 0:4], ls[:, 0:4], 1.0 / 64, None, ALU.mult, None, [lk], [lk])
                        xk, xc = xcr.next()
                        tt('dve', xc[:], hs_j, ls[:, 0:4].unsqueeze(2).broadcast_to([128, 4, 64]), ALU.subtract,
                           ['hsum%d' % j, lk], [xk])
                        qk_, xq = sqr_.next()
                        tt('pool', xq[:], xc[:], xc[:], ALU.mult, [xk], [qk_])
                        S.op('dve', lambda e, ls=ls, xq=xq: e.reduce_sum(out=ls[:, 4:8], in_=xq[:], axis=AX.X),
                             reads=[qk_], writes=[lk])
                        ts('dve', ls[:, 4:8], ls[:, 4:8], 1.0 / 64, EPS, ALU.mult, ALU.add, [lk], [lk])
                        act(ls[:, 8:12], ls[:, 4:8], AF.Sqrt, [lk], [lk])
                        S.op('dve', lambda e, ls=ls: e.reciprocal(out=ls[:, 12:16], in_=ls[:, 8:12]), reads=[lk], writes=[lk])
                        tt('dve', xc[:], xc[:], ls[:, 12:16].unsqueeze(2).broadcast_to([128, 4, 64]), ALU.mult, [xk, lk], [xk])
                        tt('pool', xq[:].rearrange("p h d -> p (h d)"), xc[:].rearrange("p h d -> p (h d)"), gmb[:], ALU.mult,
                           [xk, 'gmb', qk_], [qk_])
                        pk, pt = pqr.next()
                        for kc in range(2):
                            mm(pt[:, kc * 128:(kc + 1) * 128], xq[:].rearrange("p h d -> p (h d)")[:, kc * 128:(kc + 1) * 128],
                               ident[:], True, True, [qk_, 'ident'], [pk])
                        tt('dve', CAT[:, 4:6, tok], pt[:, 0:256].rearrange("p (c t) -> p c t", c=2), OT[:, :, tok], ALU.mult,
                           [pk, 'OT0', 'OT1'], ['CATc'])

        def mixer_a(s, l, last):
            with contextlib.ExitStack() as ph:
                QA = sb(ph, "QA", [128, 2, T], BF16)
                KA = sb(ph, "KA", [128, 2, T], BF16)
                VA = sb(ph, "VA", [128, 18, 256], BF16)
                wq = sb(ph, "wqa", [128, 8, 768], BF16)
                wload(wq[:], 'wqa', l, OFF_A, 768)
                with contextlib.ExitStack() as p1:
                    pq = [ps(p1, "pqa%d" % i, [128, 512]) for i in range(2)]
                    pqr = Rot([("pqa%d" % i, pq[i]) for i in range(2)])
                    for b in range(9):
                        t0 = 256 * b
                        v = s if b < 8 else 2
                        hk, hT = make_h(l, 0, t0, 256, v)
                        for oc in range(4):
                            if oc < 2 and last and b == 8:
                                continue
                            dst = QA if oc < 2 else KA
                            proj_fm(hT, hk, 256, wq, 'wqa', oc * 128, pqr,
                                    lambda p_, pk, oc=oc, dst=dst: act(dst[:, oc % 2, t0:t0 + 256], p_, AF.Copy, [pk], ['QKA']))
                        for sub in range(2):
                            j = 2 * b + sub
                            proj_tm(hT, hk, sub, wq, 'wqa', 512, 256, pqr,
                                    lambda p_, pk, j=j: cp('dve', VA[:, j, :], p_, [pk], ['VA']))
                S.barrier()
                with contextlib.ExitStack() as p2:
                    bt = [sb(p2, "bt%d" % i, [128, 1280], F32) for i in range(2)]
                    btr = Rot([("bt%d" % i, bt[i]) for i in range(2)])
                    tf = [sb(p2, "tf%d" % i, [128, 1280], F32) for i in range(2)]
                    tfr = Rot([("tf%d" % i, tf[i]) for i in range(2)])
                    PT = [sb(p2, "PT%d" % i, [128, 7, 256], BF16) for i in range(2)]
                    ptr_ = Rot([("PT%d" % i, PT[i]) for i in range(2)])
                    rc_ = [sb(p2, "rca%d" % i, [128, 256], F32) for i in range(2)]
                    rcr = Rot([("rca%d" % i, rc_[i]) for i in range(2)])
                    sps = ps(p2, "sps", [128, 8, 256])
                    ov = [ps(p2, "ov%d" % i, [128, 512]) for i in range(2)]
                    ovr = Rot([("ov%d" % i, ov[i]) for i in range(2)])
                    qz_ = [sb(p2, "qz%d" % i, [128, 256], BF16) for i in range(2)]
                    qzr = Rot([("qz%d" % i, qz_[i]) for i in range(2)])
                    for i in range(2):
                        S.op('pool', lambda e, i=i: e.memset(qz_[i][:], 0.0), writes=['qz%dz' % i, 'qz%d' % i])
                    nq = 16 if last else 18
                    qts = list(range(nq))
                    if ASTOP == 1:
                        qts = []
                    if ASTOP == 2:
                        qts = [16, 17]
                    if ASTOP == 3:
                        qts = [5]
                    for qt in qts:
                        ctxq = qt >= 16
                        tq = slice(128 * qt, 128 * qt + 128)
                        if ctxq:
                            kch = [16, 17]
                        else:
                            cb = min(max(qt - 2, 0), 11)
                            kch = list(range(cb, cb + 5)) + [16, 17]
                            typ = {0: 0, 1: 1, 14: 3, 15: 4}.get(qt, 2)
                        nk = len(kch)
                        for pr in range(2):
                            if not ctxq:
                                bk, bias_t = btr.next()
                                dma('sp', bias_t[:], natb[l, typ, pr], writes=[bk])
                            zk, qz = qzr.next()
                            cp('pool', qz[0:64, 0:128], QA[0:64, pr, tq], ['QKA', zk + 'z'], [zk])
                            cp('pool', qz[64:128, 128:256], QA[64:128, pr, tq], ['QKA', zk + 'z'], [zk])
                            for i, kc_ in enumerate(kch):
                                mm(sps[:, i, :], KA[:, pr, 128 * kc_:128 * kc_ + 128], qz[:], True, True, ['QKA', zk], ['sps%d' % (i // 2)])
                            pk_, P_ = ptr_.next()
                            if not ctxq:
                                fk, tfl = tfr.next()
                                for (a0, a1) in ((0, 2), (2, 4), (4, 5)):
                                    stt(tfl[:, a0 * 256:a1 * 256], sps[:, a0:a1, :].rearrange("p a b -> p (a b)"), 0.125,
                                        bias_t[:, a0 * 256:a1 * 256], ALU.mult, ALU.add, ['sps%d' % (a0 // 2), bk], [fk])
                                act(P_[:, 0:5, :].rearrange("p a b -> p (a b)"), tfl[:], AF.Exp, [fk], [pk_])
                                for a0 in (5, 6):
                                    act(P_[:, a0, :], sps[:, a0, :], AF.Exp, ['sps%d' % (a0 // 2)], [pk_], scale=0.125)
                            else:
                                act(P_[:, 0:2, :].rearrange("p a b -> p (a b)"), sps[:, 0:2, :].rearrange("p a b -> p (a b)"),
                                    AF.Exp, ['sps0'], [pk_], scale=0.125)
                            ok_, o_ = ovr.next()
                            dk_ = ok_
                            for i, kc_ in enumerate(kch):
                                mm(o_[:, 0:256], VA[:, kc_, pr * 128:(pr + 1) * 128], P_[:, i, :], i == 0, i == nk - 1,
                                   ['VA', pk_], [ok_])
                            for i, kc_ in enumerate(kch):
                                mm(o_[:, 256:512], onesb[:], P_[:, i, :], i == 0, i == nk - 1, ['onesb', pk_], [dk_])
                            rk_, r_ = rcr.next()
                            S.op('dve', lambda e, r_=r_, o_=o_: e.reciprocal(out=r_[:], in_=o_[:, 256:512]), reads=[dk_], writes=[rk_])
                            for hh in range(2):
                                po = hh * 64
                                tt('dve', CAT[po:po + 64, pr, tq], o_[po:po + 64, hh * 128:(hh + 1) * 128],
                                   r_[po:po + 64, hh * 128:(hh + 1) * 128], ALU.mult, [ok_, rk_], ['CATa'])

        def mixer_b(s, l, last):
            with contextlib.ExitStack() as ph:
                UB = sb(ph, "UB", [128, 2, T], BF16)
                ZB = sb(ph, "ZB", [128, 18, 256], BF16)
                wb_ = sb(ph, "wbb", [128, 8, 512], BF16)
                wsb = sb(ph, "wsb", [128, 4, 128], BF16)
                bsb = sb(ph, "bsb", [128, 2, 128], F32)
                ggb = sb(ph, "ggb", [128, 256], F32)
                wload(wb_[:], 'wbb', l, OFF_B, 512)
                dma('pool', wsb[:], wsT[l], writes=['wsb'])
                dma('sp', bsb[:], bsT[l], writes=['bsb'])
                dma('sp', ggb[:], ggm[l, :].partition_broadcast(128), writes=['ggb'])
                pq = [ps(ph, "pqb%d" % i, [128, 512]) for i in range(2)]
                pqr = Rot([("pqb%d" % i, pq[i]) for i in range(2)])
                zf = [sb(ph, "zf%d" % i, [128, 256], F32) for i in range(2)]
                zfr = Rot([("zf%d" % i, zf[i]) for i in range(2)])
                zq = [sb(ph, "zq%d" % i, [128, 256], F32) for i in range(2)]
                zqr = Rot([("zq%d" % i, zq[i]) for i in range(2)])
                zs = [sb(ph, "zs%d" % i, [128, 4], F32) for i in range(2)]
                zsr = Rot([("zs%d" % i, zs[i]) for i in range(2)])
                nb = 8 if last else 9
                for b in range(nb):
                    t0 = 256 * b
                    v = s if b < 8 else 2
                    hk, hT = make_h(l, 0, t0, 256, v)
                    for oc in range(2):
                        proj_fm(hT, hk, 256, wb_, 'wbb', oc * 128, pqr,
                                lambda p_, pk, oc=oc: act(UB[:, oc, t0:t0 + 256], p_, AF.Gelu_apprx_tanh, [pk], ['UB']))
                    for sub in range(2):
                        j = 2 * b + sub

                        def ev(p_, pk, j=j):
                            fk, z_ = zfr.next()
                            act(z_[:], p_, AF.Gelu_apprx_tanh, [pk], [fk])
                            qk_, q_ = zqr.next()
                            tt('pool', q_[:], z_[:], z_[:], ALU.mult, [fk], [qk_])
                            sk, s_ = zsr.next()
                            S.op('dve', lambda e: e.reduce_sum(out=s_[:, 0:1], in_=q_[:], axis=AX.X), reads=[qk_], writes=[sk])
                            ts('dve', s_[:, 0:1], s_[:, 0:1], 1.0 / 256, EPS, ALU.mult, ALU.add, [sk], [sk])
                            act(s_[:, 1:2], s_[:, 0:1], AF.Sqrt, [sk], [sk])
                            S.op('dve', lambda e: e.reciprocal(out=s_[:, 2:3], in_=s_[:, 1:2]), reads=[sk], writes=[sk])
                            stt(ZB[:, j, :], z_[:], s_[:, 2:3], ggb[:], ALU.mult, ALU.mult, [fk, sk, 'ggb'], ['ZB%d' % j])
                        proj_tm(hT, hk, sub, wb_, 'wbb', 256, 256, pqr, ev)
                mt = [sb(ph, "mt%d" % i, [128, 128], F32) for i in range(2)]
                mtr = Rot([("mt%d" % i, mt[i]) for i in range(2)])
                for j in range(16 if last else 18):
                    tok = slice(128 * j, 128 * j + 128)
                    for pr in range(2):
                        pk, pt = pqr.next()
                        for hh in range(2):
                            po = hh * 64
                            mm(pt[po:po + 64, 0:128], ZB[:, j, pr * 128 + po:pr * 128 + po + 64], wsb[:, 2 * pr + hh, :],
                               True, True, ['ZB%d' % j, 'wsb'], [pk])
                        mk, m_ = mtr.next()
                        tt('dve', m_[:], pt[:, 0:128], bsb[:, pr, :], ALU.add, [pk, 'bsb'], [mk])
                        tt('pool', CAT[:, 2 + pr, tok], m_[:], UB[:, pr, tok], ALU.mult, [mk, 'UB'], ['CATb'])

        def mixer_d(s, l, last):
            with contextlib.ExitStack() as ph:
                FT = sb(ph, "FT", [128, 18, 256], BF16)
                wfi = sb(ph, "wfi", [128, 8, 256], BF16)
                wfn = sb(ph, "wfn", [128, 2, 256], BF16)
                wload(wfi[:], 'wfi', l, OFF_D, 256)
                dma('pool', wfn[:], w_fnet[l].rearrange("(kc p) n -> p kc n", p=128), writes=['wfn'])
                pq = [ps(ph, "pqd%d" % i, [128, 512]) for i in range(2)]
                pqr = Rot([("pqd%d" % i, pq[i]) for i in range(2)])
                nb = 8 if last else 9
                for b in range(nb):
                    t0 = 256 * b
                    v = s if b < 8 else 2
                    hk, hT = make_h(l, 0, t0, 256, v)
                    for sub in range(2):
                        j = 2 * b + sub
                        proj_tm(hT, hk, sub, wfi, 'wfi', 0, 256, pqr,
                                lambda p_, pk, j=j: cp('dve', FT[:, j, :], p_, [pk], ['FT']))
                dc_ = [sb(ph, "dc%d" % i, [128, 16, 256], BF16) for i in range(2)]
                ds_ = [sb(ph, "ds%d" % i, [128, 16, 256], BF16) for i in range(2)]
                YC = [sb(ph, "YC%d" % i, [128, 2, 256], BF16) for i in range(2)]
                YS = [sb(ph, "YS%d" % i, [128, 2, 256], BF16) for i in range(2)]
                SPc = [sb(ph, "SP%d" % i, [128, 2, 256], BF16) for i in range(2)]
                ycp = ps(ph, "ycp", [128, 512])
                ysp = ps(ph, "ysp", [128, 512])
                spp = ps(ph, "spp", [128, 512])
                dpp = ps(ph, "dpp", [128, 512])
                it = 0
                segs = [(0, 16, TL, c_dftc_l, c_dfts_l)]
                if not last:
                    segs.append((16, 2, 256, c_dftc_c, c_dfts_c))
                for (jb, nchk, nT, mc, ms) in segs:
                    scale = float(1.0 / np.sqrt(64.0 * nT))
                    for tb in range(nT // 256):
                        bb = it % 2
                        it += 1
                        dma('sp', dc_[bb][:, 0:nchk, :], mc[:, tb * 256:(tb + 1) * 256].rearrange("(c p) n -> p c n", p=128),
                            writes=['dc%d' % bb])
                        dma('sp', ds_[bb][:, 0:nchk, :], ms[:, tb * 256:(tb + 1) * 256].rearrange("(c p) n -> p c n", p=128),
                            writes=['ds%d' % bb])
                        for fc in range(2):
                            for i in range(nchk):
                                mm(ycp[:, 0:256], FT[:, jb + i, fc * 128:(fc + 1) * 128], dc_[bb][:, i, :], i == 0, i == nchk - 1,
                                   ['FT', 'dc%d' % bb], ['ycp'])
                            for i in range(nchk):
                                mm(ysp[:, 0:256], FT[:, jb + i, fc * 128:(fc + 1) * 128], ds_[bb][:, i, :], i == 0, i == nchk - 1,
                                   ['FT', 'ds%d' % bb], ['ysp'])
                            act(YC[bb][:, fc, :], ycp[:, 0:256], AF.Copy, ['ycp'], ['YC%d' % bb])
                            cp('dve', YS[bb][:, fc, :], ysp[:, 0:256], ['ysp'], ['YS%d' % bb])
                            mm(spp[:, 0:256], blk[:, 0, :], YC[bb][:, fc, :], True, False, ['blk', 'YC%d' % bb], ['spp'])
                            mm(spp[:, 0:256], blk[:, 1, :], YS[bb][:, fc, :], False, True, ['blk', 'YS%d' % bb], ['spp'])
                            act(SPc[bb][:, fc, :], spp[:, 0:256], AF.Copy, ['spp'], ['SP%d' % bb], scale=scale)
                        for oc in range(2):
                            for fc in range(2):
                                mm(dpp[:, 0:256], wfn[:, fc, oc * 128:(oc + 1) * 128], SPc[bb][:, fc, :], fc == 0, fc == 1,
                                   ['wfn', 'SP%d' % bb], ['dpp'])
                            t0 = 128 * jb + tb * 256
                            cp('dve', CAT[:, 6 + oc, t0:t0 + 256], dpp[:, 0:256], ['dpp'], ['CATd'])

        class _Stop(Exception):
            pass

        def body():
          if dbg:
              S.op('pool', lambda e: e.memset(CAT[:], 0.0), writes=['CAT'])
              S.barrier()
          for s in range(2):
            for c in range(8):
                dma('sp', X[:, c, :], xin[s, :, c, :], writes=['X%d' % c])
            for l in range(DEPTH):
                last = (l == DEPTH - 1)
                compute_rs()
                S.barrier()
                if 'C' not in SKIP:
                    mixer_c(s, l, last)
                    S.barrier()
                if 'A' not in SKIP:
                    mixer_a(s, l, last)
                    S.barrier()
                if 'B' not in SKIP:
                    mixer_b(s, l, last)
                    S.barrier()
                if 'D' not in SKIP:
                    mixer_d(s, l, last)
                    S.barrier()
                if dbg == ('cat', s, l):
                    for c in range(8):
                        dma('pool', dbg_out[:, c, :], CAT[:, c, :], reads=['CAT'])
                    return
                S.barrier()
                for _once in ([] if 'wout' in SKIP else [0]):
                  with contextlib.ExitStack() as ph:
                    wo = sb(ph, "wo", [128, 8, D], BF16)
                    yps = [ps(ph, "yps%d" % i, [128, 512]) for i in range(2)]
                    for kc in range(8):
                        dma('pool', wo[:, kc, :], w_out[l, kc * 128:(kc + 1) * 128, :], writes=['wo'])
                    it = 0
                    nblk = 4 if last else 5
                    for b in range(nblk):
                        t0 = b * 512
                        n = 512 if b < 4 else 256
                        v = s if b < 4 else 2
                        for oc in range(8):
                            pb = it % 2
                            it += 1
                            for kc in range(8):
                                mm(yps[pb][:, :n], wo[:, kc, oc * 128:(oc + 1) * 128], CAT[:, kc, t0:t0 + n],
                                   kc == 0, kc == 7, ['wo', 'CAT'], ['yps%d' % pb])
                            stt(X[:, oc, t0:t0 + n], yps[pb][:, :n], modap(l, 2, oc, v), X[:, oc, t0:t0 + n],
                                ALU.mult, ALU.add, ['yps%d' % pb, 'modT', 'X%d' % oc], ['X%d' % oc])
                S.barrier()
                for _once in ([] if 'ffn' in SKIP else [0]):
                  with contextlib.ExitStack() as ph:
                    nblk = 4 if last else 5
                    H2 = CAT
                    for b2 in range(8 if last else 9):
                        t0 = b2 * 256
                        v = s if b2 < 8 else 2
                        make_h(l, 1, t0, 256, v, dst=('H2_%d' % (b2 // 2), H2[:, :, t0:t0 + 256]))
                    w1 = [sb(ph, "w1_%d" % i, [128, 8, 512], BF16) for i in range(2)]
                    w2 = [sb(ph, "w2_%d" % i, [128, 4, D], BF16) for i in range(2)]
                    ag = [sb(ph, "ag%d" % i, [128, 4, 512], BF16) for i in range(2)]
                    rl = [sb(ph, "rl%d" % i, [128, 512], F32) for i in range(2)]
                    fps = [ps(ph, "fps%d" % i, [128, 512]) for i in range(2)]
                    ops_ = [ps(ph, "ops%d" % i, [128, 512]) for i in range(2)]
                    i1 = i2 = i3 = 0
                    for g in range(8):
                        wb = g % 2
                        dma('pool', w1[wb][:], w_ff1[l, :, g * 512:(g + 1) * 512].rearrange("(kc p) n -> p kc n", p=128),
                            writes=['w1_%d' % wb])
                        dma('pool', w2[wb][:], w_ff2[l, g * 512:(g + 1) * 512, :].rearrange("(kc p) n -> p kc n", p=128),
                            writes=['w2_%d' % wb])
                        for b in range(nblk):
                            t0 = b * 512
                            n = 512 if b < 4 else 256
                            v = s if b < 4 else 2
                            ab = i1 % 2
                            i1 += 1
                            for fc in range(4):
                                pb = i2 % 2
                                i2 += 1
                                for kc in range(8):
                                    mm(fps[pb][:, :n], w1[wb][:, kc, fc * 128:(fc + 1) * 128], H2[:, kc, t0:t0 + n],
                                       kc == 0, kc == 7, ['w1_%d' % wb, 'H2_%d' % b], ['fps%d' % pb])
                                act(rl[pb][:, :n], fps[pb][:, :n], AF.Relu, ['fps%d' % pb], ['rl%d' % pb])
                                tt('pool', ag[ab][:, fc, :n], rl[pb][:, :n], rl[pb][:, :n], ALU.mult,
                                   ['rl%d' % pb], ['ag%d_%d' % (ab, fc)])
                            for oc in range(8):
                                pb = i3 % 2
                                i3 += 1
                                for kc in range(4):
                                    mm(ops_[pb][:, :n], w2[wb][:, kc, oc * 128:(oc + 1) * 128], ag[ab][:, kc, :n],
                                       kc == 0, kc == 3, ['w2_%d' % wb, 'ag%d_%d' % (ab, kc)], ['ops%d' % pb])
                                stt(X[:, oc, t0:t0 + n], ops_[pb][:, :n], modap(l, 5, oc, v), X[:, oc, t0:t0 + n],
                                    ALU.mult, ALU.add, ['ops%d' % pb, 'modT', 'X%d' % oc], ['X%d' % oc])
                S.barrier()
                if dbg == ('x', s, l):
                    for c in range(8):
                        dma('sp', dbg_out[:, c, :], X[:, c, :], reads=['X%d' % c])
                    return
            with contextlib.ExitStack() as ph:
                ob = [sb(ph, "ob%d" % i, [128, 256], F32) for i in range(2)]
                it = 0
                for b in range(8):
                    t0 = b * 256
                    pk, pst = ssqrot.next()
                    for c in range(8):
                        sk, sq = sqrot.next()
                        act(sq[:], X[:, c, t0:t0 + 256], AF.Square, ['X%d' % c], [sk])
                        mm(pst[:, 0:256], onesb[:], sq[:], c == 0, c == 7, [sk, 'onesb'], [pk])
                    rk, rs = rsrot.next()
                    act(rs[:], pst[:, 0:256], AF.Sqrt, [pk, 'epsb'], [rk], scale=1.0 / D, bias=epsb[:, 0:1])
                    S.op('dve', lambda e, rs=rs: e.reciprocal(out=rs[:], in_=rs[:]), reads=[rk], writes=[rk])
                    for c in range(8):
                        o = it % 2
                        it += 1
                        stt(ob[o][:], X[:, c, t0:t0 + 256], gv[:, 4, c:c + 1], rs[:], ALU.mult, ALU.mult,
                            ['X%d' % c, 'gv', rk], ['ob%d' % o])
                        dma('sp', outT[s, :, c, t0:t0 + 256], ob[o][:], reads=['ob%d' % o])
            S.barrier()
        body()
        S.finish('sp')
        print("ops", S.nops, "waits", S.nwaits, {e: S.cc[e] for e in S.cc}, {e: S.dc[e] for e in S.dc})
    return nc


_NC_CACHE = {}


def _prep_shared(inp):
    f32 = lambda a: np.ascontiguousarray(np.asarray(a, dtype=np.float32))
    sh = dict(_consts())
    sh['w_ada'] = f32(inp['w_ada'])
    sh['badaT'] = np.ascontiguousarray(np.moveaxis(f32(inp['b_ada']).reshape(2, 48, 128), 2, 0))
    gl = [inp['g_norm_mix'][0], inp['g_norm_ffn'][0], inp['g_norm_mix'][1], inp['g_norm_ffn'][1], inp['g_final']]
    sh['gvec'] = np.ascontiguousarray(np.stack([f32(g).reshape(8, 128).T for g in gl], 1))
    sh['w_in'] = f32(inp['w_in'])
    bgate = f32(inp['b_gate'])
    sh['bg'] = np.ascontiguousarray(bgate.reshape(2, 4, 4).transpose(2, 0, 1))
    wc = f32(inp['w_conv_qk'])
    sh['wconv'] = np.ascontiguousarray(wc.reshape(2, 3, 4, 128).transpose(3, 0, 2, 1))
    rpb = f32(inp['rpb'])
    sh['natb'] = np.ascontiguousarray(np.stack([_nat_bias(rpb[l]).reshape(5, 2, 128, 1280) for l in range(2)], 0))
    sh['wsT'] = np.ascontiguousarray(f32(inp['w_spatial']).transpose(0, 3, 1, 2))
    bs = f32(inp['b_spatial'])
    bsT = np.zeros((2, 128, 2, 128), np.float32)
    for pr in range(2):
        for hh in range(2):
            bsT[:, hh * 64:(hh + 1) * 64, pr, :] = bs[:, 2 * pr + hh, None, :]
    sh['bsT'] = bsT
    sh['ggm'] = f32(inp['g_gmlp'])
    sh['gml'] = f32(inp['g_mlstm'])
    sh['w_fnet'] = f32(inp['w_fnet'])
    sh['w_out'] = f32(inp['w_out'])
    sh['w_ff1'] = f32(inp['w_ff1'])
    sh['w_ff2'] = f32(inp['w_ff2'])
    return sh


def _fmT(a):
    return np.ascontiguousarray(a.T.reshape(8, 128, a.shape[0]).transpose(1, 0, 2))


def kernel(dbg=None, **inp):
    x = np.asarray(inp['x'], np.float32)
    c = np.asarray(inp['c'], np.float32)
    ctx = np.asarray(inp['ctx'], np.float32)
    c_ctx = np.asarray(inp['c_ctx'], np.float32)
    sh = _prep_shared(inp)
    key = repr(dbg)
    if key not in _NC_CACHE:
        _NC_CACHE[key] = build_nc(dbg)
    nc = _NC_CACHE[key]
    in_maps = []
    ncores = int(os.environ.get('MK_CORES', '8'))
    for core in range(ncores):
        m = dict(sh)
        xs = []
        for i in range(2):
            b = 2 * core + i
            xs.append(_fmT(np.concatenate([x[b], ctx[b]], 0)))
        m['xin'] = np.ascontiguousarray(np.stack(xs, 0))
        vecs = [c[2 * core], c[2 * core + 1], c_ctx]
        m['cT'] = np.ascontiguousarray(np.stack([v.reshape(8, 128).T for v in vecs], 2))
        in_maps.append(m)
    res = run_bass_kernel_spmd(nc, in_maps, core_ids=list(range(ncores)))
    out = np.zeros((16, TL, D), np.float32)
    for core in range(ncores):
        o = res.results[core]['outT']
        for i in range(2):
            out[2 * core + i] = o[i].transpose(2, 1, 0).reshape(TL, D)
    if dbg:
        return out, [res.results[core]['dbg'] for core in range(ncores)]
    return out
```

```python
import contextlib
import numpy as np
import ml_dtypes
import concourse.bass as bass
import concourse.mybir as mybir
from concourse.bass_utils import run_bass_kernel_spmd

F32 = mybir.dt.float32
BF16 = mybir.dt.bfloat16
AF = mybir.ActivationFunctionType
ALU = mybir.AluOpType
AX = mybir.AxisListType

D = 1024
T = 2304
TL = 2048
NCH = 18
DIN = 2576
DEPTH = 2
EPS = 1e-6
NEG = -30000.0
import os
SKIP = set(os.environ.get('MK_SKIP', '').split(','))
CSTOP = int(os.environ.get('MK_CSTOP', '9'))
ASTOP = int(os.environ.get('MK_ASTOP', '9'))
OFF_A, OFF_B, OFF_C, OFF_D, OFF_G = 0, 768, 1280, 2304, 2560


class Sch:
    def __init__(self, nc, stack, ndma=8, same_engine_sync=True):
        self.nc = nc
        self.E = {'pe': nc.tensor, 'act': nc.scalar, 'dve': nc.vector,
                  'pool': nc.gpsimd, 'sp': nc.sync}
        self.R = ndma
        self.same = same_engine_sync
        self.csem = {}
        self.dsem = {}
        for e in self.E:
            self.csem[e] = stack.enter_context(nc.semaphore('c_' + e))
            self.dsem[e] = [stack.enter_context(nc.semaphore('d_%s_%d' % (e, i)))
                            for i in range(ndma)]
        self.cc = {e: 0 for e in self.E}
        self.dc = {e: 0 for e in self.E}
        self.waited = {e: {} for e in self.E}
        self.lw = {}
        self.rd = {}
        self.bar = set()
        self.nops = 0
        self.nwaits = 0

    def _tok_sem(self, tok):
        kind, e, i = tok
        if kind == 'c':
            return (kind, e, 0), self.csem[e], i
        return (kind, e, i % self.R), self.dsem[e][i % self.R], 16 * (i // self.R + 1)

    def _wait(self, eng, tok):
        kind, e, i = tok
        if kind == 'c' and e == eng and (eng == 'pe' or not self.same):
            return
        key, sem, val = self._tok_sem(tok)
        if self.waited[eng].get(key, 0) >= val:
            return
        self.waited[eng][key] = val
        self.E[eng].wait_ge(sem, val)
        self.nwaits += 1

    def op(self, eng, fn, reads=(), writes=(), dma=False):
        deps = set(self.bar)
        for r in reads:
            if r in self.lw:
                deps.add(self.lw[r])
        for w in writes:
            if w in self.lw:
                lt = self.lw[w]
                if not (lt[0] == 'c' and lt[1] == eng and not dma):
                    deps.add(lt)
            for t in self.rd.get(w, ()):
                deps.add(t)
        if dma:
            k = self.dc[eng]
            self.dc[eng] += 1
            tok = ('d', eng, k)
            if k >= self.R:
                deps.add(('d', eng, k - self.R))
        else:
            self.cc[eng] += 1
            tok = ('c', eng, self.cc[eng])
        best = {}
        for t in deps:
            key, sem, val = self._tok_sem(t)
            if key not in best or best[key][1] < val:
                best[key] = (t, val)
        for key in sorted(best, key=str):
            self._wait(eng, best[key][0])
        inst = fn(self.E[eng])
        _, sem, _ = self._tok_sem(tok)
        inst.then_inc(sem, 16 if dma else 1)
        for w in writes:
            self.lw[w] = tok
            self.rd[w] = []
        for r in reads:
            self.rd.setdefault(r, []).append(tok)
        self.nops += 1
        return tok

    def barrier(self):
        self.bar = set()
        for e in self.E:
            if self.cc[e] > 0:
                self.bar.add(('c', e, self.cc[e]))
            for j in range(max(0, self.dc[e] - self.R), self.dc[e]):
                self.bar.add(('d', e, j))
        self.lw = {}
        self.rd = {}

    def finish(self, eng='sp'):
        self.barrier()
        best = {}
        for t in self.bar:
            key, sem, val = self._tok_sem(t)
            if key not in best or best[key][1] < val:
                best[key] = (t, val)
        for key in sorted(best, key=str):
            self._wait(eng, best[key][0])


class Rot:
    def __init__(self, items):
        self.items = items
        self.i = 0

    def next(self):
        it = self.items[self.i % len(self.items)]
        self.i += 1
        return it


def _bf(a):
    return np.ascontiguousarray(a.astype(ml_dtypes.bfloat16))


def _consts():
    c = {}
    c['ident'] = np.eye(128, dtype=np.float32)
    c['jrev'] = np.ascontiguousarray(np.eye(128, dtype=np.float32)[::-1])
    s = np.arange(128)
    mf = (s[:, None] <= s[None, :]).astype(np.float32)
    mb = (s[:, None] >= s[None, :]).astype(np.float32)
    c['masks'] = _bf(np.stack([mf, mb], 1))
    t = np.arange(TL)
    rows = (t // 64).astype(np.float32)
    cols = (t % 64).astype(np.float32)
    p = np.arange(128)
    d = p % 64
    half = d // 32
    i = (d % 16).astype(np.float32)
    inv = (10000.0 ** (-i / 16.0)).astype(np.float32)
    pos = np.where(half[:, None] == 0, rows[None, :], cols[None, :]).astype(np.float32)
    ang = pos * inv[:, None]
    c['rope'] = _bf(np.stack([np.cos(ang), np.sin(ang)], 1))
    rm = np.zeros((128, 128), np.float32)
    for m in range(128):
        dd = m % 32
        if dd < 16:
            rm[m + 16, m] = -1.0
        else:
            rm[m - 16, m] = 1.0
    c['rm'] = _bf(rm)
    for n, name in ((2048, 'l'), (256, 'c')):
        k = np.arange(n, dtype=np.float64)
        ang = 2.0 * np.pi * np.outer(k, k) / n
        c['dftc_' + name] = _bf(np.cos(ang))
        c['dfts_' + name] = _bf(-np.sin(ang))
    k = np.arange(64, dtype=np.float64)
    ang = 2.0 * np.pi * np.outer(k, k) / 64
    bc = np.zeros((128, 128)); bs = np.zeros((128, 128))
    for g in range(2):
        bc[g * 64:(g + 1) * 64, g * 64:(g + 1) * 64] = np.cos(ang)
        bs[g * 64:(g + 1) * 64, g * 64:(g + 1) * 64] = np.sin(ang)
    c['blk'] = _bf(np.stack([bc, bs], 1))
    sel = np.zeros((4, 2, 128), np.float32)
    for pr in range(2):
        sel[2 * pr, pr, :64] = 1.0
        sel[2 * pr + 1, pr, 64:] = 1.0
    c['sel'] = sel
    return c


def _nat_bias(rpb_l):
    out = np.full((5, 2, 128, 5, 2, 128), NEG, np.float32)
    tsel = [0, 1, 2, 14, 15]
    kk = np.arange(128)
    qq = np.arange(128)
    for ti, t in enumerate(tsel):
        cb = min(max(t - 2, 0), 11)
        for j in range(5):
            kr = (cb + j) * 2 + kk // 64
            kc = kk % 64
            r = 2 * t + qq // 64
            qc = qq % 64
            rs = np.clip(r - 4, 0, 24)
            row_ok = (kr[:, None] >= rs[None, :]) & (kr[:, None] < rs[None, :] + 8)
            qs = np.clip(qc - 8, 0, 48)
            col_ok = (kc[:, None] >= qs[None, :]) & (kc[:, None] < qs[None, :] + 16)
            ok = row_ok & col_ok
            drow = np.clip(kr[:, None] - r[None, :] + 7, 0, 14)
            dcol = np.clip(kc[:, None] - qc[None, :] + 15, 0, 30)
            for h in range(4):
                val = rpb_l[h][drow, dcol]
                out[ti, h // 2, :, j, h % 2, :] = np.where(ok, val, NEG)
    return out


def _fm(vec):
    v = vec.reshape(vec.shape[:-1] + (8, 128))
    return np.ascontiguousarray(np.moveaxis(v, -1, 0))


def build_nc(dbg=None):
    nc = bass.Bass("TRN2", target_bir_lowering=False)

    def din(name, shape, dt=F32):
        return nc.dram_tensor(name, list(shape), dt, kind="ExternalInput").ap()

    xin = din("xin", [2, 128, 8, T])
    cT = din("cT", [128, 8, 3])
    w_ada = din("w_ada", [2, D, 6 * D])
    badaT = din("badaT", [128, 2, 48])
    gvec = din("gvec", [128, 5, 8])
    w_in = din("w_in", [2, D, DIN])
    bg = din("bg", [4, 2, 4])
    wconv = din("wconv", [128, 2, 4, 3])
    natb = din("natb", [2, 5, 2, 128, 1280])
    wsT = din("wsT", [2, 128, 4, 128])
    bsT = din("bsT", [2, 128, 2, 128])
    ggm = din("ggm", [2, 256])
    gml = din("gml", [2, 256])
    w_fnet = din("w_fnet", [2, 256, 256])
    w_out = din("w_out", [2, D, D])
    w_ff1 = din("w_ff1", [2, D, 4 * D])
    w_ff2 = din("w_ff2", [2, 4 * D, D])
    c_ident = din("ident", [128, 128])
    c_jrev = din("jrev", [128, 128])
    c_masks = din("masks", [128, 2, 128], BF16)
    c_rope = din("rope", [128, 2, TL], BF16)
    c_rm = din("rm", [128, 128], BF16)
    c_dftc_l = din("dftc_l", [TL, TL], BF16)
    c_dfts_l = din("dfts_l", [TL, TL], BF16)
    c_dftc_c = din("dftc_c", [256, 256], BF16)
    c_dfts_c = din("dfts_c", [256, 256], BF16)
    c_blk = din("blk", [128, 2, 128], BF16)
    c_sel = din("sel", [4, 2, 128])
    outT = nc.dram_tensor("outT", [2, 128, 8, TL], F32, kind="ExternalOutput").ap()
    dbg_out = None
    if dbg:
        dbg_out = nc.dram_tensor("dbg", [128, 8, T], F32, kind="ExternalOutput").ap()

    with contextlib.ExitStack() as st:
        S = Sch(nc, st)

        uid = [0]

        def sb(stack, name, shape, dt):
            uid[0] += 1
            return stack.enter_context(nc.sbuf_tensor("s%d_%s" % (uid[0], name), list(shape), dt))

        def ps(stack, name, shape, dt=F32):
            uid[0] += 1
            return stack.enter_context(nc.psum_tensor("p%d_%s" % (uid[0], name), list(shape), dt))

        def dma(eng, out, in_, reads=(), writes=()):
            S.op(eng, lambda e: e.dma_start(out=out, in_=in_), reads=reads, writes=writes, dma=True)

        def mm(out, lhsT, rhs, start, stop, reads, writes):
            S.op('pe', lambda e: e.matmul(out, lhsT=lhsT, rhs=rhs, start=start, stop=stop),
                 reads=reads, writes=writes)

        def act(out, in_, func, reads, writes, scale=1.0, bias=None):
            if bias is None:
                S.op('act', lambda e: e.activation(out=out, in_=in_, func=func, scale=scale),
                     reads=reads, writes=writes)
            else:
                S.op('act', lambda e: e.activation(out=out, in_=in_, func=func, scale=scale, bias=bias),
                     reads=reads, writes=writes)

        def tt(eng, out, in0, in1, op, reads, writes):
            S.op(eng, lambda e: e.tensor_tensor(out=out, in0=in0, in1=in1, op=op), reads=reads, writes=writes)

        def stt(out, in0, scalar, in1, op0, op1, reads, writes):
            S.op('dve', lambda e: e.scalar_tensor_tensor(out=out, in0=in0, scalar=scalar, in1=in1, op0=op0, op1=op1),
                 reads=reads, writes=writes)

        def ts(eng, out, in0, s1, s2, op0, op1, reads, writes):
            if s2 is None:
                S.op(eng, lambda e: e.tensor_scalar(out=out, in0=in0, scalar1=s1, scalar2=None, op0=op0),
                     reads=reads, writes=writes)
            else:
                S.op(eng, lambda e: e.tensor_scalar(out=out, in0=in0, scalar1=s1, scalar2=s2, op0=op0, op1=op1),
                     reads=reads, writes=writes)

        def cp(eng, out, in_, reads, writes):
            S.op(eng, lambda e: e.tensor_copy(out=out, in_=in_), reads=reads, writes=writes)

        X = sb(st, "X", [128, 8, T], F32)
        CAT = sb(st, "CAT", [128, 8, T], BF16)
        ident = sb(st, "ident", [128, 128], F32)
        jrev = sb(st, "jrev", [128, 128], F32)
        identb = sb(st, "identb", [128, 128], BF16)
        masks = sb(st, "masks", [128, 2, 128], BF16)
        rm = sb(st, "rm", [128, 128], BF16)
        blk = sb(st, "blk", [128, 2, 128], BF16)
        sel = sb(st, "sel", [4, 2, 128], F32)
        onesb = sb(st, "onesb", [128, 128], BF16)
        onesf = sb(st, "onesf", [128, 2], F32)
        RS = sb(st, "RS", [128, T], F32)
        modT = sb(st, "modT", [128, 2, 48, 3], F32)
        gmT = sb(st, "gmT", [128, 2, 2, 8, 3], F32)
        gv = sb(st, "gv", [128, 5, 8], F32)
        bada = sb(st, "bada", [128, 2, 48], F32)
        csb = sb(st, "csb", [128, 8, 3], F32)
        scb = sb(st, "scb", [128, 8, 3], BF16)
        bgs = sb(st, "bgs", [4, 2, 4], F32)
        wcv = sb(st, "wcv", [128, 2, 4, 3], F32)

        dma('sp', ident[:], c_ident, writes=['ident'])
        dma('sp', jrev[:], c_jrev, writes=['jrev'])
        dma('sp', masks[:], c_masks, writes=['masks'])
        dma('sp', rm[:], c_rm, writes=['rm'])
        dma('sp', blk[:], c_blk, writes=['blk'])
        dma('sp', sel[:], c_sel, writes=['sel'])
        dma('sp', gv[:], gvec, writes=['gv'])
        dma('sp', bada[:], badaT, writes=['bada'])
        dma('sp', csb[:], cT, writes=['csb'])
        dma('sp', bgs[:], bg, writes=['bgs'])
        dma('sp', wcv[:], wconv, writes=['wcv'])
        S.op('dve', lambda e: e.memset(onesb[:], 1.0), writes=['onesb'])
        S.op('dve', lambda e: e.memset(onesf[:], 1.0), writes=['onesf'])
        cp('dve', identb[:], ident[:], ['ident'], ['identb'])
        act(scb[:], csb[:], AF.Silu, ['csb'], ['scb'])

        with contextlib.ExitStack() as ph:
            wa = [sb(ph, "wa%d" % i, [128, 8, 512], BF16) for i in range(2)]
            mps = [ps(ph, "mps%d" % i, [128, 4, 3]) for i in range(2)]
            it = 0
            for l in range(DEPTH):
                for pc in range(12):
                    b = it % 2
                    it += 1
                    dma('pool', wa[b][:], w_ada[l, :, pc * 512:(pc + 1) * 512].rearrange("(kc p) n -> p kc n", p=128),
                        writes=['wa%d' % b])
                    for oc in range(4):
                        for kc in range(8):
                            mm(mps[b][:, oc, :], wa[b][:, kc, oc * 128:(oc + 1) * 128], scb[:, kc, :],
                               kc == 0, kc == 7, ['wa%d' % b, 'scb'], ['mps%d' % b])
                    tt('dve', modT[:, l, pc * 4:(pc + 1) * 4, :], mps[b][:],
                       bada[:, l, pc * 4:(pc + 1) * 4].unsqueeze(2).broadcast_to([128, 4, 3]), ALU.add,
                       ['mps%d' % b, 'bada'], ['modT'])
            for l in range(DEPTH):
                for kind in range(2):
                    sc_j = 1 if kind == 0 else 4
                    for v in range(3):
                        stt(gmT[:, l, kind, :, v], modT[:, l, sc_j * 8:(sc_j + 1) * 8, v], 1.0,
                            gv[:, 2 * l + kind, :], ALU.add, ALU.mult, ['modT', 'gv'], ['gmT'])
        S.barrier()

        def modap(l, j, c, v):
            return modT[:, l, j * 8 + c, v:v + 1]

        hbuf = [sb(st, "hT%d" % i, [128, 8, 256], BF16) for i in range(2)]
        hrot = Rot([("hT%d" % i, hbuf[i]) for i in range(2)])
        sqb = [sb(st, "sq%d" % i, [128, 256], BF16) for i in range(2)]
        sqrot = Rot([("sq%d" % i, sqb[i]) for i in range(2)])
        rsb = [sb(st, "rs%d" % i, [128, 256], F32) for i in range(2)]
        rsrot = Rot([("rs%d" % i, rsb[i]) for i in range(2)])
        tmb = [sb(st, "tm%d" % i, [128, 256], F32) for i in range(2)]
        tmrot = Rot([("tm%d" % i, tmb[i]) for i in range(2)])
        ssq = [ps(st, "ssq%d" % i, [128, 512]) for i in range(1)]
        ssqrot = Rot([("ssq%d" % i, ssq[i]) for i in range(1)])

        def make_h(l, kind, t0, n, v, dst=None):
            if dst is None:
                hk, hT = hrot.next()
            else:
                hk, hT = dst
            if kind == 0:
                return _mk_tail(l, kind, t0, n, v, hk, hT, 'RS', RS[:, t0:t0 + n])
            pk, pst = ssqrot.next()
            for c in range(8):
                sk, sq = sqrot.next()
                act(sq[:, :n], X[:, c, t0:t0 + n], AF.Square, ['X%d' % c], [sk])
                mm(pst[:, :n], onesb[:], sq[:, :n], c == 0, c == 7, [sk, 'onesb'], [pk])
            rk, rs = rsrot.next()
            act(rs[:, :n], pst[:, :n], AF.Sqrt, [pk, 'epsb'], [rk], scale=1.0 / D, bias=epsb[:, 0:1])
            S.op('dve', lambda e: e.reciprocal(out=rs[:, :n], in_=rs[:, :n]), reads=[rk], writes=[rk])
            return _mk_tail(l, kind, t0, n, v, hk, hT, rk, rs)

        def _mk_tail(l, kind, t0, n, v, hk, hT, rk, rs):
            sh_j = 0 if kind == 0 else 3
            for c in range(8):
                stt(hT[:, c, :n], X[:, c, t0:t0 + n], gmT[:, l, kind, c, v:v + 1], rs[:, :n], ALU.mult, ALU.mult,
                    ['X%d' % c, 'gmT', rk], [hk + '_%d' % c])
            for c in range(8):
                act(hT[:, c, :n], hT[:, c, :n], AF.Identity, [hk + '_%d' % c, 'modT'], [hk + '_%d' % c],
                    bias=modap(l, sh_j, c, v))
            return hk, hT

        def compute_rs():
            for b in range(9):
                t0 = 256 * b
                n = 256
                pk, pst = ssqrot.next()
                for c in range(8):
                    sk, sq = sqrot.next()
                    act(sq[:, :n], X[:, c, t0:t0 + n], AF.Square, ['X%d' % c], [sk])
                    mm(pst[:, :n], onesb[:], sq[:, :n], c == 0, c == 7, [sk, 'onesb'], [pk])
                act(RS[:, t0:t0 + n], pst[:, :n], AF.Sqrt, [pk, 'epsb'], ['RS'], scale=1.0 / D, bias=epsb[:, 0:1])
                S.op('dve', lambda e, t0=t0, n=n: e.reciprocal(out=RS[:, t0:t0 + n], in_=RS[:, t0:t0 + n]),
                     reads=['RS'], writes=['RS'])

        epsb = sb(st, "epsb", [128, 1], F32)
        S.op('dve', lambda e: e.memset(epsb[:], EPS), writes=['epsb'])

        def proj_fm(hT, hk, n, w, wk, col0, ppool, evac):
            pk, pt = ppool.next()
            for kc in range(8):
                mm(pt[:, :n], w[:, kc, col0:col0 + 128], hT[:, kc, :n], kc == 0, kc == 7, [wk, hk + '_%d' % kc], [pk])
            evac(pt[:, :n], pk)

        def proj_tm(hT, hk, sub, w, wk, col0, ncol, ppool, evac):
            pk, pt = ppool.next()
            for kc in range(8):
                mm(pt[:, :ncol], hT[:, kc, sub * 128:(sub + 1) * 128], w[:, kc, col0:col0 + ncol],
                   kc == 0, kc == 7, [wk, hk + '_%d' % kc], [pk])
            evac(pt[:, :ncol], pk)

        def wload(wt, key, l, col0, ncol):
            dma('pool', wt, w_in[l, :, col0:col0 + ncol].rearrange("(kc p) n -> p kc n", p=128), writes=[key])

        ORD = [[16, 17] + list(range(16)), [17, 16] + list(range(15, -1, -1))]

        def mixer_c(s, l, last):
            with contextlib.ExitStack() as ph:
                cf03 = CAT[:, 0:4, :].rearrange("p c t -> p (c t)")
                vaug = cf03[:, 0:4752].rearrange("p (j h d) -> p j h d", j=18, h=4)[:, :, :, 0:65]
                wqk = cf03[:, 4752:4752 + 4096].rearrange("p (k n) -> p k n", k=8)
                ktm = CAT[:, 6:8, :].rearrange("p c t -> p (c t)").rearrange("p (j f) -> p j f", f=256)
                R1 = sb(ph, "R1", [128, 2 * T], F32)
                R2 = sb(ph, "R2", [128, 4, T], BF16)
                RAW = R1[:].bitcast(BF16).rearrange("p (c t) -> p c t", c=4)
                QK = R2
                wv = sb(ph, "wv", [128, 8, 256], BF16)
                wg = sb(ph, "wg", [128, 8, 16], BF16)
                gtm = sb(ph, "gtm", [128, 18, 16], F32)
                gmb = sb(ph, "gmb", [128, 256], F32)
                dma('sp', gmb[:], gml[l, :].partition_broadcast(128), writes=['gmb'])
                wload(wqk, 'wqk', l, OFF_C, 512)
                wload(wv[:], 'wv', l, OFF_C + 512, 256)
                wload(wg[:], 'wg', l, OFF_G, 16)
                S.op('pool', lambda e: e.memset(vaug[:, :, :, 64:65], 1.0), writes=['vaug1'])
                with contextlib.ExitStack() as p1:
                    pq = [ps(p1, "pq%d" % i, [128, 512]) for i in range(2)]
                    pqr = Rot([("pq%d" % i, pq[i]) for i in range(2)])
                    pvv = [ps(p1, "pvv%d" % i, [128, 512]) for i in range(2)]
                    pvr = Rot([("pvv%d" % i, pvv[i]) for i in range(2)])
                    rope = sb(p1, "rope", [128, 2, TL], BF16)
                    dma('sp', rope[:], c_rope, writes=['rope'])
                    for b in range(9):
                        t0 = 256 * b
                        v = s if b < 8 else 2
                        hk, hT = make_h(l, 0, t0, 256, v)
                        for oc in range(4):
                            proj_fm(hT, hk, 256, wqk, 'wqk', oc * 128, pqr,
                                    lambda p_, pk, oc=oc: act(RAW[:, oc, t0:t0 + 256], p_, AF.Copy, [pk], ['RAW%d' % oc]))
                        for sub in range(2):
                            j = 2 * b + sub
                            proj_tm(hT, hk, sub, wv, 'wv', 0, 256, pvr,
                                    lambda p_, pk, j=j: cp('dve', vaug[:, j, :, 0:64], p_.rearrange("p (h d) -> p h d", h=4),
                                                           [pk], ['vaug%d' % j]))
                            proj_tm(hT, hk, sub, wg, 'wg', 0, 16, pvr,
                                    lambda p_, pk, j=j: cp('dve', gtm[:, j, :], p_, [pk], ['gtm']))
                    if CSTOP == 1:
                        return
                    cvt = [sb(p1, "cvt%d" % i, [128, 512], F32) for i in range(2)]
                    cvr = Rot([("cvt%d" % i, cvt[i]) for i in range(2)])
                    cst = [sb(p1, "cst%d" % i, [128, 512], BF16) for i in range(2)]
                    csr = Rot([("cst%d" % i, cst[i]) for i in range(2)])
                    r2t = [sb(p1, "r2t%d" % i, [128, 512], F32) for i in range(2)]
                    r2r = Rot([("r2t%d" % i, r2t[i]) for i in range(2)])
                    r3t = [sb(p1, "r3t%d" % i, [128, 512], F32) for i in range(2)]
                    r3r = Rot([("r3t%d" % i, r3t[i]) for i in range(2)])
                    for oc in range(4):
                        scl = 0.125 if oc >= 2 else 1.0
                        for (g0, g1) in ((0, TL), (TL, T)):
                            for t0 in range(g0, g1, 512):
                                n = min(512, g1 - t0)
                                ck, ct = cvr.next()
                                ts('dve', ct[:, :n], RAW[:, oc, t0:t0 + n], wcv[:, l, oc, 1:2], None, ALU.mult, None,
                                   ['RAW%d' % oc, 'wcv'], [ck])
                                a = 1 if t0 == g0 else 0
                                stt(ct[:, a:n], RAW[:, oc, t0 + a - 1:t0 + n - 1], wcv[:, l, oc, 0:1], ct[:, a:n],
                                    ALU.mult, ALU.add, ['RAW%d' % oc, 'wcv', ck], [ck])
                                bnd = n - 1 if t0 + n == g1 else n
                                stt(ct[:, 0:bnd], RAW[:, oc, t0 + 1:t0 + 1 + bnd], wcv[:, l, oc, 2:3], ct[:, 0:bnd],
                                    ALU.mult, ALU.add, ['RAW%d' % oc, 'wcv', ck], [ck])
                                if g0 == 0:
                                    sk, cs_ = csr.next()
                                    act(cs_[:, :n], ct[:, :n], AF.Silu, [ck], [sk])
                                    pk, pt = pqr.next()
                                    mm(pt[:, :n], rm[:], cs_[:, :n], True, True, ['rm', sk], [pk])
                                    k2, t2 = r2r.next()
                                    stt(t2[:, :n], pt[:, :n], scl, rope[:, 1, t0:t0 + n], ALU.mult, ALU.mult, [pk, 'rope'], [k2])
                                    k3, t3 = r3r.next()
                                    stt(t3[:, :n], cs_[:, :n], scl, rope[:, 0, t0:t0 + n], ALU.mult, ALU.mult, [sk, 'rope'], [k3])
                                    tt('pool', QK[:, oc, t0:t0 + n], t2[:, :n], t3[:, :n], ALU.add, [k2, k3], ['QK%d' % oc])
                                else:
                                    act(QK[:, oc, t0:t0 + n], ct[:, :n], AF.Silu, [ck], ['QK%d' % oc], scale=1.0)
                                    if scl != 1.0:
                                        ts('dve', QK[:, oc, t0:t0 + n], QK[:, oc, t0:t0 + n], scl, None, ALU.mult, None,
                                           ['QK%d' % oc], ['QK%d' % oc])
                    for j in range(18):
                        pk, pt = pqr.next()
                        for kc in range(2):
                            mm(pt[:, kc * 128:(kc + 1) * 128], QK[:, 2 + kc, 128 * j:128 * j + 128], identb[:], True, True,
                               ['QK%d' % (2 + kc), 'identb'], [pk])
                        cp('dve', ktm[:, j, :], pt[:, 0:256], [pk], ['ktm%d' % j])
                S.barrier()
                if CSTOP == 2:
                    return
                R3 = sb(ph, "R3", [128, 18 * 256], F32)
                hsum = R3[:].rearrange("p (j f) -> p j f", f=256)
                colq = [sb(ph, "colq%d" % i, [128, 18, 12], F32) for i in range(2)]
                dcol = [sb(ph, "dcol%d" % i, [128, 2, 18], F32) for i in range(2)]
                with contextlib.ExitStack() as p3:
                    R1f = R1
                    rI = R1f[0:4, 0:T]
                    rF = R1f[0:4, T:2 * T]
                    rG = R3[0:4, 0:T]
                    rA = R3[0:4, T:2 * T]
                    rrow = sb(p3, "rrow", [4, 20], F32)
                    drow = sb(p3, "drow", [4, 18], F32)
                    colraw = sb(p3, "colraw", [128, 18, 12], F32)
                    prw = [ps(p3, "prw%d" % i, [128, 512]) for i in range(2)]
                    pcol = ps(p3, "pcol", [128, 18, 12])
                    pcol2 = ps(p3, "pcol2", [128, 18, 12])
                    pd = ps(p3, "pd", [128, 2, 18])
                    for dr in range(2):
                        tr_m = ident if dr == 0 else jrev
                        trk = 'ident' if dr == 0 else 'jrev'
                        for g in range(5):
                            idxs = list(range(4 * g, min(4 * g + 4, 18)))
                            for qi, (dst, pw) in enumerate(((rI, prw[0]), (rF, prw[1]))):
                                for ii, idx in enumerate(idxs):
                                    j = ORD[dr][idx]
                                    c0 = dr * 8 + qi * 4
                                    mm(pw[0:4, ii * 128:(ii + 1) * 128], gtm[:, j, c0:c0 + 4], tr_m[:], True, True,
                                       ['gtm', trk], ['prw%d' % qi])
                                n = 128 * len(idxs)
                                ts('dve', dst[:, 512 * g:512 * g + n], pw[0:4, 0:n], bgs[:, l, dr * 2 + qi:dr * 2 + qi + 1], None,
                                   ALU.add, None, ['prw%d' % qi, 'bgs'], ['row%d' % qi])
                        act(rF, rF, AF.Exp, ['row1'], ['row1'], scale=-1.0)
                        act(rF, rF, AF.Ln, ['row1'], ['row1'], bias=onesf[0:4, 0:1])
                        S.op('dve', lambda e: e.tensor_tensor_scan(out=rG, data0=onesf[0:4, 0:1].broadcast_to([4, T]), data1=rF,
                                                                   initial=0.0, op0=ALU.mult, op1=ALU.add),
                             reads=['row1', 'onesf'], writes=['row2'])
                        tt('dve', rI, rI, rG, ALU.add, ['row0', 'row2'], ['row0'])
                        S.op('dve', lambda e: e.tensor_tensor_scan(out=rA, data0=onesf[0:4, 0:1].broadcast_to([4, T]), data1=rI,
                                                                   initial=0.0, op0=ALU.mult, op1=ALU.max),
                             reads=['row0', 'onesf'], writes=['row3'])
                        S.op('dve', lambda e: e.memset(rrow[:, 0:1], 0.0), writes=['rrow'])
                        cp('dve', rrow[:, 1:19], rA.rearrange("p (j t) -> p j t", t=128)[:, :, 127], ['row3'], ['rrow'])
                        rfull = rrow[:, 0:18].unsqueeze(2).broadcast_to([4, 18, 128])
                        v3 = lambda r_: r_.rearrange("p (j t) -> p j t", t=128)
                        tt('dve', v3(rI), v3(rI), rfull, ALU.subtract, ['row0', 'rrow'], ['row0'])
                        act(rI, rI, AF.Exp, ['row0'], ['row0'])
                        tt('dve', rG, rG, rA, ALU.subtract, ['row2', 'row3'], ['row2'])
                        act(rG, rG, AF.Exp, ['row2'], ['row2'])
                        tt('dve', v3(rA), rfull, v3(rA), ALU.subtract, ['row3', 'rrow'], ['row3'])
                        act(rA, rA, AF.Exp, ['row3'], ['row3'])
                        tt('dve', drow[:, 0:18], rrow[:, 0:18], rrow[:, 1:19], ALU.subtract, ['rrow'], ['drow'])
                        act(drow[:], drow[:], AF.Exp, ['drow'], ['drow'])
                        for idx in range(18):
                            for qi, rw in enumerate((rI, rA, rG)):
                                mm(pcol[:, idx, qi * 4:(qi + 1) * 4], rw[:, idx * 128:(idx + 1) * 128], ident[0:4, 0:4], True, True,
                                   ['row0', 'row2', 'row3', 'ident'], ['pcol'])
                        if dr == 0:
                            cp('dve', colq[0][:], pcol[:], ['pcol'], ['colq0'])
                        else:
                            cp('dve', colraw[:], pcol[:], ['pcol'], ['colraw'])
                            mm(pcol2[:].rearrange("p a b -> p (a b)"), jrev[:], colraw[:].rearrange("p a b -> p (a b)"), True, True,
                               ['colraw', 'jrev'], ['pcol2'])
                            cp('dve', colq[1][:], pcol2[:], ['pcol2'], ['colq1'])
                        for pr in range(2):
                            mm(pd[:, pr, :], sel[:, pr, :], drow[:], True, True, ['sel', 'drow'], ['pd'])
                        cp('dve', dcol[dr][:], pd[:], ['pd'], ['dcol%d' % dr])
                S.barrier()
                if CSTOP == 3:
                    return
                with contextlib.ExitStack() as p4:
                    qz4_ = [sb(p4, "qzc%d" % i, [128, 256], BF16) for i in range(2)]
                    qzr4 = Rot([("qzc%d" % i, qz4_[i]) for i in range(2)])
                    for i in range(2):
                        S.op('pool', lambda e, i=i: e.memset(qz4_[i][:], 0.0), writes=['qzc%dz' % i, 'qzc%d' % i])
                    vs_ = [sb(p4, "vs%d" % i, [128, 4, 80], BF16) for i in range(2)]
                    vsr = Rot([("vs%d" % i, vs_[i]) for i in range(2)])
                    wt_ = [sb(p4, "wt%d" % i, [128, 4, 128], BF16) for i in range(2)]
                    wtr = Rot([("wt%d" % i, wt_[i]) for i in range(2)])
                    sm = [sb(p4, "sm%d" % i, [128, 16], F32) for i in range(2)]
                    smr = Rot([("sm%d" % i, sm[i]) for i in range(2)])
                    ho = [sb(p4, "ho%d" % i, [128, 4, 64], F32) for i in range(1)]
                    hor = Rot([("ho%d" % i, ho[i]) for i in range(1)])
                    stp = [ps(p4, "stp%d" % i, [128, 512]) for i in range(2)]
                    stpr = Rot([("stp%d" % i, stp[i]) for i in range(2)])
                    opp = [ps(p4, "opp%d" % i, [128, 4, 80]) for i in range(2)]
                    oppr = Rot([("opp%d" % i, opp[i]) for i in range(2)])
                    upp = [ps(p4, "upp%d" % i, [128, 2, 80]) for i in range(2)]
                    uppr = Rot([("upp%d" % i, upp[i]) for i in range(2)])
                    CstD, CbfD, Cbf4D = {}, {}, {}
                    for dr in range(2):
                        CstD[dr] = sb(p4, "CstD%d" % dr, [128, 2, 80], F32)
                        CbfD[dr] = sb(p4, "CbfD%d" % dr, [128, 4, 80], BF16)
                        Cbf4D[dr] = CbfD[dr][:].rearrange("p (c u) d -> p c u d", u=2)
                        S.op('dve', lambda e, dr=dr: e.memset(CstD[dr][:], 0.0), writes=['Cst%d' % dr])
                        S.op('dve', lambda e, dr=dr: e.memset(CbfD[dr][:], 0.0), writes=['Cbf%d' % dr])
                    written = set()
                    for idx in range(18):
                        for dr in (1, 0):
                            Cst, Cbf, Cbf4 = CstD[dr], CbfD[dr], Cbf4D[dr]
                            CK, BK = 'Cst%d' % dr, 'Cbf%d' % dr
                            j = ORD[dr][idx]
                            tok = slice(128 * j, 128 * j + 128)
                            emit = not (last and j >= 16)
                            vk, vs = vsr.next()
                            tt('dve', vs[:, :, 0:65], vaug[:, j, :, :], colq[dr][:, idx, 0:4].unsqueeze(2).broadcast_to([128, 4, 65]),
                               ALU.mult, ['vaug%d' % j, 'vaug1', 'colq%d' % dr], [vk])
                            if emit:
                                sk, sp_ = stpr.next()
                                for c2 in range(2):
                                    zk, qz = qzr4.next()
                                    act(qz[0:64, 0:128], QK[0:64, c2, tok], AF.Copy, ['QK', zk + 'z'], [zk])
                                    act(qz[64:128, 128:256], QK[64:128, c2, tok], AF.Copy, ['QK', zk + 'z'], [zk])
                                    mm(sp_[:, c2 * 256:(c2 + 1) * 256], QK[:, 2 + c2, tok], qz[:], True, True, ['QK', zk], [sk])
                                wk, wt = wtr.next()
                                tt('dve', wt[:], sp_[:].rearrange("p (h t) -> p h t", h=4),
                                   masks[:, dr, :].unsqueeze(1).broadcast_to([128, 4, 128]), ALU.mult, [sk, 'masks'], [wk])
                                ok_, op_ = oppr.next()
                                for h in range(4):
                                    c2, po = h // 2, (h % 2) * 64
                                    mm(op_[:, h, 0:65], wt[:, h, :], vs[:, h, 0:65], True, False, [wk, vk], [ok_])
                                    mm(op_[:, h, 0:65], QK[:, c2, tok], Cbf[:, h, 0:65], False, True,
                                       ['QK', BK], [ok_])
                                mk, m_ = smr.next()
                                eo = colq[dr][:, idx, 4:8]
                                eb = colq[dr][:, idx, 8:12]
                                tt('dve', m_[:, 0:4], op_[:, :, 64], eo, ALU.mult, [ok_, 'colq%d' % dr], [mk])
                                stt(m_[:, 4:8], m_[:, 0:4], -1.0, m_[:, 0:4], ALU.mult, ALU.max, [mk], [mk])
                                tt('dve', m_[:, 4:8], m_[:, 4:8], eb, ALU.max, [mk, 'colq%d' % dr], [mk])
                                S.op('dve', lambda e, m_=m_: e.reciprocal(out=m_[:, 8:12], in_=m_[:, 4:8]), reads=[mk], writes=[mk])
                                tt('dve', m_[:, 12:16], m_[:, 8:12], eo, ALU.mult, [mk, 'colq%d' % dr], [mk])
                                hs_j = hsum[:, j, :].rearrange("p (h d) -> p h d", h=4)
                                rcb = m_[:, 12:16].unsqueeze(2).broadcast_to([128, 4, 64])
                                if j not in written:
                                    written.add(j)
                                    tt('dve', hs_j, op_[:, :, 0:64], rcb, ALU.mult, [ok_, mk], ['hsum%d' % j])
                                else:
                                    hk2, h2 = hor.next()
                                    tt('dve', h2[:], op_[:, :, 0:64], rcb, ALU.mult, [ok_, mk], [hk2])
                                    tt('pool', hs_j, hs_j, h2[:], ALU.add, [hk2, 'hsum%d' % j], ['hsum%d' % j])
                            if idx < 17:
                                uk, up = uppr.next()
                                for pr in range(2):
                                    for hh in range(2):
                                        mm(up[hh * 64:(hh + 1) * 64, pr, 0:65], ktm[:, j, pr * 128 + hh * 64:pr * 128 + hh * 64 + 64],
                                           vs[:, 2 * pr + hh, 0:65], True, True, ['ktm%d' % j, vk], [uk])
                                tt('dve', Cst[:, :, 0:65], up[:, :, 0:65], Cst[:, :, 0:65], ALU.add, [uk, CK], [CK])
                                tt('dve', Cst[:, :, 0:65], Cst[:, :, 0:65], dcol[dr][:, :, idx].unsqueeze(2).broadcast_to([128, 2, 65]), ALU.mult,
                                   [CK, 'dcol%d' % dr], [CK])
                                cp('pool', Cbf4[0:64, :, 0, 0:65], Cst[0:64, :, 0:65], [CK], [BK])
                                cp('pool', Cbf4[64:128, :, 1, 0:65], Cst[64:128, :, 0:65], [CK], [BK])
                S.barrier()
                if CSTOP == 4:
                    return
                with contextlib.ExitStack() as p5:
                    OT = R2[:, 0:2, :]
                    wload(wv[:], 'wv', l, OFF_C + 768, 256)
                    pq = [ps(p5, "pq5_%d" % i, [128, 512]) for i in range(2)]
                    pqr = Rot([("pq5_%d" % i, pq[i]) for i in range(2)])
                    nb = 8 if last else 9
                    for b in range(nb):
                        t0 = 256 * b
                        v = s if b < 8 else 2
                        hk, hT = make_h(l, 0, t0, 256, v)
                        for oc in range(2):
                            proj_fm(hT, hk, 256, wv, 'wv', oc * 128, pqr,
                                    lambda p_, pk, oc=oc: act(OT[:, oc, t0:t0 + 256], p_, AF.Sigmoid, [pk], ['OT%d' % oc]))
                    st_ = [sb(p5, "lst%d" % i, [128, 16], F32) for i in range(2)]
                    str_ = Rot([("lst%d" % i, st_[i]) for i in range(2)])
                    xc_ = [sb(p5, "xc%d" % i, [128, 4, 64], F32) for i in range(2)]
                    xcr = Rot([("xc%d" % i, xc_[i]) for i in range(2)])
                    sq_ = [sb(p5, "xsq%d" % i, [128, 4, 64], F32) for i in range(2)]
                    sqr_ = Rot([("xsq%d" % i, sq_[i]) for i in range(2)])
                    for j in range(16 if last else 18):
                        tok = slice(128 * j, 128 * j + 128)
                        hs_j = hsum[:, j, :].rearrange("p (h d) -> p h d", h=4)
                        lk, ls = str_.next()
                        S.op('dve', lambda e, ls=ls, hs_j=hs_j: e.reduce_sum(out=ls[:, 0:4], in_=hs_j, axis=AX.X),
                             reads=['hsum%d' % j], writes=[lk])
                        ts('dve', ls[:, 0:4], ls[:, 0:4], 1.0 / 64, None, ALU.mult, None, [lk], [lk])
                        xk, xc = xcr.next()
                        tt('dve', xc[:], hs_j, ls[:, 0:4].unsqueeze(2).broadcast_to([128, 4, 64]), ALU.subtract,
                           ['hsum%d' % j, lk], [xk])
                        qk_, xq = sqr_.next()
                        tt('pool', xq[:], xc[:], xc[:], ALU.mult, [xk], [qk_])
                        S.op('dve', lambda e, ls=ls, xq=xq: e.reduce_sum(out=ls[:, 4:8], in_=xq[:], axis=AX.X),
                             reads=[qk_], writes=[lk])
                        ts('dve', ls[:, 4:8], ls[:, 4:8], 1.0 / 64, EPS, ALU.mult, ALU.add, [lk], [lk])
                        act(ls[:, 8:12], ls[:, 4:8], AF.Sqrt, [lk], [lk])
                        S.op('dve', lambda e, ls=ls: e.reciprocal(out=ls[:, 12:16], in_=ls[:, 8:12]), reads=[lk], writes=[lk])
                        tt('dve', xc[:], xc[:], ls[:, 12:16].unsqueeze(2).broadcast_to([128, 4, 64]), ALU.mult, [xk, lk], [xk])
                        tt('pool', xq[:].rearrange("p h d -> p (h d)"), xc[:].rearrange("p h d -> p (h d)"), gmb[:], ALU.mult,
                           [xk, 'gmb', qk_], [qk_])
                        pk, pt = pqr.next()
                        for kc in range(2):
                            mm(pt[:, kc * 128:(kc + 1) * 128], xq[:].rearrange("p h d -> p (h d)")[:, kc * 128:(kc + 1) * 128],
                               ident[:], True, True, [qk_, 'ident'], [pk])
                        tt('dve', CAT[:, 4:6, tok], pt[:, 0:256].rearrange("p (c t) -> p c t", c=2), OT[:, :, tok], ALU.mult,
                           [pk, 'OT0', 'OT1'], ['CATc'])

        def mixer_a(s, l, last):
            with contextlib.ExitStack() as ph:
                QA = sb(ph, "QA", [128, 2, T], BF16)
                KA = sb(ph, "KA", [128, 2, T], BF16)
                VA = sb(ph, "VA", [128, 18, 256], BF16)
                wq = sb(ph, "wqa", [128, 8, 768], BF16)
                wload(wq[:], 'wqa', l, OFF_A, 768)
                with contextlib.ExitStack() as p1:
                    pq = [ps(p1, "pqa%d" % i, [128, 512]) for i in range(2)]
                    pqr = Rot([("pqa%d" % i, pq[i]) for i in range(2)])
                    for b in range(9):
                        t0 = 256 * b
                        v = s if b < 8 else 2
                        hk, hT = make_h(l, 0, t0, 256, v)
                        for oc in range(4):
                            if oc < 2 and last and b == 8:
                                continue
                            dst = QA if oc < 2 else KA
                            proj_fm(hT, hk, 256, wq, 'wqa', oc * 128, pqr,
                                    lambda p_, pk, oc=oc, dst=dst: act(dst[:, oc % 2, t0:t0 + 256], p_, AF.Copy, [pk], ['QKA']))
                        for sub in range(2):
                            j = 2 * b + sub
                            proj_tm(hT, hk, sub, wq, 'wqa', 512, 256, pqr,
                                    lambda p_, pk, j=j: cp('dve', VA[:, j, :], p_, [pk], ['VA']))
                S.barrier()
                with contextlib.ExitStack() as p2:
                    bt = [sb(p2, "bt%d" % i, [128, 1280], F32) for i in range(2)]
                    btr = Rot([("bt%d" % i, bt[i]) for i in range(2)])
                    tf = [sb(p2, "tf%d" % i, [128, 1280], F32) for i in range(2)]
                    tfr = Rot([("tf%d" % i, tf[i]) for i in range(2)])
                    PT = [sb(p2, "PT%d" % i, [128, 7, 256], BF16) for i in range(2)]
                    ptr_ = Rot([("PT%d" % i, PT[i]) for i in range(2)])
                    rc_ = [sb(p2, "rca%d" % i, [128, 256], F32) for i in range(2)]
                    rcr = Rot([("rca%d" % i, rc_[i]) for i in range(2)])
                    sps = ps(p2, "sps", [128, 8, 256])
                    ov = [ps(p2, "ov%d" % i, [128, 512]) for i in range(2)]
                    ovr = Rot([("ov%d" % i, ov[i]) for i in range(2)])
                    qz_ = [sb(p2, "qz%d" % i, [128, 256], BF16) for i in range(2)]
                    qzr = Rot([("qz%d" % i, qz_[i]) for i in range(2)])
                    for i in range(2):
                        S.op('pool', lambda e, i=i: e.memset(qz_[i][:], 0.0), writes=['qz%dz' % i, 'qz%d' % i])
                    nq = 16 if last else 18
                    qts = list(range(nq))
                    if ASTOP == 1:
                        qts = []
                    if ASTOP == 2:
                        qts = [16, 17]
                    if ASTOP == 3:
                        qts = [5]
                    units = [(qt, pr) for qt in qts for pr in range(2)]

                    def stage1(qt, pr):
                        ctxq = qt >= 16
                        tq = slice(128 * qt, 128 * qt + 128)
                        if ctxq:
                            kch = [16, 17]
                            bk = bias_t = None
                        else:
                            cb = min(max(qt - 2, 0), 11)
                            kch = list(range(cb, cb + 5)) + [16, 17]
                            typ = {0: 0, 1: 1, 14: 3, 15: 4}.get(qt, 2)
                            bk, bias_t = btr.next()
                            dma('sp', bias_t[:], natb[l, typ, pr], writes=[bk])
                        zk, qz = qzr.next()
                        cp('pool', qz[0:64, 0:128], QA[0:64, pr, tq], ['QKA', zk + 'z'], [zk])
                        cp('pool', qz[64:128, 128:256], QA[64:128, pr, tq], ['QKA', zk + 'z'], [zk])
                        for i, kc_ in enumerate(kch):
                            mm(sps[:, i, :], KA[:, pr, 128 * kc_:128 * kc_ + 128], qz[:], True, True, ['QKA', zk], ['sps%d' % (i // 2)])
                        return (qt, pr, ctxq, tq, kch, bk, bias_t)

                    def stage2(st_):
                        qt, pr, ctxq, tq, kch, bk, bias_t = st_
                        nk = len(kch)
                        pk_, P_ = ptr_.next()
                        if not ctxq:
                            fk, tfl = tfr.next()
                            for (a0, a1) in ((0, 2), (2, 4), (4, 5)):
                                stt(tfl[:, a0 * 256:a1 * 256], sps[:, a0:a1, :].rearrange("p a b -> p (a b)"), 0.125,
                                    bias_t[:, a0 * 256:a1 * 256], ALU.mult, ALU.add, ['sps%d' % (a0 // 2), bk], [fk])
                            for a0 in (5, 6):
                                act(P_[:, a0, :], sps[:, a0, :], AF.Exp, ['sps%d' % (a0 // 2)], [pk_ + 'c'], scale=0.125)
                            act(P_[:, 0:5, :].rearrange("p a b -> p (a b)"), tfl[:], AF.Exp, [fk], [pk_])
                        else:
                            act(P_[:, 0:2, :].rearrange("p a b -> p (a b)"), sps[:, 0:2, :].rearrange("p a b -> p (a b)"),
                                AF.Exp, ['sps0'], [pk_, pk_ + 'c'], scale=0.125)
                        return (qt, pr, tq, kch, pk_, P_)

                    def stage3(st_):
                        qt, pr, tq, kch, pk_, P_ = st_
                        nk = len(kch)
                        ok_, o_ = ovr.next()
                        for i, kc_ in enumerate(kch):
                            mm(o_[:, 0:256], VA[:, kc_, pr * 128:(pr + 1) * 128], P_[:, i, :], i == 0, i == nk - 1,
                               ['VA', pk_, pk_ + 'c'], [ok_])
                        for i, kc_ in enumerate(kch):
                            mm(o_[:, 256:512], onesb[:], P_[:, i, :], i == 0, i == nk - 1, ['onesb', pk_, pk_ + 'c'], [ok_])
                        rk_, r_ = rcr.next()
                        S.op('dve', lambda e, r_=r_, o_=o_: e.reciprocal(out=r_[:], in_=o_[:, 256:512]), reads=[ok_], writes=[rk_])
                        for hh in range(2):
                            po = hh * 64
                            tt('dve', CAT[po:po + 64, pr, tq], o_[po:po + 64, hh * 128:(hh + 1) * 128],
                               r_[po:po + 64, hh * 128:(hh + 1) * 128], ALU.mult, [ok_, rk_], ['CATa'])

                    if units:
                        s1 = stage1(*units[0])
                        for ui in range(len(units)):
                            s2 = stage2(s1)
                            if ui + 1 < len(units):
                                s1 = stage1(*units[ui + 1])
                            stage3(s2)

        def mixer_b(s, l, last):
            with contextlib.ExitStack() as ph:
                UB = sb(ph, "UB", [128, 2, T], BF16)
                ZB = sb(ph, "ZB", [128, 18, 256], BF16)
                wb_ = sb(ph, "wbb", [128, 8, 512], BF16)
                wsb = sb(ph, "wsb", [128, 4, 128], BF16)
                bsb = sb(ph, "bsb", [128, 2, 128], F32)
                ggb = sb(ph, "ggb", [128, 256], F32)
                wload(wb_[:], 'wbb', l, OFF_B, 512)
                dma('pool', wsb[:], wsT[l], writes=['wsb'])
                dma('sp', bsb[:], bsT[l], writes=['bsb'])
                dma('sp', ggb[:], ggm[l, :].partition_broadcast(128), writes=['ggb'])
                pq = [ps(ph, "pqb%d" % i, [128, 512]) for i in range(2)]
                pqr = Rot([("pqb%d" % i, pq[i]) for i in range(2)])
                zf = [sb(ph, "zf%d" % i, [128, 256], F32) for i in range(2)]
                zfr = Rot([("zf%d" % i, zf[i]) for i in range(2)])
                zq = [sb(ph, "zq%d" % i, [128, 256], F32) for i in range(2)]
                zqr = Rot([("zq%d" % i, zq[i]) for i in range(2)])
                zs = [sb(ph, "zs%d" % i, [128, 4], F32) for i in range(2)]
                zsr = Rot([("zs%d" % i, zs[i]) for i in range(2)])
                nb = 8 if last else 9
                for b in range(nb):
                    t0 = 256 * b
                    v = s if b < 8 else 2
                    hk, hT = make_h(l, 0, t0, 256, v)
                    for oc in range(2):
                        proj_fm(hT, hk, 256, wb_, 'wbb', oc * 128, pqr,
                                lambda p_, pk, oc=oc: act(UB[:, oc, t0:t0 + 256], p_, AF.Gelu_apprx_tanh, [pk], ['UB']))
                    for sub in range(2):
                        j = 2 * b + sub

                        def ev(p_, pk, j=j):
                            fk, z_ = zfr.next()
                            act(z_[:], p_, AF.Gelu_apprx_tanh, [pk], [fk])
                            qk_, q_ = zqr.next()
                            tt('pool', q_[:], z_[:], z_[:], ALU.mult, [fk], [qk_])
                            sk, s_ = zsr.next()
                            S.op('dve', lambda e: e.reduce_sum(out=s_[:, 0:1], in_=q_[:], axis=AX.X), reads=[qk_], writes=[sk])
                            ts('dve', s_[:, 0:1], s_[:, 0:1], 1.0 / 256, EPS, ALU.mult, ALU.add, [sk], [sk])
                            act(s_[:, 1:2], s_[:, 0:1], AF.Sqrt, [sk], [sk])
                            S.op('dve', lambda e: e.reciprocal(out=s_[:, 2:3], in_=s_[:, 1:2]), reads=[sk], writes=[sk])
                            stt(ZB[:, j, :], z_[:], s_[:, 2:3], ggb[:], ALU.mult, ALU.mult, [fk, sk, 'ggb'], ['ZB%d' % j])
                        proj_tm(hT, hk, sub, wb_, 'wbb', 256, 256, pqr, ev)
                mt = [sb(ph, "mt%d" % i, [128, 128], F32) for i in range(2)]
                mtr = Rot([("mt%d" % i, mt[i]) for i in range(2)])
                for j in range(16 if last else 18):
                    tok = slice(128 * j, 128 * j + 128)
                    for pr in range(2):
                        pk, pt = pqr.next()
                        for hh in range(2):
                            po = hh * 64
                            mm(pt[po:po + 64, 0:128], ZB[:, j, pr * 128 + po:pr * 128 + po + 64], wsb[:, 2 * pr + hh, :],
                               True, True, ['ZB%d' % j, 'wsb'], [pk])
                        mk, m_ = mtr.next()
                        tt('dve', m_[:], pt[:, 0:128], bsb[:, pr, :], ALU.add, [pk, 'bsb'], [mk])
                        tt('pool', CAT[:, 2 + pr, tok], m_[:], UB[:, pr, tok], ALU.mult, [mk, 'UB'], ['CATb'])

        def mixer_d(s, l, last):
            with contextlib.ExitStack() as ph:
                FT = sb(ph, "FT", [128, 18, 256], BF16)
                wfi = sb(ph, "wfi", [128, 8, 256], BF16)
                wfn = sb(ph, "wfn", [128, 2, 256], BF16)
                wload(wfi[:], 'wfi', l, OFF_D, 256)
                dma('pool', wfn[:], w_fnet[l].rearrange("(kc p) n -> p kc n", p=128), writes=['wfn'])
                pq = [ps(ph, "pqd%d" % i, [128, 512]) for i in range(2)]
                pqr = Rot([("pqd%d" % i, pq[i]) for i in range(2)])
                nb = 8 if last else 9
                for b in range(nb):
                    t0 = 256 * b
                    v = s if b < 8 else 2
                    hk, hT = make_h(l, 0, t0, 256, v)
                    for sub in range(2):
                        j = 2 * b + sub
                        proj_tm(hT, hk, sub, wfi, 'wfi', 0, 256, pqr,
                                lambda p_, pk, j=j: cp('dve', FT[:, j, :], p_, [pk], ['FT']))
                dc_ = [sb(ph, "dc%d" % i, [128, 16, 256], BF16) for i in range(2)]
                ds_ = [sb(ph, "ds%d" % i, [128, 16, 256], BF16) for i in range(2)]
                YC = [sb(ph, "YC%d" % i, [128, 2, 256], BF16) for i in range(2)]
                YS = [sb(ph, "YS%d" % i, [128, 2, 256], BF16) for i in range(2)]
                SPc = [sb(ph, "SP%d" % i, [128, 2, 256], BF16) for i in range(2)]
                ycp = ps(ph, "ycp", [128, 512])
                ysp = ps(ph, "ysp", [128, 512])
                spp = ps(ph, "spp", [128, 512])
                dpp = ps(ph, "dpp", [128, 512])
                it = 0
                segs = [(0, 16, TL, c_dftc_l, c_dfts_l)]
                if not last:
                    segs.append((16, 2, 256, c_dftc_c, c_dfts_c))
                for (jb, nchk, nT, mc, ms) in segs:
                    scale = float(1.0 / np.sqrt(64.0 * nT))
                    for tb in range(nT // 256):
                        bb = it % 2
                        it += 1
                        dma('sp', dc_[bb][:, 0:nchk, :], mc[:, tb * 256:(tb + 1) * 256].rearrange("(c p) n -> p c n", p=128),
                            writes=['dc%d' % bb])
                        dma('sp', ds_[bb][:, 0:nchk, :], ms[:, tb * 256:(tb + 1) * 256].rearrange("(c p) n -> p c n", p=128),
                            writes=['ds%d' % bb])
                        for fc in range(2):
                            for i in range(nchk):
                                mm(ycp[:, 0:256], FT[:, jb + i, fc * 128:(fc + 1) * 128], dc_[bb][:, i, :], i == 0, i == nchk - 1,
                                   ['FT', 'dc%d' % bb], ['ycp'])
                            for i in range(nchk):
                                mm(ysp[:, 0:256], FT[:, jb + i, fc * 128:(fc + 1) * 128], ds_[bb][:, i, :], i == 0, i == nchk - 1,
                                   ['FT', 'ds%d' % bb], ['ysp'])
                            act(YC[bb][:, fc, :], ycp[:, 0:256], AF.Copy, ['ycp'], ['YC%d' % bb])
                            cp('dve', YS[bb][:, fc, :], ysp[:, 0:256], ['ysp'], ['YS%d' % bb])
                            mm(spp[:, 0:256], blk[:, 0, :], YC[bb][:, fc, :], True, False, ['blk', 'YC%d' % bb], ['spp'])
                            mm(spp[:, 0:256], blk[:, 1, :], YS[bb][:, fc, :], False, True, ['blk', 'YS%d' % bb], ['spp'])
                            act(SPc[bb][:, fc, :], spp[:, 0:256], AF.Copy, ['spp'], ['SP%d' % bb], scale=scale)
                        for oc in range(2):
                            for fc in range(2):
                                mm(dpp[:, 0:256], wfn[:, fc, oc * 128:(oc + 1) * 128], SPc[bb][:, fc, :], fc == 0, fc == 1,
                                   ['wfn', 'SP%d' % bb], ['dpp'])
                            t0 = 128 * jb + tb * 256
                            cp('dve', CAT[:, 6 + oc, t0:t0 + 256], dpp[:, 0:256], ['dpp'], ['CATd'])

        class _Stop(Exception):
            pass

        def body():
          if dbg:
              S.op('pool', lambda e: e.memset(CAT[:], 0.0), writes=['CAT'])
              S.barrier()
          for s in range(2):
            for c in range(8):
                dma('sp', X[:, c, :], xin[s, :, c, :], writes=['X%d' % c])
            for l in range(DEPTH):
                last = (l == DEPTH - 1)
                compute_rs()
                S.barrier()
                if 'C' not in SKIP:
                    mixer_c(s, l, last)
                    S.barrier()
                if 'A' not in SKIP:
                    mixer_a(s, l, last)
                    S.barrier()
                if 'B' not in SKIP:
                    mixer_b(s, l, last)
                    S.barrier()
                if 'D' not in SKIP:
                    mixer_d(s, l, last)
                    S.barrier()
                if dbg == ('cat', s, l):
                    for c in range(8):
                        dma('pool', dbg_out[:, c, :], CAT[:, c, :], reads=['CAT'])
                    return
                S.barrier()
                for _once in ([] if 'wout' in SKIP else [0]):
                  with contextlib.ExitStack() as ph:
                    wo = sb(ph, "wo", [128, 8, D], BF16)
                    yps = [ps(ph, "yps%d" % i, [128, 512]) for i in range(2)]
                    for kc in range(8):
                        dma('pool', wo[:, kc, :], w_out[l, kc * 128:(kc + 1) * 128, :], writes=['wo'])
                    it = 0
                    nblk = 4 if last else 5
                    for b in range(nblk):
                        t0 = b * 512
                        n = 512 if b < 4 else 256
                        v = s if b < 4 else 2
                        for oc in range(8):
                            pb = it % 2
                            it += 1
                            for kc in range(8):
                                mm(yps[pb][:, :n], wo[:, kc, oc * 128:(oc + 1) * 128], CAT[:, kc, t0:t0 + n],
                                   kc == 0, kc == 7, ['wo', 'CAT'], ['yps%d' % pb])
                            stt(X[:, oc, t0:t0 + n], yps[pb][:, :n], modap(l, 2, oc, v), X[:, oc, t0:t0 + n],
                                ALU.mult, ALU.add, ['yps%d' % pb, 'modT', 'X%d' % oc], ['X%d' % oc])
                S.barrier()
                for _once in ([] if 'ffn' in SKIP else [0]):
                  with contextlib.ExitStack() as ph:
                    nblk = 4 if last else 5
                    H2 = CAT
                    for b2 in range(8 if last else 9):
                        t0 = b2 * 256
                        v = s if b2 < 8 else 2
                        make_h(l, 1, t0, 256, v, dst=('H2_%d' % (b2 // 2), H2[:, :, t0:t0 + 256]))
                    w1 = [sb(ph, "w1_%d" % i, [128, 8, 512], BF16) for i in range(2)]
                    w2 = [sb(ph, "w2_%d" % i, [128, 4, D], BF16) for i in range(2)]
                    ag = [sb(ph, "ag%d" % i, [128, 4, 512], BF16) for i in range(2)]
                    rl = [sb(ph, "rl%d" % i, [128, 512], F32) for i in range(2)]
                    fps = [ps(ph, "fps%d" % i, [128, 512]) for i in range(2)]
                    ops_ = [ps(ph, "ops%d" % i, [128, 512]) for i in range(2)]
                    i1 = i2 = i3 = 0
                    for g in range(8):
                        wb = g % 2
                        dma('pool', w1[wb][:], w_ff1[l, :, g * 512:(g + 1) * 512].rearrange("(kc p) n -> p kc n", p=128),
                            writes=['w1_%d' % wb])
                        dma('pool', w2[wb][:], w_ff2[l, g * 512:(g + 1) * 512, :].rearrange("(kc p) n -> p kc n", p=128),
                            writes=['w2_%d' % wb])
                        for b in range(nblk):
                            t0 = b * 512
                            n = 512 if b < 4 else 256
                            v = s if b < 4 else 2
                            ab = i1 % 2
                            i1 += 1
                            for fc in range(4):
                                pb = i2 % 2
                                i2 += 1
                                for kc in range(8):
                                    mm(fps[pb][:, :n], w1[wb][:, kc, fc * 128:(fc + 1) * 128], H2[:, kc, t0:t0 + n],
                                       kc == 0, kc == 7, ['w1_%d' % wb, 'H2_%d_%d' % (b, kc)], ['fps%d' % pb])
                                act(rl[pb][:, :n], fps[pb][:, :n], AF.Relu, ['fps%d' % pb], ['rl%d' % pb])
                                tt('pool', ag[ab][:, fc, :n], rl[pb][:, :n], rl[pb][:, :n], ALU.mult,
                                   ['rl%d' % pb], ['ag%d_%d' % (ab, fc)])
                            for oc in range(8):
                                pb = i3 % 2
                                i3 += 1
                                for kc in range(4):
                                    mm(ops_[pb][:, :n], w2[wb][:, kc, oc * 128:(oc + 1) * 128], ag[ab][:, kc, :n],
                                       kc == 0, kc == 3, ['w2_%d' % wb, 'ag%d_%d' % (ab, kc)], ['ops%d' % pb])
                                stt(X[:, oc, t0:t0 + n], ops_[pb][:, :n], modap(l, 5, oc, v), X[:, oc, t0:t0 + n],
                                    ALU.mult, ALU.add, ['ops%d' % pb, 'modT', 'X%d' % oc], ['X%d' % oc])
                S.barrier()
                if dbg == ('x', s, l):
                    for c in range(8):
                        dma('sp', dbg_out[:, c, :], X[:, c, :], reads=['X%d' % c])
                    return
            with contextlib.ExitStack() as ph:
                ob = [sb(ph, "ob%d" % i, [128, 256], F32) for i in range(2)]
                it = 0
                for b in range(8):
                    t0 = b * 256
                    pk, pst = ssqrot.next()
                    for c in range(8):
                        sk, sq = sqrot.next()
                        act(sq[:], X[:, c, t0:t0 + 256], AF.Square, ['X%d' % c], [sk])
                        mm(pst[:, 0:256], onesb[:], sq[:], c == 0, c == 7, [sk, 'onesb'], [pk])
                    rk, rs = rsrot.next()
                    act(rs[:], pst[:, 0:256], AF.Sqrt, [pk, 'epsb'], [rk], scale=1.0 / D, bias=epsb[:, 0:1])
                    S.op('dve', lambda e, rs=rs: e.reciprocal(out=rs[:], in_=rs[:]), reads=[rk], writes=[rk])
                    for c in range(8):
                        o = it % 2
                        it += 1
                        stt(ob[o][:], X[:, c, t0:t0 + 256], gv[:, 4, c:c + 1], rs[:], ALU.mult, ALU.mult,
                            ['X%d' % c, 'gv', rk], ['ob%d' % o])
                        dma('sp', outT[s, :, c, t0:t0 + 256], ob[o][:], reads=['ob%d' % o])
            S.barrier()
        body()
        S.finish('sp')
        print("ops", S.nops, "waits", S.nwaits, {e: S.cc[e] for e in S.cc}, {e: S.dc[e] for e in S.dc})
    return nc


_NC_CACHE = {}


def _prep_shared(inp):
    f32 = lambda a: np.ascontiguousarray(np.asarray(a, dtype=np.float32))
    sh = dict(_consts())
    sh['w_ada'] = f32(inp['w_ada'])
    sh['badaT'] = np.ascontiguousarray(np.moveaxis(f32(inp['b_ada']).reshape(2, 48, 128), 2, 0))
    gl = [inp['g_norm_mix'][0], inp['g_norm_ffn'][0], inp['g_norm_mix'][1], inp['g_norm_ffn'][1], inp['g_final']]
    sh['gvec'] = np.ascontiguousarray(np.stack([f32(g).reshape(8, 128).T for g in gl], 1))
    sh['w_in'] = f32(inp['w_in'])
    bgate = f32(inp['b_gate'])
    sh['bg'] = np.ascontiguousarray(bgate.reshape(2, 4, 4).transpose(2, 0, 1))
    wc = f32(inp['w_conv_qk'])
    sh['wconv'] = np.ascontiguousarray(wc.reshape(2, 3, 4, 128).transpose(3, 0, 2, 1))
    rpb = f32(inp['rpb'])
    sh['natb'] = np.ascontiguousarray(np.stack([_nat_bias(rpb[l]).reshape(5, 2, 128, 1280) for l in range(2)], 0))
    sh['wsT'] = np.ascontiguousarray(f32(inp['w_spatial']).transpose(0, 3, 1, 2))
    bs = f32(inp['b_spatial'])
    bsT = np.zeros((2, 128, 2, 128), np.float32)
    for pr in range(2):
        for hh in range(2):
            bsT[:, hh * 64:(hh + 1) * 64, pr, :] = bs[:, 2 * pr + hh, None, :]
    sh['bsT'] = bsT
    sh['ggm'] = f32(inp['g_gmlp'])
    sh['gml'] = f32(inp['g_mlstm'])
    sh['w_fnet'] = f32(inp['w_fnet'])
    sh['w_out'] = f32(inp['w_out'])
    sh['w_ff1'] = f32(inp['w_ff1'])
    sh['w_ff2'] = f32(inp['w_ff2'])
    return sh


def _fmT(a):
    return np.ascontiguousarray(a.T.reshape(8, 128, a.shape[0]).transpose(1, 0, 2))


def kernel(dbg=None, **inp):
    x = np.asarray(inp['x'], np.float32)
    c = np.asarray(inp['c'], np.float32)
    ctx = np.asarray(inp['ctx'], np.float32)
    c_ctx = np.asarray(inp['c_ctx'], np.float32)
    sh = _prep_shared(inp)
    key = repr(dbg)
    if key not in _NC_CACHE:
        _NC_CACHE[key] = build_nc(dbg)
    nc = _NC_CACHE[key]
    in_maps = []
    ncores = int(os.environ.get('MK_CORES', '8'))
    for core in range(ncores):
        m = dict(sh)
        xs = []
        for i in range(2):
            b = 2 * core + i
            xs.append(_fmT(np.concatenate([x[b], ctx[b]], 0)))
        m['xin'] = np.ascontiguousarray(np.stack(xs, 0))
        vecs = [c[2 * core], c[2 * core + 1], c_ctx]
        m['cT'] = np.ascontiguousarray(np.stack([v.reshape(8, 128).T for v in vecs], 2))
        in_maps.append(m)
    res = run_bass_kernel_spmd(nc, in_maps, core_ids=list(range(ncores)))
    out = np.zeros((16, TL, D), np.float32)
    for core in range(ncores):
        o = res.results[core]['outT']
        for i in range(2):
            out[2 * core + i] = o[i].transpose(2, 1, 0).reshape(TL, D)
    if dbg:
        return out, [res.results[core]['dbg'] for core in range(ncores)]
    return out
```

```python
import contextlib
import numpy as np
import ml_dtypes
import concourse.bass as bass
import concourse.mybir as mybir
from concourse.bass_utils import run_bass_kernel_spmd

F32 = mybir.dt.float32
BF16 = mybir.dt.bfloat16
AF = mybir.ActivationFunctionType
ALU = mybir.AluOpType
AX = mybir.AxisListType

D = 1024
T = 2304
TL = 2048
NCH = 18
DIN = 2576
DEPTH = 2
EPS = 1e-6
NEG = -30000.0
import os
SKIP = set(os.environ.get('MK_SKIP', '').split(','))
CSTOP = int(os.environ.get('MK_CSTOP', '9'))
ASTOP = int(os.environ.get('MK_ASTOP', '9'))
OFF_A, OFF_B, OFF_C, OFF_D, OFF_G = 0, 768, 1280, 2304, 2560


class Sch:
    def __init__(self, nc, stack, ndma=8, same_engine_sync=True):
        self.nc = nc
        self.E = {'pe': nc.tensor, 'act': nc.scalar, 'dve': nc.vector,
                  'pool': nc.gpsimd, 'sp': nc.sync}
        self.R = ndma
        self.same = same_engine_sync
        self.csem = {}
        self.dsem = {}
        for e in self.E:
            self.csem[e] = stack.enter_context(nc.semaphore('c_' + e))
            self.dsem[e] = [stack.enter_context(nc.semaphore('d_%s_%d' % (e, i)))
                            for i in range(ndma)]
        self.cc = {e: 0 for e in self.E}
        self.dc = {e: 0 for e in self.E}
        self.waited = {e: {} for e in self.E}
        self.lw = {}
        self.rd = {}
        self.bar = set()
        self.nops = 0
        self.nwaits = 0

    def _tok_sem(self, tok):
        kind, e, i = tok
        if kind == 'c':
            return (kind, e, 0), self.csem[e], i
        return (kind, e, i % self.R), self.dsem[e][i % self.R], 16 * (i // self.R + 1)

    def _wait(self, eng, tok):
        kind, e, i = tok
        if kind == 'c' and e == eng and (eng == 'pe' or not self.same):
            return
        key, sem, val = self._tok_sem(tok)
        if self.waited[eng].get(key, 0) >= val:
            return
        self.waited[eng][key] = val
        self.E[eng].wait_ge(sem, val)
        self.nwaits += 1

    def op(self, eng, fn, reads=(), writes=(), dma=False):
        deps = set(self.bar)
        for r in reads:
            if r in self.lw:
                deps.add(self.lw[r])
        for w in writes:
            if w in self.lw:
                lt = self.lw[w]
                if not (lt[0] == 'c' and lt[1] == eng and not dma):
                    deps.add(lt)
            for t in self.rd.get(w, ()):
                deps.add(t)
        if dma:
            k = self.dc[eng]
            self.dc[eng] += 1
            tok = ('d', eng, k)
            if k >= self.R:
                deps.add(('d', eng, k - self.R))
        else:
            self.cc[eng] += 1
            tok = ('c', eng, self.cc[eng])
        best = {}
        for t in deps:
            key, sem, val = self._tok_sem(t)
            if key not in best or best[key][1] < val:
                best[key] = (t, val)
        for key in sorted(best, key=str):
            self._wait(eng, best[key][0])
        inst = fn(self.E[eng])
        _, sem, _ = self._tok_sem(tok)
        inst.then_inc(sem, 16 if dma else 1)
        for w in writes:
            self.lw[w] = tok
            self.rd[w] = []
        for r in reads:
            self.rd.setdefault(r, []).append(tok)
        self.nops += 1
        return tok

    def barrier(self):
        self.bar = set()
        for e in self.E:
            if self.cc[e] > 0:
                self.bar.add(('c', e, self.cc[e]))
            for j in range(max(0, self.dc[e] - self.R), self.dc[e]):
                self.bar.add(('d', e, j))
        self.lw = {}
        self.rd = {}

    def finish(self, eng='sp'):
        self.barrier()
        best = {}
        for t in self.bar:
            key, sem, val = self._tok_sem(t)
            if key not in best or best[key][1] < val:
                best[key] = (t, val)
        for key in sorted(best, key=str):
            self._wait(eng, best[key][0])


class Rot:
    def __init__(self, items):
        self.items = items
        self.i = 0

    def next(self):
        it = self.items[self.i % len(self.items)]
        self.i += 1
        return it


def _bf(a):
    return np.ascontiguousarray(a.astype(ml_dtypes.bfloat16))


def _consts():
    c = {}
    c['ident'] = np.eye(128, dtype=np.float32)
    c['jrev'] = np.ascontiguousarray(np.eye(128, dtype=np.float32)[::-1])
    s = np.arange(128)
    mf = (s[:, None] <= s[None, :]).astype(np.float32)
    mb = (s[:, None] >= s[None, :]).astype(np.float32)
    c['masks'] = _bf(np.stack([mf, mb], 1))
    t = np.arange(TL)
    rows = (t // 64).astype(np.float32)
    cols = (t % 64).astype(np.float32)
    p = np.arange(128)
    d = p % 64
    half = d // 32
    i = (d % 16).astype(np.float32)
    inv = (10000.0 ** (-i / 16.0)).astype(np.float32)
    pos = np.where(half[:, None] == 0, rows[None, :], cols[None, :]).astype(np.float32)
    ang = pos * inv[:, None]
    c['rope'] = _bf(np.stack([np.cos(ang), np.sin(ang)], 1))
    rm = np.zeros((128, 128), np.float32)
    for m in range(128):
        dd = m % 32
        if dd < 16:
            rm[m + 16, m] = -1.0
        else:
            rm[m - 16, m] = 1.0
    c['rm'] = _bf(rm)
    for n, name in ((2048, 'l'), (256, 'c')):
        k = np.arange(n, dtype=np.float64)
        ang = 2.0 * np.pi * np.outer(k, k) / n
        c['dftc_' + name] = _bf(np.cos(ang))
        c['dfts_' + name] = _bf(-np.sin(ang))
    k = np.arange(64, dtype=np.float64)
    ang = 2.0 * np.pi * np.outer(k, k) / 64
    bc = np.zeros((128, 128)); bs = np.zeros((128, 128))
    for g in range(2):
        bc[g * 64:(g + 1) * 64, g * 64:(g + 1) * 64] = np.cos(ang)
        bs[g * 64:(g + 1) * 64, g * 64:(g + 1) * 64] = np.sin(ang)
    c['blk'] = _bf(np.stack([bc, bs], 1))
    sel = np.zeros((4, 2, 128), np.float32)
    for pr in range(2):
        sel[2 * pr, pr, :64] = 1.0
        sel[2 * pr + 1, pr, 64:] = 1.0
    c['sel'] = sel
    return c


def _nat_bias(rpb_l):
    out = np.full((5, 2, 128, 5, 2, 128), NEG, np.float32)
    tsel = [0, 1, 2, 14, 15]
    kk = np.arange(128)
    qq = np.arange(128)
    for ti, t in enumerate(tsel):
        cb = min(max(t - 2, 0), 11)
        for j in range(5):
            kr = (cb + j) * 2 + kk // 64
            kc = kk % 64
            r = 2 * t + qq // 64
            qc = qq % 64
            rs = np.clip(r - 4, 0, 24)
            row_ok = (kr[:, None] >= rs[None, :]) & (kr[:, None] < rs[None, :] + 8)
            qs = np.clip(qc - 8, 0, 48)
            col_ok = (kc[:, None] >= qs[None, :]) & (kc[:, None] < qs[None, :] + 16)
            ok = row_ok & col_ok
            drow = np.clip(kr[:, None] - r[None, :] + 7, 0, 14)
            dcol = np.clip(kc[:, None] - qc[None, :] + 15, 0, 30)
            for h in range(4):
                val = rpb_l[h][drow, dcol]
                out[ti, h // 2, :, j, h % 2, :] = np.where(ok, val, NEG)
    return out


def _fm(vec):
    v = vec.reshape(vec.shape[:-1] + (8, 128))
    return np.ascontiguousarray(np.moveaxis(v, -1, 0))


def build_nc(dbg=None):
    nc = bass.Bass("TRN2", target_bir_lowering=False)

    def din(name, shape, dt=F32):
        return nc.dram_tensor(name, list(shape), dt, kind="ExternalInput").ap()

    xin = din("xin", [2, 128, 8, T])
    cT = din("cT", [128, 8, 3])
    w_ada = din("w_ada", [2, D, 6 * D])
    badaT = din("badaT", [128, 2, 48])
    gvec = din("gvec", [128, 5, 8])
    w_in = din("w_in", [2, D, DIN])
    bg = din("bg", [4, 2, 4])
    wconv = din("wconv", [128, 2, 4, 3])
    natb = din("natb", [2, 5, 2, 128, 1280])
    wsT = din("wsT", [2, 128, 4, 128])
    bsT = din("bsT", [2, 128, 2, 128])
    ggm = din("ggm", [2, 256])
    gml = din("gml", [2, 256])
    w_fnet = din("w_fnet", [2, 256, 256])
    w_out = din("w_out", [2, D, D])
    w_ff1 = din("w_ff1", [2, D, 4 * D])
    w_ff2 = din("w_ff2", [2, 4 * D, D])
    c_ident = din("ident", [128, 128])
    c_jrev = din("jrev", [128, 128])
    c_masks = din("masks", [128, 2, 128], BF16)
    c_rope = din("rope", [128, 2, TL], BF16)
    c_rm = din("rm", [128, 128], BF16)
    c_dftc_l = din("dftc_l", [TL, TL], BF16)
    c_dfts_l = din("dfts_l", [TL, TL], BF16)
    c_dftc_c = din("dftc_c", [256, 256], BF16)
    c_dfts_c = din("dfts_c", [256, 256], BF16)
    c_blk = din("blk", [128, 2, 128], BF16)
    c_sel = din("sel", [4, 2, 128])
    outT = nc.dram_tensor("outT", [2, 128, 8, TL], F32, kind="ExternalOutput").ap()
    dbg_out = None
    if dbg:
        dbg_out = nc.dram_tensor("dbg", [128, 8, T], F32, kind="ExternalOutput").ap()

    with contextlib.ExitStack() as st:
        S = Sch(nc, st)

        uid = [0]

        def sb(stack, name, shape, dt):
            uid[0] += 1
            return stack.enter_context(nc.sbuf_tensor("s%d_%s" % (uid[0], name), list(shape), dt))

        def ps(stack, name, shape, dt=F32):
            uid[0] += 1
            return stack.enter_context(nc.psum_tensor("p%d_%s" % (uid[0], name), list(shape), dt))

        def dma(eng, out, in_, reads=(), writes=()):
            S.op(eng, lambda e: e.dma_start(out=out, in_=in_), reads=reads, writes=writes, dma=True)

        def mm(out, lhsT, rhs, start, stop, reads, writes):
            S.op('pe', lambda e: e.matmul(out, lhsT=lhsT, rhs=rhs, start=start, stop=stop),
                 reads=reads, writes=writes)

        def act(out, in_, func, reads, writes, scale=1.0, bias=None):
            if bias is None:
                S.op('act', lambda e: e.activation(out=out, in_=in_, func=func, scale=scale),
                     reads=reads, writes=writes)
            else:
                S.op('act', lambda e: e.activation(out=out, in_=in_, func=func, scale=scale, bias=bias),
                     reads=reads, writes=writes)

        def tt(eng, out, in0, in1, op, reads, writes):
            S.op(eng, lambda e: e.tensor_tensor(out=out, in0=in0, in1=in1, op=op), reads=reads, writes=writes)

        def stt(out, in0, scalar, in1, op0, op1, reads, writes):
            S.op('dve', lambda e: e.scalar_tensor_tensor(out=out, in0=in0, scalar=scalar, in1=in1, op0=op0, op1=op1),
                 reads=reads, writes=writes)

        def ts(eng, out, in0, s1, s2, op0, op1, reads, writes):
            if s2 is None:
                S.op(eng, lambda e: e.tensor_scalar(out=out, in0=in0, scalar1=s1, scalar2=None, op0=op0),
                     reads=reads, writes=writes)
            else:
                S.op(eng, lambda e: e.tensor_scalar(out=out, in0=in0, scalar1=s1, scalar2=s2, op0=op0, op1=op1),
                     reads=reads, writes=writes)

        def cp(eng, out, in_, reads, writes):
            S.op(eng, lambda e: e.tensor_copy(out=out, in_=in_), reads=reads, writes=writes)

        X = sb(st, "X", [128, 8, T], F32)
        CAT = sb(st, "CAT", [128, 8, T], BF16)
        ident = sb(st, "ident", [128, 128], F32)
        jrev = sb(st, "jrev", [128, 128], F32)
        identb = sb(st, "identb", [128, 128], BF16)
        masks = sb(st, "masks", [128, 2, 128], BF16)
        rm = sb(st, "rm", [128, 128], BF16)
        blk = sb(st, "blk", [128, 2, 128], BF16)
        sel = sb(st, "sel", [4, 2, 128], F32)
        onesb = sb(st, "onesb", [128, 128], BF16)
        onesf = sb(st, "onesf", [128, 2], F32)
        RS = sb(st, "RS", [128, T], F32)
        modT = sb(st, "modT", [128, 2, 48, 3], F32)
        gmT = sb(st, "gmT", [128, 2, 2, 8, 3], F32)
        gv = sb(st, "gv", [128, 5, 8], F32)
        bada = sb(st, "bada", [128, 2, 48], F32)
        csb = sb(st, "csb", [128, 8, 3], F32)
        scb = sb(st, "scb", [128, 8, 3], BF16)
        bgs = sb(st, "bgs", [4, 2, 4], F32)
        wcv = sb(st, "wcv", [128, 2, 4, 3], F32)

        dma('sp', ident[:], c_ident, writes=['ident'])
        dma('sp', jrev[:], c_jrev, writes=['jrev'])
        dma('sp', masks[:], c_masks, writes=['masks'])
        dma('sp', rm[:], c_rm, writes=['rm'])
        dma('sp', blk[:], c_blk, writes=['blk'])
        dma('sp', sel[:], c_sel, writes=['sel'])
        dma('sp', gv[:], gvec, writes=['gv'])
        dma('sp', bada[:], badaT, writes=['bada'])
        dma('sp', csb[:], cT, writes=['csb'])
        dma('sp', bgs[:], bg, writes=['bgs'])
        dma('sp', wcv[:], wconv, writes=['wcv'])
        S.op('dve', lambda e: e.memset(onesb[:], 1.0), writes=['onesb'])
        S.op('dve', lambda e: e.memset(onesf[:], 1.0), writes=['onesf'])
        cp('dve', identb[:], ident[:], ['ident'], ['identb'])
        act(scb[:], csb[:], AF.Silu, ['csb'], ['scb'])

        with contextlib.ExitStack() as ph:
            wa = [sb(ph, "wa%d" % i, [128, 8, 512], BF16) for i in range(2)]
            mps = [ps(ph, "mps%d" % i, [128, 4, 3]) for i in range(2)]
            it = 0
            for l in range(DEPTH):
                for pc in range(12):
                    b = it % 2
                    it += 1
                    dma('pool', wa[b][:], w_ada[l, :, pc * 512:(pc + 1) * 512].rearrange("(kc p) n -> p kc n", p=128),
                        writes=['wa%d' % b])
                    for oc in range(4):
                        for kc in range(8):
                            mm(mps[b][:, oc, :], wa[b][:, kc, oc * 128:(oc + 1) * 128], scb[:, kc, :],
                               kc == 0, kc == 7, ['wa%d' % b, 'scb'], ['mps%d' % b])
                    tt('dve', modT[:, l, pc * 4:(pc + 1) * 4, :], mps[b][:],
                       bada[:, l, pc * 4:(pc + 1) * 4].unsqueeze(2).broadcast_to([128, 4, 3]), ALU.add,
                       ['mps%d' % b, 'bada'], ['modT'])
            for l in range(DEPTH):
                for kind in range(2):
                    sc_j = 1 if kind == 0 else 4
                    for v in range(3):
                        stt(gmT[:, l, kind, :, v], modT[:, l, sc_j * 8:(sc_j + 1) * 8, v], 1.0,
                            gv[:, 2 * l + kind, :], ALU.add, ALU.mult, ['modT', 'gv'], ['gmT'])
        S.barrier()

        def modap(l, j, c, v):
            return modT[:, l, j * 8 + c, v:v + 1]

        hbuf = [sb(st, "hT%d" % i, [128, 8, 256], BF16) for i in range(2)]
        hrot = Rot([("hT%d" % i, hbuf[i]) for i in range(2)])
        sqb = [sb(st, "sq%d" % i, [128, 256], BF16) for i in range(2)]
        sqrot = Rot([("sq%d" % i, sqb[i]) for i in range(2)])
        rsb = [sb(st, "rs%d" % i, [128, 256], F32) for i in range(2)]
        rsrot = Rot([("rs%d" % i, rsb[i]) for i in range(2)])
        tmb = [sb(st, "tm%d" % i, [128, 256], F32) for i in range(2)]
        tmrot = Rot([("tm%d" % i, tmb[i]) for i in range(2)])
        ssq = [ps(st, "ssq%d" % i, [128, 512]) for i in range(1)]
        ssqrot = Rot([("ssq%d" % i, ssq[i]) for i in range(1)])

        def make_h(l, kind, t0, n, v, dst=None):
            if dst is None:
                hk, hT = hrot.next()
            else:
                hk, hT = dst
            if kind == 0:
                return _mk_tail(l, kind, t0, n, v, hk, hT, 'RS', RS[:, t0:t0 + n])
            pk, pst = ssqrot.next()
            for c in range(8):
                sk, sq = sqrot.next()
                act(sq[:, :n], X[:, c, t0:t0 + n], AF.Square, ['X%d' % c], [sk])
                mm(pst[:, :n], onesb[:], sq[:, :n], c == 0, c == 7, [sk, 'onesb'], [pk])
            rk, rs = rsrot.next()
            act(rs[:, :n], pst[:, :n], AF.Sqrt, [pk, 'epsb'], [rk], scale=1.0 / D, bias=epsb[:, 0:1])
            S.op('dve', lambda e: e.reciprocal(out=rs[:, :n], in_=rs[:, :n]), reads=[rk], writes=[rk])
            return _mk_tail(l, kind, t0, n, v, hk, hT, rk, rs)

        def _mk_tail(l, kind, t0, n, v, hk, hT, rk, rs):
            sh_j = 0 if kind == 0 else 3
            for c in range(8):
                stt(hT[:, c, :n], X[:, c, t0:t0 + n], gmT[:, l, kind, c, v:v + 1], rs[:, :n], ALU.mult, ALU.mult,
                    ['X%d' % c, 'gmT', rk], [hk + '_%d' % c])
            for c in range(8):
                act(hT[:, c, :n], hT[:, c, :n], AF.Identity, [hk + '_%d' % c, 'modT'], [hk + '_%d' % c],
                    bias=modap(l, sh_j, c, v))
            return hk, hT

        def hblocks(l, nb, s):
            nxt = make_h(l, 0, 0, 256, s)
            for b in range(nb):
                cur = nxt
                if b + 1 < nb:
                    nxt = make_h(l, 0, 256 * (b + 1), 256, s if b + 1 < 8 else 2)
                yield b, 256 * b, (s if b < 8 else 2), cur[0], cur[1]

        def compute_rs():
            for b in range(9):
                t0 = 256 * b
                n = 256
                pk, pst = ssqrot.next()
                for c in range(8):
                    sk, sq = sqrot.next()
                    act(sq[:, :n], X[:, c, t0:t0 + n], AF.Square, ['X%d' % c], [sk])
                    mm(pst[:, :n], onesb[:], sq[:, :n], c == 0, c == 7, [sk, 'onesb'], [pk])
                act(RS[:, t0:t0 + n], pst[:, :n], AF.Sqrt, [pk, 'epsb'], ['RS'], scale=1.0 / D, bias=epsb[:, 0:1])
                S.op('dve', lambda e, t0=t0, n=n: e.reciprocal(out=RS[:, t0:t0 + n], in_=RS[:, t0:t0 + n]),
                     reads=['RS'], writes=['RS'])

        epsb = sb(st, "epsb", [128, 1], F32)
        S.op('dve', lambda e: e.memset(epsb[:], EPS), writes=['epsb'])

        def proj_fm(hT, hk, n, w, wk, col0, ppool, evac):
            pk, pt = ppool.next()
            for kc in range(8):
                mm(pt[:, :n], w[:, kc, col0:col0 + 128], hT[:, kc, :n], kc == 0, kc == 7, [wk, hk + '_%d' % kc], [pk])
            evac(pt[:, :n], pk)

        def proj_tm(hT, hk, sub, w, wk, col0, ncol, ppool, evac):
            pk, pt = ppool.next()
            for kc in range(8):
                mm(pt[:, :ncol], hT[:, kc, sub * 128:(sub + 1) * 128], w[:, kc, col0:col0 + ncol],
                   kc == 0, kc == 7, [wk, hk + '_%d' % kc], [pk])
            evac(pt[:, :ncol], pk)

        def wload(wt, key, l, col0, ncol):
            dma('pool', wt, w_in[l, :, col0:col0 + ncol].rearrange("(kc p) n -> p kc n", p=128), writes=[key])

        ORD = [[16, 17] + list(range(16)), [17, 16] + list(range(15, -1, -1))]

        def mixer_c(s, l, last):
            with contextlib.ExitStack() as ph:
                cf03 = CAT[:, 0:4, :].rearrange("p c t -> p (c t)")
                vaug = cf03[:, 0:4752].rearrange("p (j h d) -> p j h d", j=18, h=4)[:, :, :, 0:65]
                wqk = cf03[:, 4752:4752 + 4096].rearrange("p (k n) -> p k n", k=8)
                ktm = CAT[:, 6:8, :].rearrange("p c t -> p (c t)").rearrange("p (j f) -> p j f", f=256)
                R1 = sb(ph, "R1", [128, 2 * T], F32)
                R2 = sb(ph, "R2", [128, 4, T], BF16)
                RAW = R1[:].bitcast(BF16).rearrange("p (c t) -> p c t", c=4)
                QK = R2
                wv = sb(ph, "wv", [128, 8, 256], BF16)
                wg = sb(ph, "wg", [128, 8, 16], BF16)
                gtm = sb(ph, "gtm", [128, 18, 16], F32)
                gmb = sb(ph, "gmb", [128, 256], F32)
                dma('sp', gmb[:], gml[l, :].partition_broadcast(128), writes=['gmb'])
                wload(wqk, 'wqk', l, OFF_C, 512)
                wload(wv[:], 'wv', l, OFF_C + 512, 256)
                wload(wg[:], 'wg', l, OFF_G, 16)
                S.op('pool', lambda e: e.memset(vaug[:, :, :, 64:65], 1.0), writes=['vaug1'])
                with contextlib.ExitStack() as p1:
                    pq = [ps(p1, "pq%d" % i, [128, 512]) for i in range(2)]
                    pqr = Rot([("pq%d" % i, pq[i]) for i in range(2)])
                    pvv = [ps(p1, "pvv%d" % i, [128, 512]) for i in range(2)]
                    pvr = Rot([("pvv%d" % i, pvv[i]) for i in range(2)])
                    rope = sb(p1, "rope", [128, 2, TL], BF16)
                    dma('sp', rope[:], c_rope, writes=['rope'])
                    for b, t0, v, hk, hT in hblocks(l, 9, s):
                        for oc in range(4):
                            proj_fm(hT, hk, 256, wqk, 'wqk', oc * 128, pqr,
                                    lambda p_, pk, oc=oc: act(RAW[:, oc, t0:t0 + 256], p_, AF.Copy, [pk], ['RAW%d' % oc]))
                        for sub in range(2):
                            j = 2 * b + sub
                            proj_tm(hT, hk, sub, wv, 'wv', 0, 256, pvr,
                                    lambda p_, pk, j=j: cp('dve', vaug[:, j, :, 0:64], p_.rearrange("p (h d) -> p h d", h=4),
                                                           [pk], ['vaug%d' % j]))
                            proj_tm(hT, hk, sub, wg, 'wg', 0, 16, pvr,
                                    lambda p_, pk, j=j: cp('dve', gtm[:, j, :], p_, [pk], ['gtm']))
                    if CSTOP == 1:
                        return
                    cvt = [sb(p1, "cvt%d" % i, [128, 512], F32) for i in range(2)]
                    cvr = Rot([("cvt%d" % i, cvt[i]) for i in range(2)])
                    cst = [sb(p1, "cst%d" % i, [128, 512], BF16) for i in range(2)]
                    csr = Rot([("cst%d" % i, cst[i]) for i in range(2)])
                    r2t = [sb(p1, "r2t%d" % i, [128, 512], F32) for i in range(2)]
                    r2r = Rot([("r2t%d" % i, r2t[i]) for i in range(2)])
                    r3t = [sb(p1, "r3t%d" % i, [128, 512], F32) for i in range(2)]
                    r3r = Rot([("r3t%d" % i, r3t[i]) for i in range(2)])
                    for oc in range(4):
                        scl = 0.125 if oc >= 2 else 1.0
                        for (g0, g1) in ((0, TL), (TL, T)):
                            for t0 in range(g0, g1, 512):
                                n = min(512, g1 - t0)
                                ck, ct = cvr.next()
                                ts('dve', ct[:, :n], RAW[:, oc, t0:t0 + n], wcv[:, l, oc, 1:2], None, ALU.mult, None,
                                   ['RAW%d' % oc, 'wcv'], [ck])
                                a = 1 if t0 == g0 else 0
                                stt(ct[:, a:n], RAW[:, oc, t0 + a - 1:t0 + n - 1], wcv[:, l, oc, 0:1], ct[:, a:n],
                                    ALU.mult, ALU.add, ['RAW%d' % oc, 'wcv', ck], [ck])
                                bnd = n - 1 if t0 + n == g1 else n
                                stt(ct[:, 0:bnd], RAW[:, oc, t0 + 1:t0 + 1 + bnd], wcv[:, l, oc, 2:3], ct[:, 0:bnd],
                                    ALU.mult, ALU.add, ['RAW%d' % oc, 'wcv', ck], [ck])
                                if g0 == 0:
                                    sk, cs_ = csr.next()
                                    act(cs_[:, :n], ct[:, :n], AF.Silu, [ck], [sk])
                                    pk, pt = pqr.next()
                                    mm(pt[:, :n], rm[:], cs_[:, :n], True, True, ['rm', sk], [pk])
                                    k2, t2 = r2r.next()
                                    stt(t2[:, :n], pt[:, :n], scl, rope[:, 1, t0:t0 + n], ALU.mult, ALU.mult, [pk, 'rope'], [k2])
                                    k3, t3 = r3r.next()
                                    stt(t3[:, :n], cs_[:, :n], scl, rope[:, 0, t0:t0 + n], ALU.mult, ALU.mult, [sk, 'rope'], [k3])
                                    tt('pool', QK[:, oc, t0:t0 + n], t2[:, :n], t3[:, :n], ALU.add, [k2, k3], ['QK%d' % oc])
                                else:
                                    act(QK[:, oc, t0:t0 + n], ct[:, :n], AF.Silu, [ck], ['QK%d' % oc], scale=1.0)
                                    if scl != 1.0:
                                        ts('dve', QK[:, oc, t0:t0 + n], QK[:, oc, t0:t0 + n], scl, None, ALU.mult, None,
                                           ['QK%d' % oc], ['QK%d' % oc])
                    for j in range(18):
                        pk, pt = pqr.next()
                        for kc in range(2):
                            mm(pt[:, kc * 128:(kc + 1) * 128], QK[:, 2 + kc, 128 * j:128 * j + 128], identb[:], True, True,
                               ['QK%d' % (2 + kc), 'identb'], [pk])
                        cp('dve', ktm[:, j, :], pt[:, 0:256], [pk], ['ktm%d' % j])
                S.barrier()
                if CSTOP == 2:
                    return
                R3 = sb(ph, "R3", [128, 18 * 256], F32)
                hsum = R3[:].rearrange("p (j f) -> p j f", f=256)
                colq = [sb(ph, "colq%d" % i, [128, 18, 12], F32) for i in range(2)]
                dcol = [sb(ph, "dcol%d" % i, [128, 2, 18], F32) for i in range(2)]
                with contextlib.ExitStack() as p3:
                    R1f = R1
                    rI = R1f[0:4, 0:T]
                    rF = R1f[0:4, T:2 * T]
                    rG = R3[0:4, 0:T]
                    rA = R3[0:4, T:2 * T]
                    rrow = sb(p3, "rrow", [4, 20], F32)
                    drow = sb(p3, "drow", [4, 18], F32)
                    colraw = sb(p3, "colraw", [128, 18, 12], F32)
                    prw = [ps(p3, "prw%d" % i, [128, 512]) for i in range(2)]
                    pcol = ps(p3, "pcol", [128, 18, 12])
                    pcol2 = ps(p3, "pcol2", [128, 18, 12])
                    pd = ps(p3, "pd", [128, 2, 18])
                    for dr in range(2):
                        tr_m = ident if dr == 0 else jrev
                        trk = 'ident' if dr == 0 else 'jrev'
                        for g in range(5):
                            idxs = list(range(4 * g, min(4 * g + 4, 18)))
                            for qi, (dst, pw) in enumerate(((rI, prw[0]), (rF, prw[1]))):
                                for ii, idx in enumerate(idxs):
                                    j = ORD[dr][idx]
                                    c0 = dr * 8 + qi * 4
                                    mm(pw[0:4, ii * 128:(ii + 1) * 128], gtm[:, j, c0:c0 + 4], tr_m[:], True, True,
                                       ['gtm', trk], ['prw%d' % qi])
                                n = 128 * len(idxs)
                                ts('dve', dst[:, 512 * g:512 * g + n], pw[0:4, 0:n], bgs[:, l, dr * 2 + qi:dr * 2 + qi + 1], None,
                                   ALU.add, None, ['prw%d' % qi, 'bgs'], ['row%d' % qi])
                        act(rF, rF, AF.Exp, ['row1'], ['row1'], scale=-1.0)
                        act(rF, rF, AF.Ln, ['row1'], ['row1'], bias=onesf[0:4, 0:1])
                        S.op('dve', lambda e: e.tensor_tensor_scan(out=rG, data0=onesf[0:4, 0:1].broadcast_to([4, T]), data1=rF,
                                                                   initial=0.0, op0=ALU.mult, op1=ALU.add),
                             reads=['row1', 'onesf'], writes=['row2'])
                        tt('dve', rI, rI, rG, ALU.add, ['row0', 'row2'], ['row0'])
                        S.op('dve', lambda e: e.tensor_tensor_scan(out=rA, data0=onesf[0:4, 0:1].broadcast_to([4, T]), data1=rI,
                                                                   initial=0.0, op0=ALU.mult, op1=ALU.max),
                             reads=['row0', 'onesf'], writes=['row3'])
                        S.op('dve', lambda e: e.memset(rrow[:, 0:1], 0.0), writes=['rrow'])
                        cp('dve', rrow[:, 1:19], rA.rearrange("p (j t) -> p j t", t=128)[:, :, 127], ['row3'], ['rrow'])
                        rfull = rrow[:, 0:18].unsqueeze(2).broadcast_to([4, 18, 128])
                        v3 = lambda r_: r_.rearrange("p (j t) -> p j t", t=128)
                        tt('dve', v3(rI), v3(rI), rfull, ALU.subtract, ['row0', 'rrow'], ['row0'])
                        act(rI, rI, AF.Exp, ['row0'], ['row0'])
                        tt('dve', rG, rG, rA, ALU.subtract, ['row2', 'row3'], ['row2'])
                        act(rG, rG, AF.Exp, ['row2'], ['row2'])
                        tt('dve', v3(rA), rfull, v3(rA), ALU.subtract, ['row3', 'rrow'], ['row3'])
                        act(rA, rA, AF.Exp, ['row3'], ['row3'])
                        tt('dve', drow[:, 0:18], rrow[:, 0:18], rrow[:, 1:19], ALU.subtract, ['rrow'], ['drow'])
                        act(drow[:], drow[:], AF.Exp, ['drow'], ['drow'])
                        for idx in range(18):
                            for qi, rw in enumerate((rI, rA, rG)):
                                mm(pcol[:, idx, qi * 4:(qi + 1) * 4], rw[:, idx * 128:(idx + 1) * 128], ident[0:4, 0:4], True, True,
                                   ['row0', 'row2', 'row3', 'ident'], ['pcol'])
                        if dr == 0:
                            cp('dve', colq[0][:], pcol[:], ['pcol'], ['colq0'])
                        else:
                            cp('dve', colraw[:], pcol[:], ['pcol'], ['colraw'])
                            mm(pcol2[:].rearrange("p a b -> p (a b)"), jrev[:], colraw[:].rearrange("p a b -> p (a b)"), True, True,
                               ['colraw', 'jrev'], ['pcol2'])
                            cp('dve', colq[1][:], pcol2[:], ['pcol2'], ['colq1'])
                        for pr in range(2):
                            mm(pd[:, pr, :], sel[:, pr, :], drow[:], True, True, ['sel', 'drow'], ['pd'])
                        cp('dve', dcol[dr][:], pd[:], ['pd'], ['dcol%d' % dr])
                S.barrier()
                if CSTOP == 3:
                    return
                with contextlib.ExitStack() as p4:
                    qz4_ = [sb(p4, "qzc%d" % i, [128, 256], BF16) for i in range(2)]
                    qzr4 = Rot([("qzc%d" % i, qz4_[i]) for i in range(2)])
                    for i in range(2):
                        S.op('pool', lambda e, i=i: e.memset(qz4_[i][:], 0.0), writes=['qzc%dz' % i, 'qzc%d' % i])
                    vs_ = [sb(p4, "vs%d" % i, [128, 4, 80], BF16) for i in range(2)]
                    vsr = Rot([("vs%d" % i, vs_[i]) for i in range(2)])
                    wt_ = [sb(p4, "wt%d" % i, [128, 4, 128], BF16) for i in range(2)]
                    wtr = Rot([("wt%d" % i, wt_[i]) for i in range(2)])
                    sm = [sb(p4, "sm%d" % i, [128, 16], F32) for i in range(2)]
                    smr = Rot([("sm%d" % i, sm[i]) for i in range(2)])
                    ho = [sb(p4, "ho%d" % i, [128, 4, 64], F32) for i in range(1)]
                    hor = Rot([("ho%d" % i, ho[i]) for i in range(1)])
                    stp = [ps(p4, "stp%d" % i, [128, 512]) for i in range(2)]
                    stpr = Rot([("stp%d" % i, stp[i]) for i in range(2)])
                    opp = [ps(p4, "opp%d" % i, [128, 4, 80]) for i in range(2)]
                    oppr = Rot([("opp%d" % i, opp[i]) for i in range(2)])
                    upp = [ps(p4, "upp%d" % i, [128, 2, 80]) for i in range(2)]
                    uppr = Rot([("upp%d" % i, upp[i]) for i in range(2)])
                    CstD, CbfD, Cbf4D = {}, {}, {}
                    for dr in range(2):
                        CstD[dr] = sb(p4, "CstD%d" % dr, [128, 2, 80], F32)
                        CbfD[dr] = sb(p4, "CbfD%d" % dr, [128, 4, 80], BF16)
                        Cbf4D[dr] = CbfD[dr][:].rearrange("p (c u) d -> p c u d", u=2)
                        S.op('dve', lambda e, dr=dr: e.memset(CstD[dr][:], 0.0), writes=['Cst%d' % dr])
                        S.op('dve', lambda e, dr=dr: e.memset(CbfD[dr][:], 0.0), writes=['Cbf%d' % dr])
                    written = set()
                    for idx in range(18):
                        for dr in (1, 0):
                            Cst, Cbf, Cbf4 = CstD[dr], CbfD[dr], Cbf4D[dr]
                            CK, BK = 'Cst%d' % dr, 'Cbf%d' % dr
                            j = ORD[dr][idx]
                            tok = slice(128 * j, 128 * j + 128)
                            emit = not (last and j >= 16)
                            vk, vs = vsr.next()
                            tt('dve', vs[:, :, 0:65], vaug[:, j, :, :], colq[dr][:, idx, 0:4].unsqueeze(2).broadcast_to([128, 4, 65]),
                               ALU.mult, ['vaug%d' % j, 'vaug1', 'colq%d' % dr], [vk])
                            if emit:
                                sk, sp_ = stpr.next()
                                for c2 in range(2):
                                    zk, qz = qzr4.next()
                                    act(qz[0:64, 0:128], QK[0:64, c2, tok], AF.Copy, ['QK', zk + 'z'], [zk])
                                    act(qz[64:128, 128:256], QK[64:128, c2, tok], AF.Copy, ['QK', zk + 'z'], [zk])
                                    mm(sp_[:, c2 * 256:(c2 + 1) * 256], QK[:, 2 + c2, tok], qz[:], True, True, ['QK', zk], [sk])
                                wk, wt = wtr.next()
                                tt('dve', wt[:], sp_[:].rearrange("p (h t) -> p h t", h=4),
                                   masks[:, dr, :].unsqueeze(1).broadcast_to([128, 4, 128]), ALU.mult, [sk, 'masks'], [wk])
                                ok_, op_ = oppr.next()
                                for h in range(4):
                                    c2, po = h // 2, (h % 2) * 64
                                    mm(op_[:, h, 0:65], wt[:, h, :], vs[:, h, 0:65], True, False, [wk, vk], [ok_])
                                    mm(op_[:, h, 0:65], QK[:, c2, tok], Cbf[:, h, 0:65], False, True,
                                       ['QK', BK], [ok_])
                                mk, m_ = smr.next()
                                eo = colq[dr][:, idx, 4:8]
                                eb = colq[dr][:, idx, 8:12]
                                tt('dve', m_[:, 0:4], op_[:, :, 64], eo, ALU.mult, [ok_, 'colq%d' % dr], [mk])
                                stt(m_[:, 4:8], m_[:, 0:4], -1.0, m_[:, 0:4], ALU.mult, ALU.max, [mk], [mk])
                                tt('dve', m_[:, 4:8], m_[:, 4:8], eb, ALU.max, [mk, 'colq%d' % dr], [mk])
                                S.op('dve', lambda e, m_=m_: e.reciprocal(out=m_[:, 8:12], in_=m_[:, 4:8]), reads=[mk], writes=[mk])
                                tt('dve', m_[:, 12:16], m_[:, 8:12], eo, ALU.mult, [mk, 'colq%d' % dr], [mk])
                                hs_j = hsum[:, j, :].rearrange("p (h d) -> p h d", h=4)
                                rcb = m_[:, 12:16].unsqueeze(2).broadcast_to([128, 4, 64])
                                if j not in written:
                                    written.add(j)
                                    tt('dve', hs_j, op_[:, :, 0:64], rcb, ALU.mult, [ok_, mk], ['hsum%d' % j])
                                else:
                                    hk2, h2 = hor.next()
                                    tt('dve', h2[:], op_[:, :, 0:64], rcb, ALU.mult, [ok_, mk], [hk2])
                                    tt('pool', hs_j, hs_j, h2[:], ALU.add, [hk2, 'hsum%d' % j], ['hsum%d' % j])
                            if idx < 17:
                                uk, up = uppr.next()
                                for pr in range(2):
                                    for hh in range(2):
                                        mm(up[hh * 64:(hh + 1) * 64, pr, 0:65], ktm[:, j, pr * 128 + hh * 64:pr * 128 + hh * 64 + 64],
                                           vs[:, 2 * pr + hh, 0:65], True, True, ['ktm%d' % j, vk], [uk])
                                tt('dve', Cst[:, :, 0:65], up[:, :, 0:65], Cst[:, :, 0:65], ALU.add, [uk, CK], [CK])
                                tt('dve', Cst[:, :, 0:65], Cst[:, :, 0:65], dcol[dr][:, :, idx].unsqueeze(2).broadcast_to([128, 2, 65]), ALU.mult,
                                   [CK, 'dcol%d' % dr], [CK])
                                cp('pool', Cbf4[0:64, :, 0, 0:65], Cst[0:64, :, 0:65], [CK], [BK])
                                cp('pool', Cbf4[64:128, :, 1, 0:65], Cst[64:128, :, 0:65], [CK], [BK])
                S.barrier()
                if CSTOP == 4:
                    return
                with contextlib.ExitStack() as p5:
                    OT = R2[:, 0:2, :]
                    wload(wv[:], 'wv', l, OFF_C + 768, 256)
                    pq = [ps(p5, "pq5_%d" % i, [128, 512]) for i in range(2)]
                    pqr = Rot([("pq5_%d" % i, pq[i]) for i in range(2)])
                    nb = 8 if last else 9
                    for b, t0, v, hk, hT in hblocks(l, nb, s):
                        for oc in range(2):
                            proj_fm(hT, hk, 256, wv, 'wv', oc * 128, pqr,
                                    lambda p_, pk, oc=oc: act(OT[:, oc, t0:t0 + 256], p_, AF.Sigmoid, [pk], ['OT%d' % oc]))
                    st_ = [sb(p5, "lst%d" % i, [128, 16], F32) for i in range(2)]
                    str_ = Rot([("lst%d" % i, st_[i]) for i in range(2)])
                    xc_ = [sb(p5, "xc%d" % i, [128, 4, 64], F32) for i in range(2)]
                    xcr = Rot([("xc%d" % i, xc_[i]) for i in range(2)])
                    sq_ = [sb(p5, "xsq%d" % i, [128, 4, 64], F32) for i in range(2)]
                    sqr_ = Rot([("xsq%d" % i, sq_[i]) for i in range(2)])
                    for j in range(16 if last else 18):
                        tok = slice(128 * j, 128 * j + 128)
                        hs_j = hsum[:, j, :].rearrange("p (h d) -> p h d", h=4)
                        lk, ls = str_.next()
                        S.op('dve', lambda e, ls=ls, hs_j=hs_j: e.reduce_sum(out=ls[:, 0:4], in_=hs_j, axis=AX.X),
                             reads=['hsum%d' % j], writes=[lk])
                        ts('dve', ls[:, 0:4], ls[:, 0:4], 1.0 / 64, None, ALU.mult, None, [lk], [lk])
                        xk, xc = xcr.next()
                        tt('dve', xc[:], hs_j, ls[:, 0:4].unsqueeze(2).broadcast_to([128, 4, 64]), ALU.subtract,
                           ['hsum%d' % j, lk], [xk])
                        qk_, xq = sqr_.next()
                        tt('pool', xq[:], xc[:], xc[:], ALU.mult, [xk], [qk_])
                        S.op('dve', lambda e, ls=ls, xq=xq: e.reduce_sum(out=ls[:, 4:8], in_=xq[:], axis=AX.X),
                             reads=[qk_], writes=[lk])
                        ts('dve', ls[:, 4:8], ls[:, 4:8], 1.0 / 64, EPS, ALU.mult, ALU.add, [lk], [lk])
                        act(ls[:, 8:12], ls[:, 4:8], AF.Sqrt, [lk], [lk])
                        S.op('dve', lambda e, ls=ls: e.reciprocal(out=ls[:, 12:16], in_=ls[:, 8:12]), reads=[lk], writes=[lk])
                        tt('dve', xc[:], xc[:], ls[:, 12:16].unsqueeze(2).broadcast_to([128, 4, 64]), ALU.mult, [xk, lk], [xk])
                        tt('pool', xq[:].rearrange("p h d -> p (h d)"), xc[:].rearrange("p h d -> p (h d)"), gmb[:], ALU.mult,
                           [xk, 'gmb', qk_], [qk_])
                        pk, pt = pqr.next()
                        for kc in range(2):
                            mm(pt[:, kc * 128:(kc + 1) * 128], xq[:].rearrange("p h d -> p (h d)")[:, kc * 128:(kc + 1) * 128],
                               ident[:], True, True, [qk_, 'ident'], [pk])
                        tt('dve', CAT[:, 4:6, tok], pt[:, 0:256].rearrange("p (c t) -> p c t", c=2), OT[:, :, tok], ALU.mult,
                           [pk, 'OT0', 'OT1'], ['CATc'])

        def mixer_a(s, l, last):
            with contextlib.ExitStack() as ph:
                QA = sb(ph, "QA", [128, 2, T], BF16)
                KA = sb(ph, "KA", [128, 2, T], BF16)
                VA = sb(ph, "VA", [128, 18, 256], BF16)
                wq = sb(ph, "wqa", [128, 8, 768], BF16)
                wload(wq[:], 'wqa', l, OFF_A, 768)
                with contextlib.ExitStack() as p1:
                    pq = [ps(p1, "pqa%d" % i, [128, 512]) for i in range(2)]
                    pqr = Rot([("pqa%d" % i, pq[i]) for i in range(2)])
                    for b, t0, v, hk, hT in hblocks(l, 9, s):
                        for oc in range(4):
                            if oc < 2 and last and b == 8:
                                continue
                            dst = QA if oc < 2 else KA
                            proj_fm(hT, hk, 256, wq, 'wqa', oc * 128, pqr,
                                    lambda p_, pk, oc=oc, dst=dst: act(dst[:, oc % 2, t0:t0 + 256], p_, AF.Copy, [pk], ['QKA']))
                        for sub in range(2):
                            j = 2 * b + sub
                            proj_tm(hT, hk, sub, wq, 'wqa', 512, 256, pqr,
                                    lambda p_, pk, j=j: cp('dve', VA[:, j, :], p_, [pk], ['VA']))
                S.barrier()
                with contextlib.ExitStack() as p2:
                    bt = [sb(p2, "bt%d" % i, [128, 1280], F32) for i in range(2)]
                    btr = Rot([("bt%d" % i, bt[i]) for i in range(2)])
                    tf = [sb(p2, "tf%d" % i, [128, 1280], F32) for i in range(2)]
                    tfr = Rot([("tf%d" % i, tf[i]) for i in range(2)])
                    PT = [sb(p2, "PT%d" % i, [128, 7, 256], BF16) for i in range(2)]
                    ptr_ = Rot([("PT%d" % i, PT[i]) for i in range(2)])
                    rc_ = [sb(p2, "rca%d" % i, [128, 256], F32) for i in range(2)]
                    rcr = Rot([("rca%d" % i, rc_[i]) for i in range(2)])
                    sps = ps(p2, "sps", [128, 8, 256])
                    ov = [ps(p2, "ov%d" % i, [128, 512]) for i in range(2)]
                    ovr = Rot([("ov%d" % i, ov[i]) for i in range(2)])
                    qz_ = [sb(p2, "qz%d" % i, [128, 256], BF16) for i in range(2)]
                    qzr = Rot([("qz%d" % i, qz_[i]) for i in range(2)])
                    for i in range(2):
                        S.op('pool', lambda e, i=i: e.memset(qz_[i][:], 0.0), writes=['qz%dz' % i, 'qz%d' % i])
                    nq = 16 if last else 18
                    qts = list(range(nq))
                    if ASTOP == 1:
                        qts = []
                    if ASTOP == 2:
                        qts = [16, 17]
                    if ASTOP == 3:
                        qts = [5]
                    units = [(qt, pr) for qt in qts for pr in range(2)]

                    def stage1(qt, pr):
                        ctxq = qt >= 16
                        tq = slice(128 * qt, 128 * qt + 128)
                        if ctxq:
                            kch = [16, 17]
                            bk = bias_t = None
                        else:
                            cb = min(max(qt - 2, 0), 11)
                            kch = list(range(cb, cb + 5)) + [16, 17]
                            typ = {0: 0, 1: 1, 14: 3, 15: 4}.get(qt, 2)
                            bk, bias_t = btr.next()
                            dma('sp', bias_t[:], natb[l, typ, pr], writes=[bk])
                        zk, qz = qzr.next()
                        cp('pool', qz[0:64, 0:128], QA[0:64, pr, tq], ['QKA', zk + 'z'], [zk])
                        cp('pool', qz[64:128, 128:256], QA[64:128, pr, tq], ['QKA', zk + 'z'], [zk])
                        for i, kc_ in enumerate(kch):
                            mm(sps[:, i, :], KA[:, pr, 128 * kc_:128 * kc_ + 128], qz[:], True, True, ['QKA', zk], ['sps%d' % (i // 2)])
                        return (qt, pr, ctxq, tq, kch, bk, bias_t)

                    def stage2(st_):
                        qt, pr, ctxq, tq, kch, bk, bias_t = st_
                        nk = len(kch)
                        pk_, P_ = ptr_.next()
                        if not ctxq:
                            fk, tfl = tfr.next()
                            for (a0, a1) in ((0, 2), (2, 4), (4, 5)):
                                stt(tfl[:, a0 * 256:a1 * 256], sps[:, a0:a1, :].rearrange("p a b -> p (a b)"), 0.125,
                                    bias_t[:, a0 * 256:a1 * 256], ALU.mult, ALU.add, ['sps%d' % (a0 // 2), bk], [fk])
                            for a0 in (5, 6):
                                act(P_[:, a0, :], sps[:, a0, :], AF.Exp, ['sps%d' % (a0 // 2)], [pk_ + 'c'], scale=0.125)
                            act(P_[:, 0:5, :].rearrange("p a b -> p (a b)"), tfl[:], AF.Exp, [fk], [pk_])
                        else:
                            act(P_[:, 0:2, :].rearrange("p a b -> p (a b)"), sps[:, 0:2, :].rearrange("p a b -> p (a b)"),
                                AF.Exp, ['sps0'], [pk_, pk_ + 'c'], scale=0.125)
                        return (qt, pr, tq, kch, pk_, P_)

                    def stage3(st_):
                        qt, pr, tq, kch, pk_, P_ = st_
                        nk = len(kch)
                        ok_, o_ = ovr.next()
                        for i, kc_ in enumerate(kch):
                            mm(o_[:, 0:256], VA[:, kc_, pr * 128:(pr + 1) * 128], P_[:, i, :], i == 0, i == nk - 1,
                               ['VA', pk_, pk_ + 'c'], [ok_])
                        for i, kc_ in enumerate(kch):
                            mm(o_[:, 256:512], onesb[:], P_[:, i, :], i == 0, i == nk - 1, ['onesb', pk_, pk_ + 'c'], [ok_])
                        rk_, r_ = rcr.next()
                        S.op('dve', lambda e, r_=r_, o_=o_: e.reciprocal(out=r_[:], in_=o_[:, 256:512]), reads=[ok_], writes=[rk_])
                        for hh in range(2):
                            po = hh * 64
                            tt('dve', CAT[po:po + 64, pr, tq], o_[po:po + 64, hh * 128:(hh + 1) * 128],
                               r_[po:po + 64, hh * 128:(hh + 1) * 128], ALU.mult, [ok_, rk_], ['CATa'])

                    if units:
                        s1 = stage1(*units[0])
                        for ui in range(len(units)):
                            s2 = stage2(s1)
                            if ui + 1 < len(units):
                                s1 = stage1(*units[ui + 1])
                            stage3(s2)

        def mixer_b(s, l, last):
            with contextlib.ExitStack() as ph:
                UB = sb(ph, "UB", [128, 2, T], BF16)
                ZB = sb(ph, "ZB", [128, 18, 256], BF16)
                wb_ = sb(ph, "wbb", [128, 8, 512], BF16)
                wsb = sb(ph, "wsb", [128, 4, 128], BF16)
                bsb = sb(ph, "bsb", [128, 2, 128], F32)
                ggb = sb(ph, "ggb", [128, 256], F32)
                wload(wb_[:], 'wbb', l, OFF_B, 512)
                dma('pool', wsb[:], wsT[l], writes=['wsb'])
                dma('sp', bsb[:], bsT[l], writes=['bsb'])
                dma('sp', ggb[:], ggm[l, :].partition_broadcast(128), writes=['ggb'])
                pq = [ps(ph, "pqb%d" % i, [128, 512]) for i in range(2)]
                pqr = Rot([("pqb%d" % i, pq[i]) for i in range(2)])
                zf = [sb(ph, "zf%d" % i, [128, 256], F32) for i in range(2)]
                zfr = Rot([("zf%d" % i, zf[i]) for i in range(2)])
                zq = [sb(ph, "zq%d" % i, [128, 256], F32) for i in range(2)]
                zqr = Rot([("zq%d" % i, zq[i]) for i in range(2)])
                zs = [sb(ph, "zs%d" % i, [128, 4], F32) for i in range(2)]
                zsr = Rot([("zs%d" % i, zs[i]) for i in range(2)])
                nb = 8 if last else 9
                for b, t0, v, hk, hT in hblocks(l, nb, s):
                    for oc in range(2):
                        proj_fm(hT, hk, 256, wb_, 'wbb', oc * 128, pqr,
                                lambda p_, pk, oc=oc: act(UB[:, oc, t0:t0 + 256], p_, AF.Gelu_apprx_tanh, [pk], ['UB']))
                    for sub in range(2):
                        j = 2 * b + sub

                        def ev(p_, pk, j=j):
                            fk, z_ = zfr.next()
                            act(z_[:], p_, AF.Gelu_apprx_tanh, [pk], [fk])
                            qk_, q_ = zqr.next()
                            tt('pool', q_[:], z_[:], z_[:], ALU.mult, [fk], [qk_])
                            sk, s_ = zsr.next()
                            S.op('dve', lambda e: e.reduce_sum(out=s_[:, 0:1], in_=q_[:], axis=AX.X), reads=[qk_], writes=[sk])
                            ts('dve', s_[:, 0:1], s_[:, 0:1], 1.0 / 256, EPS, ALU.mult, ALU.add, [sk], [sk])
                            act(s_[:, 1:2], s_[:, 0:1], AF.Sqrt, [sk], [sk])
                            S.op('dve', lambda e: e.reciprocal(out=s_[:, 2:3], in_=s_[:, 1:2]), reads=[sk], writes=[sk])
                            stt(ZB[:, j, :], z_[:], s_[:, 2:3], ggb[:], ALU.mult, ALU.mult, [fk, sk, 'ggb'], ['ZB%d' % j])
                        proj_tm(hT, hk, sub, wb_, 'wbb', 256, 256, pqr, ev)
                mt = [sb(ph, "mt%d" % i, [128, 128], F32) for i in range(2)]
                mtr = Rot([("mt%d" % i, mt[i]) for i in range(2)])
                for j in range(16 if last else 18):
                    tok = slice(128 * j, 128 * j + 128)
                    for pr in range(2):
                        pk, pt = pqr.next()
                        for hh in range(2):
                            po = hh * 64
                            mm(pt[po:po + 64, 0:128], ZB[:, j, pr * 128 + po:pr * 128 + po + 64], wsb[:, 2 * pr + hh, :],
                               True, True, ['ZB%d' % j, 'wsb'], [pk])
                        mk, m_ = mtr.next()
                        tt('dve', m_[:], pt[:, 0:128], bsb[:, pr, :], ALU.add, [pk, 'bsb'], [mk])
                        tt('pool', CAT[:, 2 + pr, tok], m_[:], UB[:, pr, tok], ALU.mult, [mk, 'UB'], ['CATb'])

        def mixer_d(s, l, last):
            with contextlib.ExitStack() as ph:
                FT = sb(ph, "FT", [128, 18, 256], BF16)
                wfi = sb(ph, "wfi", [128, 8, 256], BF16)
                wfn = sb(ph, "wfn", [128, 2, 256], BF16)
                wload(wfi[:], 'wfi', l, OFF_D, 256)
                dma('pool', wfn[:], w_fnet[l].rearrange("(kc p) n -> p kc n", p=128), writes=['wfn'])
                pq = [ps(ph, "pqd%d" % i, [128, 512]) for i in range(2)]
                pqr = Rot([("pqd%d" % i, pq[i]) for i in range(2)])
                nb = 8 if last else 9
                for b, t0, v, hk, hT in hblocks(l, nb, s):
                    for sub in range(2):
                        j = 2 * b + sub
                        proj_tm(hT, hk, sub, wfi, 'wfi', 0, 256, pqr,
                                lambda p_, pk, j=j: cp('dve', FT[:, j, :], p_, [pk], ['FT']))
                dc_ = [sb(ph, "dc%d" % i, [128, 16, 256], BF16) for i in range(2)]
                ds_ = [sb(ph, "ds%d" % i, [128, 16, 256], BF16) for i in range(2)]
                YC = [sb(ph, "YC%d" % i, [128, 2, 256], BF16) for i in range(2)]
                YS = [sb(ph, "YS%d" % i, [128, 2, 256], BF16) for i in range(2)]
                SPc = [sb(ph, "SP%d" % i, [128, 2, 256], BF16) for i in range(2)]
                ycp = ps(ph, "ycp", [128, 512])
                ysp = ps(ph, "ysp", [128, 512])
                spp = ps(ph, "spp", [128, 512])
                dpp = ps(ph, "dpp", [128, 512])
                it = 0
                segs = [(0, 16, TL, c_dftc_l, c_dfts_l)]
                if not last:
                    segs.append((16, 2, 256, c_dftc_c, c_dfts_c))
                for (jb, nchk, nT, mc, ms) in segs:
                    scale = float(1.0 / np.sqrt(64.0 * nT))
                    for tb in range(nT // 256):
                        bb = it % 2
                        it += 1
                        dma('sp', dc_[bb][:, 0:nchk, :], mc[:, tb * 256:(tb + 1) * 256].rearrange("(c p) n -> p c n", p=128),
                            writes=['dc%d' % bb])
                        dma('sp', ds_[bb][:, 0:nchk, :], ms[:, tb * 256:(tb + 1) * 256].rearrange("(c p) n -> p c n", p=128),
                            writes=['ds%d' % bb])
                        for fc in range(2):
                            for i in range(nchk):
                                mm(ycp[:, 0:256], FT[:, jb + i, fc * 128:(fc + 1) * 128], dc_[bb][:, i, :], i == 0, i == nchk - 1,
                                   ['FT', 'dc%d' % bb], ['ycp'])
                            for i in range(nchk):
                                mm(ysp[:, 0:256], FT[:, jb + i, fc * 128:(fc + 1) * 128], ds_[bb][:, i, :], i == 0, i == nchk - 1,
                                   ['FT', 'ds%d' % bb], ['ysp'])
                            act(YC[bb][:, fc, :], ycp[:, 0:256], AF.Copy, ['ycp'], ['YC%d' % bb])
                            cp('dve', YS[bb][:, fc, :], ysp[:, 0:256], ['ysp'], ['YS%d' % bb])
                            mm(spp[:, 0:256], blk[:, 0, :], YC[bb][:, fc, :], True, False, ['blk', 'YC%d' % bb], ['spp'])
                            mm(spp[:, 0:256], blk[:, 1, :], YS[bb][:, fc, :], False, True, ['blk', 'YS%d' % bb], ['spp'])
                            act(SPc[bb][:, fc, :], spp[:, 0:256], AF.Copy, ['spp'], ['SP%d' % bb], scale=scale)
                        for oc in range(2):
                            for fc in range(2):
                                mm(dpp[:, 0:256], wfn[:, fc, oc * 128:(oc + 1) * 128], SPc[bb][:, fc, :], fc == 0, fc == 1,
                                   ['wfn', 'SP%d' % bb], ['dpp'])
                            t0 = 128 * jb + tb * 256
                            cp('dve', CAT[:, 6 + oc, t0:t0 + 256], dpp[:, 0:256], ['dpp'], ['CATd'])

        class _Stop(Exception):
            pass

        def body():
          if dbg:
              S.op('pool', lambda e: e.memset(CAT[:], 0.0), writes=['CAT'])
              S.barrier()
          for s in range(2):
            for c in range(8):
                dma('sp', X[:, c, :], xin[s, :, c, :], writes=['X%d' % c])
            for l in range(DEPTH):
                last = (l == DEPTH - 1)
                compute_rs()
                S.barrier()
                if 'C' not in SKIP:
                    mixer_c(s, l, last)
                    S.barrier()
                if 'A' not in SKIP:
                    mixer_a(s, l, last)
                    S.barrier()
                if 'B' not in SKIP:
                    mixer_b(s, l, last)
                    S.barrier()
                if 'D' not in SKIP:
                    mixer_d(s, l, last)
                    S.barrier()
                if dbg == ('cat', s, l):
                    for c in range(8):
                        dma('pool', dbg_out[:, c, :], CAT[:, c, :], reads=['CAT'])
                    return
                S.barrier()
                for _once in ([] if 'wout' in SKIP else [0]):
                  with contextlib.ExitStack() as ph:
                    wo = sb(ph, "wo", [128, 8, D], BF16)
                    yps = [ps(ph, "yps%d" % i, [128, 512]) for i in range(2)]
                    for kc in range(8):
                        dma('pool', wo[:, kc, :], w_out[l, kc * 128:(kc + 1) * 128, :], writes=['wo'])
                    it = 0
                    nblk = 4 if last else 5
                    for b in range(nblk):
                        t0 = b * 512
                        n = 512 if b < 4 else 256
                        v = s if b < 4 else 2
                        for oc in range(8):
                            pb = it % 2
                            it += 1
                            for kc in range(8):
                                mm(yps[pb][:, :n], wo[:, kc, oc * 128:(oc + 1) * 128], CAT[:, kc, t0:t0 + n],
                                   kc == 0, kc == 7, ['wo', 'CAT'], ['yps%d' % pb])
                            stt(X[:, oc, t0:t0 + n], yps[pb][:, :n], modap(l, 2, oc, v), X[:, oc, t0:t0 + n],
                                ALU.mult, ALU.add, ['yps%d' % pb, 'modT', 'X%d' % oc], ['X%d' % oc])
                S.barrier()
                for _once in ([] if 'ffn' in SKIP else [0]):
                  with contextlib.ExitStack() as ph:
                    nblk = 4 if last else 5
                    H2 = CAT
                    for b2 in range(8 if last else 9):
                        t0 = b2 * 256
                        v = s if b2 < 8 else 2
                        make_h(l, 1, t0, 256, v, dst=('H2_%d' % (b2 // 2), H2[:, :, t0:t0 + 256]))
                    w1 = [sb(ph, "w1_%d" % i, [128, 8, 512], BF16) for i in range(2)]
                    w2 = [sb(ph, "w2_%d" % i, [128, 4, D], BF16) for i in range(2)]
                    ag = [sb(ph, "ag%d" % i, [128, 4, 512], BF16) for i in range(2)]
                    rl = [sb(ph, "rl%d" % i, [128, 512], F32) for i in range(2)]
                    fps = [ps(ph, "fps%d" % i, [128, 512]) for i in range(2)]
                    ops_ = [ps(ph, "ops%d" % i, [128, 512]) for i in range(2)]
                    cnt = {'i1': 0, 'i2': 0, 'i3': 0}

                    def ffn1(g, b):
                        wb = g % 2
                        gl = None
                        if g == 0 and b == 0:
                            gl = 0
                        if b == 1 and g + 1 < 8:
                            gl = g + 1
                        if gl is not None:
                            wl = gl % 2
                            dma('pool', w1[wl][:], w_ff1[l, :, gl * 512:(gl + 1) * 512].rearrange("(kc p) n -> p kc n", p=128),
                                writes=['w1_%d' % wl])
                            dma('pool', w2[wl][:], w_ff2[l, gl * 512:(gl + 1) * 512, :].rearrange("(kc p) n -> p kc n", p=128),
                                writes=['w2_%d' % wl])
                        t0 = b * 512
                        n = 512 if b < 4 else 256
                        ab = cnt['i1'] % 2
                        cnt['i1'] += 1
                        for fc in range(4):
                            pb = cnt['i2'] % 2
                            cnt['i2'] += 1
                            for kc in range(8):
                                mm(fps[pb][:, :n], w1[wb][:, kc, fc * 128:(fc + 1) * 128], H2[:, kc, t0:t0 + n],
                                   kc == 0, kc == 7, ['w1_%d' % wb, 'H2_%d_%d' % (b, kc)], ['fps%d' % pb])
                            act(rl[pb][:, :n], fps[pb][:, :n], AF.Relu, ['fps%d' % pb], ['rl%d' % pb])
                            tt('pool', ag[ab][:, fc, :n], rl[pb][:, :n], rl[pb][:, :n], ALU.mult,
                               ['rl%d' % pb], ['ag%d_%d' % (ab, fc)])
                        return ab

                    def ffn2(g, b, ab):
                        wb = g % 2
                        t0 = b * 512
                        n = 512 if b < 4 else 256
                        v = s if b < 4 else 2
                        for oc in range(8):
                            pb = cnt['i3'] % 2
                            cnt['i3'] += 1
                            for kc in range(4):
                                mm(ops_[pb][:, :n], w2[wb][:, kc, oc * 128:(oc + 1) * 128], ag[ab][:, kc, :n],
                                   kc == 0, kc == 3, ['w2_%d' % wb, 'ag%d_%d' % (ab, kc)], ['ops%d' % pb])
                            stt(X[:, oc, t0:t0 + n], ops_[pb][:, :n], modap(l, 5, oc, v), X[:, oc, t0:t0 + n],
                                ALU.mult, ALU.add, ['ops%d' % pb, 'modT', 'X%d' % oc], ['X%d' % oc])

                    funits = [(g, b) for g in range(8) for b in range(nblk)]
                    ab_cur = ffn1(*funits[0])
                    for ui in range(len(funits)):
                        ab_nxt = ffn1(*funits[ui + 1]) if ui + 1 < len(funits) else None
                        ffn2(funits[ui][0], funits[ui][1], ab_cur)
                        ab_cur = ab_nxt
                S.barrier()
                if dbg == ('x', s, l):
                    for c in range(8):
                        dma('sp', dbg_out[:, c, :], X[:, c, :], reads=['X%d' % c])
                    return
            with contextlib.ExitStack() as ph:
                ob = [sb(ph, "ob%d" % i, [128, 256], F32) for i in range(2)]
                it = 0
                for b in range(8):
                    t0 = b * 256
                    pk, pst = ssqrot.next()
                    for c in range(8):
                        sk, sq = sqrot.next()
                        act(sq[:], X[:, c, t0:t0 + 256], AF.Square, ['X%d' % c], [sk])
                        mm(pst[:, 0:256], onesb[:], sq[:], c == 0, c == 7, [sk, 'onesb'], [pk])
                    rk, rs = rsrot.next()
                    act(rs[:], pst[:, 0:256], AF.Sqrt, [pk, 'epsb'], [rk], scale=1.0 / D, bias=epsb[:, 0:1])
                    S.op('dve', lambda e, rs=rs: e.reciprocal(out=rs[:], in_=rs[:]), reads=[rk], writes=[rk])
                    for c in range(8):
                        o = it % 2
                        it += 1
                        stt(ob[o][:], X[:, c, t0:t0 + 256], gv[:, 4, c:c + 1], rs[:], ALU.mult, ALU.mult,
                            ['X%d' % c, 'gv', rk], ['ob%d' % o])
                        dma('sp', outT[s, :, c, t0:t0 + 256], ob[o][:], reads=['ob%d' % o])
            S.barrier()
        body()
        S.finish('sp')
        print("ops", S.nops, "waits", S.nwaits, {e: S.cc[e] for e in S.cc}, {e: S.dc[e] for e in S.dc})
    return nc


_NC_CACHE = {}


def _prep_shared(inp):
    f32 = lambda a: np.ascontiguousarray(np.asarray(a, dtype=np.float32))
    sh = dict(_consts())
    sh['w_ada'] = f32(inp['w_ada'])
    sh['badaT'] = np.ascontiguousarray(np.moveaxis(f32(inp['b_ada']).reshape(2, 48, 128), 2, 0))
    gl = [inp['g_norm_mix'][0], inp['g_norm_ffn'][0], inp['g_norm_mix'][1], inp['g_norm_ffn'][1], inp['g_final']]
    sh['gvec'] = np.ascontiguousarray(np.stack([f32(g).reshape(8, 128).T for g in gl], 1))
    sh['w_in'] = f32(inp['w_in'])
    bgate = f32(inp['b_gate'])
    sh['bg'] = np.ascontiguousarray(bgate.reshape(2, 4, 4).transpose(2, 0, 1))
    wc = f32(inp['w_conv_qk'])
    sh['wconv'] = np.ascontiguousarray(wc.reshape(2, 3, 4, 128).transpose(3, 0, 2, 1))
    rpb = f32(inp['rpb'])
    sh['natb'] = np.ascontiguousarray(np.stack([_nat_bias(rpb[l]).reshape(5, 2, 128, 1280) for l in range(2)], 0))
    sh['wsT'] = np.ascontiguousarray(f32(inp['w_spatial']).transpose(0, 3, 1, 2))
    bs = f32(inp['b_spatial'])
    bsT = np.zeros((2, 128, 2, 128), np.float32)
    for pr in range(2):
        for hh in range(2):
            bsT[:, hh * 64:(hh + 1) * 64, pr, :] = bs[:, 2 * pr + hh, None, :]
    sh['bsT'] = bsT
    sh['ggm'] = f32(inp['g_gmlp'])
    sh['gml'] = f32(inp['g_mlstm'])
    sh['w_fnet'] = f32(inp['w_fnet'])
    sh['w_out'] = f32(inp['w_out'])
    sh['w_ff1'] = f32(inp['w_ff1'])
    sh['w_ff2'] = f32(inp['w_ff2'])
    return sh


def _fmT(a):
    return np.ascontiguousarray(a.T.reshape(8, 128, a.shape[0]).transpose(1, 0, 2))


def kernel(dbg=None, **inp):
    x = np.asarray(inp['x'], np.float32)
    c = np.asarray(inp['c'], np.float32)
    ctx = np.asarray(inp['ctx'], np.float32)
    c_ctx = np.asarray(inp['c_ctx'], np.float32)
    sh = _prep_shared(inp)
    key = repr(dbg)
    if key not in _NC_CACHE:
        _NC_CACHE[key] = build_nc(dbg)
    nc = _NC_CACHE[key]
    in_maps = []
    ncores = int(os.environ.get('MK_CORES', '8'))
    for core in range(ncores):
        m = dict(sh)
        xs = []
        for i in range(2):
            b = 2 * core + i
            xs.append(_fmT(np.concatenate([x[b], ctx[b]], 0)))
        m['xin'] = np.ascontiguousarray(np.stack(xs, 0))
        vecs = [c[2 * core], c[2 * core + 1], c_ctx]
        m['cT'] = np.ascontiguousarray(np.stack([v.reshape(8, 128).T for v in vecs], 2))
        in_maps.append(m)
    res = run_bass_kernel_spmd(nc, in_maps, core_ids=list(range(ncores)))
    out = np.zeros((16, TL, D), np.float32)
    for core in range(ncores):
        o = res.results[core]['outT']
        for i in range(2):
            out[2 * core + i] = o[i].transpose(2, 1, 0).reshape(TL, D)
    if dbg:
        return out, [res.results[core]['dbg'] for core in range(ncores)]
    return out
```

```python
import contextlib
import numpy as np
import ml_dtypes
import concourse.bass as bass
import concourse.mybir as mybir
from concourse.bass_utils import run_bass_kernel_spmd

F32 = mybir.dt.float32
BF16 = mybir.dt.bfloat16
AF = mybir.ActivationFunctionType
ALU = mybir.AluOpType
AX = mybir.AxisListType

D = 1024
T = 2304
TL = 2048
NCH = 18
DIN = 2576
DEPTH = 2
EPS = 1e-6
NEG = -30000.0
import os
SKIP = set(os.environ.get('MK_SKIP', '').split(','))
CSTOP = int(os.environ.get('MK_CSTOP', '9'))
ASTOP = int(os.environ.get('MK_ASTOP', '9'))
OFF_A, OFF_B, OFF_C, OFF_D, OFF_G = 0, 768, 1280, 2304, 2560


class Sch:
    def __init__(self, nc, stack, ndma=8, same_engine_sync=True):
        self.nc = nc
        self.E = {'pe': nc.tensor, 'act': nc.scalar, 'dve': nc.vector,
                  'pool': nc.gpsimd, 'sp': nc.sync}
        self.R = ndma
        self.same = same_engine_sync
        self.csem = {}
        self.dsem = {}
        for e in self.E:
            self.csem[e] = stack.enter_context(nc.semaphore('c_' + e))
            self.dsem[e] = [stack.enter_context(nc.semaphore('d_%s_%d' % (e, i)))
                            for i in range(ndma)]
        self.cc = {e: 0 for e in self.E}
        self.dc = {e: 0 for e in self.E}
        self.waited = {e: {} for e in self.E}
        self.lw = {}
        self.rd = {}
        self.bar = set()
        self.nops = 0
        self.nwaits = 0

    def _tok_sem(self, tok):
        kind, e, i = tok
        if kind == 'c':
            return (kind, e, 0), self.csem[e], i
        return (kind, e, i % self.R), self.dsem[e][i % self.R], 16 * (i // self.R + 1)

    def _wait(self, eng, tok):
        kind, e, i = tok
        if kind == 'c' and e == eng and (eng == 'pe' or not self.same):
            return
        key, sem, val = self._tok_sem(tok)
        if self.waited[eng].get(key, 0) >= val:
            return
        self.waited[eng][key] = val
        self.E[eng].wait_ge(sem, val)
        self.nwaits += 1

    def op(self, eng, fn, reads=(), writes=(), dma=False):
        deps = set(self.bar)
        for r in reads:
            if r in self.lw:
                deps.add(self.lw[r])
        for w in writes:
            if w in self.lw:
                lt = self.lw[w]
                if not (lt[0] == 'c' and lt[1] == eng and not dma):
                    deps.add(lt)
            for t in self.rd.get(w, ()):
                deps.add(t)
        if dma:
            k = self.dc[eng]
            self.dc[eng] += 1
            tok = ('d', eng, k)
            if k >= self.R:
                deps.add(('d', eng, k - self.R))
        else:
            self.cc[eng] += 1
            tok = ('c', eng, self.cc[eng])
        best = {}
        for t in deps:
            key, sem, val = self._tok_sem(t)
            if key not in best or best[key][1] < val:
                best[key] = (t, val)
        for key in sorted(best, key=str):
            self._wait(eng, best[key][0])
        inst = fn(self.E[eng])
        _, sem, _ = self._tok_sem(tok)
        inst.then_inc(sem, 16 if dma else 1)
        for w in writes:
            self.lw[w] = tok
            self.rd[w] = []
        for r in reads:
            self.rd.setdefault(r, []).append(tok)
        self.nops += 1
        return tok

    def barrier(self):
        self.bar = set()
        for e in self.E:
            if self.cc[e] > 0:
                self.bar.add(('c', e, self.cc[e]))
            for j in range(max(0, self.dc[e] - self.R), self.dc[e]):
                self.bar.add(('d', e, j))
        self.lw = {}
        self.rd = {}

    def finish(self, eng='sp'):
        self.barrier()
        best = {}
        for t in self.bar:
            key, sem, val = self._tok_sem(t)
            if key not in best or best[key][1] < val:
                best[key] = (t, val)
        for key in sorted(best, key=str):
            self._wait(eng, best[key][0])


class Rot:
    def __init__(self, items):
        self.items = items
        self.i = 0

    def next(self):
        it = self.items[self.i % len(self.items)]
        self.i += 1
        return it


def _bf(a):
    return np.ascontiguousarray(a.astype(ml_dtypes.bfloat16))


def _consts():
    c = {}
    c['ident'] = np.eye(128, dtype=np.float32)
    c['jrev'] = np.ascontiguousarray(np.eye(128, dtype=np.float32)[::-1])
    s = np.arange(128)
    mf = (s[:, None] <= s[None, :]).astype(np.float32)
    mb = (s[:, None] >= s[None, :]).astype(np.float32)
    c['masks'] = _bf(np.stack([mf, mb], 1))
    t = np.arange(TL)
    rows = (t // 64).astype(np.float32)
    cols = (t % 64).astype(np.float32)
    p = np.arange(128)
    d = p % 64
    half = d // 32
    i = (d % 16).astype(np.float32)
    inv = (10000.0 ** (-i / 16.0)).astype(np.float32)
    pos = np.where(half[:, None] == 0, rows[None, :], cols[None, :]).astype(np.float32)
    ang = pos * inv[:, None]
    c['rope'] = _bf(np.stack([np.cos(ang), np.sin(ang)], 1))
    rm = np.zeros((128, 128), np.float32)
    for m in range(128):
        dd = m % 32
        if dd < 16:
            rm[m + 16, m] = -1.0
        else:
            rm[m - 16, m] = 1.0
    c['rm'] = _bf(rm)
    for n, name in ((2048, 'l'), (256, 'c')):
        k = np.arange(n, dtype=np.float64)
        ang = 2.0 * np.pi * np.outer(k, k) / n
        c['dftc_' + name] = _bf(np.cos(ang))
        c['dfts_' + name] = _bf(-np.sin(ang))
    k = np.arange(64, dtype=np.float64)
    ang = 2.0 * np.pi * np.outer(k, k) / 64
    bc = np.zeros((128, 128)); bs = np.zeros((128, 128))
    for g in range(2):
        bc[g * 64:(g + 1) * 64, g * 64:(g + 1) * 64] = np.cos(ang)
        bs[g * 64:(g + 1) * 64, g * 64:(g + 1) * 64] = np.sin(ang)
    c['blk'] = _bf(np.stack([bc, bs], 1))
    sel = np.zeros((4, 2, 128), np.float32)
    for pr in range(2):
        sel[2 * pr, pr, :64] = 1.0
        sel[2 * pr + 1, pr, 64:] = 1.0
    c['sel'] = sel
    return c


def _nat_bias(rpb_l):
    out = np.full((5, 2, 128, 5, 2, 128), NEG, np.float32)
    tsel = [0, 1, 2, 14, 15]
    kk = np.arange(128)
    qq = np.arange(128)
    for ti, t in enumerate(tsel):
        cb = min(max(t - 2, 0), 11)
        for j in range(5):
            kr = (cb + j) * 2 + kk // 64
            kc = kk % 64
            r = 2 * t + qq // 64
            qc = qq % 64
            rs = np.clip(r - 4, 0, 24)
            row_ok = (kr[:, None] >= rs[None, :]) & (kr[:, None] < rs[None, :] + 8)
            qs = np.clip(qc - 8, 0, 48)
            col_ok = (kc[:, None] >= qs[None, :]) & (kc[:, None] < qs[None, :] + 16)
            ok = row_ok & col_ok
            drow = np.clip(kr[:, None] - r[None, :] + 7, 0, 14)
            dcol = np.clip(kc[:, None] - qc[None, :] + 15, 0, 30)
            for h in range(4):
                val = rpb_l[h][drow, dcol]
                out[ti, h // 2, :, j, h % 2, :] = np.where(ok, val, NEG)
    return out


def _fm(vec):
    v = vec.reshape(vec.shape[:-1] + (8, 128))
    return np.ascontiguousarray(np.moveaxis(v, -1, 0))


def build_nc(dbg=None):
    nc = bass.Bass("TRN2", target_bir_lowering=False)

    def din(name, shape, dt=F32):
        return nc.dram_tensor(name, list(shape), dt, kind="ExternalInput").ap()

    xin = din("xin", [2, 128, 8, T])
    cT = din("cT", [128, 8, 3])
    w_ada = din("w_ada", [2, D, 6 * D])
    badaT = din("badaT", [128, 2, 48])
    gvec = din("gvec", [128, 5, 8])
    w_in = din("w_in", [2, D, DIN])
    bg = din("bg", [4, 2, 4])
    wconv = din("wconv", [128, 2, 4, 3])
    natb = din("natb", [2, 5, 2, 128, 1280])
    wsT = din("wsT", [2, 128, 4, 128])
    bsT = din("bsT", [2, 128, 2, 128])
    ggm = din("ggm", [2, 256])
    gml = din("gml", [2, 256])
    w_fnet = din("w_fnet", [2, 256, 256])
    w_out = din("w_out", [2, D, D])
    w_ff1 = din("w_ff1", [2, D, 4 * D])
    w_ff2 = din("w_ff2", [2, 4 * D, D])
    c_ident = din("ident", [128, 128])
    c_jrev = din("jrev", [128, 128])
    c_masks = din("masks", [128, 2, 128], BF16)
    c_rope = din("rope", [128, 2, TL], BF16)
    c_rm = din("rm", [128, 128], BF16)
    c_dftc_l = din("dftc_l", [TL, TL], BF16)
    c_dfts_l = din("dfts_l", [TL, TL], BF16)
    c_dftc_c = din("dftc_c", [256, 256], BF16)
    c_dfts_c = din("dfts_c", [256, 256], BF16)
    c_blk = din("blk", [128, 2, 128], BF16)
    c_sel = din("sel", [4, 2, 128])
    outT = nc.dram_tensor("outT", [2, 128, 8, TL], F32, kind="ExternalOutput").ap()
    dbg_out = None
    if dbg:
        dbg_out = nc.dram_tensor("dbg", [128, 8, T], F32, kind="ExternalOutput").ap()

    with contextlib.ExitStack() as st:
        S = Sch(nc, st)

        uid = [0]

        def sb(stack, name, shape, dt):
            uid[0] += 1
            return stack.enter_context(nc.sbuf_tensor("s%d_%s" % (uid[0], name), list(shape), dt))

        def ps(stack, name, shape, dt=F32):
            uid[0] += 1
            return stack.enter_context(nc.psum_tensor("p%d_%s" % (uid[0], name), list(shape), dt))

        def dma(eng, out, in_, reads=(), writes=()):
            S.op(eng, lambda e: e.dma_start(out=out, in_=in_), reads=reads, writes=writes, dma=True)

        def mm(out, lhsT, rhs, start, stop, reads, writes):
            S.op('pe', lambda e: e.matmul(out, lhsT=lhsT, rhs=rhs, start=start, stop=stop),
                 reads=reads, writes=writes)

        def act(out, in_, func, reads, writes, scale=1.0, bias=None):
            if bias is None:
                S.op('act', lambda e: e.activation(out=out, in_=in_, func=func, scale=scale),
                     reads=reads, writes=writes)
            else:
                S.op('act', lambda e: e.activation(out=out, in_=in_, func=func, scale=scale, bias=bias),
                     reads=reads, writes=writes)

        def tt(eng, out, in0, in1, op, reads, writes):
            S.op(eng, lambda e: e.tensor_tensor(out=out, in0=in0, in1=in1, op=op), reads=reads, writes=writes)

        def stt(out, in0, scalar, in1, op0, op1, reads, writes):
            S.op('dve', lambda e: e.scalar_tensor_tensor(out=out, in0=in0, scalar=scalar, in1=in1, op0=op0, op1=op1),
                 reads=reads, writes=writes)

        def ts(eng, out, in0, s1, s2, op0, op1, reads, writes):
            if s2 is None:
                S.op(eng, lambda e: e.tensor_scalar(out=out, in0=in0, scalar1=s1, scalar2=None, op0=op0),
                     reads=reads, writes=writes)
            else:
                S.op(eng, lambda e: e.tensor_scalar(out=out, in0=in0, scalar1=s1, scalar2=s2, op0=op0, op1=op1),
                     reads=reads, writes=writes)

        def cp(eng, out, in_, reads, writes):
            S.op(eng, lambda e: e.tensor_copy(out=out, in_=in_), reads=reads, writes=writes)

        X = sb(st, "X", [128, 8, T], F32)
        CAT = sb(st, "CAT", [128, 8, T], BF16)
        ident = sb(st, "ident", [128, 128], F32)
        jrev = sb(st, "jrev", [128, 128], F32)
        identb = sb(st, "identb", [128, 128], BF16)
        masks = sb(st, "masks", [128, 2, 128], BF16)
        rm = sb(st, "rm", [128, 128], BF16)
        blk = sb(st, "blk", [128, 2, 128], BF16)
        sel = sb(st, "sel", [4, 2, 128], F32)
        onesb = sb(st, "onesb", [128, 128], BF16)
        onesf = sb(st, "onesf", [128, 2], F32)
        RS = sb(st, "RS", [128, T], F32)
        modT = sb(st, "modT", [128, 2, 48, 3], F32)
        gmT = sb(st, "gmT", [128, 2, 2, 8, 3], F32)
        gv = sb(st, "gv", [128, 5, 8], F32)
        bada = sb(st, "bada", [128, 2, 48], F32)
        csb = sb(st, "csb", [128, 8, 3], F32)
        scb = sb(st, "scb", [128, 8, 3], BF16)
        bgs = sb(st, "bgs", [4, 2, 4], F32)
        wcv = sb(st, "wcv", [128, 2, 4, 3], F32)

        dma('sp', ident[:], c_ident, writes=['ident'])
        dma('sp', jrev[:], c_jrev, writes=['jrev'])
        dma('sp', masks[:], c_masks, writes=['masks'])
        dma('sp', rm[:], c_rm, writes=['rm'])
        dma('sp', blk[:], c_blk, writes=['blk'])
        dma('sp', sel[:], c_sel, writes=['sel'])
        dma('sp', gv[:], gvec, writes=['gv'])
        dma('sp', bada[:], badaT, writes=['bada'])
        dma('sp', csb[:], cT, writes=['csb'])
        dma('sp', bgs[:], bg, writes=['bgs'])
        dma('sp', wcv[:], wconv, writes=['wcv'])
        S.op('dve', lambda e: e.memset(onesb[:], 1.0), writes=['onesb'])
        S.op('dve', lambda e: e.memset(onesf[:], 1.0), writes=['onesf'])
        cp('dve', identb[:], ident[:], ['ident'], ['identb'])
        act(scb[:], csb[:], AF.Silu, ['csb'], ['scb'])

        with contextlib.ExitStack() as ph:
            wa = [sb(ph, "wa%d" % i, [128, 8, 512], BF16) for i in range(2)]
            mps = [ps(ph, "mps%d" % i, [128, 4, 3]) for i in range(2)]
            it = 0
            for l in range(DEPTH):
                for pc in range(12):
                    b = it % 2
                    it += 1
                    dma('pool', wa[b][:], w_ada[l, :, pc * 512:(pc + 1) * 512].rearrange("(kc p) n -> p kc n", p=128),
                        writes=['wa%d' % b])
                    for oc in range(4):
                        for kc in range(8):
                            mm(mps[b][:, oc, :], wa[b][:, kc, oc * 128:(oc + 1) * 128], scb[:, kc, :],
                               kc == 0, kc == 7, ['wa%d' % b, 'scb'], ['mps%d' % b])
                    tt('dve', modT[:, l, pc * 4:(pc + 1) * 4, :], mps[b][:],
                       bada[:, l, pc * 4:(pc + 1) * 4].unsqueeze(2).broadcast_to([128, 4, 3]), ALU.add,
                       ['mps%d' % b, 'bada'], ['modT'])
            for l in range(DEPTH):
                for kind in range(2):
                    sc_j = 1 if kind == 0 else 4
                    for v in range(3):
                        stt(gmT[:, l, kind, :, v], modT[:, l, sc_j * 8:(sc_j + 1) * 8, v], 1.0,
                            gv[:, 2 * l + kind, :], ALU.add, ALU.mult, ['modT', 'gv'], ['gmT'])
        S.barrier()

        def modap(l, j, c, v):
            return modT[:, l, j * 8 + c, v:v + 1]

        hbuf = [sb(st, "hT%d" % i, [128, 8, 256], BF16) for i in range(2)]
        hrot = Rot([("hT%d" % i, hbuf[i]) for i in range(2)])
        sqb = [sb(st, "sq%d" % i, [128, 256], BF16) for i in range(2)]
        sqrot = Rot([("sq%d" % i, sqb[i]) for i in range(2)])
        rsb = [sb(st, "rs%d" % i, [128, 256], F32) for i in range(2)]
        rsrot = Rot([("rs%d" % i, rsb[i]) for i in range(2)])
        tmb = [sb(st, "tm%d" % i, [128, 256], F32) for i in range(2)]
        tmrot = Rot([("tm%d" % i, tmb[i]) for i in range(2)])
        ssq = [ps(st, "ssq%d" % i, [128, 512]) for i in range(1)]
        ssqrot = Rot([("ssq%d" % i, ssq[i]) for i in range(1)])

        def make_h(l, kind, t0, n, v, dst=None):
            if dst is None:
                hk, hT = hrot.next()
            else:
                hk, hT = dst
            if kind == 0:
                return _mk_tail(l, kind, t0, n, v, hk, hT, 'RS', RS[:, t0:t0 + n])
            pk, pst = ssqrot.next()
            for c in range(8):
                sk, sq = sqrot.next()
                act(sq[:, :n], X[:, c, t0:t0 + n], AF.Square, ['X%d' % c], [sk])
                mm(pst[:, :n], onesb[:], sq[:, :n], c == 0, c == 7, [sk, 'onesb'], [pk])
            rk, rs = rsrot.next()
            act(rs[:, :n], pst[:, :n], AF.Sqrt, [pk, 'epsb'], [rk], scale=1.0 / D, bias=epsb[:, 0:1])
            S.op('dve', lambda e: e.reciprocal(out=rs[:, :n], in_=rs[:, :n]), reads=[rk], writes=[rk])
            return _mk_tail(l, kind, t0, n, v, hk, hT, rk, rs)

        def _mk_tail(l, kind, t0, n, v, hk, hT, rk, rs):
            sh_j = 0 if kind == 0 else 3
            for c in range(8):
                stt(hT[:, c, :n], X[:, c, t0:t0 + n], gmT[:, l, kind, c, v:v + 1], rs[:, :n], ALU.mult, ALU.mult,
                    ['X%d' % c, 'gmT', rk], [hk + '_%d' % c])
            for c in range(8):
                act(hT[:, c, :n], hT[:, c, :n], AF.Identity, [hk + '_%d' % c, 'modT'], [hk + '_%d' % c],
                    bias=modap(l, sh_j, c, v))
            return hk, hT

        def hblocks(l, nb, s):
            nxt = make_h(l, 0, 0, 256, s)
            for b in range(nb):
                cur = nxt
                if b + 1 < nb:
                    nxt = make_h(l, 0, 256 * (b + 1), 256, s if b + 1 < 8 else 2)
                yield b, 256 * b, (s if b < 8 else 2), cur[0], cur[1]

        def compute_rs():
            for b in range(9):
                t0 = 256 * b
                n = 256
                pk, pst = ssqrot.next()
                for c in range(8):
                    sk, sq = sqrot.next()
                    act(sq[:, :n], X[:, c, t0:t0 + n], AF.Square, ['X%d' % c], [sk])
                    mm(pst[:, :n], onesb[:], sq[:, :n], c == 0, c == 7, [sk, 'onesb'], [pk])
                act(RS[:, t0:t0 + n], pst[:, :n], AF.Sqrt, [pk, 'epsb'], ['RS'], scale=1.0 / D, bias=epsb[:, 0:1])
                S.op('dve', lambda e, t0=t0, n=n: e.reciprocal(out=RS[:, t0:t0 + n], in_=RS[:, t0:t0 + n]),
                     reads=['RS'], writes=['RS'])

        epsb = sb(st, "epsb", [128, 1], F32)
        S.op('dve', lambda e: e.memset(epsb[:], EPS), writes=['epsb'])

        def proj_fm(hT, hk, n, w, wk, col0, ppool, evac):
            pk, pt = ppool.next()
            for kc in range(8):
                mm(pt[:, :n], w[:, kc, col0:col0 + 128], hT[:, kc, :n], kc == 0, kc == 7, [wk, hk + '_%d' % kc], [pk])
            evac(pt[:, :n], pk)

        def proj_tm(hT, hk, sub, w, wk, col0, ncol, ppool, evac):
            pk, pt = ppool.next()
            for kc in range(8):
                mm(pt[:, :ncol], hT[:, kc, sub * 128:(sub + 1) * 128], w[:, kc, col0:col0 + ncol],
                   kc == 0, kc == 7, [wk, hk + '_%d' % kc], [pk])
            evac(pt[:, :ncol], pk)

        def wload(wt, key, l, col0, ncol):
            dma('pool', wt, w_in[l, :, col0:col0 + ncol].rearrange("(kc p) n -> p kc n", p=128), writes=[key])

        ORD = [[16, 17] + list(range(16)), [17, 16] + list(range(15, -1, -1))]

        def mixer_c(s, l, last):
            with contextlib.ExitStack() as ph:
                cf03 = CAT[:, 0:4, :].rearrange("p c t -> p (c t)")
                vaug = cf03[:, 0:4752].rearrange("p (j h d) -> p j h d", j=18, h=4)[:, :, :, 0:65]
                wqk = cf03[:, 4752:4752 + 4096].rearrange("p (k n) -> p k n", k=8)
                ktm = CAT[:, 6:8, :].rearrange("p c t -> p (c t)").rearrange("p (j f) -> p j f", f=256)
                R1 = sb(ph, "R1", [128, 2 * T], F32)
                R2 = sb(ph, "R2", [128, 4, T], BF16)
                RAW = R1[:].bitcast(BF16).rearrange("p (c t) -> p c t", c=4)
                QK = R2
                wv = sb(ph, "wv", [128, 8, 256], BF16)
                wg = sb(ph, "wg", [128, 8, 16], BF16)
                gtm = sb(ph, "gtm", [128, 18, 16], F32)
                gmb = sb(ph, "gmb", [128, 256], F32)
                dma('sp', gmb[:], gml[l, :].partition_broadcast(128), writes=['gmb'])
                wload(wqk, 'wqk', l, OFF_C, 512)
                wload(wv[:], 'wv', l, OFF_C + 512, 256)
                wload(wg[:], 'wg', l, OFF_G, 16)
                S.op('pool', lambda e: e.memset(vaug[:, :, :, 64:65], 1.0), writes=['vaug1'])
                with contextlib.ExitStack() as p1:
                    pq = [ps(p1, "pq%d" % i, [128, 512]) for i in range(2)]
                    pqr = Rot([("pq%d" % i, pq[i]) for i in range(2)])
                    pvv = [ps(p1, "pvv%d" % i, [128, 512]) for i in range(2)]
                    pvr = Rot([("pvv%d" % i, pvv[i]) for i in range(2)])
                    rope = sb(p1, "rope", [128, 2, TL], BF16)
                    dma('sp', rope[:], c_rope, writes=['rope'])
                    for b, t0, v, hk, hT in hblocks(l, 9, s):
                        for oc in range(4):
                            proj_fm(hT, hk, 256, wqk, 'wqk', oc * 128, pqr,
                                    lambda p_, pk, oc=oc: act(RAW[:, oc, t0:t0 + 256], p_, AF.Copy, [pk], ['RAW%d' % oc]))
                        for sub in range(2):
                            j = 2 * b + sub
                            proj_tm(hT, hk, sub, wv, 'wv', 0, 256, pvr,
                                    lambda p_, pk, j=j: cp('dve', vaug[:, j, :, 0:64], p_.rearrange("p (h d) -> p h d", h=4),
                                                           [pk], ['vaug%d' % j]))
                            proj_tm(hT, hk, sub, wg, 'wg', 0, 16, pvr,
                                    lambda p_, pk, j=j: cp('dve', gtm[:, j, :], p_, [pk], ['gtm']))
                    if CSTOP == 1:
                        return
                    cvt = [sb(p1, "cvt%d" % i, [128, 512], F32) for i in range(2)]
                    cvr = Rot([("cvt%d" % i, cvt[i]) for i in range(2)])
                    cst = [sb(p1, "cst%d" % i, [128, 512], BF16) for i in range(2)]
                    csr = Rot([("cst%d" % i, cst[i]) for i in range(2)])
                    r2t = [sb(p1, "r2t%d" % i, [128, 512], F32) for i in range(2)]
                    r2r = Rot([("r2t%d" % i, r2t[i]) for i in range(2)])
                    r3t = [sb(p1, "r3t%d" % i, [128, 512], F32) for i in range(2)]
                    r3r = Rot([("r3t%d" % i, r3t[i]) for i in range(2)])
                    for oc in range(4):
                        scl = 0.125 if oc >= 2 else 1.0
                        for (g0, g1) in ((0, TL), (TL, T)):
                            for t0 in range(g0, g1, 512):
                                n = min(512, g1 - t0)
                                ck, ct = cvr.next()
                                ts('dve', ct[:, :n], RAW[:, oc, t0:t0 + n], wcv[:, l, oc, 1:2], None, ALU.mult, None,
                                   ['RAW%d' % oc, 'wcv'], [ck])
                                a = 1 if t0 == g0 else 0
                                stt(ct[:, a:n], RAW[:, oc, t0 + a - 1:t0 + n - 1], wcv[:, l, oc, 0:1], ct[:, a:n],
                                    ALU.mult, ALU.add, ['RAW%d' % oc, 'wcv', ck], [ck])
                                bnd = n - 1 if t0 + n == g1 else n
                                stt(ct[:, 0:bnd], RAW[:, oc, t0 + 1:t0 + 1 + bnd], wcv[:, l, oc, 2:3], ct[:, 0:bnd],
                                    ALU.mult, ALU.add, ['RAW%d' % oc, 'wcv', ck], [ck])
                                if g0 == 0:
                                    sk, cs_ = csr.next()
                                    act(cs_[:, :n], ct[:, :n], AF.Silu, [ck], [sk])
                                    pk, pt = pqr.next()
                                    mm(pt[:, :n], rm[:], cs_[:, :n], True, True, ['rm', sk], [pk])
                                    k2, t2 = r2r.next()
                                    stt(t2[:, :n], pt[:, :n], scl, rope[:, 1, t0:t0 + n], ALU.mult, ALU.mult, [pk, 'rope'], [k2])
                                    k3, t3 = r3r.next()
                                    stt(t3[:, :n], cs_[:, :n], scl, rope[:, 0, t0:t0 + n], ALU.mult, ALU.mult, [sk, 'rope'], [k3])
                                    tt('pool', QK[:, oc, t0:t0 + n], t2[:, :n], t3[:, :n], ALU.add, [k2, k3], ['QK%d' % oc])
                                else:
                                    act(QK[:, oc, t0:t0 + n], ct[:, :n], AF.Silu, [ck], ['QK%d' % oc], scale=1.0)
                                    if scl != 1.0:
                                        ts('dve', QK[:, oc, t0:t0 + n], QK[:, oc, t0:t0 + n], scl, None, ALU.mult, None,
                                           ['QK%d' % oc], ['QK%d' % oc])
                    for j in range(18):
                        pk, pt = pqr.next()
                        for kc in range(2):
                            mm(pt[:, kc * 128:(kc + 1) * 128], QK[:, 2 + kc, 128 * j:128 * j + 128], identb[:], True, True,
                               ['QK%d' % (2 + kc), 'identb'], [pk])
                        cp('dve', ktm[:, j, :], pt[:, 0:256], [pk], ['ktm%d' % j])
                S.barrier()
                if CSTOP == 2:
                    return
                R3 = sb(ph, "R3", [128, 18 * 256], F32)
                hsum = R3[:].rearrange("p (j f) -> p j f", f=256)
                colq = [sb(ph, "colq%d" % i, [128, 18, 12], F32) for i in range(2)]
                dcol = [sb(ph, "dcol%d" % i, [128, 2, 18], F32) for i in range(2)]
                with contextlib.ExitStack() as p3:
                    R1f = R1
                    rI = R1f[0:4, 0:T]
                    rF = R1f[0:4, T:2 * T]
                    rG = R3[0:4, 0:T]
                    rA = R3[0:4, T:2 * T]
                    rrow = sb(p3, "rrow", [4, 20], F32)
                    drow = sb(p3, "drow", [4, 18], F32)
                    colraw = sb(p3, "colraw", [128, 18, 12], F32)
                    prw = [ps(p3, "prw%d" % i, [128, 512]) for i in range(2)]
                    pcol = ps(p3, "pcol", [128, 18, 12])
                    pcol2 = ps(p3, "pcol2", [128, 18, 12])
                    pd = ps(p3, "pd", [128, 2, 18])
                    for dr in range(2):
                        tr_m = ident if dr == 0 else jrev
                        trk = 'ident' if dr == 0 else 'jrev'
                        for g in range(5):
                            idxs = list(range(4 * g, min(4 * g + 4, 18)))
                            for qi, (dst, pw) in enumerate(((rI, prw[0]), (rF, prw[1]))):
                                for ii, idx in enumerate(idxs):
                                    j = ORD[dr][idx]
                                    c0 = dr * 8 + qi * 4
                                    mm(pw[0:4, ii * 128:(ii + 1) * 128], gtm[:, j, c0:c0 + 4], tr_m[:], True, True,
                                       ['gtm', trk], ['prw%d' % qi])
                                n = 128 * len(idxs)
                                ts('dve', dst[:, 512 * g:512 * g + n], pw[0:4, 0:n], bgs[:, l, dr * 2 + qi:dr * 2 + qi + 1], None,
                                   ALU.add, None, ['prw%d' % qi, 'bgs'], ['row%d' % qi])
                        act(rF, rF, AF.Exp, ['row1'], ['row1'], scale=-1.0)
                        act(rF, rF, AF.Ln, ['row1'], ['row1'], bias=onesf[0:4, 0:1])
                        S.op('dve', lambda e: e.tensor_tensor_scan(out=rG, data0=onesf[0:4, 0:1].broadcast_to([4, T]), data1=rF,
                                                                   initial=0.0, op0=ALU.mult, op1=ALU.add),
                             reads=['row1', 'onesf'], writes=['row2'])
                        tt('dve', rI, rI, rG, ALU.add, ['row0', 'row2'], ['row0'])
                        S.op('dve', lambda e: e.tensor_tensor_scan(out=rA, data0=onesf[0:4, 0:1].broadcast_to([4, T]), data1=rI,
                                                                   initial=0.0, op0=ALU.mult, op1=ALU.max),
                             reads=['row0', 'onesf'], writes=['row3'])
                        S.op('dve', lambda e: e.memset(rrow[:, 0:1], 0.0), writes=['rrow'])
                        cp('dve', rrow[:, 1:19], rA.rearrange("p (j t) -> p j t", t=128)[:, :, 127], ['row3'], ['rrow'])
                        rfull = rrow[:, 0:18].unsqueeze(2).broadcast_to([4, 18, 128])
                        v3 = lambda r_: r_.rearrange("p (j t) -> p j t", t=128)
                        tt('dve', v3(rI), v3(rI), rfull, ALU.subtract, ['row0', 'rrow'], ['row0'])
                        act(rI, rI, AF.Exp, ['row0'], ['row0'])
                        tt('dve', rG, rG, rA, ALU.subtract, ['row2', 'row3'], ['row2'])
                        act(rG, rG, AF.Exp, ['row2'], ['row2'])
                        tt('dve', v3(rA), rfull, v3(rA), ALU.subtract, ['row3', 'rrow'], ['row3'])
                        act(rA, rA, AF.Exp, ['row3'], ['row3'])
                        tt('dve', drow[:, 0:18], rrow[:, 0:18], rrow[:, 1:19], ALU.subtract, ['rrow'], ['drow'])
                        act(drow[:], drow[:], AF.Exp, ['drow'], ['drow'])
                        for idx in range(18):
                            for qi, rw in enumerate((rI, rA, rG)):
                                mm(pcol[:, idx, qi * 4:(qi + 1) * 4], rw[:, idx * 128:(idx + 1) * 128], ident[0:4, 0:4], True, True,
                                   ['row0', 'row2', 'row3', 'ident'], ['pcol'])
                        if dr == 0:
                            cp('dve', colq[0][:], pcol[:], ['pcol'], ['colq0'])
                        else:
                            cp('dve', colraw[:], pcol[:], ['pcol'], ['colraw'])
                            mm(pcol2[:].rearrange("p a b -> p (a b)"), jrev[:], colraw[:].rearrange("p a b -> p (a b)"), True, True,
                               ['colraw', 'jrev'], ['pcol2'])
                            cp('dve', colq[1][:], pcol2[:], ['pcol2'], ['colq1'])
                        for pr in range(2):
                            mm(pd[:, pr, :], sel[:, pr, :], drow[:], True, True, ['sel', 'drow'], ['pd'])
                        cp('dve', dcol[dr][:], pd[:], ['pd'], ['dcol%d' % dr])
                S.barrier()
                if CSTOP == 3:
                    return
                with contextlib.ExitStack() as p4:
                    qz4_ = [sb(p4, "qzc%d" % i, [128, 256], BF16) for i in range(2)]
                    qzr4 = Rot([("qzc%d" % i, qz4_[i]) for i in range(2)])
                    for i in range(2):
                        S.op('pool', lambda e, i=i: e.memset(qz4_[i][:], 0.0), writes=['qzc%dz' % i, 'qzc%d' % i])
                    vs_ = [sb(p4, "vs%d" % i, [128, 4, 80], BF16) for i in range(2)]
                    vsr = Rot([("vs%d" % i, vs_[i]) for i in range(2)])
                    wt_ = [sb(p4, "wt%d" % i, [128, 4, 128], BF16) for i in range(2)]
                    wtr = Rot([("wt%d" % i, wt_[i]) for i in range(2)])
                    sm = [sb(p4, "sm%d" % i, [128, 16], F32) for i in range(2)]
                    smr = Rot([("sm%d" % i, sm[i]) for i in range(2)])
                    ho = [sb(p4, "ho%d" % i, [128, 4, 64], F32) for i in range(1)]
                    hor = Rot([("ho%d" % i, ho[i]) for i in range(1)])
                    stp = [ps(p4, "stp%d" % i, [128, 512]) for i in range(2)]
                    stpr = Rot([("stp%d" % i, stp[i]) for i in range(2)])
                    opp = [ps(p4, "opp%d" % i, [128, 4, 80]) for i in range(2)]
                    oppr = Rot([("opp%d" % i, opp[i]) for i in range(2)])
                    upp = [ps(p4, "upp%d" % i, [128, 2, 80]) for i in range(2)]
                    uppr = Rot([("upp%d" % i, upp[i]) for i in range(2)])
                    CstD, CbfD, Cbf4D = {}, {}, {}
                    for dr in range(2):
                        CstD[dr] = sb(p4, "CstD%d" % dr, [128, 2, 80], F32)
                        CbfD[dr] = sb(p4, "CbfD%d" % dr, [128, 4, 80], BF16)
                        Cbf4D[dr] = CbfD[dr][:].rearrange("p (c u) d -> p c u d", u=2)
                        S.op('dve', lambda e, dr=dr: e.memset(CstD[dr][:], 0.0), writes=['Cst%d' % dr])
                        S.op('dve', lambda e, dr=dr: e.memset(CbfD[dr][:], 0.0), writes=['Cbf%d' % dr])
                    written = set()
                    for idx in range(18):
                        for dr in (1, 0):
                            Cst, Cbf, Cbf4 = CstD[dr], CbfD[dr], Cbf4D[dr]
                            CK, BK = 'Cst%d' % dr, 'Cbf%d' % dr
                            j = ORD[dr][idx]
                            tok = slice(128 * j, 128 * j + 128)
                            emit = not (last and j >= 16)
                            vk, vs = vsr.next()
                            tt('dve', vs[:, :, 0:65], vaug[:, j, :, :], colq[dr][:, idx, 0:4].unsqueeze(2).broadcast_to([128, 4, 65]),
                               ALU.mult, ['vaug%d' % j, 'vaug1', 'colq%d' % dr], [vk])
                            if emit:
                                sk, sp_ = stpr.next()
                                for c2 in range(2):
                                    zk, qz = qzr4.next()
                                    act(qz[0:64, 0:128], QK[0:64, c2, tok], AF.Copy, ['QK', zk + 'z'], [zk])
                                    act(qz[64:128, 128:256], QK[64:128, c2, tok], AF.Copy, ['QK', zk + 'z'], [zk])
                                    mm(sp_[:, c2 * 256:(c2 + 1) * 256], QK[:, 2 + c2, tok], qz[:], True, True, ['QK', zk], [sk])
                                wk, wt = wtr.next()
                                tt('dve', wt[:], sp_[:].rearrange("p (h t) -> p h t", h=4),
                                   masks[:, dr, :].unsqueeze(1).broadcast_to([128, 4, 128]), ALU.mult, [sk, 'masks'], [wk])
                                ok_, op_ = oppr.next()
                                for h in range(4):
                                    c2, po = h // 2, (h % 2) * 64
                                    mm(op_[:, h, 0:65], wt[:, h, :], vs[:, h, 0:65], True, False, [wk, vk], [ok_])
                                    mm(op_[:, h, 0:65], QK[:, c2, tok], Cbf[:, h, 0:65], False, True,
                                       ['QK', BK], [ok_])
                                mk, m_ = smr.next()
                                eo = colq[dr][:, idx, 4:8]
                                eb = colq[dr][:, idx, 8:12]
                                tt('dve', m_[:, 0:4], op_[:, :, 64], eo, ALU.mult, [ok_, 'colq%d' % dr], [mk])
                                stt(m_[:, 4:8], m_[:, 0:4], -1.0, m_[:, 0:4], ALU.mult, ALU.max, [mk], [mk])
                                tt('dve', m_[:, 4:8], m_[:, 4:8], eb, ALU.max, [mk, 'colq%d' % dr], [mk])
                                S.op('dve', lambda e, m_=m_: e.reciprocal(out=m_[:, 8:12], in_=m_[:, 4:8]), reads=[mk], writes=[mk])
                                tt('dve', m_[:, 12:16], m_[:, 8:12], eo, ALU.mult, [mk, 'colq%d' % dr], [mk])
                                hs_j = hsum[:, j, :].rearrange("p (h d) -> p h d", h=4)
                                rcb = m_[:, 12:16].unsqueeze(2).broadcast_to([128, 4, 64])
                                if j not in written:
                                    written.add(j)
                                    tt('dve', hs_j, op_[:, :, 0:64], rcb, ALU.mult, [ok_, mk], ['hsum%d' % j])
                                else:
                                    hk2, h2 = hor.next()
                                    tt('dve', h2[:], op_[:, :, 0:64], rcb, ALU.mult, [ok_, mk], [hk2])
                                    tt('pool', hs_j, hs_j, h2[:], ALU.add, [hk2, 'hsum%d' % j], ['hsum%d' % j])
                            if idx < 17:
                                uk, up = uppr.next()
                                for pr in range(2):
                                    for hh in range(2):
                                        mm(up[hh * 64:(hh + 1) * 64, pr, 0:65], ktm[:, j, pr * 128 + hh * 64:pr * 128 + hh * 64 + 64],
                                           vs[:, 2 * pr + hh, 0:65], True, True, ['ktm%d' % j, vk], [uk])
                                tt('dve', Cst[:, :, 0:65], up[:, :, 0:65], Cst[:, :, 0:65], ALU.add, [uk, CK], [CK])
                                tt('dve', Cst[:, :, 0:65], Cst[:, :, 0:65], dcol[dr][:, :, idx].unsqueeze(2).broadcast_to([128, 2, 65]), ALU.mult,
                                   [CK, 'dcol%d' % dr], [CK])
                                cp('pool', Cbf4[0:64, :, 0, 0:65], Cst[0:64, :, 0:65], [CK], [BK])
                                cp('pool', Cbf4[64:128, :, 1, 0:65], Cst[64:128, :, 0:65], [CK], [BK])
                S.barrier()
                if CSTOP == 4:
                    return
                with contextlib.ExitStack() as p5:
                    OT = R2[:, 0:2, :]
                    wload(wv[:], 'wv', l, OFF_C + 768, 256)
                    pq = [ps(p5, "pq5_%d" % i, [128, 512]) for i in range(2)]
                    pqr = Rot([("pq5_%d" % i, pq[i]) for i in range(2)])
                    nb = 8 if last else 9
                    for b, t0, v, hk, hT in hblocks(l, nb, s):
                        for oc in range(2):
                            proj_fm(hT, hk, 256, wv, 'wv', oc * 128, pqr,
                                    lambda p_, pk, oc=oc: act(OT[:, oc, t0:t0 + 256], p_, AF.Sigmoid, [pk], ['OT%d' % oc]))
                    nj = 16 if last else 18
                    G = nj * 4
                    H3 = R3[:, 0:nj * 256].rearrange("p (g d) -> p g d", d=64)
                    SQ3 = R1[:, 0:nj * 256].rearrange("p (g d) -> p g d", d=64)
                    lst = sb(p5, "lnst", [128, 4, 72], F32)
                    HK = ['hsum%d' % j for j in range(nj)]
                    S.op('dve', lambda e: e.reduce_sum(out=lst[:, 0, 0:G], in_=H3, axis=AX.X), reads=HK, writes=['lnst'])
                    ts('dve', lst[:, 0, 0:G], lst[:, 0, 0:G], 1.0 / 64, None, ALU.mult, None, ['lnst'], ['lnst'])
                    tt('dve', H3, H3, lst[:, 0, 0:G].unsqueeze(2).broadcast_to([128, G, 64]), ALU.subtract, HK + ['lnst'], HK)
                    tt('pool', SQ3, H3, H3, ALU.mult, HK, ['SQ3'])
                    S.op('dve', lambda e: e.reduce_sum(out=lst[:, 1, 0:G], in_=SQ3, axis=AX.X), reads=['SQ3'], writes=['lnst'])
                    ts('dve', lst[:, 1, 0:G], lst[:, 1, 0:G], 1.0 / 64, EPS, ALU.mult, ALU.add, ['lnst'], ['lnst'])
                    act(lst[:, 2, 0:G], lst[:, 1, 0:G], AF.Sqrt, ['lnst'], ['lnst'])
                    S.op('dve', lambda e: e.reciprocal(out=lst[:, 3, 0:G], in_=lst[:, 2, 0:G]), reads=['lnst'], writes=['lnst'])
                    tt('dve', H3, H3, lst[:, 3, 0:G].unsqueeze(2).broadcast_to([128, G, 64]), ALU.mult, HK + ['lnst'], HK)
                    H2v = R3[:, 0:nj * 256].rearrange("p (j f) -> p j f", f=256)
                    tt('pool', H2v, H2v, gmb[:].unsqueeze(1).broadcast_to([128, nj, 256]), ALU.mult, HK + ['gmb'], HK)
                    for j in range(nj):
                        tok = slice(128 * j, 128 * j + 128)
                        pk, pt = pqr.next()
                        for kc in range(2):
                            mm(pt[:, kc * 128:(kc + 1) * 128], hsum[:, j, kc * 128:(kc + 1) * 128],
                               ident[:], True, True, ['hsum%d' % j, 'ident'], [pk])
                        tt('dve', CAT[:, 4:6, tok], pt[:, 0:256].rearrange("p (c t) -> p c t", c=2), OT[:, :, tok], ALU.mult,
                           [pk, 'OT0', 'OT1'], ['CATc'])

        def mixer_a(s, l, last):
            with contextlib.ExitStack() as ph:
                QA = sb(ph, "QA", [128, 2, T], BF16)
                KA = sb(ph, "KA", [128, 2, T], BF16)
                VA = sb(ph, "VA", [128, 18, 256], BF16)
                wq = sb(ph, "wqa", [128, 8, 768], BF16)
                wload(wq[:], 'wqa', l, OFF_A, 768)
                with contextlib.ExitStack() as p1:
                    pq = [ps(p1, "pqa%d" % i, [128, 512]) for i in range(2)]
                    pqr = Rot([("pqa%d" % i, pq[i]) for i in range(2)])
                    for b, t0, v, hk, hT in hblocks(l, 9, s):
                        for oc in range(4):
                            if oc < 2 and last and b == 8:
                                continue
                            dst = QA if oc < 2 else KA
                            proj_fm(hT, hk, 256, wq, 'wqa', oc * 128, pqr,
                                    lambda p_, pk, oc=oc, dst=dst: act(dst[:, oc % 2, t0:t0 + 256], p_, AF.Copy, [pk], ['QKA']))
                        for sub in range(2):
                            j = 2 * b + sub
                            proj_tm(hT, hk, sub, wq, 'wqa', 512, 256, pqr,
                                    lambda p_, pk, j=j: cp('dve', VA[:, j, :], p_, [pk], ['VA']))
                S.barrier()
                with contextlib.ExitStack() as p2:
                    bt = [sb(p2, "bt%d" % i, [128, 1280], F32) for i in range(2)]
                    btr = Rot([("bt%d" % i, bt[i]) for i in range(2)])
                    tf = [sb(p2, "tf%d" % i, [128, 1280], F32) for i in range(2)]
                    tfr = Rot([("tf%d" % i, tf[i]) for i in range(2)])
                    PT = [sb(p2, "PT%d" % i, [128, 7, 256], BF16) for i in range(2)]
                    ptr_ = Rot([("PT%d" % i, PT[i]) for i in range(2)])
                    rc_ = [sb(p2, "rca%d" % i, [128, 256], F32) for i in range(2)]
                    rcr = Rot([("rca%d" % i, rc_[i]) for i in range(2)])
                    sps = ps(p2, "sps", [128, 8, 256])
                    ov = [ps(p2, "ov%d" % i, [128, 512]) for i in range(2)]
                    ovr = Rot([("ov%d" % i, ov[i]) for i in range(2)])
                    qz_ = [sb(p2, "qz%d" % i, [128, 256], BF16) for i in range(2)]
                    qzr = Rot([("qz%d" % i, qz_[i]) for i in range(2)])
                    for i in range(2):
                        S.op('pool', lambda e, i=i: e.memset(qz_[i][:], 0.0), writes=['qz%dz' % i, 'qz%d' % i])
                    nq = 16 if last else 18
                    qts = list(range(nq))
                    if ASTOP == 1:
                        qts = []
                    if ASTOP == 2:
                        qts = [16, 17]
                    if ASTOP == 3:
                        qts = [5]
                    units = [(qt, pr) for qt in qts for pr in range(2)]

                    def stage1(qt, pr):
                        ctxq = qt >= 16
                        tq = slice(128 * qt, 128 * qt + 128)
                        if ctxq:
                            kch = [16, 17]
                            bk = bias_t = None
                        else:
                            cb = min(max(qt - 2, 0), 11)
                            kch = list(range(cb, cb + 5)) + [16, 17]
                            typ = {0: 0, 1: 1, 14: 3, 15: 4}.get(qt, 2)
                            bk, bias_t = btr.next()
                            dma('sp', bias_t[:], natb[l, typ, pr], writes=[bk])
                        zk, qz = qzr.next()
                        cp('pool', qz[0:64, 0:128], QA[0:64, pr, tq], ['QKA', zk + 'z'], [zk])
                        cp('pool', qz[64:128, 128:256], QA[64:128, pr, tq], ['QKA', zk + 'z'], [zk])
                        for i, kc_ in enumerate(kch):
                            mm(sps[:, i, :], KA[:, pr, 128 * kc_:128 * kc_ + 128], qz[:], True, True, ['QKA', zk], ['sps%d' % (i // 2)])
                        return (qt, pr, ctxq, tq, kch, bk, bias_t)

                    def stage2(st_):
                        qt, pr, ctxq, tq, kch, bk, bias_t = st_
                        nk = len(kch)
                        pk_, P_ = ptr_.next()
                        if not ctxq:
                            fk, tfl = tfr.next()
                            for (a0, a1) in ((0, 2), (2, 4), (4, 5)):
                                stt(tfl[:, a0 * 256:a1 * 256], sps[:, a0:a1, :].rearrange("p a b -> p (a b)"), 0.125,
                                    bias_t[:, a0 * 256:a1 * 256], ALU.mult, ALU.add, ['sps%d' % (a0 // 2), bk], [fk])
                            for a0 in (5, 6):
                                act(P_[:, a0, :], sps[:, a0, :], AF.Exp, ['sps%d' % (a0 // 2)], [pk_ + 'c'], scale=0.125)
                            act(P_[:, 0:5, :].rearrange("p a b -> p (a b)"), tfl[:], AF.Exp, [fk], [pk_])
                        else:
                            act(P_[:, 0:2, :].rearrange("p a b -> p (a b)"), sps[:, 0:2, :].rearrange("p a b -> p (a b)"),
                                AF.Exp, ['sps0'], [pk_, pk_ + 'c'], scale=0.125)
                        return (qt, pr, tq, kch, pk_, P_)

                    def stage3(st_):
                        qt, pr, tq, kch, pk_, P_ = st_
                        nk = len(kch)
                        ok_, o_ = ovr.next()
                        for i, kc_ in enumerate(kch):
                            mm(o_[:, 0:256], VA[:, kc_, pr * 128:(pr + 1) * 128], P_[:, i, :], i == 0, i == nk - 1,
                               ['VA', pk_, pk_ + 'c'], [ok_])
                        for i, kc_ in enumerate(kch):
                            mm(o_[:, 256:512], onesb[:], P_[:, i, :], i == 0, i == nk - 1, ['onesb', pk_, pk_ + 'c'], [ok_])
                        rk_, r_ = rcr.next()
                        S.op('dve', lambda e, r_=r_, o_=o_: e.reciprocal(out=r_[:], in_=o_[:, 256:512]), reads=[ok_], writes=[rk_])
                        for hh in range(2):
                            po = hh * 64
                            tt('dve', CAT[po:po + 64, pr, tq], o_[po:po + 64, hh * 128:(hh + 1) * 128],
                               r_[po:po + 64, hh * 128:(hh + 1) * 128], ALU.mult, [ok_, rk_], ['CATa'])

                    if units:
                        s1 = stage1(*units[0])
                        for ui in range(len(units)):
                            s2 = stage2(s1)
                            if ui + 1 < len(units):
                                s1 = stage1(*units[ui + 1])
                            stage3(s2)

        def mixer_b(s, l, last):
            with contextlib.ExitStack() as ph:
                UB = sb(ph, "UB", [128, 2, T], BF16)
                ZB = sb(ph, "ZB", [128, 18, 256], BF16)
                wb_ = sb(ph, "wbb", [128, 8, 512], BF16)
                wsb = sb(ph, "wsb", [128, 4, 128], BF16)
                bsb = sb(ph, "bsb", [128, 2, 128], F32)
                ggb = sb(ph, "ggb", [128, 256], F32)
                wload(wb_[:], 'wbb', l, OFF_B, 512)
                dma('pool', wsb[:], wsT[l], writes=['wsb'])
                dma('sp', bsb[:], bsT[l], writes=['bsb'])
                dma('sp', ggb[:], ggm[l, :].partition_broadcast(128), writes=['ggb'])
                pq = [ps(ph, "pqb%d" % i, [128, 512]) for i in range(2)]
                pqr = Rot([("pqb%d" % i, pq[i]) for i in range(2)])
                zf = [sb(ph, "zf%d" % i, [128, 256], F32) for i in range(2)]
                zfr = Rot([("zf%d" % i, zf[i]) for i in range(2)])
                zq = [sb(ph, "zq%d" % i, [128, 256], F32) for i in range(2)]
                zqr = Rot([("zq%d" % i, zq[i]) for i in range(2)])
                zs = [sb(ph, "zs%d" % i, [128, 4], F32) for i in range(2)]
                zsr = Rot([("zs%d" % i, zs[i]) for i in range(2)])
                nb = 8 if last else 9
                for b, t0, v, hk, hT in hblocks(l, nb, s):
                    for oc in range(2):
                        proj_fm(hT, hk, 256, wb_, 'wbb', oc * 128, pqr,
                                lambda p_, pk, oc=oc: act(UB[:, oc, t0:t0 + 256], p_, AF.Gelu_apprx_tanh, [pk], ['UB']))
                    for sub in range(2):
                        j = 2 * b + sub

                        def ev(p_, pk, j=j):
                            fk, z_ = zfr.next()
                            act(z_[:], p_, AF.Gelu_apprx_tanh, [pk], [fk])
                            qk_, q_ = zqr.next()
                            tt('pool', q_[:], z_[:], z_[:], ALU.mult, [fk], [qk_])
                            sk, s_ = zsr.next()
                            S.op('dve', lambda e: e.reduce_sum(out=s_[:, 0:1], in_=q_[:], axis=AX.X), reads=[qk_], writes=[sk])
                            ts('dve', s_[:, 0:1], s_[:, 0:1], 1.0 / 256, EPS, ALU.mult, ALU.add, [sk], [sk])
                            act(s_[:, 1:2], s_[:, 0:1], AF.Sqrt, [sk], [sk])
                            S.op('dve', lambda e: e.reciprocal(out=s_[:, 2:3], in_=s_[:, 1:2]), reads=[sk], writes=[sk])
                            stt(ZB[:, j, :], z_[:], s_[:, 2:3], ggb[:], ALU.mult, ALU.mult, [fk, sk, 'ggb'], ['ZB%d' % j])
                        proj_tm(hT, hk, sub, wb_, 'wbb', 256, 256, pqr, ev)
                mt = [sb(ph, "mt%d" % i, [128, 128], F32) for i in range(2)]
                mtr = Rot([("mt%d" % i, mt[i]) for i in range(2)])
                for j in range(16 if last else 18):
                    tok = slice(128 * j, 128 * j + 128)
                    for pr in range(2):
                        pk, pt = pqr.next()
                        for hh in range(2):
                            po = hh * 64
                            mm(pt[po:po + 64, 0:128], ZB[:, j, pr * 128 + po:pr * 128 + po + 64], wsb[:, 2 * pr + hh, :],
                               True, True, ['ZB%d' % j, 'wsb'], [pk])
                        mk, m_ = mtr.next()
                        tt('dve', m_[:], pt[:, 0:128], bsb[:, pr, :], ALU.add, [pk, 'bsb'], [mk])
                        tt('pool', CAT[:, 2 + pr, tok], m_[:], UB[:, pr, tok], ALU.mult, [mk, 'UB'], ['CATb'])

        def mixer_d(s, l, last):
            with contextlib.ExitStack() as ph:
                FT = sb(ph, "FT", [128, 18, 256], BF16)
                wfi = sb(ph, "wfi", [128, 8, 256], BF16)
                wfn = sb(ph, "wfn", [128, 2, 256], BF16)
                wload(wfi[:], 'wfi', l, OFF_D, 256)
                dma('pool', wfn[:], w_fnet[l].rearrange("(kc p) n -> p kc n", p=128), writes=['wfn'])
                pq = [ps(ph, "pqd%d" % i, [128, 512]) for i in range(2)]
                pqr = Rot([("pqd%d" % i, pq[i]) for i in range(2)])
                nb = 8 if last else 9
                for b, t0, v, hk, hT in hblocks(l, nb, s):
                    for sub in range(2):
                        j = 2 * b + sub
                        proj_tm(hT, hk, sub, wfi, 'wfi', 0, 256, pqr,
                                lambda p_, pk, j=j: cp('dve', FT[:, j, :], p_, [pk], ['FT']))
                dc_ = [sb(ph, "dc%d" % i, [128, 16, 256], BF16) for i in range(2)]
                ds_ = [sb(ph, "ds%d" % i, [128, 16, 256], BF16) for i in range(2)]
                YC = [sb(ph, "YC%d" % i, [128, 2, 256], BF16) for i in range(2)]
                YS = [sb(ph, "YS%d" % i, [128, 2, 256], BF16) for i in range(2)]
                SPc = [sb(ph, "SP%d" % i, [128, 2, 256], BF16) for i in range(2)]
                ycp = ps(ph, "ycp", [128, 512])
                ysp = ps(ph, "ysp", [128, 512])
                spp = ps(ph, "spp", [128, 512])
                dpp = ps(ph, "dpp", [128, 512])
                it = 0
                segs = [(0, 16, TL, c_dftc_l, c_dfts_l)]
                if not last:
                    segs.append((16, 2, 256, c_dftc_c, c_dfts_c))
                for (jb, nchk, nT, mc, ms) in segs:
                    scale = float(1.0 / np.sqrt(64.0 * nT))
                    for tb in range(nT // 256):
                        bb = it % 2
                        it += 1
                        dma('sp', dc_[bb][:, 0:nchk, :], mc[:, tb * 256:(tb + 1) * 256].rearrange("(c p) n -> p c n", p=128),
                            writes=['dc%d' % bb])
                        dma('sp', ds_[bb][:, 0:nchk, :], ms[:, tb * 256:(tb + 1) * 256].rearrange("(c p) n -> p c n", p=128),
                            writes=['ds%d' % bb])
                        for fc in range(2):
                            for i in range(nchk):
                                mm(ycp[:, 0:256], FT[:, jb + i, fc * 128:(fc + 1) * 128], dc_[bb][:, i, :], i == 0, i == nchk - 1,
                                   ['FT', 'dc%d' % bb], ['ycp'])
                            for i in range(nchk):
                                mm(ysp[:, 0:256], FT[:, jb + i, fc * 128:(fc + 1) * 128], ds_[bb][:, i, :], i == 0, i == nchk - 1,
                                   ['FT', 'ds%d' % bb], ['ysp'])
                            act(YC[bb][:, fc, :], ycp[:, 0:256], AF.Copy, ['ycp'], ['YC%d' % bb])
                            cp('dve', YS[bb][:, fc, :], ysp[:, 0:256], ['ysp'], ['YS%d' % bb])
                            mm(spp[:, 0:256], blk[:, 0, :], YC[bb][:, fc, :], True, False, ['blk', 'YC%d' % bb], ['spp'])
                            mm(spp[:, 0:256], blk[:, 1, :], YS[bb][:, fc, :], False, True, ['blk', 'YS%d' % bb], ['spp'])
                            act(SPc[bb][:, fc, :], spp[:, 0:256], AF.Copy, ['spp'], ['SP%d' % bb], scale=scale)
                        for oc in range(2):
                            for fc in range(2):
                                mm(dpp[:, 0:256], wfn[:, fc, oc * 128:(oc + 1) * 128], SPc[bb][:, fc, :], fc == 0, fc == 1,
                                   ['wfn', 'SP%d' % bb], ['dpp'])
                            t0 = 128 * jb + tb * 256
                            cp('dve', CAT[:, 6 + oc, t0:t0 + 256], dpp[:, 0:256], ['dpp'], ['CATd'])

        class _Stop(Exception):
            pass

        def body():
          if dbg:
              S.op('pool', lambda e: e.memset(CAT[:], 0.0), writes=['CAT'])
              S.barrier()
          for s in range(2):
            for c in range(8):
                dma('sp', X[:, c, :], xin[s, :, c, :], writes=['X%d' % c])
            for l in range(DEPTH):
                last = (l == DEPTH - 1)
                compute_rs()
                S.barrier()
                if 'C' not in SKIP:
                    mixer_c(s, l, last)
                    S.barrier()
                if 'A' not in SKIP:
                    mixer_a(s, l, last)
                    S.barrier()
                if 'B' not in SKIP:
                    mixer_b(s, l, last)
                    S.barrier()
                if 'D' not in SKIP:
                    mixer_d(s, l, last)
                    S.barrier()
                if dbg == ('cat', s, l):
                    for c in range(8):
                        dma('pool', dbg_out[:, c, :], CAT[:, c, :], reads=['CAT'])
                    return
                S.barrier()
                for _once in ([] if 'wout' in SKIP else [0]):
                  with contextlib.ExitStack() as ph:
                    wo = sb(ph, "wo", [128, 8, D], BF16)
                    yps = [ps(ph, "yps%d" % i, [128, 512]) for i in range(2)]
                    for kc in range(8):
                        dma('pool', wo[:, kc, :], w_out[l, kc * 128:(kc + 1) * 128, :], writes=['wo'])
                    it = 0
                    nblk = 4 if last else 5
                    for b in range(nblk):
                        t0 = b * 512
                        n = 512 if b < 4 else 256
                        v = s if b < 4 else 2
                        for oc in range(8):
                            pb = it % 2
                            it += 1
                            for kc in range(8):
                                mm(yps[pb][:, :n], wo[:, kc, oc * 128:(oc + 1) * 128], CAT[:, kc, t0:t0 + n],
                                   kc == 0, kc == 7, ['wo', 'CAT'], ['yps%d' % pb])
                            stt(X[:, oc, t0:t0 + n], yps[pb][:, :n], modap(l, 2, oc, v), X[:, oc, t0:t0 + n],
                                ALU.mult, ALU.add, ['yps%d' % pb, 'modT', 'X%d' % oc], ['X%d' % oc])
                S.barrier()
                for _once in ([] if 'ffn' in SKIP else [0]):
                  with contextlib.ExitStack() as ph:
                    nblk = 4 if last else 5
                    H2 = CAT
                    for b2 in range(8 if last else 9):
                        t0 = b2 * 256
                        v = s if b2 < 8 else 2
                        make_h(l, 1, t0, 256, v, dst=('H2_%d' % (b2 // 2), H2[:, :, t0:t0 + 256]))
                    w1 = [sb(ph, "w1_%d" % i, [128, 8, 512], BF16) for i in range(2)]
                    w2 = [sb(ph, "w2_%d" % i, [128, 4, D], BF16) for i in range(2)]
                    ag = [sb(ph, "ag%d" % i, [128, 4, 512], BF16) for i in range(2)]
                    rl = [sb(ph, "rl%d" % i, [128, 512], F32) for i in range(2)]
                    fps = [ps(ph, "fps%d" % i, [128, 512]) for i in range(2)]
                    ops_ = [ps(ph, "ops%d" % i, [128, 512]) for i in range(2)]
                    cnt = {'i1': 0, 'i2': 0, 'i3': 0}

                    def ffn1(g, b):
                        wb = g % 2
                        gl = None
                        if g == 0 and b == 0:
                            gl = 0
                        if b == 1 and g + 1 < 8:
                            gl = g + 1
                        if gl is not None:
                            wl = gl % 2
                            dma('pool', w1[wl][:], w_ff1[l, :, gl * 512:(gl + 1) * 512].rearrange("(kc p) n -> p kc n", p=128),
                                writes=['w1_%d' % wl])
                            dma('pool', w2[wl][:], w_ff2[l, gl * 512:(gl + 1) * 512, :].rearrange("(kc p) n -> p kc n", p=128),
                                writes=['w2_%d' % wl])
                        t0 = b * 512
                        n = 512 if b < 4 else 256
                        ab = cnt['i1'] % 2
                        cnt['i1'] += 1
                        for fc in range(4):
                            pb = cnt['i2'] % 2
                            cnt['i2'] += 1
                            for kc in range(8):
                                mm(fps[pb][:, :n], w1[wb][:, kc, fc * 128:(fc + 1) * 128], H2[:, kc, t0:t0 + n],
                                   kc == 0, kc == 7, ['w1_%d' % wb, 'H2_%d_%d' % (b, kc)], ['fps%d' % pb])
                            act(rl[pb][:, :n], fps[pb][:, :n], AF.Relu, ['fps%d' % pb], ['rl%d' % pb])
                            tt('pool', ag[ab][:, fc, :n], rl[pb][:, :n], rl[pb][:, :n], ALU.mult,
                               ['rl%d' % pb], ['ag%d_%d' % (ab, fc)])
                        return ab

                    def ffn2(g, b, ab):
                        wb = g % 2
                        t0 = b * 512
                        n = 512 if b < 4 else 256
                        v = s if b < 4 else 2
                        for oc in range(8):
                            pb = cnt['i3'] % 2
                            cnt['i3'] += 1
                            for kc in range(4):
                                mm(ops_[pb][:, :n], w2[wb][:, kc, oc * 128:(oc + 1) * 128], ag[ab][:, kc, :n],
                                   kc == 0, kc == 3, ['w2_%d' % wb, 'ag%d_%d' % (ab, kc)], ['ops%d' % pb])
                            stt(X[:, oc, t0:t0 + n], ops_[pb][:, :n], modap(l, 5, oc, v), X[:, oc, t0:t0 + n],
                                ALU.mult, ALU.add, ['ops%d' % pb, 'modT', 'X%d' % oc], ['X%d' % oc])

                    funits = [(g, b) for g in range(8) for b in range(nblk)]
                    ab_cur = ffn1(*funits[0])
                    for ui in range(len(funits)):
                        ab_nxt = ffn1(*funits[ui + 1]) if ui + 1 < len(funits) else None
                        ffn2(funits[ui][0], funits[ui][1], ab_cur)
                        ab_cur = ab_nxt
                S.barrier()
                if dbg == ('x', s, l):
                    for c in range(8):
                        dma('sp', dbg_out[:, c, :], X[:, c, :], reads=['X%d' % c])
                    return
            with contextlib.ExitStack() as ph:
                ob = [sb(ph, "ob%d" % i, [128, 256], F32) for i in range(2)]
                it = 0
                for b in range(8):
                    t0 = b * 256
                    pk, pst = ssqrot.next()
                    for c in range(8):
                        sk, sq = sqrot.next()
                        act(sq[:], X[:, c, t0:t0 + 256], AF.Square, ['X%d' % c], [sk])
                        mm(pst[:, 0:256], onesb[:], sq[:], c == 0, c == 7, [sk, 'onesb'], [pk])
                    rk, rs = rsrot.next()
                    act(rs[:], pst[:, 0:256], AF.Sqrt, [pk, 'epsb'], [rk], scale=1.0 / D, bias=epsb[:, 0:1])
                    S.op('dve', lambda e, rs=rs: e.reciprocal(out=rs[:], in_=rs[:]), reads=[rk], writes=[rk])
                    for c in range(8):
                        o = it % 2
                        it += 1
                        stt(ob[o][:], X[:, c, t0:t0 + 256], gv[:, 4, c:c + 1], rs[:], ALU.mult, ALU.mult,
                            ['X%d' % c, 'gv', rk], ['ob%d' % o])
                        dma('sp', outT[s, :, c, t0:t0 + 256], ob[o][:], reads=['ob%d' % o])
            S.barrier()
        body()
        S.finish('sp')
        print("ops", S.nops, "waits", S.nwaits, {e: S.cc[e] for e in S.cc}, {e: S.dc[e] for e in S.dc})
    return nc


_NC_CACHE = {}


def _prep_shared(inp):
    f32 = lambda a: np.ascontiguousarray(np.asarray(a, dtype=np.float32))
    sh = dict(_consts())
    sh['w_ada'] = f32(inp['w_ada'])
    sh['badaT'] = np.ascontiguousarray(np.moveaxis(f32(inp['b_ada']).reshape(2, 48, 128), 2, 0))
    gl = [inp['g_norm_mix'][0], inp['g_norm_ffn'][0], inp['g_norm_mix'][1], inp['g_norm_ffn'][1], inp['g_final']]
    sh['gvec'] = np.ascontiguousarray(np.stack([f32(g).reshape(8, 128).T for g in gl], 1))
    sh['w_in'] = f32(inp['w_in'])
    bgate = f32(inp['b_gate'])
    sh['bg'] = np.ascontiguousarray(bgate.reshape(2, 4, 4).transpose(2, 0, 1))
    wc = f32(inp['w_conv_qk'])
    sh['wconv'] = np.ascontiguousarray(wc.reshape(2, 3, 4, 128).transpose(3, 0, 2, 1))
    rpb = f32(inp['rpb'])
    sh['natb'] = np.ascontiguousarray(np.stack([_nat_bias(rpb[l]).reshape(5, 2, 128, 1280) for l in range(2)], 0))
    sh['wsT'] = np.ascontiguousarray(f32(inp['w_spatial']).transpose(0, 3, 1, 2))
    bs = f32(inp['b_spatial'])
    bsT = np.zeros((2, 128, 2, 128), np.float32)
    for pr in range(2):
        for hh in range(2):
            bsT[:, hh * 64:(hh + 1) * 64, pr, :] = bs[:, 2 * pr + hh, None, :]
    sh['bsT'] = bsT
    sh['ggm'] = f32(inp['g_gmlp'])
    sh['gml'] = f32(inp['g_mlstm'])
    sh['w_fnet'] = f32(inp['w_fnet'])
    sh['w_out'] = f32(inp['w_out'])
    sh['w_ff1'] = f32(inp['w_ff1'])
    sh['w_ff2'] = f32(inp['w_ff2'])
    return sh


def _fmT(a):
    return np.ascontiguousarray(a.T.reshape(8, 128, a.shape[0]).transpose(1, 0, 2))


def kernel(dbg=None, **inp):
    x = np.asarray(inp['x'], np.float32)
    c = np.asarray(inp['c'], np.float32)
    ctx = np.asarray(inp['ctx'], np.float32)
    c_ctx = np.asarray(inp['c_ctx'], np.float32)
    sh = _prep_shared(inp)
    key = repr(dbg)
    if key not in _NC_CACHE:
        _NC_CACHE[key] = build_nc(dbg)
    nc = _NC_CACHE[key]
    in_maps = []
    ncores = int(os.environ.get('MK_CORES', '8'))
    for core in range(ncores):
        m = dict(sh)
        xs = []
        for i in range(2):
            b = 2 * core + i
            xs.append(_fmT(np.concatenate([x[b], ctx[b]], 0)))
        m['xin'] = np.ascontiguousarray(np.stack(xs, 0))
        vecs = [c[2 * core], c[2 * core + 1], c_ctx]
        m['cT'] = np.ascontiguousarray(np.stack([v.reshape(8, 128).T for v in vecs], 2))
        in_maps.append(m)
    res = run_bass_kernel_spmd(nc, in_maps, core_ids=list(range(ncores)))
    out = np.zeros((16, TL, D), np.float32)
    for core in range(ncores):
        o = res.results[core]['outT']
        for i in range(2):
            out[2 * core + i] = o[i].transpose(2, 1, 0).reshape(TL, D)
    if dbg:
        return out, [res.results[core]['dbg'] for core in range(ncores)]
    return out
```

```python
import contextlib
import numpy as np
import ml_dtypes
import concourse.bass as bass
import concourse.mybir as mybir
from concourse.bass_utils import run_bass_kernel_spmd

F32 = mybir.dt.float32
BF16 = mybir.dt.bfloat16
AF = mybir.ActivationFunctionType
ALU = mybir.AluOpType
AX = mybir.AxisListType

D = 1024
T = 2304
TL = 2048
NCH = 18
DIN = 2576
DEPTH = 2
EPS = 1e-6
NEG = -30000.0
import os
SKIP = set(os.environ.get('MK_SKIP', '').split(','))
CSTOP = int(os.environ.get('MK_CSTOP', '9'))
ASTOP = int(os.environ.get('MK_ASTOP', '9'))
OFF_A, OFF_B, OFF_C, OFF_D, OFF_G = 0, 768, 1280, 2304, 2560


class Sch:
    def __init__(self, nc, stack, ndma=8, same_engine_sync=True):
        self.nc = nc
        self.E = {'pe': nc.tensor, 'act': nc.scalar, 'dve': nc.vector,
                  'pool': nc.gpsimd, 'sp': nc.sync}
        self.R = ndma
        self.same = same_engine_sync
        self.csem = {}
        self.dsem = {}
        for e in self.E:
            self.csem[e] = stack.enter_context(nc.semaphore('c_' + e))
            self.dsem[e] = [stack.enter_context(nc.semaphore('d_%s_%d' % (e, i)))
                            for i in range(ndma)]
        self.cc = {e: 0 for e in self.E}
        self.dc = {e: 0 for e in self.E}
        self.waited = {e: {} for e in self.E}
        self.lw = {}
        self.rd = {}
        self.bar = set()
        self.nops = 0
        self.nwaits = 0

    def _tok_sem(self, tok):
        kind, e, i = tok
        if kind == 'c':
            return (kind, e, 0), self.csem[e], i
        return (kind, e, i % self.R), self.dsem[e][i % self.R], 16 * (i // self.R + 1)

    def _wait(self, eng, tok):
        kind, e, i = tok
        if kind == 'c' and e == eng and (eng == 'pe' or not self.same):
            return
        key, sem, val = self._tok_sem(tok)
        if self.waited[eng].get(key, 0) >= val:
            return
        self.waited[eng][key] = val
        self.E[eng].wait_ge(sem, val)
        self.nwaits += 1

    def op(self, eng, fn, reads=(), writes=(), dma=False):
        deps = set(self.bar)
        for r in reads:
            if r in self.lw:
                deps.add(self.lw[r])
        for w in writes:
            if w in self.lw:
                lt = self.lw[w]
                if not (lt[0] == 'c' and lt[1] == eng and not dma):
                    deps.add(lt)
            for t in self.rd.get(w, ()):
                deps.add(t)
        if dma:
            k = self.dc[eng]
            self.dc[eng] += 1
            tok = ('d', eng, k)
            if k >= self.R:
                deps.add(('d', eng, k - self.R))
        else:
            self.cc[eng] += 1
            tok = ('c', eng, self.cc[eng])
        best = {}
        for t in deps:
            key, sem, val = self._tok_sem(t)
            if key not in best or best[key][1] < val:
                best[key] = (t, val)
        for key in sorted(best, key=str):
            self._wait(eng, best[key][0])
        inst = fn(self.E[eng])
        _, sem, _ = self._tok_sem(tok)
        inst.then_inc(sem, 16 if dma else 1)
        for w in writes:
            self.lw[w] = tok
            self.rd[w] = []
        for r in reads:
            self.rd.setdefault(r, []).append(tok)
        self.nops += 1
        return tok

    def barrier(self):
        self.bar = set()
        for e in self.E:
            if self.cc[e] > 0:
                self.bar.add(('c', e, self.cc[e]))
            for j in range(max(0, self.dc[e] - self.R), self.dc[e]):
                self.bar.add(('d', e, j))
        self.lw = {}
        self.rd = {}

    def finish(self, eng='sp'):
        self.barrier()
        best = {}
        for t in self.bar:
            key, sem, val = self._tok_sem(t)
            if key not in best or best[key][1] < val:
                best[key] = (t, val)
        for key in sorted(best, key=str):
            self._wait(eng, best[key][0])


class Rot:
    def __init__(self, items):
        self.items = items
        self.i = 0

    def next(self):
        it = self.items[self.i % len(self.items)]
        self.i += 1
        return it


def _bf(a):
    return np.ascontiguousarray(a.astype(ml_dtypes.bfloat16))


def _consts():
    c = {}
    c['ident'] = np.eye(128, dtype=np.float32)
    c['jrev'] = np.ascontiguousarray(np.eye(128, dtype=np.float32)[::-1])
    s = np.arange(128)
    mf = (s[:, None] <= s[None, :]).astype(np.float32)
    mb = (s[:, None] >= s[None, :]).astype(np.float32)
    c['masks'] = _bf(np.stack([mf, mb], 1))
    t = np.arange(TL)
    rows = (t // 64).astype(np.float32)
    cols = (t % 64).astype(np.float32)
    p = np.arange(128)
    d = p % 64
    half = d // 32
    i = (d % 16).astype(np.float32)
    inv = (10000.0 ** (-i / 16.0)).astype(np.float32)
    pos = np.where(half[:, None] == 0, rows[None, :], cols[None, :]).astype(np.float32)
    ang = pos * inv[:, None]
    c['rope'] = _bf(np.stack([np.cos(ang), np.sin(ang)], 1))
    rm = np.zeros((128, 128), np.float32)
    for m in range(128):
        dd = m % 32
        if dd < 16:
            rm[m + 16, m] = -1.0
        else:
            rm[m - 16, m] = 1.0
    c['rm'] = _bf(rm)
    for n, name in ((2048, 'l'), (256, 'c')):
        k = np.arange(n, dtype=np.float64)
        ang = 2.0 * np.pi * np.outer(k, k) / n
        def _tile(m):
            return np.ascontiguousarray(m.reshape(n // 128, 128, n // 256, 256).transpose(2, 1, 0, 3))
        c['dftc_' + name] = _bf(_tile(np.cos(ang)))
        c['dfts_' + name] = _bf(_tile(-np.sin(ang)))
    k = np.arange(64, dtype=np.float64)
    ang = 2.0 * np.pi * np.outer(k, k) / 64
    bc = np.zeros((128, 128)); bs = np.zeros((128, 128))
    for g in range(2):
        bc[g * 64:(g + 1) * 64, g * 64:(g + 1) * 64] = np.cos(ang)
        bs[g * 64:(g + 1) * 64, g * 64:(g + 1) * 64] = np.sin(ang)
    c['blk'] = _bf(np.stack([bc, bs], 1))
    sel = np.zeros((4, 2, 128), np.float32)
    for pr in range(2):
        sel[2 * pr, pr, :64] = 1.0
        sel[2 * pr + 1, pr, 64:] = 1.0
    c['sel'] = sel
    return c


def _nat_bias(rpb_l):
    out = np.full((5, 2, 128, 5, 2, 128), NEG, np.float32)
    tsel = [0, 1, 2, 14, 15]
    kk = np.arange(128)
    qq = np.arange(128)
    for ti, t in enumerate(tsel):
        cb = min(max(t - 2, 0), 11)
        for j in range(5):
            kr = (cb + j) * 2 + kk // 64
            kc = kk % 64
            r = 2 * t + qq // 64
            qc = qq % 64
            rs = np.clip(r - 4, 0, 24)
            row_ok = (kr[:, None] >= rs[None, :]) & (kr[:, None] < rs[None, :] + 8)
            qs = np.clip(qc - 8, 0, 48)
            col_ok = (kc[:, None] >= qs[None, :]) & (kc[:, None] < qs[None, :] + 16)
            ok = row_ok & col_ok
            drow = np.clip(kr[:, None] - r[None, :] + 7, 0, 14)
            dcol = np.clip(kc[:, None] - qc[None, :] + 15, 0, 30)
            for h in range(4):
                val = rpb_l[h][drow, dcol]
                out[ti, h // 2, :, j, h % 2, :] = np.where(ok, val, NEG)
    return out


def _fm(vec):
    v = vec.reshape(vec.shape[:-1] + (8, 128))
    return np.ascontiguousarray(np.moveaxis(v, -1, 0))


def build_nc(dbg=None):
    nc = bass.Bass("TRN2", target_bir_lowering=False)

    def din(name, shape, dt=F32):
        return nc.dram_tensor(name, list(shape), dt, kind="ExternalInput").ap()

    xin = din("xin", [2, 128, 8, T])
    cT = din("cT", [128, 8, 3])
    w_ada = din("w_ada", [2, D, 6 * D])
    badaT = din("badaT", [128, 2, 48])
    gvec = din("gvec", [128, 5, 8])
    w_in = din("w_in", [2, D, DIN])
    bg = din("bg", [4, 2, 4])
    wconv = din("wconv", [128, 2, 4, 3])
    natb = din("natb", [2, 5, 2, 128, 1280])
    wsT = din("wsT", [2, 128, 4, 128])
    bsT = din("bsT", [2, 128, 2, 128])
    ggm = din("ggm", [2, 256])
    gml = din("gml", [2, 256])
    w_fnet = din("w_fnet", [2, 256, 256])
    w_out = din("w_out", [2, D, D])
    w_ff1 = din("w_ff1", [2, D, 4 * D])
    w_ff2 = din("w_ff2", [2, 4 * D, D])
    c_ident = din("ident", [128, 128])
    c_jrev = din("jrev", [128, 128])
    c_masks = din("masks", [128, 2, 128], BF16)
    c_rope = din("rope", [128, 2, TL], BF16)
    c_rm = din("rm", [128, 128], BF16)
    c_dftc_l = din("dftc_l", [8, 128, 16, 256], BF16)
    c_dfts_l = din("dfts_l", [8, 128, 16, 256], BF16)
    c_dftc_c = din("dftc_c", [1, 128, 2, 256], BF16)
    c_dfts_c = din("dfts_c", [1, 128, 2, 256], BF16)
    c_blk = din("blk", [128, 2, 128], BF16)
    c_sel = din("sel", [4, 2, 128])
    outT = nc.dram_tensor("outT", [2, 128, 8, TL], F32, kind="ExternalOutput").ap()
    dbg_out = None
    if dbg:
        dbg_out = nc.dram_tensor("dbg", [128, 8, T], F32, kind="ExternalOutput").ap()

    with contextlib.ExitStack() as st:
        S = Sch(nc, st)

        uid = [0]

        def sb(stack, name, shape, dt):
            uid[0] += 1
            return stack.enter_context(nc.sbuf_tensor("s%d_%s" % (uid[0], name), list(shape), dt))

        def ps(stack, name, shape, dt=F32):
            uid[0] += 1
            return stack.enter_context(nc.psum_tensor("p%d_%s" % (uid[0], name), list(shape), dt))

        def dma(eng, out, in_, reads=(), writes=()):
            S.op(eng, lambda e: e.dma_start(out=out, in_=in_), reads=reads, writes=writes, dma=True)

        def mm(out, lhsT, rhs, start, stop, reads, writes):
            S.op('pe', lambda e: e.matmul(out, lhsT=lhsT, rhs=rhs, start=start, stop=stop),
                 reads=reads, writes=writes)

        def act(out, in_, func, reads, writes, scale=1.0, bias=None):
            if bias is None:
                S.op('act', lambda e: e.activation(out=out, in_=in_, func=func, scale=scale),
                     reads=reads, writes=writes)
            else:
                S.op('act', lambda e: e.activation(out=out, in_=in_, func=func, scale=scale, bias=bias),
                     reads=reads, writes=writes)

        def tt(eng, out, in0, in1, op, reads, writes):
            S.op(eng, lambda e: e.tensor_tensor(out=out, in0=in0, in1=in1, op=op), reads=reads, writes=writes)

        def stt(out, in0, scalar, in1, op0, op1, reads, writes):
            S.op('dve', lambda e: e.scalar_tensor_tensor(out=out, in0=in0, scalar=scalar, in1=in1, op0=op0, op1=op1),
                 reads=reads, writes=writes)

        def ts(eng, out, in0, s1, s2, op0, op1, reads, writes):
            if s2 is None:
                S.op(eng, lambda e: e.tensor_scalar(out=out, in0=in0, scalar1=s1, scalar2=None, op0=op0),
                     reads=reads, writes=writes)
            else:
                S.op(eng, lambda e: e.tensor_scalar(out=out, in0=in0, scalar1=s1, scalar2=s2, op0=op0, op1=op1),
                     reads=reads, writes=writes)

        def cp(eng, out, in_, reads, writes):
            S.op(eng, lambda e: e.tensor_copy(out=out, in_=in_), reads=reads, writes=writes)

        X = sb(st, "X", [128, 8, T], F32)
        CAT = sb(st, "CAT", [128, 8, T], BF16)
        ident = sb(st, "ident", [128, 128], F32)
        jrev = sb(st, "jrev", [128, 128], F32)
        identb = sb(st, "identb", [128, 128], BF16)
        masks = sb(st, "masks", [128, 2, 128], BF16)
        rm = sb(st, "rm", [128, 128], BF16)
        blk = sb(st, "blk", [128, 2, 128], BF16)
        sel = sb(st, "sel", [4, 2, 128], F32)
        onesb = sb(st, "onesb", [128, 128], BF16)
        onesf = sb(st, "onesf", [128, 2], F32)
        RS = sb(st, "RS", [128, T], F32)
        modT = sb(st, "modT", [128, 2, 48, 3], F32)
        gmT = sb(st, "gmT", [128, 2, 2, 8, 3], F32)
        gv = sb(st, "gv", [128, 5, 8], F32)
        bada = sb(st, "bada", [128, 2, 48], F32)
        csb = sb(st, "csb", [128, 8, 3], F32)
        scb = sb(st, "scb", [128, 8, 3], BF16)
        bgs = sb(st, "bgs", [4, 2, 4], F32)
        wcv = sb(st, "wcv", [128, 2, 4, 3], F32)

        dma('sp', ident[:], c_ident, writes=['ident'])
        dma('sp', jrev[:], c_jrev, writes=['jrev'])
        dma('sp', masks[:], c_masks, writes=['masks'])
        dma('sp', rm[:], c_rm, writes=['rm'])
        dma('sp', blk[:], c_blk, writes=['blk'])
        dma('sp', sel[:], c_sel, writes=['sel'])
        dma('sp', gv[:], gvec, writes=['gv'])
        dma('sp', bada[:], badaT, writes=['bada'])
        dma('sp', csb[:], cT, writes=['csb'])
        dma('sp', bgs[:], bg, writes=['bgs'])
        dma('sp', wcv[:], wconv, writes=['wcv'])
        S.op('dve', lambda e: e.memset(onesb[:], 1.0), writes=['onesb'])
        S.op('dve', lambda e: e.memset(onesf[:], 1.0), writes=['onesf'])
        cp('dve', identb[:], ident[:], ['ident'], ['identb'])
        act(scb[:], csb[:], AF.Silu, ['csb'], ['scb'])

        with contextlib.ExitStack() as ph:
            wa = [sb(ph, "wa%d" % i, [128, 8, 512], BF16) for i in range(2)]
            mps = [ps(ph, "mps%d" % i, [128, 4, 3]) for i in range(2)]
            it = 0
            for l in range(DEPTH):
                for pc in range(12):
                    b = it % 2
                    it += 1
                    dma('pool', wa[b][:], w_ada[l, :, pc * 512:(pc + 1) * 512].rearrange("(kc p) n -> p kc n", p=128),
                        writes=['wa%d' % b])
                    for oc in range(4):
                        for kc in range(8):
                            mm(mps[b][:, oc, :], wa[b][:, kc, oc * 128:(oc + 1) * 128], scb[:, kc, :],
                               kc == 0, kc == 7, ['wa%d' % b, 'scb'], ['mps%d' % b])
                    tt('dve', modT[:, l, pc * 4:(pc + 1) * 4, :], mps[b][:],
                       bada[:, l, pc * 4:(pc + 1) * 4].unsqueeze(2).broadcast_to([128, 4, 3]), ALU.add,
                       ['mps%d' % b, 'bada'], ['modT'])
            for l in range(DEPTH):
                for kind in range(2):
                    sc_j = 1 if kind == 0 else 4
                    for v in range(3):
                        stt(gmT[:, l, kind, :, v], modT[:, l, sc_j * 8:(sc_j + 1) * 8, v], 1.0,
                            gv[:, 2 * l + kind, :], ALU.add, ALU.mult, ['modT', 'gv'], ['gmT'])
        S.barrier()

        def modap(l, j, c, v):
            return modT[:, l, j * 8 + c, v:v + 1]

        hbuf = [sb(st, "hT%d" % i, [128, 8, 256], BF16) for i in range(2)]
        hrot = Rot([("hT%d" % i, hbuf[i]) for i in range(2)])
        sqb = [sb(st, "sq%d" % i, [128, 256], BF16) for i in range(2)]
        sqrot = Rot([("sq%d" % i, sqb[i]) for i in range(2)])
        rsb = [sb(st, "rs%d" % i, [128, 256], F32) for i in range(2)]
        rsrot = Rot([("rs%d" % i, rsb[i]) for i in range(2)])
        tmb = [sb(st, "tm%d" % i, [128, 256], F32) for i in range(2)]
        tmrot = Rot([("tm%d" % i, tmb[i]) for i in range(2)])
        ssq = [ps(st, "ssq%d" % i, [128, 512]) for i in range(1)]
        ssqrot = Rot([("ssq%d" % i, ssq[i]) for i in range(1)])

        def make_h(l, kind, t0, n, v, dst=None):
            if dst is None:
                hk, hT = hrot.next()
            else:
                hk, hT = dst
            if kind == 0:
                return _mk_tail(l, kind, t0, n, v, hk, hT, 'RS', RS[:, t0:t0 + n])
            pk, pst = ssqrot.next()
            for c in range(8):
                sk, sq = sqrot.next()
                act(sq[:, :n], X[:, c, t0:t0 + n], AF.Square, ['X%d' % c], [sk])
                mm(pst[:, :n], onesb[:], sq[:, :n], c == 0, c == 7, [sk, 'onesb'], [pk])
            rk, rs = rsrot.next()
            act(rs[:, :n], pst[:, :n], AF.Sqrt, [pk, 'epsb'], [rk], scale=1.0 / D, bias=epsb[:, 0:1])
            S.op('dve', lambda e: e.reciprocal(out=rs[:, :n], in_=rs[:, :n]), reads=[rk], writes=[rk])
            return _mk_tail(l, kind, t0, n, v, hk, hT, rk, rs)

        def _mk_tail(l, kind, t0, n, v, hk, hT, rk, rs):
            sh_j = 0 if kind == 0 else 3
            for c in range(8):
                stt(hT[:, c, :n], X[:, c, t0:t0 + n], gmT[:, l, kind, c, v:v + 1], rs[:, :n], ALU.mult, ALU.mult,
                    ['X%d' % c, 'gmT', rk], [hk + '_%d' % c])
            for c in range(8):
                act(hT[:, c, :n], hT[:, c, :n], AF.Identity, [hk + '_%d' % c, 'modT'], [hk + '_%d' % c],
                    bias=modap(l, sh_j, c, v))
            return hk, hT

        def hblocks(l, nb, s):
            nxt = make_h(l, 0, 0, 256, s)
            for b in range(nb):
                cur = nxt
                if b + 1 < nb:
                    nxt = make_h(l, 0, 256 * (b + 1), 256, s if b + 1 < 8 else 2)
                yield b, 256 * b, (s if b < 8 else 2), cur[0], cur[1]

        def compute_rs():
            for b in range(9):
                t0 = 256 * b
                n = 256
                pk, pst = ssqrot.next()
                for c in range(8):
                    sk, sq = sqrot.next()
                    act(sq[:, :n], X[:, c, t0:t0 + n], AF.Square, ['X%d' % c], [sk])
                    mm(pst[:, :n], onesb[:], sq[:, :n], c == 0, c == 7, [sk, 'onesb'], [pk])
                act(RS[:, t0:t0 + n], pst[:, :n], AF.Sqrt, [pk, 'epsb'], ['RS'], scale=1.0 / D, bias=epsb[:, 0:1])
                S.op('dve', lambda e, t0=t0, n=n: e.reciprocal(out=RS[:, t0:t0 + n], in_=RS[:, t0:t0 + n]),
                     reads=['RS'], writes=['RS'])

        epsb = sb(st, "epsb", [128, 1], F32)
        S.op('dve', lambda e: e.memset(epsb[:], EPS), writes=['epsb'])

        def proj_fm(hT, hk, n, w, wk, col0, ppool, evac):
            pk, pt = ppool.next()
            for kc in range(8):
                mm(pt[:, :n], w[:, kc, col0:col0 + 128], hT[:, kc, :n], kc == 0, kc == 7, [wk, hk + '_%d' % kc], [pk])
            evac(pt[:, :n], pk)

        def proj_tm(hT, hk, sub, w, wk, col0, ncol, ppool, evac):
            pk, pt = ppool.next()
            for kc in range(8):
                mm(pt[:, :ncol], hT[:, kc, sub * 128:(sub + 1) * 128], w[:, kc, col0:col0 + ncol],
                   kc == 0, kc == 7, [wk, hk + '_%d' % kc], [pk])
            evac(pt[:, :ncol], pk)

        def wload(wt, key, l, col0, ncol):
            dma('pool', wt, w_in[l, :, col0:col0 + ncol].rearrange("(kc p) n -> p kc n", p=128), writes=[key])

        ORD = [[16, 17] + list(range(16)), [17, 16] + list(range(15, -1, -1))]

        def mixer_c(s, l, last):
            with contextlib.ExitStack() as ph:
                cf03 = CAT[:, 0:4, :].rearrange("p c t -> p (c t)")
                vaug = cf03[:, 0:4752].rearrange("p (j h d) -> p j h d", j=18, h=4)[:, :, :, 0:65]
                wqk = cf03[:, 4752:4752 + 4096].rearrange("p (k n) -> p k n", k=8)
                ktm = CAT[:, 6:8, :].rearrange("p c t -> p (c t)").rearrange("p (j f) -> p j f", f=256)
                R1 = sb(ph, "R1", [128, 2 * T], F32)
                R2 = sb(ph, "R2", [128, 4, T], BF16)
                RAW = R1[:].bitcast(BF16).rearrange("p (c t) -> p c t", c=4)
                QK = R2
                wv = sb(ph, "wv", [128, 8, 256], BF16)
                wg = sb(ph, "wg", [128, 8, 16], BF16)
                gtm = sb(ph, "gtm", [128, 18, 16], F32)
                gmb = sb(ph, "gmb", [128, 256], F32)
                dma('sp', gmb[:], gml[l, :].partition_broadcast(128), writes=['gmb'])
                wload(wqk, 'wqk', l, OFF_C, 512)
                wload(wv[:], 'wv', l, OFF_C + 512, 256)
                wload(wg[:], 'wg', l, OFF_G, 16)
                S.op('pool', lambda e: e.memset(vaug[:, :, :, 64:65], 1.0), writes=['vaug1'])
                with contextlib.ExitStack() as p1:
                    pq = [ps(p1, "pq%d" % i, [128, 512]) for i in range(2)]
                    pqr = Rot([("pq%d" % i, pq[i]) for i in range(2)])
                    pvv = [ps(p1, "pvv%d" % i, [128, 512]) for i in range(2)]
                    pvr = Rot([("pvv%d" % i, pvv[i]) for i in range(2)])
                    rope = sb(p1, "rope", [128, 2, TL], BF16)
                    dma('sp', rope[:], c_rope, writes=['rope'])
                    for b, t0, v, hk, hT in hblocks(l, 9, s):
                        for oc in range(4):
                            proj_fm(hT, hk, 256, wqk, 'wqk', oc * 128, pqr,
                                    lambda p_, pk, oc=oc: act(RAW[:, oc, t0:t0 + 256], p_, AF.Copy, [pk], ['RAW%d' % oc]))
                        for sub in range(2):
                            j = 2 * b + sub
                            proj_tm(hT, hk, sub, wv, 'wv', 0, 256, pvr,
                                    lambda p_, pk, j=j: cp('dve', vaug[:, j, :, 0:64], p_.rearrange("p (h d) -> p h d", h=4),
                                                           [pk], ['vaug%d' % j]))
                            proj_tm(hT, hk, sub, wg, 'wg', 0, 16, pvr,
                                    lambda p_, pk, j=j: cp('dve', gtm[:, j, :], p_, [pk], ['gtm']))
                    if CSTOP == 1:
                        return
                    cvt = [sb(p1, "cvt%d" % i, [128, 512], F32) for i in range(2)]
                    cvr = Rot([("cvt%d" % i, cvt[i]) for i in range(2)])
                    cst = [sb(p1, "cst%d" % i, [128, 512], BF16) for i in range(2)]
                    csr = Rot([("cst%d" % i, cst[i]) for i in range(2)])
                    r2t = [sb(p1, "r2t%d" % i, [128, 512], F32) for i in range(2)]
                    r2r = Rot([("r2t%d" % i, r2t[i]) for i in range(2)])
                    r3t = [sb(p1, "r3t%d" % i, [128, 512], F32) for i in range(2)]
                    r3r = Rot([("r3t%d" % i, r3t[i]) for i in range(2)])
                    for oc in range(4):
                        scl = 0.125 if oc >= 2 else 1.0
                        for (g0, g1) in ((0, TL), (TL, T)):
                            for t0 in range(g0, g1, 512):
                                n = min(512, g1 - t0)
                                ck, ct = cvr.next()
                                ts('dve', ct[:, :n], RAW[:, oc, t0:t0 + n], wcv[:, l, oc, 1:2], None, ALU.mult, None,
                                   ['RAW%d' % oc, 'wcv'], [ck])
                                a = 1 if t0 == g0 else 0
                                stt(ct[:, a:n], RAW[:, oc, t0 + a - 1:t0 + n - 1], wcv[:, l, oc, 0:1], ct[:, a:n],
                                    ALU.mult, ALU.add, ['RAW%d' % oc, 'wcv', ck], [ck])
                                bnd = n - 1 if t0 + n == g1 else n
                                stt(ct[:, 0:bnd], RAW[:, oc, t0 + 1:t0 + 1 + bnd], wcv[:, l, oc, 2:3], ct[:, 0:bnd],
                                    ALU.mult, ALU.add, ['RAW%d' % oc, 'wcv', ck], [ck])
                                if g0 == 0:
                                    sk, cs_ = csr.next()
                                    act(cs_[:, :n], ct[:, :n], AF.Silu, [ck], [sk])
                                    pk, pt = pqr.next()
                                    mm(pt[:, :n], rm[:], cs_[:, :n], True, True, ['rm', sk], [pk])
                                    k2, t2 = r2r.next()
                                    stt(t2[:, :n], pt[:, :n], scl, rope[:, 1, t0:t0 + n], ALU.mult, ALU.mult, [pk, 'rope'], [k2])
                                    k3, t3 = r3r.next()
                                    stt(t3[:, :n], cs_[:, :n], scl, rope[:, 0, t0:t0 + n], ALU.mult, ALU.mult, [sk, 'rope'], [k3])
                                    tt('pool', QK[:, oc, t0:t0 + n], t2[:, :n], t3[:, :n], ALU.add, [k2, k3], ['QK%d' % oc])
                                else:
                                    act(QK[:, oc, t0:t0 + n], ct[:, :n], AF.Silu, [ck], ['QK%d' % oc], scale=1.0)
                                    if scl != 1.0:
                                        ts('dve', QK[:, oc, t0:t0 + n], QK[:, oc, t0:t0 + n], scl, None, ALU.mult, None,
                                           ['QK%d' % oc], ['QK%d' % oc])
                    for j in range(18):
                        pk, pt = pqr.next()
                        for kc in range(2):
                            mm(pt[:, kc * 128:(kc + 1) * 128], QK[:, 2 + kc, 128 * j:128 * j + 128], identb[:], True, True,
                               ['QK%d' % (2 + kc), 'identb'], [pk])
                        cp('dve', ktm[:, j, :], pt[:, 0:256], [pk], ['ktm%d' % j])
                S.barrier()
                if CSTOP == 2:
                    return
                R3 = sb(ph, "R3", [128, 18 * 256], F32)
                hsum = R3[:].rearrange("p (j f) -> p j f", f=256)
                colq = [sb(ph, "colq%d" % i, [128, 18, 12], F32) for i in range(2)]
                dcol = [sb(ph, "dcol%d" % i, [128, 2, 18], F32) for i in range(2)]
                with contextlib.ExitStack() as p3:
                    R1f = R1
                    rI = R1f[0:4, 0:T]
                    rF = R1f[0:4, T:2 * T]
                    rG = R3[0:4, 0:T]
                    rA = R3[0:4, T:2 * T]
                    rrow = sb(p3, "rrow", [4, 20], F32)
                    drow = sb(p3, "drow", [4, 18], F32)
                    colraw = sb(p3, "colraw", [128, 18, 12], F32)
                    prw = [ps(p3, "prw%d" % i, [128, 512]) for i in range(2)]
                    pcol = ps(p3, "pcol", [128, 18, 12])
                    pcol2 = ps(p3, "pcol2", [128, 18, 12])
                    pd = ps(p3, "pd", [128, 2, 18])
                    for dr in range(2):
                        tr_m = ident if dr == 0 else jrev
                        trk = 'ident' if dr == 0 else 'jrev'
                        for g in range(5):
                            idxs = list(range(4 * g, min(4 * g + 4, 18)))
                            for qi, (dst, pw) in enumerate(((rI, prw[0]), (rF, prw[1]))):
                                for ii, idx in enumerate(idxs):
                                    j = ORD[dr][idx]
                                    c0 = dr * 8 + qi * 4
                                    mm(pw[0:4, ii * 128:(ii + 1) * 128], gtm[:, j, c0:c0 + 4], tr_m[:], True, True,
                                       ['gtm', trk], ['prw%d' % qi])
                                n = 128 * len(idxs)
                                ts('dve', dst[:, 512 * g:512 * g + n], pw[0:4, 0:n], bgs[:, l, dr * 2 + qi:dr * 2 + qi + 1], None,
                                   ALU.add, None, ['prw%d' % qi, 'bgs'], ['row%d' % qi])
                        act(rF, rF, AF.Exp, ['row1'], ['row1'], scale=-1.0)
                        act(rF, rF, AF.Ln, ['row1'], ['row1'], bias=onesf[0:4, 0:1])
                        S.op('dve', lambda e: e.tensor_tensor_scan(out=rG, data0=onesf[0:4, 0:1].broadcast_to([4, T]), data1=rF,
                                                                   initial=0.0, op0=ALU.mult, op1=ALU.add),
                             reads=['row1', 'onesf'], writes=['row2'])
                        tt('dve', rI, rI, rG, ALU.add, ['row0', 'row2'], ['row0'])
                        S.op('dve', lambda e: e.tensor_tensor_scan(out=rA, data0=onesf[0:4, 0:1].broadcast_to([4, T]), data1=rI,
                                                                   initial=0.0, op0=ALU.mult, op1=ALU.max),
                             reads=['row0', 'onesf'], writes=['row3'])
                        S.op('dve', lambda e: e.memset(rrow[:, 0:1], 0.0), writes=['rrow'])
                        cp('dve', rrow[:, 1:19], rA.rearrange("p (j t) -> p j t", t=128)[:, :, 127], ['row3'], ['rrow'])
                        rfull = rrow[:, 0:18].unsqueeze(2).broadcast_to([4, 18, 128])
                        v3 = lambda r_: r_.rearrange("p (j t) -> p j t", t=128)
                        tt('dve', v3(rI), v3(rI), rfull, ALU.subtract, ['row0', 'rrow'], ['row0'])
                        act(rI, rI, AF.Exp, ['row0'], ['row0'])
                        tt('dve', rG, rG, rA, ALU.subtract, ['row2', 'row3'], ['row2'])
                        act(rG, rG, AF.Exp, ['row2'], ['row2'])
                        tt('dve', v3(rA), rfull, v3(rA), ALU.subtract, ['row3', 'rrow'], ['row3'])
                        act(rA, rA, AF.Exp, ['row3'], ['row3'])
                        tt('dve', drow[:, 0:18], rrow[:, 0:18], rrow[:, 1:19], ALU.subtract, ['rrow'], ['drow'])
                        act(drow[:], drow[:], AF.Exp, ['drow'], ['drow'])
                        for idx in range(18):
                            for qi, rw in enumerate((rI, rA, rG)):
                                mm(pcol[:, idx, qi * 4:(qi + 1) * 4], rw[:, idx * 128:(idx + 1) * 128], ident[0:4, 0:4], True, True,
                                   ['row0', 'row2', 'row3', 'ident'], ['pcol'])
                        if dr == 0:
                            cp('dve', colq[0][:], pcol[:], ['pcol'], ['colq0'])
                        else:
                            cp('dve', colraw[:], pcol[:], ['pcol'], ['colraw'])
                            mm(pcol2[:].rearrange("p a b -> p (a b)"), jrev[:], colraw[:].rearrange("p a b -> p (a b)"), True, True,
                               ['colraw', 'jrev'], ['pcol2'])
                            cp('dve', colq[1][:], pcol2[:], ['pcol2'], ['colq1'])
                        for pr in range(2):
                            mm(pd[:, pr, :], sel[:, pr, :], drow[:], True, True, ['sel', 'drow'], ['pd'])
                        cp('dve', dcol[dr][:], pd[:], ['pd'], ['dcol%d' % dr])
                S.barrier()
                if CSTOP == 3:
                    return
                with contextlib.ExitStack() as p4:
                    qz4_ = [sb(p4, "qzc%d" % i, [128, 256], BF16) for i in range(2)]
                    qzr4 = Rot([("qzc%d" % i, qz4_[i]) for i in range(2)])
                    for i in range(2):
                        S.op('pool', lambda e, i=i: e.memset(qz4_[i][:], 0.0), writes=['qzc%dz' % i, 'qzc%d' % i])
                    vs_ = [sb(p4, "vs%d" % i, [128, 4, 80], BF16) for i in range(2)]
                    vsr = Rot([("vs%d" % i, vs_[i]) for i in range(2)])
                    wt_ = [sb(p4, "wt%d" % i, [128, 4, 128], BF16) for i in range(2)]
                    wtr = Rot([("wt%d" % i, wt_[i]) for i in range(2)])
                    sm = [sb(p4, "sm%d" % i, [128, 16], F32) for i in range(2)]
                    smr = Rot([("sm%d" % i, sm[i]) for i in range(2)])
                    ho = [sb(p4, "ho%d" % i, [128, 4, 64], F32) for i in range(1)]
                    hor = Rot([("ho%d" % i, ho[i]) for i in range(1)])
                    stp = [ps(p4, "stp%d" % i, [128, 512]) for i in range(2)]
                    stpr = Rot([("stp%d" % i, stp[i]) for i in range(2)])
                    opp = [ps(p4, "opp%d" % i, [128, 4, 80]) for i in range(2)]
                    oppr = Rot([("opp%d" % i, opp[i]) for i in range(2)])
                    upp = [ps(p4, "upp%d" % i, [128, 2, 80]) for i in range(2)]
                    uppr = Rot([("upp%d" % i, upp[i]) for i in range(2)])
                    CstD, CbfD, Cbf4D = {}, {}, {}
                    for dr in range(2):
                        CstD[dr] = sb(p4, "CstD%d" % dr, [128, 2, 80], F32)
                        CbfD[dr] = sb(p4, "CbfD%d" % dr, [128, 4, 80], BF16)
                        Cbf4D[dr] = CbfD[dr][:].rearrange("p (c u) d -> p c u d", u=2)
                        S.op('dve', lambda e, dr=dr: e.memset(CstD[dr][:], 0.0), writes=['Cst%d' % dr])
                        S.op('dve', lambda e, dr=dr: e.memset(CbfD[dr][:], 0.0), writes=['Cbf%d' % dr])
                    written = set()
                    for idx in range(18):
                        for dr in (1, 0):
                            Cst, Cbf, Cbf4 = CstD[dr], CbfD[dr], Cbf4D[dr]
                            CK, BK = 'Cst%d' % dr, 'Cbf%d' % dr
                            j = ORD[dr][idx]
                            tok = slice(128 * j, 128 * j + 128)
                            emit = not (last and j >= 16)
                            vk, vs = vsr.next()
                            tt('dve', vs[:, :, 0:65], vaug[:, j, :, :], colq[dr][:, idx, 0:4].unsqueeze(2).broadcast_to([128, 4, 65]),
                               ALU.mult, ['vaug%d' % j, 'vaug1', 'colq%d' % dr], [vk])
                            if emit:
                                sk, sp_ = stpr.next()
                                for c2 in range(2):
                                    zk, qz = qzr4.next()
                                    act(qz[0:64, 0:128], QK[0:64, c2, tok], AF.Copy, ['QK', zk + 'z'], [zk])
                                    act(qz[64:128, 128:256], QK[64:128, c2, tok], AF.Copy, ['QK', zk + 'z'], [zk])
                                    mm(sp_[:, c2 * 256:(c2 + 1) * 256], QK[:, 2 + c2, tok], qz[:], True, True, ['QK', zk], [sk])
                                wk, wt = wtr.next()
                                tt('dve', wt[:], sp_[:].rearrange("p (h t) -> p h t", h=4),
                                   masks[:, dr, :].unsqueeze(1).broadcast_to([128, 4, 128]), ALU.mult, [sk, 'masks'], [wk])
                                ok_, op_ = oppr.next()
                                for h in range(4):
                                    c2, po = h // 2, (h % 2) * 64
                                    mm(op_[:, h, 0:65], wt[:, h, :], vs[:, h, 0:65], True, False, [wk, vk], [ok_])
                                    mm(op_[:, h, 0:65], QK[:, c2, tok], Cbf[:, h, 0:65], False, True,
                                       ['QK', BK], [ok_])
                                mk, m_ = smr.next()
                                eo = colq[dr][:, idx, 4:8]
                                eb = colq[dr][:, idx, 8:12]
                                tt('dve', m_[:, 0:4], op_[:, :, 64], eo, ALU.mult, [ok_, 'colq%d' % dr], [mk])
                                stt(m_[:, 4:8], m_[:, 0:4], -1.0, m_[:, 0:4], ALU.mult, ALU.max, [mk], [mk])
                                tt('dve', m_[:, 4:8], m_[:, 4:8], eb, ALU.max, [mk, 'colq%d' % dr], [mk])
                                S.op('dve', lambda e, m_=m_: e.reciprocal(out=m_[:, 8:12], in_=m_[:, 4:8]), reads=[mk], writes=[mk])
                                tt('dve', m_[:, 12:16], m_[:, 8:12], eo, ALU.mult, [mk, 'colq%d' % dr], [mk])
                                hs_j = hsum[:, j, :].rearrange("p (h d) -> p h d", h=4)
                                rcb = m_[:, 12:16].unsqueeze(2).broadcast_to([128, 4, 64])
                                if j not in written:
                                    written.add(j)
                                    tt('dve', hs_j, op_[:, :, 0:64], rcb, ALU.mult, [ok_, mk], ['hsum%d' % j])
                                else:
                                    hk2, h2 = hor.next()
                                    tt('dve', h2[:], op_[:, :, 0:64], rcb, ALU.mult, [ok_, mk], [hk2])
                                    tt('pool', hs_j, hs_j, h2[:], ALU.add, [hk2, 'hsum%d' % j], ['hsum%d' % j])
                            if idx < 17:
                                uk, up = uppr.next()
                                for pr in range(2):
                                    for hh in range(2):
                                        mm(up[hh * 64:(hh + 1) * 64, pr, 0:65], ktm[:, j, pr * 128 + hh * 64:pr * 128 + hh * 64 + 64],
                                           vs[:, 2 * pr + hh, 0:65], True, True, ['ktm%d' % j, vk], [uk])
                                tt('dve', Cst[:, :, 0:65], up[:, :, 0:65], Cst[:, :, 0:65], ALU.add, [uk, CK], [CK])
                                tt('dve', Cst[:, :, 0:65], Cst[:, :, 0:65], dcol[dr][:, :, idx].unsqueeze(2).broadcast_to([128, 2, 65]), ALU.mult,
                                   [CK, 'dcol%d' % dr], [CK])
                                cp('pool', Cbf4[0:64, :, 0, 0:65], Cst[0:64, :, 0:65], [CK], [BK])
                                cp('pool', Cbf4[64:128, :, 1, 0:65], Cst[64:128, :, 0:65], [CK], [BK])
                S.barrier()
                if CSTOP == 4:
                    return
                with contextlib.ExitStack() as p5:
                    OT = R2[:, 0:2, :]
                    wload(wv[:], 'wv', l, OFF_C + 768, 256)
                    pq = [ps(p5, "pq5_%d" % i, [128, 512]) for i in range(2)]
                    pqr = Rot([("pq5_%d" % i, pq[i]) for i in range(2)])
                    nb = 8 if last else 9
                    for b, t0, v, hk, hT in hblocks(l, nb, s):
                        for oc in range(2):
                            proj_fm(hT, hk, 256, wv, 'wv', oc * 128, pqr,
                                    lambda p_, pk, oc=oc: act(OT[:, oc, t0:t0 + 256], p_, AF.Sigmoid, [pk], ['OT%d' % oc]))
                    nj = 16 if last else 18
                    G = nj * 4
                    H3 = R3[:, 0:nj * 256].rearrange("p (g d) -> p g d", d=64)
                    SQ3 = R1[:, 0:nj * 256].rearrange("p (g d) -> p g d", d=64)
                    lst = sb(p5, "lnst", [128, 4, 72], F32)
                    HK = ['hsum%d' % j for j in range(nj)]
                    S.op('dve', lambda e: e.reduce_sum(out=lst[:, 0, 0:G], in_=H3, axis=AX.X), reads=HK, writes=['lnst'])
                    ts('dve', lst[:, 0, 0:G], lst[:, 0, 0:G], 1.0 / 64, None, ALU.mult, None, ['lnst'], ['lnst'])
                    tt('dve', H3, H3, lst[:, 0, 0:G].unsqueeze(2).broadcast_to([128, G, 64]), ALU.subtract, HK + ['lnst'], HK)
                    tt('pool', SQ3, H3, H3, ALU.mult, HK, ['SQ3'])
                    S.op('dve', lambda e: e.reduce_sum(out=lst[:, 1, 0:G], in_=SQ3, axis=AX.X), reads=['SQ3'], writes=['lnst'])
                    ts('dve', lst[:, 1, 0:G], lst[:, 1, 0:G], 1.0 / 64, EPS, ALU.mult, ALU.add, ['lnst'], ['lnst'])
                    act(lst[:, 2, 0:G], lst[:, 1, 0:G], AF.Sqrt, ['lnst'], ['lnst'])
                    S.op('dve', lambda e: e.reciprocal(out=lst[:, 3, 0:G], in_=lst[:, 2, 0:G]), reads=['lnst'], writes=['lnst'])
                    tt('dve', H3, H3, lst[:, 3, 0:G].unsqueeze(2).broadcast_to([128, G, 64]), ALU.mult, HK + ['lnst'], HK)
                    H2v = R3[:, 0:nj * 256].rearrange("p (j f) -> p j f", f=256)
                    tt('pool', H2v, H2v, gmb[:].unsqueeze(1).broadcast_to([128, nj, 256]), ALU.mult, HK + ['gmb'], HK)
                    for j in range(nj):
                        tok = slice(128 * j, 128 * j + 128)
                        pk, pt = pqr.next()
                        for kc in range(2):
                            mm(pt[:, kc * 128:(kc + 1) * 128], hsum[:, j, kc * 128:(kc + 1) * 128],
                               ident[:], True, True, ['hsum%d' % j, 'ident'], [pk])
                        tt('dve', CAT[:, 4:6, tok], pt[:, 0:256].rearrange("p (c t) -> p c t", c=2), OT[:, :, tok], ALU.mult,
                           [pk, 'OT0', 'OT1'], ['CATc'])

        def mixer_a(s, l, last):
            with contextlib.ExitStack() as ph:
                QA = sb(ph, "QA", [128, 2, T], BF16)
                KA = sb(ph, "KA", [128, 2, T], BF16)
                VA = sb(ph, "VA", [128, 18, 256], BF16)
                wq = sb(ph, "wqa", [128, 8, 768], BF16)
                wload(wq[:], 'wqa', l, OFF_A, 768)
                with contextlib.ExitStack() as p1:
                    pq = [ps(p1, "pqa%d" % i, [128, 512]) for i in range(2)]
                    pqr = Rot([("pqa%d" % i, pq[i]) for i in range(2)])
                    for b, t0, v, hk, hT in hblocks(l, 9, s):
                        for oc in range(4):
                            if oc < 2 and last and b == 8:
                                continue
                            dst = QA if oc < 2 else KA
                            proj_fm(hT, hk, 256, wq, 'wqa', oc * 128, pqr,
                                    lambda p_, pk, oc=oc, dst=dst: act(dst[:, oc % 2, t0:t0 + 256], p_, AF.Copy, [pk], ['QKA']))
                        for sub in range(2):
                            j = 2 * b + sub
                            proj_tm(hT, hk, sub, wq, 'wqa', 512, 256, pqr,
                                    lambda p_, pk, j=j: cp('dve', VA[:, j, :], p_, [pk], ['VA']))
                S.barrier()
                with contextlib.ExitStack() as p2:
                    bt = [sb(p2, "bt%d" % i, [128, 1280], F32) for i in range(2)]
                    btr = Rot([("bt%d" % i, bt[i]) for i in range(2)])
                    tf = [sb(p2, "tf%d" % i, [128, 1280], F32) for i in range(2)]
                    tfr = Rot([("tf%d" % i, tf[i]) for i in range(2)])
                    PT = [sb(p2, "PT%d" % i, [128, 7, 256], BF16) for i in range(2)]
                    ptr_ = Rot([("PT%d" % i, PT[i]) for i in range(2)])
                    rc_ = [sb(p2, "rca%d" % i, [128, 256], F32) for i in range(2)]
                    rcr = Rot([("rca%d" % i, rc_[i]) for i in range(2)])
                    sps = ps(p2, "sps", [128, 8, 256])
                    ov = [ps(p2, "ov%d" % i, [128, 512]) for i in range(2)]
                    ovr = Rot([("ov%d" % i, ov[i]) for i in range(2)])
                    qz_ = [sb(p2, "qz%d" % i, [128, 256], BF16) for i in range(2)]
                    qzr = Rot([("qz%d" % i, qz_[i]) for i in range(2)])
                    for i in range(2):
                        S.op('pool', lambda e, i=i: e.memset(qz_[i][:], 0.0), writes=['qz%dz' % i, 'qz%d' % i])
                    nq = 16 if last else 18
                    qts = list(range(nq))
                    if ASTOP == 1:
                        qts = []
                    if ASTOP == 2:
                        qts = [16, 17]
                    if ASTOP == 3:
                        qts = [5]
                    units = [(qt, pr) for qt in qts for pr in range(2)]

                    def stage1(qt, pr):
                        ctxq = qt >= 16
                        tq = slice(128 * qt, 128 * qt + 128)
                        if ctxq:
                            kch = [16, 17]
                            bk = bias_t = None
                        else:
                            cb = min(max(qt - 2, 0), 11)
                            kch = list(range(cb, cb + 5)) + [16, 17]
                            typ = {0: 0, 1: 1, 14: 3, 15: 4}.get(qt, 2)
                            bk, bias_t = btr.next()
                            dma('sp', bias_t[:], natb[l, typ, pr], writes=[bk])
                        zk, qz = qzr.next()
                        cp('pool', qz[0:64, 0:128], QA[0:64, pr, tq], ['QKA', zk + 'z'], [zk])
                        cp('pool', qz[64:128, 128:256], QA[64:128, pr, tq], ['QKA', zk + 'z'], [zk])
                        for i, kc_ in enumerate(kch):
                            mm(sps[:, i, :], KA[:, pr, 128 * kc_:128 * kc_ + 128], qz[:], True, True, ['QKA', zk], ['sps%d' % (i // 2)])
                        return (qt, pr, ctxq, tq, kch, bk, bias_t)

                    def stage2(st_):
                        qt, pr, ctxq, tq, kch, bk, bias_t = st_
                        nk = len(kch)
                        pk_, P_ = ptr_.next()
                        if not ctxq:
                            fk, tfl = tfr.next()
                            for (a0, a1) in ((0, 2), (2, 4), (4, 5)):
                                stt(tfl[:, a0 * 256:a1 * 256], sps[:, a0:a1, :].rearrange("p a b -> p (a b)"), 0.125,
                                    bias_t[:, a0 * 256:a1 * 256], ALU.mult, ALU.add, ['sps%d' % (a0 // 2), bk], [fk])
                            for a0 in (5, 6):
                                act(P_[:, a0, :], sps[:, a0, :], AF.Exp, ['sps%d' % (a0 // 2)], [pk_ + 'c'], scale=0.125)
                            act(P_[:, 0:5, :].rearrange("p a b -> p (a b)"), tfl[:], AF.Exp, [fk], [pk_])
                        else:
                            act(P_[:, 0:2, :].rearrange("p a b -> p (a b)"), sps[:, 0:2, :].rearrange("p a b -> p (a b)"),
                                AF.Exp, ['sps0'], [pk_, pk_ + 'c'], scale=0.125)
                        return (qt, pr, tq, kch, pk_, P_)

                    def stage3(st_):
                        qt, pr, tq, kch, pk_, P_ = st_
                        nk = len(kch)
                        ok_, o_ = ovr.next()
                        for i, kc_ in enumerate(kch):
                            mm(o_[:, 0:256], VA[:, kc_, pr * 128:(pr + 1) * 128], P_[:, i, :], i == 0, i == nk - 1,
                               ['VA', pk_, pk_ + 'c'], [ok_])
                        for i, kc_ in enumerate(kch):
                            mm(o_[:, 256:512], onesb[:], P_[:, i, :], i == 0, i == nk - 1, ['onesb', pk_, pk_ + 'c'], [ok_])
                        rk_, r_ = rcr.next()
                        S.op('dve', lambda e, r_=r_, o_=o_: e.reciprocal(out=r_[:], in_=o_[:, 256:512]), reads=[ok_], writes=[rk_])
                        for hh in range(2):
                            po = hh * 64
                            tt('dve', CAT[po:po + 64, pr, tq], o_[po:po + 64, hh * 128:(hh + 1) * 128],
                               r_[po:po + 64, hh * 128:(hh + 1) * 128], ALU.mult, [ok_, rk_], ['CATa'])

                    if units:
                        s1 = stage1(*units[0])
                        for ui in range(len(units)):
                            s2 = stage2(s1)
                            if ui + 1 < len(units):
                                s1 = stage1(*units[ui + 1])
                            stage3(s2)

        def mixer_b(s, l, last):
            with contextlib.ExitStack() as ph:
                UB = sb(ph, "UB", [128, 2, T], BF16)
                ZB = sb(ph, "ZB", [128, 18, 256], BF16)
                wb_ = sb(ph, "wbb", [128, 8, 512], BF16)
                wsb = sb(ph, "wsb", [128, 4, 128], BF16)
                bsb = sb(ph, "bsb", [128, 2, 128], F32)
                ggb = sb(ph, "ggb", [128, 256], F32)
                wload(wb_[:], 'wbb', l, OFF_B, 512)
                dma('pool', wsb[:], wsT[l], writes=['wsb'])
                dma('sp', bsb[:], bsT[l], writes=['bsb'])
                dma('sp', ggb[:], ggm[l, :].partition_broadcast(128), writes=['ggb'])
                pq = [ps(ph, "pqb%d" % i, [128, 512]) for i in range(2)]
                pqr = Rot([("pqb%d" % i, pq[i]) for i in range(2)])
                zf = [sb(ph, "zf%d" % i, [128, 256], F32) for i in range(2)]
                zfr = Rot([("zf%d" % i, zf[i]) for i in range(2)])
                zq = [sb(ph, "zq%d" % i, [128, 256], F32) for i in range(2)]
                zqr = Rot([("zq%d" % i, zq[i]) for i in range(2)])
                ZF = sb(ph, "ZF", [128, 18, 256], F32)
                zss = sb(ph, "zss", [128, 32], F32)
                zst = sb(ph, "zst", [128, 64], F32)
                nb = 8 if last else 9
                for b, t0, v, hk, hT in hblocks(l, nb, s):
                    for oc in range(2):
                        proj_fm(hT, hk, 256, wb_, 'wbb', oc * 128, pqr,
                                lambda p_, pk, oc=oc: act(UB[:, oc, t0:t0 + 256], p_, AF.Gelu_apprx_tanh, [pk], ['UB']))
                    for sub in range(2):
                        j = 2 * b + sub

                        def ev(p_, pk, j=j):
                            act(ZF[:, j, :], p_, AF.Gelu_apprx_tanh, [pk], ['ZF%d' % j])
                            qk_, q_ = zqr.next()
                            S.op('act', lambda e, q_=q_, j=j: e.activation(out=q_[:], in_=ZF[:, j, :], func=AF.Square,
                                                                      accum_out=zss[:, j:j + 1]),
                                 reads=['ZF%d' % j], writes=[qk_, 'zss%d' % j])
                        proj_tm(hT, hk, sub, wb_, 'wbb', 256, 256, pqr, ev)
                njb = 16 if last else 18
                ZSK = ['zss%d' % j for j in range(njb)]
                ts('dve', zss[:, 0:njb], zss[:, 0:njb], 1.0 / 256, EPS, ALU.mult, ALU.add, ZSK, ['zst'])
                act(zst[:, 0:njb], zss[:, 0:njb], AF.Sqrt, ['zst'], ['zst2'])
                S.op('dve', lambda e: e.reciprocal(out=zst[:, 32:32 + njb], in_=zst[:, 0:njb]), reads=['zst2'], writes=['zst3'])
                for j in range(njb):
                    stt(ZB[:, j, :], ZF[:, j, :], zst[:, 32 + j:33 + j], ggb[:], ALU.mult, ALU.mult, ['ZF%d' % j, 'zst3', 'ggb'], ['ZB%d' % j])
                mt = [sb(ph, "mt%d" % i, [128, 128], F32) for i in range(2)]
                mtr = Rot([("mt%d" % i, mt[i]) for i in range(2)])
                for j in range(16 if last else 18):
                    tok = slice(128 * j, 128 * j + 128)
                    for pr in range(2):
                        pk, pt = pqr.next()
                        for hh in range(2):
                            po = hh * 64
                            mm(pt[po:po + 64, 0:128], ZB[:, j, pr * 128 + po:pr * 128 + po + 64], wsb[:, 2 * pr + hh, :],
                               True, True, ['ZB%d' % j, 'wsb'], [pk])
                        mk, m_ = mtr.next()
                        tt('dve', m_[:], pt[:, 0:128], bsb[:, pr, :], ALU.add, [pk, 'bsb'], [mk])
                        tt('pool', CAT[:, 2 + pr, tok], m_[:], UB[:, pr, tok], ALU.mult, [mk, 'UB'], ['CATb'])

        def mixer_d(s, l, last):
            with contextlib.ExitStack() as ph:
                FT = sb(ph, "FT", [128, 18, 256], BF16)
                wfi = sb(ph, "wfi", [128, 8, 256], BF16)
                wfn = sb(ph, "wfn", [128, 2, 256], BF16)
                wload(wfi[:], 'wfi', l, OFF_D, 256)
                dma('pool', wfn[:], w_fnet[l].rearrange("(kc p) n -> p kc n", p=128), writes=['wfn'])
                pq = [ps(ph, "pqd%d" % i, [128, 512]) for i in range(2)]
                pqr = Rot([("pqd%d" % i, pq[i]) for i in range(2)])
                nb = 8 if last else 9
                for b, t0, v, hk, hT in hblocks(l, nb, s):
                    for sub in range(2):
                        j = 2 * b + sub
                        proj_tm(hT, hk, sub, wfi, 'wfi', 0, 256, pqr,
                                lambda p_, pk, j=j: cp('dve', FT[:, j, :], p_, [pk], ['FT']))
                dc_ = [sb(ph, "dc%d" % i, [128, 16, 256], BF16) for i in range(2)]
                ds_ = [sb(ph, "ds%d" % i, [128, 16, 256], BF16) for i in range(2)]
                YC = [sb(ph, "YC%d" % i, [128, 2, 256], BF16) for i in range(2)]
                YS = [sb(ph, "YS%d" % i, [128, 2, 256], BF16) for i in range(2)]
                SPc = [sb(ph, "SP%d" % i, [128, 2, 256], BF16) for i in range(2)]
                ycp = ps(ph, "ycp", [128, 512])
                ysp = ps(ph, "ysp", [128, 512])
                spp = ps(ph, "spp", [128, 512])
                dpp = ps(ph, "dpp", [128, 512])
                it = 0
                segs = [(0, 16, TL, c_dftc_l, c_dfts_l)]
                if not last:
                    segs.append((16, 2, 256, c_dftc_c, c_dfts_c))
                for (jb, nchk, nT, mc, ms) in segs:
                    scale = float(1.0 / np.sqrt(64.0 * nT))
                    for tb in range(nT // 256):
                        bb = it % 2
                        it += 1
                        dma('sp', dc_[bb][:, 0:nchk, :], mc[tb], writes=['dc%d' % bb])
                        dma('sp', ds_[bb][:, 0:nchk, :], ms[tb], writes=['ds%d' % bb])
                        for fc in range(2):
                            for i in range(nchk):
                                mm(ycp[:, 0:256], FT[:, jb + i, fc * 128:(fc + 1) * 128], dc_[bb][:, i, :], i == 0, i == nchk - 1,
                                   ['FT', 'dc%d' % bb], ['ycp'])
                            for i in range(nchk):
                                mm(ysp[:, 0:256], FT[:, jb + i, fc * 128:(fc + 1) * 128], ds_[bb][:, i, :], i == 0, i == nchk - 1,
                                   ['FT', 'ds%d' % bb], ['ysp'])
                            act(YC[bb][:, fc, :], ycp[:, 0:256], AF.Copy, ['ycp'], ['YC%d' % bb])
                            cp('dve', YS[bb][:, fc, :], ysp[:, 0:256], ['ysp'], ['YS%d' % bb])
                            mm(spp[:, 0:256], blk[:, 0, :], YC[bb][:, fc, :], True, False, ['blk', 'YC%d' % bb], ['spp'])
                            mm(spp[:, 0:256], blk[:, 1, :], YS[bb][:, fc, :], False, True, ['blk', 'YS%d' % bb], ['spp'])
                            act(SPc[bb][:, fc, :], spp[:, 0:256], AF.Copy, ['spp'], ['SP%d' % bb], scale=scale)
                        for oc in range(2):
                            for fc in range(2):
                                mm(dpp[:, 0:256], wfn[:, fc, oc * 128:(oc + 1) * 128], SPc[bb][:, fc, :], fc == 0, fc == 1,
                                   ['wfn', 'SP%d' % bb], ['dpp'])
                            t0 = 128 * jb + tb * 256
                            cp('dve', CAT[:, 6 + oc, t0:t0 + 256], dpp[:, 0:256], ['dpp'], ['CATd'])

        class _Stop(Exception):
            pass

        def body():
          if dbg:
              S.op('pool', lambda e: e.memset(CAT[:], 0.0), writes=['CAT'])
              S.barrier()
          for s in range(2):
            for c in range(8):
                dma('sp', X[:, c, :], xin[s, :, c, :], writes=['X%d' % c])
            for l in range(DEPTH):
                last = (l == DEPTH - 1)
                compute_rs()
                S.barrier()
                if 'C' not in SKIP:
                    mixer_c(s, l, last)
                    S.barrier()
                if 'A' not in SKIP:
                    mixer_a(s, l, last)
                    S.barrier()
                if 'B' not in SKIP:
                    mixer_b(s, l, last)
                    S.barrier()
                if 'D' not in SKIP:
                    mixer_d(s, l, last)
                    S.barrier()
                if dbg == ('cat', s, l):
                    for c in range(8):
                        dma('pool', dbg_out[:, c, :], CAT[:, c, :], reads=['CAT'])
                    return
                S.barrier()
                for _once in ([] if 'wout' in SKIP else [0]):
                  with contextlib.ExitStack() as ph:
                    wo = sb(ph, "wo", [128, 8, D], BF16)
                    yps = [ps(ph, "yps%d" % i, [128, 512]) for i in range(2)]
                    for kc in range(8):
                        dma('pool', wo[:, kc, :], w_out[l, kc * 128:(kc + 1) * 128, :], writes=['wo'])
                    it = 0
                    nblk = 4 if last else 5
                    for b in range(nblk):
                        t0 = b * 512
                        n = 512 if b < 4 else 256
                        v = s if b < 4 else 2
                        for oc in range(8):
                            pb = it % 2
                            it += 1
                            for kc in range(8):
                                mm(yps[pb][:, :n], wo[:, kc, oc * 128:(oc + 1) * 128], CAT[:, kc, t0:t0 + n],
                                   kc == 0, kc == 7, ['wo', 'CAT'], ['yps%d' % pb])
                            stt(X[:, oc, t0:t0 + n], yps[pb][:, :n], modap(l, 2, oc, v), X[:, oc, t0:t0 + n],
                                ALU.mult, ALU.add, ['yps%d' % pb, 'modT', 'X%d' % oc], ['X%d' % oc])
                S.barrier()
                for _once in ([] if 'ffn' in SKIP else [0]):
                  with contextlib.ExitStack() as ph:
                    nblk = 4 if last else 5
                    H2 = CAT
                    for b2 in range(8 if last else 9):
                        t0 = b2 * 256
                        v = s if b2 < 8 else 2
                        make_h(l, 1, t0, 256, v, dst=('H2_%d' % (b2 // 2), H2[:, :, t0:t0 + 256]))
                    w1 = [sb(ph, "w1_%d" % i, [128, 8, 512], BF16) for i in range(2)]
                    w2 = [sb(ph, "w2_%d" % i, [128, 4, D], BF16) for i in range(2)]
                    ag = [sb(ph, "ag%d" % i, [128, 4, 512], BF16) for i in range(2)]
                    rl = [sb(ph, "rl%d" % i, [128, 512], F32) for i in range(2)]
                    fps = [ps(ph, "fps%d" % i, [128, 512]) for i in range(2)]
                    ops_ = [ps(ph, "ops%d" % i, [128, 512]) for i in range(2)]
                    cnt = {'i1': 0, 'i2': 0, 'i3': 0}

                    def ffn1(g, b):
                        wb = g % 2
                        gl = None
                        if g == 0 and b == 0:
                            gl = 0
                        if b == 1 and g + 1 < 8:
                            gl = g + 1
                        if gl is not None:
                            wl = gl % 2
                            dma('pool', w1[wl][:], w_ff1[l, :, gl * 512:(gl + 1) * 512].rearrange("(kc p) n -> p kc n", p=128),
                                writes=['w1_%d' % wl])
                            dma('pool', w2[wl][:], w_ff2[l, gl * 512:(gl + 1) * 512, :].rearrange("(kc p) n -> p kc n", p=128),
                                writes=['w2_%d' % wl])
                        t0 = b * 512
                        n = 512 if b < 4 else 256
                        ab = cnt['i1'] % 2
                        cnt['i1'] += 1
                        for fc in range(4):
                            pb = cnt['i2'] % 2
                            cnt['i2'] += 1
                            for kc in range(8):
                                mm(fps[pb][:, :n], w1[wb][:, kc, fc * 128:(fc + 1) * 128], H2[:, kc, t0:t0 + n],
                                   kc == 0, kc == 7, ['w1_%d' % wb, 'H2_%d_%d' % (b, kc)], ['fps%d' % pb])
                            act(rl[pb][:, :n], fps[pb][:, :n], AF.Relu, ['fps%d' % pb], ['rl%d' % pb])
                            tt('pool', ag[ab][:, fc, :n], rl[pb][:, :n], rl[pb][:, :n], ALU.mult,
                               ['rl%d' % pb], ['ag%d_%d' % (ab, fc)])
                        return ab

                    def ffn2(g, b, ab):
                        wb = g % 2
                        t0 = b * 512
                        n = 512 if b < 4 else 256
                        v = s if b < 4 else 2
                        for oc in range(8):
                            pb = cnt['i3'] % 2
                            cnt['i3'] += 1
                            for kc in range(4):
                                mm(ops_[pb][:, :n], w2[wb][:, kc, oc * 128:(oc + 1) * 128], ag[ab][:, kc, :n],
                                   kc == 0, kc == 3, ['w2_%d' % wb, 'ag%d_%d' % (ab, kc)], ['ops%d' % pb])
                            stt(X[:, oc, t0:t0 + n], ops_[pb][:, :n], modap(l, 5, oc, v), X[:, oc, t0:t0 + n],
                                ALU.mult, ALU.add, ['ops%d' % pb, 'modT', 'X%d' % oc], ['X%d' % oc])

                    funits = [(g, b) for g in range(8) for b in range(nblk)]
                    ab_cur = ffn1(*funits[0])
                    for ui in range(len(funits)):
                        ab_nxt = ffn1(*funits[ui + 1]) if ui + 1 < len(funits) else None
                        ffn2(funits[ui][0], funits[ui][1], ab_cur)
                        ab_cur = ab_nxt
                S.barrier()
                if dbg == ('x', s, l):
                    for c in range(8):
                        dma('sp', dbg_out[:, c, :], X[:, c, :], reads=['X%d' % c])
                    return
            with contextlib.ExitStack() as ph:
                ob = [sb(ph, "ob%d" % i, [128, 256], F32) for i in range(2)]
                it = 0
                for b in range(8):
                    t0 = b * 256
                    pk, pst = ssqrot.next()
                    for c in range(8):
                        sk, sq = sqrot.next()
                        act(sq[:], X[:, c, t0:t0 + 256], AF.Square, ['X%d' % c], [sk])
                        mm(pst[:, 0:256], onesb[:], sq[:], c == 0, c == 7, [sk, 'onesb'], [pk])
                    rk, rs = rsrot.next()
                    act(rs[:], pst[:, 0:256], AF.Sqrt, [pk, 'epsb'], [rk], scale=1.0 / D, bias=epsb[:, 0:1])
                    S.op('dve', lambda e, rs=rs: e.reciprocal(out=rs[:], in_=rs[:]), reads=[rk], writes=[rk])
                    for c in range(8):
                        o = it % 2
                        it += 1
                        stt(ob[o][:], X[:, c, t0:t0 + 256], gv[:, 4, c:c + 1], rs[:], ALU.mult, ALU.mult,
                            ['X%d' % c, 'gv', rk], ['ob%d' % o])
                        dma('sp', outT[s, :, c, t0:t0 + 256], ob[o][:], reads=['ob%d' % o])
            S.barrier()
        body()
        S.finish('sp')
        print("ops", S.nops, "waits", S.nwaits, {e: S.cc[e] for e in S.cc}, {e: S.dc[e] for e in S.dc})
    return nc


_NC_CACHE = {}


def _prep_shared(inp):
    f32 = lambda a: np.ascontiguousarray(np.asarray(a, dtype=np.float32))
    sh = dict(_consts())
    sh['w_ada'] = f32(inp['w_ada'])
    sh['badaT'] = np.ascontiguousarray(np.moveaxis(f32(inp['b_ada']).reshape(2, 48, 128), 2, 0))
    gl = [inp['g_norm_mix'][0], inp['g_norm_ffn'][0], inp['g_norm_mix'][1], inp['g_norm_ffn'][1], inp['g_final']]
    sh['gvec'] = np.ascontiguousarray(np.stack([f32(g).reshape(8, 128).T for g in gl], 1))
    sh['w_in'] = f32(inp['w_in'])
    bgate = f32(inp['b_gate'])
    sh['bg'] = np.ascontiguousarray(bgate.reshape(2, 4, 4).transpose(2, 0, 1))
    wc = f32(inp['w_conv_qk'])
    sh['wconv'] = np.ascontiguousarray(wc.reshape(2, 3, 4, 128).transpose(3, 0, 2, 1))
    rpb = f32(inp['rpb'])
    sh['natb'] = np.ascontiguousarray(np.stack([_nat_bias(rpb[l]).reshape(5, 2, 128, 1280) for l in range(2)], 0))
    sh['wsT'] = np.ascontiguousarray(f32(inp['w_spatial']).transpose(0, 3, 1, 2))
    bs = f32(inp['b_spatial'])
    bsT = np.zeros((2, 128, 2, 128), np.float32)
    for pr in range(2):
        for hh in range(2):
            bsT[:, hh * 64:(hh + 1) * 64, pr, :] = bs[:, 2 * pr + hh, None, :]
    sh['bsT'] = bsT
    sh['ggm'] = f32(inp['g_gmlp'])
    sh['gml'] = f32(inp['g_mlstm'])
    sh['w_fnet'] = f32(inp['w_fnet'])
    sh['w_out'] = f32(inp['w_out'])
    sh['w_ff1'] = f32(inp['w_ff1'])
    sh['w_ff2'] = f32(inp['w_ff2'])
    return sh


def _fmT(a):
    return np.ascontiguousarray(a.T.reshape(8, 128, a.shape[0]).transpose(1, 0, 2))


def kernel(dbg=None, **inp):
    x = np.asarray(inp['x'], np.float32)
    c = np.asarray(inp['c'], np.float32)
    ctx = np.asarray(inp['ctx'], np.float32)
    c_ctx = np.asarray(inp['c_ctx'], np.float32)
    sh = _prep_shared(inp)
    key = repr(dbg)
    if key not in _NC_CACHE:
        _NC_CACHE[key] = build_nc(dbg)
    nc = _NC_CACHE[key]
    in_maps = []
    ncores = int(os.environ.get('MK_CORES', '8'))
    for core in range(ncores):
        m = dict(sh)
        xs = []
        for i in range(2):
            b = 2 * core + i
            xs.append(_fmT(np.concatenate([x[b], ctx[b]], 0)))
        m['xin'] = np.ascontiguousarray(np.stack(xs, 0))
        vecs = [c[2 * core], c[2 * core + 1], c_ctx]
        m['cT'] = np.ascontiguousarray(np.stack([v.reshape(8, 128).T for v in vecs], 2))
        in_maps.append(m)
    res = run_bass_kernel_spmd(nc, in_maps, core_ids=list(range(ncores)))
    out = np.zeros((16, TL, D), np.float32)
    for core in range(ncores):
        o = res.results[core]['outT']
        for i in range(2):
            out[2 * core + i] = o[i].transpose(2, 1, 0).reshape(TL, D)
    if dbg:
        return out, [res.results[core]['dbg'] for core in range(ncores)]
    return out
```
